# Optimizing a Trainium2 kernel written in Bass

```python
import math
import jax
import jax.numpy as jnp
from jax import lax
import numpy as np

D_MODEL = 1024
BATCH = 2
SEQ = 16384
DEPTH = 2

CHUNK = 64
HEAD_DIM = 64
N_BRANCH = 4
BRANCH_WIDTH = D_MODEL // N_BRANCH
CONV_WIDTH = 3
SC_WIDTH = BRANCH_WIDTH
SB_HEADS = BRANCH_WIDTH // HEAD_DIM
SB_WIDTH = SB_HEADS * HEAD_DIM
SB_BLOCK = 128
RW_HEADS = BRANCH_WIDTH // HEAD_DIM
RW_WIDTH = RW_HEADS * HEAD_DIM
RW_DECAY_LORA = 64
RW_A_LORA = 64
RW_GATE_LORA = 128
RW_GN_EPS = 64e-5
RW_DECAY_SCALE = math.exp(-0.5)
SG_GROUPS = 4
SG_WIDTH = BRANCH_WIDTH
SG_GROUP_WIDTH = SG_WIDTH // SG_GROUPS
SG_CHUNK = 128
D_FF = ((8 * D_MODEL // 3 + 127) // 128) * 128
RMS_EPS = 1e-6
LN_EPS = 1e-5

SC_OFF = 0
SB_OFF = SC_OFF + 3 * SC_WIDTH
RW_OFF = SB_OFF + 3 * SB_WIDTH
RW_COLS = 3 * RW_WIDTH + RW_DECAY_LORA + RW_A_LORA + RW_GATE_LORA
SG_OFF = RW_OFF + RW_COLS
GATE_OFF = SG_OFF + 2 * SG_WIDTH
IN_COLS = GATE_OFF + N_BRANCH * D_MODEL

kernel_name = "hybrid_gated_parallel_streaming_encoder"


def rms_norm(x, g):
    xf = x.astype(jnp.float32)
    y = xf * lax.rsqrt(jnp.mean(xf * xf, axis=-1, keepdims=True) + RMS_EPS)
    return (y * g.astype(jnp.float32)).astype(x.dtype)


def layer_norm(x, g, b):
    xf = x.astype(jnp.float32)
    mu = jnp.mean(xf, axis=-1, keepdims=True)
    var = jnp.mean(jnp.square(xf - mu), axis=-1, keepdims=True)
    y = (xf - mu) * lax.rsqrt(var + LN_EPS) * g.astype(jnp.float32) + b.astype(jnp.float32)
    return y.astype(x.dtype)


def causal_dwconv(x, w):
    k = w.shape[0]
    t = x.shape[1]
    xp = jnp.pad(x, ((0, 0), (k - 1, 0), (0, 0)))
    out = xp[:, 0:t] * w[0]
    for i in range(1, k):
        out = out + xp[:, i:i + t] * w[i]
    return out


def token_shift(x):
    return jnp.pad(x, ((0, 0), (1, 0), (0, 0)))[:, :-1]


def short_conv_mixer(z, conv_w):
    b_gate, c_gate, xin = jnp.split(z, 3, axis=-1)
    return b_gate * causal_dwconv(c_gate * xin, conv_w)


def stick_breaking_attention(q, k, v):
    bsz, t, h, dh = q.shape
    nb = t // SB_BLOCK
    scale = 1.0 / math.sqrt(dh)
    qb = q.reshape(bsz, nb, SB_BLOCK, h, dh).transpose(1, 0, 3, 2, 4)
    kh = k.transpose(0, 2, 1, 3)
    vh = v.transpose(0, 2, 1, 3)
    kpos = jnp.arange(t)

    def block(args):
        qi, bi = args
        z = jnp.einsum('bhqd,bhkd->bhqk', qi, kh).astype(jnp.float32) * scale
        qpos = bi * SB_BLOCK + jnp.arange(SB_BLOCK)
        mask = kpos[None, :] < qpos[:, None]
        log_beta = jax.nn.log_sigmoid(z)
        log_keep = jnp.where(mask, log_beta - z, 0.0)
        suffix = lax.cumsum(log_keep, axis=3, reverse=True) - log_keep
        attn = jnp.where(mask, jnp.exp(log_beta + suffix), 0.0)
        return jnp.einsum('bhqk,bhkd->bhqd', attn.astype(vh.dtype), vh)

    out = lax.map(block, (qb, jnp.arange(nb)))
    return out.transpose(1, 0, 3, 2, 4).reshape(bsz, t, h * dh)


def rwkv7_scan(r, w, k, v, a_vec, b_vec):
    bsz, t, h, n = r.shape

    def step(s, inp):
        r_t, w_t, k_t, v_t, a_t, b_t = inp
        sa = jnp.einsum('bhvk,bhk->bhv', s, a_t)
        s = s * w_t[:, :, None, :] + sa[..., None] * b_t[:, :, None, :] + v_t[..., None] * k_t[:, :, None, :]
        return s, jnp.einsum('bhvk,bhk->bhv', s, r_t)

    xs = tuple(u.astype(jnp.float32).transpose(1, 0, 2, 3) for u in (r, w, k, v, a_vec, b_vec))
    s0 = jnp.zeros((bsz, h, n, n), jnp.float32)
    _, y = lax.scan(step, s0, xs)
    return y.transpose(1, 0, 2, 3)


def rwkv7_mixer(z, mu, w0, w2, a0, a2, g2, k_k, k_a, r_k, gn_g, gn_b):
    bsz, t, _ = z.shape
    z = z + (token_shift(z) - z) * mu
    c0, c1, c2 = RW_WIDTH, 2 * RW_WIDTH, 3 * RW_WIDTH
    c3, c4 = c2 + RW_DECAY_LORA, c2 + RW_DECAY_LORA + RW_A_LORA
    r, k, v = z[..., :c0], z[..., c0:c1], z[..., c1:c2]
    xw, xa, xg = z[..., c2:c3], z[..., c3:c4], z[..., c4:]
    decay = jnp.exp(-RW_DECAY_SCALE * jax.nn.sigmoid((w0 + jnp.tanh(xw) @ w2).astype(jnp.float32)))
    a = jax.nn.sigmoid(a0 + xa @ a2)
    g = jax.nn.sigmoid(xg) @ g2
    hs = (bsz, t, RW_HEADS, HEAD_DIM)
    kk = (k * k_k).reshape(hs).astype(jnp.float32)
    kk = kk / jnp.maximum(jnp.sqrt(jnp.sum(kk * kk, axis=-1, keepdims=True)), 1e-12)
    k = k * (1.0 + (a - 1.0) * k_a)
    rh, kh, vh, ah = r.reshape(hs), k.reshape(hs), v.reshape(hs), a.reshape(hs)
    y = rwkv7_scan(rh, decay.reshape(hs), kh, vh, -kk, kk * ah.astype(jnp.float32))
    mean = jnp.mean(y, axis=-1, keepdims=True)
    var = jnp.mean(jnp.square(y - mean), axis=-1, keepdims=True)
    y = ((y - mean) * lax.rsqrt(var + RW_GN_EPS)).reshape(bsz, t, RW_WIDTH)
    y = y * gn_g.astype(jnp.float32) + gn_b.astype(jnp.float32)
    bonus = jnp.sum((rh * kh * r_k).astype(jnp.float32), axis=-1, keepdims=True) * vh.astype(jnp.float32)
    y = (y + bonus.reshape(bsz, t, RW_WIDTH)).astype(z.dtype)
    return y * g


def spatial_gating(z, ln_g, ln_b, w_s, b_s):
    bsz, t, _ = z.shape
    u, v = jnp.split(jax.nn.gelu(z), 2, axis=-1)
    v = layer_norm(v, ln_g, ln_b)
    vb = v.reshape(bsz, t // SG_CHUNK, SG_CHUNK, SG_GROUPS, SG_GROUP_WIDTH)
    w_causal = jnp.tril(w_s)
    mixed = jnp.einsum('gts,bnsgc->bntgc', w_causal, vb) + b_s.T[:, :, None]
    return u * mixed.reshape(bsz, t, SG_WIDTH)


def hybrid_mixer(h, w_in, gate_b, sc_conv_w, sc_out, sb_out, rw_mu, rw_w0, rw_w2, rw_a0, rw_a2,
                 rw_g2, rw_k_k, rw_k_a, rw_r_k, rw_gn_g, rw_gn_b, rw_out, sg_ln_g, sg_ln_b,
                 sg_w, sg_b, sg_out, w_o):
    bsz, t, _ = h.shape
    proj = h @ w_in
    y_a = short_conv_mixer(proj[..., SC_OFF:SB_OFF], sc_conv_w)
    q, k, v = jnp.split(proj[..., SB_OFF:RW_OFF], 3, axis=-1)
    hs = (bsz, t, SB_HEADS, HEAD_DIM)
    y_b = stick_breaking_attention(q.reshape(hs), k.reshape(hs), v.reshape(hs))
    y_c = rwkv7_mixer(proj[..., RW_OFF:SG_OFF], rw_mu, rw_w0, rw_w2, rw_a0, rw_a2, rw_g2,
                      rw_k_k, rw_k_a, rw_r_k, rw_gn_g, rw_gn_b)
    y_d = spatial_gating(proj[..., SG_OFF:GATE_OFF], sg_ln_g, sg_ln_b, sg_w, sg_b)
    gates = jax.nn.sigmoid((proj[..., GATE_OFF:] + gate_b).astype(jnp.float32)).astype(h.dtype)
    gates = gates.reshape(bsz, t, N_BRANCH, D_MODEL)
    merged = (gates[:, :, 0] * (y_a @ sc_out) + gates[:, :, 1] * (y_b @ sb_out)
              + gates[:, :, 2] * (y_c @ rw_out) + gates[:, :, 3] * (y_d @ sg_out))
    return merged @ w_o


def conv_glu_ffn(h, w_up, conv_w, w_down):
    up = causal_dwconv(h @ w_up, conv_w)
    gate, val = jnp.split(up, 2, axis=-1)
    return (jax.nn.silu(gate) * val) @ w_down


def setup_inputs(seed: int = 0) -> dict:
    key = jax.random.key(seed)
    keys = iter(jax.random.split(key, 40))
    f32 = jnp.float32
    L = DEPTH

    def nrm(shape, scale):
        return jax.random.normal(next(keys), shape, f32) * scale

    return {
        "x": nrm((BATCH, SEQ, D_MODEL), 1.0),
        "mix_norm_g": 1.0 + nrm((L, D_MODEL), 0.05),
        "w_in": nrm((L, D_MODEL, IN_COLS), D_MODEL ** -0.5),
        "gate_b": nrm((L, N_BRANCH * D_MODEL), 0.1),
        "sc_conv_w": nrm((L, CONV_WIDTH, SC_WIDTH), CONV_WIDTH ** -0.5),
        "sc_out": nrm((L, SC_WIDTH, D_MODEL), SC_WIDTH ** -0.5),
        "sb_out": nrm((L, SB_WIDTH, D_MODEL), SB_WIDTH ** -0.5),
        "rw_mu": jax.random.uniform(next(keys), (L, RW_COLS), f32),
        "rw_w0": nrm((L, RW_WIDTH), 0.5),
        "rw_w2": nrm((L, RW_DECAY_LORA, RW_WIDTH), 0.1),
        "rw_a0": nrm((L, RW_WIDTH), 0.1),
        "rw_a2": nrm((L, RW_A_LORA, RW_WIDTH), 0.1),
        "rw_g2": nrm((L, RW_GATE_LORA, RW_WIDTH), RW_GATE_LORA ** -0.5),
        "rw_k_k": 0.85 + nrm((L, RW_WIDTH), 0.05),
        "rw_k_a": 1.0 + nrm((L, RW_WIDTH), 0.05),
        "rw_r_k": nrm((L, RW_HEADS, HEAD_DIM), 0.1),
        "rw_gn_g": 1.0 + nrm((L, RW_WIDTH), 0.05),
        "rw_gn_b": nrm((L, RW_WIDTH), 0.01),
        "rw_out": nrm((L, RW_WIDTH, D_MODEL), RW_WIDTH ** -0.5),
        "sg_ln_g": 1.0 + nrm((L, SG_WIDTH), 0.05),
        "sg_ln_b": nrm((L, SG_WIDTH), 0.01),
        "sg_w": nrm((L, SG_GROUPS, SG_CHUNK, SG_CHUNK), 0.05),
        "sg_b": 1.0 + nrm((L, SG_GROUPS, SG_CHUNK), 0.05),
        "sg_out": nrm((L, SG_WIDTH, D_MODEL), SG_WIDTH ** -0.5),
        "w_o": nrm((L, D_MODEL, D_MODEL), 0.5 * D_MODEL ** -0.5),
        "ffn_norm_g": 1.0 + nrm((L, D_MODEL), 0.05),
        "w_up": nrm((L, D_MODEL, 2 * D_FF), D_MODEL ** -0.5),
        "ffn_conv_w": nrm((L, CONV_WIDTH, 2 * D_FF), CONV_WIDTH ** -0.5),
        "w_down": nrm((L, D_FF, D_MODEL), 0.5 * D_FF ** -0.5),
        "final_norm_g": 1.0 + nrm((D_MODEL,), 0.05),
    }


def reference(x, mix_norm_g, w_in, gate_b, sc_conv_w, sc_out, sb_out, rw_mu, rw_w0, rw_w2,
              rw_a0, rw_a2, rw_g2, rw_k_k, rw_k_a, rw_r_k, rw_gn_g, rw_gn_b, rw_out, sg_ln_g,
              sg_ln_b, sg_w, sg_b, sg_out, w_o, ffn_norm_g, w_up, ffn_conv_w, w_down,
              final_norm_g):
    for l in range(DEPTH):
        h = rms_norm(x, mix_norm_g[l])
        x = x + hybrid_mixer(h, w_in[l], gate_b[l], sc_conv_w[l], sc_out[l], sb_out[l], rw_mu[l],
                             rw_w0[l], rw_w2[l], rw_a0[l], rw_a2[l], rw_g2[l], rw_k_k[l],
                             rw_k_a[l], rw_r_k[l], rw_gn_g[l], rw_gn_b[l], rw_out[l], sg_ln_g[l],
                             sg_ln_b[l], sg_w[l], sg_b[l], sg_out[l], w_o[l])
        h = rms_norm(x, ffn_norm_g[l])
        x = x + conv_glu_ffn(h, w_up[l], ffn_conv_w[l], w_down[l])
    return rms_norm(x, final_norm_g)
```

```python
import math
from contextlib import ExitStack
import numpy as np
import concourse.bass as bass
import concourse.mybir as mybir
from concourse.bass_utils import run_bass_kernel_spmd

F32 = mybir.dt.float32
BF16 = mybir.dt.bfloat16
AF = mybir.ActivationFunctionType
ALU = mybir.AluOpType
AX = mybir.AxisListType

D = 1024
KC = 8
DFF = 2816
NJ = 22
HALO = 128
RMS_EPS = 1e-6
LN_EPS = 1e-5
GN_EPS = 64e-5
DECAY_SCALE = math.exp(-0.5)


class Buf:
    def __init__(self, t, name, semkey=None):
        self.t = t
        self.name = name
        self.last_w = None
        self.readers = {}
        self.dma_sem = None
        self.semkey = semkey

    def __getitem__(self, idx):
        return self.t[idx]


class _Rec:
    def __init__(self):
        self.call = None

    def __getattr__(self, name):
        def f(*a, **k):
            self.call = (name, a, k)
            return self
        return f


def _rec(fn):
    r = _Rec()
    fn(r)
    assert r.call is not None
    return r.call


class Prog:
    ENGS = ("pe", "act", "dve", "pool", "sp")

    def __init__(self, nc):
        self.nc = nc
        self.ops = {e: [] for e in self.ENGS}
        self.sems = {}
        self.cnt = {}
        self.is_dma = {}
        self.waited = {e: {} for e in self.ENGS}
        self.pending = {e: False for e in self.ENGS}
        self.ekey = {}
        for e in ("pe", "act", "dve", "pool"):
            self._mksem(e, False)
            self.ekey[e] = e
        self.nphase = 0
        self.nbuf = 0
        self.rr = 0
        self.stack = None
        self.ext_in = []

    def begin_phase(self):
        self.stack = ExitStack()
        self.nphase += 1
        for e in ("pe", "act", "dve", "pool"):
            key = f"{e}_p{self.nphase}"
            self._mksem(key, False)
            self.ekey[e] = key
        for e in self.ENGS:
            waits = []
            for k, v in self.cnt.items():
                if v > 0 and k != self.ekey.get(e) and self.waited[e].get(k, 0) < v:
                    self.waited[e][k] = v
                    waits.append((k, v))
            if waits:
                self.ops[e].append((waits, None, None))

    def end_phase(self):
        self.emit()
        self.ops = {e: [] for e in self.ENGS}
        self.stack.close()
        self.stack = None

    def _mksem(self, key, dma):
        self.sems[key] = self.nc.alloc_semaphore("s_" + key)
        self.cnt[key] = 0
        self.is_dma[key] = dma

    def sb(self, shape, dt=F32, name=None):
        self.nbuf += 1
        name = (name or "sb") + f"_{self.nbuf}"
        if self.stack is not None:
            return Buf(self.stack.enter_context(self.nc.sbuf_tensor(name, list(shape), dt)), name)
        return Buf(self.nc.alloc_sbuf_tensor(name, list(shape), dt), name)

    def ps(self, shape=(128, 512), dt=F32, name=None):
        self.nbuf += 1
        name = (name or "ps") + f"_{self.nbuf}"
        if self.stack is not None:
            return Buf(self.stack.enter_context(self.nc.psum_tensor(name, list(shape), dt)), name)
        return Buf(self.nc.alloc_psum_tensor(name, list(shape), dt), name)

    def dram(self, name, shape, dt=F32, kind="Internal", semkey=None):
        if kind == "ExternalInput":
            self.ext_in.append(name)
        return Buf(self.nc.dram_tensor(name, list(shape), dt, kind=kind), name, semkey)

    def _needs(self, eng, reads, writes, pe_chain):
        needs = {}

        def add(tok):
            if tok is None:
                return
            k, v = tok
            if needs.get(k, 0) < v:
                needs[k] = v

        for b in reads:
            add(b.last_w)
        for b in writes:
            add(b.last_w)
            for k, v in b.readers.items():
                add((k, v))
        out = []
        for k, v in needs.items():
            if self.is_dma[k]:
                v = self.cnt[k]
            elif eng == "pe" and k == self.ekey["pe"] and pe_chain:
                continue
            if self.waited[eng].get(k, 0) >= v:
                continue
            self.waited[eng][k] = v
            out.append((k, v))
        return out

    def _commit(self, tok, reads, writes):
        k, v = tok
        for b in reads:
            if b.readers.get(k, 0) < v:
                b.readers[k] = v
        for b in writes:
            b.last_w = tok
            b.readers = {}

    def op(self, eng, fn, reads=(), writes=(), inc=True):
        waits = self._needs(eng, reads, writes, True)
        key = self.ekey[eng]
        if inc:
            self.cnt[key] += 1
            tok = (key, self.cnt[key])
            self.pending[eng] = False
            self.ops[eng].append((waits, _rec(fn), (key, 1)))
        else:
            tok = (key, self.cnt[key] + 1)
            self.pending[eng] = True
            self.ops[eng].append((waits, _rec(fn), None))
        self._commit(tok, reads, writes)

    def dma(self, q, out_buf, fn, reads=(), inc=16):
        if out_buf.dma_sem is None:
            key = "d_" + (out_buf.semkey or out_buf.name)
            if key not in self.sems:
                self._mksem(key, True)
            out_buf.dma_sem = key
        key = out_buf.dma_sem
        waits = self._needs(q, reads, (out_buf,), False)
        self.cnt[key] += inc
        tok = (key, self.cnt[key])
        self.ops[q].append((waits, _rec(fn), (key, inc)))
        self._commit(tok, reads, (out_buf,))

    def final_wait(self, eng, bufs):
        waits = self._needs(eng, bufs, (), False)
        self.ops[eng].append((waits, None, None))

    def emit(self):
        sems = self.sems
        for e in self.ENGS:
            assert not self.pending[e], e

        def run(e_name):
            def body(engine):
                for waits, fn, inc in self.ops[e_name]:
                    for k, v in waits:
                        engine.wait_ge(sems[k], v)
                    if fn is not None:
                        ins = getattr(engine, fn[0])(*fn[1], **fn[2])
                        if inc is not None:
                            ins.then_inc(sems[inc[0]], inc[1])
            return body

        with self.nc.Block() as block:
            block.tensor(run("pe"))
            block.scalar(run("act"))
            block.vector(run("dve"))
            block.gpsimd(run("pool"))
            block.sync(run("sp"))

    def ew(self):
        self.rr ^= 1
        return "dve" if self.rr else "pool"


def mm_group(P, out_ap, out_buf, pairs, reads):
    n = len(pairs)
    for i, (l, r) in enumerate(pairs):
        P.op("pe", (lambda t, l=l, r=r, i=i: t.matmul(out_ap, lhsT=l, rhs=r, start=(i == 0), stop=(i == n - 1))),
             reads=reads, writes=(out_buf,), inc=(i == n - 1))


def emit_dense(P, NT, L, last_layer, io, W=512, last_out=True):
    TT = HALO + NT
    EI = "ExternalInput"
    sfx = f"_{L}"
    hm, ohd, ohpd = io["hm"], io["oh"], io["ohp"]
    CH = 1024
    wAD = P.dram("wAD" + sfx, [128, KC * 1280], F32, EI)
    wG = P.dram("wG" + sfx, [8, 128, KC * 512], F32, EI)
    wOut = P.dram("wOut" + sfx, [4, 128, 2 * D], F32, EI)
    wO = P.dram("wO" + sfx, [2, 128, KC * 512], F32, EI)
    wUp = P.dram("wUp" + sfx, [11, 128, KC * 512], F32, EI)
    wDn = P.dram("wDn" + sfx, [8, 128, NJ * 128], F32, EI)
    vecs = P.dram("vecs" + sfx, [128, 64], F32, EI)
    convF = P.dram("convF" + sfx, [128, 44 * 3], F32, EI)
    rowsD = P.dram("rowsD" + sfx, [128, 512], F32, EI)
    wsT = P.dram("wsT" + sfx, [128, 4 * 128], F32, EI)
    bsr = P.dram("bsr" + sfx, [1, 4 * 128], F32, EI)
    out = io["dense_out"]
    ohb = P.sb([128, 8], F32, "ohb")
    P.dma("sp", ohb, lambda q: q.dma_start(out=ohb[:, 0:4], in_=ohd.t.ap()), reads=(ohd,))
    P.dma("sp", ohb, lambda q: q.dma_start(out=ohb[:, 4:8], in_=ohpd.t.ap()), reads=(ohpd,))

    vec = P.sb([128, 64], F32, "vec")
    P.dma("sp", vec, lambda q: q.dma_start(out=vec[:, :], in_=vecs.t.ap()), reads=(vecs,))
    cvf = P.sb([128, 44 * 3], F32, "cvf")
    P.dma("sp", cvf, lambda q: q.dma_start(out=cvf[:, :], in_=convF.t.ap()), reads=(convF,))
    rows = P.sb([128, 512], F32, "rows")
    P.dma("sp", rows, lambda q: q.dma_start(out=rows[:, :], in_=rowsD.t.ap()), reads=(rowsD,))
    hmb = P.sb([128, 1], F32, "hmb")
    P.dma("sp", hmb, lambda q: q.dma_start(out=hmb[:, :], in_=hm.t.ap()), reads=(hm,))
    wsb = P.sb([128, 512], BF16, "wsb")
    wsf = P.sb([128, 512], F32, "wsf")
    P.dma("sp", wsf, lambda q: q.dma_start(out=wsf[:, :], in_=wsT.t.ap()), reads=(wsT,))
    P.op("pool", lambda g: g.affine_select(out=wsf[:, :].rearrange("p (g t) -> p g t", g=4),
                                           in_=wsf[:, :].rearrange("p (g t) -> p g t", g=4),
                                           pattern=[[0, 4], [1, 128]], compare_op=ALU.is_ge, fill=0.0,
                                           base=0, channel_multiplier=-1),
         reads=(wsf,), writes=(wsf,))
    P.op("dve", lambda v: v.tensor_copy(out=wsb[:, :], in_=wsf[:, :]), reads=(wsf,), writes=(wsb,))
    bsb = P.sb([1, 512], BF16, "bsb")
    P.dma("pool", bsb, lambda q: q.dma_start(out=bsb[:, :], in_=bsr.t.ap()), reads=(bsr,))
    onesb = P.sb([128, 128], BF16, "onesb")
    P.op("pool", lambda g: g.memset(onesb[:, :], 1.0), writes=(onesb,))
    wad = P.sb([128, KC * 1280], BF16, "wad")
    P.dma("pool", wad, lambda q: q.dma_start(out=wad[:, :], in_=wAD.t.ap()), reads=(wAD,))
    wadv = wad.t.ap().rearrange("p (kc c) -> p kc c", kc=KC)
    wout = [P.sb([128, 2 * D], BF16, f"wout{b}") for b in range(4)]
    for b in range(4):
        P.dma("pool", wout[b], lambda q, b=b: q.dma_start(out=wout[b][:, :], in_=wOut.t.ap()[b]), reads=(wOut,))

    NSLOT = 4
    ring = [P.sb([128, 4096], BF16, f"ring{i}") for i in range(NSLOT)]
    stream = []
    for oc in range(8):
        stream.append((wG, oc, KC * 512))
    for i in range(2):
        stream.append((wO, i, KC * 512))
    for jj in range(11):
        stream.append((wUp, jj, KC * 512))
    for oc in range(8):
        stream.append((wDn, oc, NJ * 128))
    state = {"issued": 0, "consumed": 0}

    def issue_block():
        i = state["issued"]
        src, idx, n = stream[i % len(stream)]
        slot = ring[i % NSLOT]
        P.dma("pool", slot, lambda q: q.dma_start(out=slot[:, 0:n], in_=src.t.ap()[idx]), reads=(src,))
        state["issued"] += 1

    def next_block(total_blocks):
        while state["issued"] < min(state["consumed"] + NSLOT - 1, total_blocks):
            issue_block()
        if state["issued"] <= state["consumed"]:
            issue_block()
        slot = ring[state["consumed"] % NSLOT]
        state["consumed"] += 1
        return slot

    xt = P.sb([128, KC, W], F32, "xt")
    h = P.sb([128, KC, W], BF16, "h")
    act = P.sb([128, NJ, W], BF16, "act")
    merged = P.sb([128, KC, W], F32, "merged")
    mb = P.sb([128, KC, W], BF16, "mb")
    ybt = P.sb([128, 4, W], BF16, "ybt")
    ycand = [P.sb([128, 4, W], BF16, f"ycand{i}") for i in range(2)]
    ya = P.sb([128, 2, W], BF16, "ya")
    yd = P.sb([128, 2, W], BF16, "yd")
    ug = P.sb([128, 2, W], F32, "ug")
    cxh = P.sb([128, 2, W + 2], F32, "cxh")
    uh = P.sb([128, 44, 2], F32, "uh")
    uc = [P.sb([128, W + 2], F32, f"uc{i}") for i in range(4)]
    tmp = [P.sb([128, W], F32, f"tmp{i}") for i in range(6)]
    rstd = P.sb([128, W], F32, "rstd")
    lnt = P.sb([128, W], F32, "lnt")
    vtk = [P.sb([128, 256], F32, f"vtk{i}") for i in range(4)]
    vnb = P.sb([128, 256], BF16, "vnb")
    st = P.sb([128, 8], F32, "st")
    banks = [P.ps([128, 512], F32, f"bank{i}") for i in range(8)]
    bstate = {"i": 0}

    def bank():
        b = banks[bstate["i"] % 8]
        bstate["i"] += 1
        return b

    P.op("pool", lambda g: g.memset(cxh[:, :, :], 0.0), writes=(cxh,))
    P.op("pool", lambda g: g.memset(uh[:, :, :], 0.0), writes=(uh,))

    def rmsnorm(w, gcol, out_buf, out_f32=False):
        P.op("dve", lambda v: v.tensor_tensor(out=act[:, 0:KC, 0:w], in0=xt[:, :, 0:w], in1=xt[:, :, 0:w], op=ALU.mult),
             reads=(xt,), writes=(act,))
        pb = bank()
        mm_group(P, pb[:, 0:w], pb, [(onesb[:, :], act[:, kc, 0:w]) for kc in range(KC)], reads=(onesb, act))
        P.op("act", lambda a: a.activation(out=lnt[:, 0:w], in_=pb[:, 0:w], func=AF.Ln, bias=RMS_EPS, scale=1.0 / D),
             reads=(pb,), writes=(lnt,))
        P.op("act", lambda a: a.activation(out=rstd[:, 0:w], in_=lnt[:, 0:w], func=AF.Exp, scale=-0.5),
             reads=(lnt,), writes=(rstd,))
        for kc in range(KC):
            e = "dve"
            P.op(e, lambda v, kc=kc: v.scalar_tensor_tensor(out=out_buf[:, kc, 0:w], in0=xt[:, kc, 0:w],
                                                            scalar=vec[:, gcol + kc:gcol + kc + 1], in1=rstd[:, 0:w],
                                                            op0=ALU.mult, op1=ALU.mult),
                 reads=(xt, vec, rstd), writes=(out_buf,))

    ntiles = NT // W
    total_blocks = len(stream) * ntiles

    def tile(off, w, is_halo, out_off):
        if L == 0:
            xsrc = io["xTh"]
            xv = xsrc.t.ap().rearrange("(kc p) t -> p kc t", p=128)
            P.dma("sp", xt, lambda q: q.dma_start(out=xt[:, :, 0:w], in_=xv[:, :, off:off + w]), reads=(xsrc,))
        elif not is_halo:
            o_ = off - HALO
            for kc in range(KC):
                xb_ = io["xok"][kc][o_ // CH]
                P.dma("sp", xt, lambda q, kc=kc, xb_=xb_: q.dma_start(out=xt[:, kc, 0:w], in_=xb_.t.ap()[:, o_ % CH:o_ % CH + w]), reads=(xb_,))
        else:
            for sgi in range(4):
                for kc in range(KC):
                    xb_ = io["xgk"][kc][NT // CH - 1]
                    P.dma("sp", merged, lambda q, kc=kc, xb_=xb_, sgi=sgi: q.dma_start(out=merged[:, kc, 0:HALO], in_=xb_.t.ap()[sgi * 128:(sgi + 1) * 128, CH - HALO:CH]), reads=(xb_,))
                if sgi == 0:
                    P.op("dve", lambda v: v.tensor_scalar(out=xt[:, :, 0:HALO], in0=merged[:, :, 0:HALO], scalar1=ohb[:, 4:5], scalar2=None, op0=ALU.mult),
                         reads=(merged, ohb), writes=(xt,))
                else:
                    P.op("dve", lambda v, sgi=sgi: v.scalar_tensor_tensor(out=xt[:, :, 0:HALO], in0=merged[:, :, 0:HALO], scalar=ohb[:, 4 + sgi:5 + sgi], in1=xt[:, :, 0:HALO], op0=ALU.mult, op1=ALU.add),
                         reads=(merged, ohb, xt), writes=(xt,))
        for sgi in range(4):
            yc_ = ycand[sgi % 2]
            t0_ = sgi * NT + off - HALO
            if t0_ < 0:
                P.op("dve", lambda v, yc_=yc_: v.memset(yc_[:, :, 0:w], 0.0), writes=(yc_,))
            else:
                gb_ = io["ygk"][t0_ // CH]
                gv_ = gb_.t.ap().rearrange("(h p) t -> p h t", p=128)
                P.dma("pool", yc_, lambda q, yc_=yc_, gv_=gv_, t0_=t0_: q.dma_start(out=yc_[:, :, 0:w], in_=gv_[:, :, t0_ % CH:t0_ % CH + w]), reads=(gb_,))
            if sgi == 0:
                P.op("dve", lambda v, yc_=yc_: v.tensor_scalar(out=ybt[:, :, 0:w], in0=yc_[:, :, 0:w], scalar1=ohb[:, 0:1], scalar2=None, op0=ALU.mult),
                     reads=(yc_, ohb), writes=(ybt,))
            else:
                P.op("dve", lambda v, yc_=yc_, sgi=sgi: v.scalar_tensor_tensor(out=ybt[:, :, 0:w], in0=yc_[:, :, 0:w], scalar=ohb[:, sgi:sgi + 1], in1=ybt[:, :, 0:w], op0=ALU.mult, op1=ALU.add),
                     reads=(yc_, ohb, ybt), writes=(ybt,))
        rmsnorm(w, 0, h)
        hr = (h, wad)
        for ch in range(2):
            pbg, pcg, pxi = bank(), bank(), bank()
            for pb, c in ((pbg, ch), (pcg, 2 + ch), (pxi, 4 + ch)):
                mm_group(P, pb[:, 0:w], pb, [(wadv[:, kc, c * 128:(c + 1) * 128], h[:, kc, 0:w]) for kc in range(KC)], reads=hr)
            t0, t1 = tmp[0], tmp[1]
            P.op("act", lambda a: a.copy(out=t0[:, 0:w], in_=pxi[:, 0:w]), reads=(pxi,), writes=(t0,))
            P.op("dve", lambda v: v.tensor_tensor(out=cxh[:, ch, 2:2 + w], in0=pcg[:, 0:w], in1=t0[:, 0:w], op=ALU.mult),
                 reads=(pcg, t0), writes=(cxh,))
            c0 = 56 + ch * 3
            P.op("dve", lambda v: v.tensor_scalar(out=t1[:, 0:w], in0=cxh[:, ch, 0:w], scalar1=vec[:, c0:c0 + 1], scalar2=None, op0=ALU.mult),
                 reads=(cxh, vec), writes=(t1,))
            P.op("dve", lambda v: v.scalar_tensor_tensor(out=t1[:, 0:w], in0=cxh[:, ch, 1:1 + w], scalar=vec[:, c0 + 1:c0 + 2], in1=t1[:, 0:w], op0=ALU.mult, op1=ALU.add),
                 reads=(cxh, vec, t1), writes=(t1,))
            P.op("dve", lambda v: v.scalar_tensor_tensor(out=t1[:, 0:w], in0=cxh[:, ch, 2:2 + w], scalar=vec[:, c0 + 2:c0 + 3], in1=t1[:, 0:w], op0=ALU.mult, op1=ALU.add),
                 reads=(cxh, vec, t1), writes=(t1,))
            P.op("dve", lambda v: v.tensor_tensor(out=ya[:, ch, 0:w], in0=pbg[:, 0:w], in1=t1[:, 0:w], op=ALU.mult),
                 reads=(pbg, t1), writes=(ya,))
            if is_halo:
                P.op("dve", lambda v: v.tensor_scalar(out=cxh[:, ch, 0:2], in0=cxh[:, ch, w:w + 2], scalar1=hmb[:, 0:1], scalar2=None, op0=ALU.mult),
                     reads=(cxh, hmb), writes=(cxh,))
            else:
                P.op("dve", lambda v: v.tensor_copy(out=cxh[:, ch, 0:2], in_=cxh[:, ch, w:w + 2]), reads=(cxh,), writes=(cxh,))

        def gelu(dst_ap, dst_buf, src_ap, src_buf, npart, n, t_a, t_b):
            P.op("act", lambda a: a.copy(out=t_a[0:npart, 0:n], in_=src_ap), reads=(src_buf,), writes=(t_a,))
            P.op("dve", lambda g: g.tensor_tensor(out=t_b[0:npart, 0:n], in0=t_a[0:npart, 0:n], in1=t_a[0:npart, 0:n], op=ALU.mult),
                 reads=(t_a,), writes=(t_b,))
            P.op("dve", lambda v: v.tensor_scalar(out=t_b[0:npart, 0:n], in0=t_b[0:npart, 0:n], scalar1=0.044715, scalar2=1.0, op0=ALU.mult, op1=ALU.add),
                 reads=(t_b,), writes=(t_b,))
            P.op("dve", lambda g: g.tensor_tensor(out=t_b[0:npart, 0:n], in0=t_b[0:npart, 0:n], in1=t_a[0:npart, 0:n], op=ALU.mult),
                 reads=(t_a, t_b), writes=(t_b,))
            P.op("act", lambda a: a.activation(out=t_b[0:npart, 0:n], in_=t_b[0:npart, 0:n], func=AF.Sigmoid, scale=2.0 * 0.7978845608028654),
                 reads=(t_b,), writes=(t_b,))
            P.op("dve", lambda v: v.tensor_tensor(out=dst_ap, in0=t_a[0:npart, 0:n], in1=t_b[0:npart, 0:n], op=ALU.mult),
                 reads=(t_a, t_b), writes=(dst_buf,))

        for ch in range(2):
            pu = bank()
            c = 768 + ch * 128
            mm_group(P, pu[:, 0:w], pu, [(wadv[:, kc, c:c + 128], h[:, kc, 0:w]) for kc in range(KC)], reads=hr)
            gelu(ug[:, ch, 0:w], ug, pu[:, 0:w], pu, 128, w, tmp[2], tmp[3])
        for tb in range(w // 128):
            pv = bank()
            mm_group(P, pv[:, 0:256], pv, [(h[:, kc, tb * 128:(tb + 1) * 128], wadv[:, kc, 1024:1280]) for kc in range(KC)], reads=hr)
            gv, sq = vtk[0], vtk[1]
            gelu(gv[:, :], gv, pv[:, 0:256], pv, 128, 256, vtk[2], vtk[3])
            P.op("dve", lambda v: v.tensor_reduce(out=st[:, 0:1], in_=gv[:, :], axis=AX.X, op=ALU.add), reads=(gv,), writes=(st,))
            P.op("dve", lambda g: g.tensor_tensor(out=sq[:, :], in0=gv[:, :], in1=gv[:, :], op=ALU.mult), reads=(gv,), writes=(sq,))
            P.op("dve", lambda v: v.tensor_reduce(out=st[:, 1:2], in_=sq[:, :], axis=AX.X, op=ALU.add), reads=(sq, st), writes=(st,))
            P.op("dve", lambda v: v.tensor_scalar(out=st[:, 2:4], in0=st[:, 0:2], scalar1=1.0 / 256, scalar2=None, op0=ALU.mult), reads=(st,), writes=(st,))
            P.op("dve", lambda v: v.tensor_tensor(out=st[:, 4:5], in0=st[:, 2:3], in1=st[:, 2:3], op=ALU.mult), reads=(st,), writes=(st,))
            P.op("dve", lambda v: v.tensor_tensor(out=st[:, 5:6], in0=st[:, 3:4], in1=st[:, 4:5], op=ALU.subtract), reads=(st,), writes=(st,))
            P.op("act", lambda a: a.activation(out=st[:, 6:7], in_=st[:, 5:6], func=AF.Ln, bias=LN_EPS, scale=1.0), reads=(st,), writes=(st,))
            P.op("act", lambda a: a.activation(out=st[:, 7:8], in_=st[:, 6:7], func=AF.Exp, scale=-0.5), reads=(st,), writes=(st,))
            P.op("dve", lambda v: v.tensor_scalar(out=gv[:, :], in0=gv[:, :], scalar1=st[:, 2:3], scalar2=st[:, 7:8], op0=ALU.subtract, op1=ALU.mult),
                 reads=(gv, st), writes=(gv,))
            P.op("dve", lambda g: g.tensor_tensor(out=gv[:, :], in0=gv[:, :], in1=rows[:, 0:256], op=ALU.mult), reads=(gv, rows), writes=(gv,))
            P.op("dve", lambda v: v.tensor_tensor(out=vnb[:, :], in0=gv[:, :], in1=rows[:, 256:512], op=ALU.add), reads=(gv, rows), writes=(vnb,))
            for g4 in range(4):
                pm = bank()
                po = (g4 % 2) * 64
                mm_group(P, pm[po:po + 64, 0:128], pm,
                         [(vnb[:, g4 * 64:(g4 + 1) * 64], wsb[:, g4 * 128:(g4 + 1) * 128]),
                          (onesb[0:1, 0:64], bsb[0:1, g4 * 128:(g4 + 1) * 128])], reads=(vnb, wsb, onesb, bsb))
                P.op("dve", lambda v, g4=g4, pm=pm, po=po: v.tensor_tensor(out=yd[po:po + 64, g4 // 2, tb * 128:(tb + 1) * 128], in0=pm[po:po + 64, 0:128],
                                                                            in1=ug[po:po + 64, g4 // 2, tb * 128:(tb + 1) * 128], op=ALU.mult),
                     reads=(pm, ug), writes=(yd,))

        for oc in range(8):
            slot = next_block(total_blocks) if not is_halo else None
            if is_halo:
                slot = ring_h
                P.dma("pool", slot, lambda q: q.dma_start(out=slot[:, 0:KC * 512], in_=wG.t.ap()[oc]), reads=(wG,))
            sv = slot.t.ap().rearrange("p (kc c) -> p kc c", kc=KC)
            for br in range(4):
                pg, pp = bank(), bank()
                mm_group(P, pg[:, 0:w], pg, [(sv[:, kc, br * 128:(br + 1) * 128], h[:, kc, 0:w]) for kc in range(KC)], reads=(slot, h))
                wo_v = wout[br].t.ap().rearrange("p (k c) -> p k c", k=2)
                if br == 0:
                    prs = [(wo_v[:, k2, oc * 128:(oc + 1) * 128], ya[:, k2, 0:w]) for k2 in range(2)]
                    rd = (wout[br], ya)
                elif br in (1, 2):
                    po_ = 0 if br == 1 else 64
                    prs = []
                    for hd_ in range(4):
                        wv_ = wout[1 + hd_ // 2].t.ap().rearrange("p (k c) -> p k c", k=2)
                        prs.append((wv_[po_:po_ + 64, hd_ % 2, oc * 128:(oc + 1) * 128], ybt[po_:po_ + 64, hd_, 0:w]))
                    rd = (wout[1], wout[2], ybt)
                else:
                    prs = [(wo_v[:, k2, oc * 128:(oc + 1) * 128], yd[:, k2, 0:w]) for k2 in range(2)]
                    rd = (wout[br], yd)
                mm_group(P, pp[:, 0:w], pp, prs, reads=rd)
                gs = tmp[4]
                gb = 24 + br * 8 + oc
                P.op("act", lambda a, pg=pg, gb=gb: a.activation(out=gs[:, 0:w], in_=pg[:, 0:w], func=AF.Sigmoid, bias=vec[:, gb:gb + 1], scale=1.0),
                     reads=(pg, vec), writes=(gs,))
                if br == 0:
                    P.op("dve", lambda v, pp=pp: v.tensor_tensor(out=merged[:, oc, 0:w], in0=pp[:, 0:w], in1=gs[:, 0:w], op=ALU.mult),
                         reads=(pp, gs), writes=(merged,))
                else:
                    t5 = tmp[5]
                    P.op("dve", lambda v, pp=pp: v.tensor_tensor(out=t5[:, 0:w], in0=pp[:, 0:w], in1=gs[:, 0:w], op=ALU.mult),
                         reads=(pp, gs), writes=(t5,))
                    dst = mb if br == 3 else merged
                    P.op("dve", lambda g, dst=dst: g.tensor_tensor(out=dst[:, oc, 0:w], in0=merged[:, oc, 0:w], in1=t5[:, 0:w], op=ALU.add),
                         reads=(merged, t5), writes=(dst,))

        for i in range(2):
            if is_halo:
                slot = ring_h
                P.dma("pool", slot, lambda q: q.dma_start(out=slot[:, 0:KC * 512], in_=wO.t.ap()[i]), reads=(wO,))
            else:
                slot = next_block(total_blocks)
            sv = slot.t.ap().rearrange("p (kc c) -> p kc c", kc=KC)
            for o4 in range(4):
                oc = i * 4 + o4
                pb = bank()
                mm_group(P, pb[:, 0:w], pb, [(sv[:, kc, o4 * 128:(o4 + 1) * 128], mb[:, kc, 0:w]) for kc in range(KC)], reads=(slot, mb))
                P.op("dve", lambda v, pb=pb, oc=oc: v.tensor_tensor(out=xt[:, oc, 0:w], in0=pb[:, 0:w], in1=xt[:, oc, 0:w], op=ALU.add),
                     reads=(pb, xt), writes=(xt,))

        rmsnorm(w, 8, h)
        for jj in range(11):
            if is_halo:
                slot = ring_h
                P.dma("pool", slot, lambda q: q.dma_start(out=slot[:, 0:KC * 512], in_=wUp.t.ap()[jj]), reads=(wUp,))
            else:
                slot = next_block(total_blocks)
            sv = slot.t.ap().rearrange("p (kc c) -> p kc c", kc=KC)
            for j2 in range(2):
                j = jj * 2 + j2
                cv = []
                for gv_i in range(2):
                    pb = bank()
                    c = (j2 * 2 + gv_i) * 128
                    mm_group(P, pb[:, 0:w], pb, [(sv[:, kc, c:c + 128], h[:, kc, 0:w]) for kc in range(KC)], reads=(slot, h))
                    idx = gv_i * NJ + j
                    u = uc[(j % 2) * 2 + gv_i]
                    P.op("act", lambda a, pb=pb, u=u: a.copy(out=u[:, 2:2 + w], in_=pb[:, 0:w]), reads=(pb,), writes=(u,))
                    P.op("dve", lambda g, u=u, idx=idx: g.tensor_copy(out=u[:, 0:2], in_=uh[:, idx, :]), reads=(uh, u), writes=(u,))
                    if is_halo:
                        P.op("dve", lambda g, u=u, idx=idx: g.tensor_scalar(out=uh[:, idx, :], in0=u[:, w:w + 2], scalar1=hmb[:, 0:1], scalar2=None, op0=ALU.mult),
                             reads=(u, hmb, uh), writes=(uh,))
                        continue
                    P.op("dve", lambda g, u=u, idx=idx: g.tensor_copy(out=uh[:, idx, :], in_=u[:, w:w + 2]), reads=(u, uh), writes=(uh,))
                    t = tmp[gv_i] if j % 2 == 0 else tmp[3 + gv_i]
                    P.op("act", lambda a, pb=pb, t=t, idx=idx: a.activation(out=t[:, 0:w], in_=pb[:, 0:w], func=AF.Copy, scale=cvf[:, idx * 3 + 2:idx * 3 + 3]),
                         reads=(pb, cvf), writes=(t,))
                    for tap in (0, 1):
                        P.op("dve", lambda v, u=u, t=t, idx=idx, tap=tap: v.scalar_tensor_tensor(out=t[:, 0:w], in0=u[:, tap:tap + w], scalar=cvf[:, idx * 3 + tap:idx * 3 + tap + 1], in1=t[:, 0:w], op0=ALU.mult, op1=ALU.add),
                             reads=(u, cvf, t), writes=(t,))
                    cv.append(t)
                if is_halo:
                    continue
                sg = tmp[2] if j % 2 == 0 else tmp[5]
                P.op("act", lambda a: a.activation(out=sg[:, 0:w], in_=cv[0][:, 0:w], func=AF.Sigmoid), reads=(cv[0],), writes=(sg,))
                P.op("dve", lambda v: v.tensor_tensor(out=sg[:, 0:w], in0=sg[:, 0:w], in1=cv[0][:, 0:w], op=ALU.mult), reads=(sg, cv[0]), writes=(sg,))
                P.op("dve", lambda g, j=j: g.tensor_tensor(out=act[:, j, 0:w], in0=sg[:, 0:w], in1=cv[1][:, 0:w], op=ALU.mult), reads=(sg, cv[1]), writes=(act,))
        if is_halo:
            return
        for oc in range(8):
            slot = next_block(total_blocks)
            sv = slot.t.ap().rearrange("p (j c) -> p j c", c=128)
            pb = bank()
            mm_group(P, pb[:, 0:w], pb, [(sv[:, j, :], act[:, j, 0:w]) for j in range(NJ)], reads=(slot, act))
            P.op("dve", lambda v, pb=pb, oc=oc: v.tensor_tensor(out=xt[:, oc, 0:w], in0=pb[:, 0:w], in1=xt[:, oc, 0:w], op=ALU.add),
                 reads=(pb, xt), writes=(xt,))
        if last_layer:
            rmsnorm(w, 16, merged, out_f32=True)
            src = merged
        else:
            src = xt
        if last_out:
            outv = out.t.ap().rearrange("(kc p) t -> p kc t", p=128)
            P.dma("sp", out, lambda q: q.dma_start(out=outv[:, :, out_off:out_off + w], in_=src[:, :, 0:w]), reads=(src,))
        else:
            for kc in range(KC):
                xb_ = io["xok"][kc][out_off // CH]
                P.dma("sp", xb_, lambda q, kc=kc, xb_=xb_: q.dma_start(out=xb_.t.ap()[:, out_off % CH:out_off % CH + w], in_=src[:, kc, 0:w]), reads=(src,))

    ring_h = P.sb([128, 4096], BF16, "ring_h")
    tile(0, HALO, True, None)
    for ti in range(ntiles):
        tile(HALO + ti * W, W, False, ti * W)
        if io.get("after_dense_tile") is not None:
            io["after_dense_tile"](ti)


SB_OFF, RW_OFF, SG_OFF, GATE_OFF = 768, 1536, 2560, 3072


def _kc_layout(w):
    K, C = w.shape
    return np.ascontiguousarray(w.reshape(K // 128, 128, C).transpose(1, 0, 2).reshape(128, (K // 128) * C))


def _pvec(v):
    return np.ascontiguousarray(v.reshape(-1, 128).T)


def pack_dense_weights(inp, l):
    f = np.float32
    w_in = inp["w_in"][l]
    cols = np.concatenate([np.arange(0, 768), np.arange(SG_OFF, SG_OFF + 512)])
    d = {}
    d["wAD"] = _kc_layout(w_in[:, cols])
    wg = w_in[:, GATE_OFF:].reshape(D, 4, 8, 128).transpose(2, 0, 1, 3).reshape(8, D, 512)
    d["wG"] = np.stack([_kc_layout(wg[oc]) for oc in range(8)])
    sbo, rwo = inp["sb_out"][l], inp["rw_out"][l]

    def bc(h0):
        a = np.zeros((128, 2, D), np.float32)
        for k in range(2):
            hd = h0 + k
            a[0:64, k] = sbo[hd * 64:(hd + 1) * 64]
            a[64:128, k] = rwo[hd * 64:(hd + 1) * 64]
        return a.reshape(128, 2 * D)
    d["wOut"] = np.stack([_kc_layout(inp["sc_out"][l]), bc(0), bc(2), _kc_layout(inp["sg_out"][l])])
    d["wO"] = np.stack([_kc_layout(inp["w_o"][l][:, i * 512:(i + 1) * 512]) for i in range(2)])
    wu = inp["w_up"][l].reshape(D, 2, 11, 2, 128).transpose(2, 0, 3, 1, 4).reshape(11, D, 512)
    d["wUp"] = np.stack([_kc_layout(wu[jj]) for jj in range(11)])
    d["wDn"] = np.stack([_kc_layout(inp["w_down"][l][:, oc * 128:(oc + 1) * 128]) for oc in range(8)])
    vec = np.zeros((128, 64), f)
    vec[:, 0:8] = _pvec(inp["mix_norm_g"][l])
    vec[:, 8:16] = _pvec(inp["ffn_norm_g"][l])
    vec[:, 16:24] = _pvec(inp["final_norm_g"])
    vec[:, 24:56] = _pvec(inp["gate_b"][l])
    cw = inp["sc_conv_w"][l]
    for ch in range(2):
        for tap in range(3):
            vec[:, 56 + ch * 3 + tap] = cw[tap, ch * 128:(ch + 1) * 128]
    d["vecs"] = vec
    fc = inp["ffn_conv_w"][l]
    d["convF"] = np.ascontiguousarray(fc.reshape(3, 44, 128).transpose(2, 1, 0).reshape(128, 132))
    rows = np.zeros((128, 512), f)
    rows[:, 0:256] = inp["sg_ln_g"][l][None, :]
    rows[:, 256:512] = inp["sg_ln_b"][l][None, :]
    d["rowsD"] = rows
    d["wsT"] = np.ascontiguousarray(inp["sg_w"][l].transpose(2, 0, 1).reshape(128, 512))
    d["bsr"] = np.ascontiguousarray(inp["sg_b"][l].reshape(1, 512))
    return {k: np.ascontiguousarray(v, dtype=f) for k, v in d.items()}


def dense_core_inputs(xT_b, ybc_b, seg, NT):
    def halo(a):
        o = np.zeros((a.shape[0], HALO + NT), np.float32)
        s = seg * NT
        if seg > 0:
            o[:, :] = a[:, s - HALO:s + NT]
        else:
            o[:, HALO:] = a[:, 0:NT]
        return o
    hm = np.full((128, 1), 0.0 if seg == 0 else 1.0, np.float32)
    return {"xT": halo(xT_b), "ybc": halo(ybc_b), "hm": hm}


def emit_seqmix(P, T, L, io, do_attn=True, do_rwkv=True, interleave=True):
    EI = "ExternalInput"
    W = 512
    NTL = T // W
    NT = T // 4
    sfx = f"_{L}"
    wS = P.dram("wS" + sfx, [128, KC * 640], F32, EI)
    vecS = P.dram("vecS" + sfx, [128, 32], F32, EI)
    w2d = P.dram("w2h" + sfx, [64, 64], F32, EI)
    a2d = P.dram("a2h" + sfx, [64, 64], F32, EI)
    g2d = P.dram("g2h" + sfx, [128, 64], F32, EI)
    yxk = io["yxk"]
    CH = 1024

    vec = P.sb([128, 32], F32, "vec")
    P.dma("sp", vec, lambda q: q.dma_start(out=vec[:, :], in_=vecS.t.ap()), reads=(vecS,))
    ws = P.sb([128, KC * 640], BF16, "ws")
    P.dma("pool", ws, lambda q: q.dma_start(out=ws[:, :], in_=wS.t.ap()), reads=(wS,))
    wsv = ws.t.ap().rearrange("p (kc c) -> p kc c", kc=KC)
    w2h = P.sb([64, 64], F32, "w2h_s"); a2h = P.sb([64, 64], F32, "a2h_s"); g2h = P.sb([128, 64], F32, "g2h_s")
    P.dma("sp", w2h, lambda q: q.dma_start(out=w2h[:, :], in_=w2d.t.ap()), reads=(w2d,))
    P.dma("sp", a2h, lambda q: q.dma_start(out=a2h[:, :], in_=a2d.t.ap()), reads=(a2d,))
    P.dma("sp", g2h, lambda q: q.dma_start(out=g2h[:, :], in_=g2d.t.ap()), reads=(g2d,))

    onesb = P.sb([128, 128], BF16, "onesb")
    P.op("pool", lambda g: g.memset(onesb[:, :], 1.0), writes=(onesb,))
    onesf = P.sb([128, 128], F32, "onesf")
    P.op("pool", lambda g: g.memset(onesf[:, :], 1.0), writes=(onesf,))
    identf = P.sb([128, 128], F32, "identf")
    P.op("pool", lambda g: g.affine_select(out=identf[:, :], in_=onesf[:, 0:128], pattern=[[-1, 128]], compare_op=ALU.is_equal,
                                           fill=0.0, base=0, channel_multiplier=1), reads=(onesf,), writes=(identf,))

    qTt = [P.sb([64, W], BF16, f"qT{i}") for i in range(NTL)]
    kTt = [P.sb([64, W], BF16, f"kT{i}") for i in range(NTL)]
    vtt = [P.sb([128, 4, 64], BF16, f"vt{i}") for i in range(NTL)]

    xt = P.sb([128, KC, W], F32, "xt")
    h = P.sb([128, KC, W], BF16, "h")
    sq = h
    rstd = P.sb([128, W], F32, "rstd")
    lnt = rstd
    banks = [P.ps([128, 512], F32, f"bank{i}") for i in range(8)]
    bstate = {"i": 0}

    def bank():
        b = banks[bstate["i"] % 4]
        bstate["i"] += 1
        return b

    if do_rwkv:
        rmask = P.sb([64, W], F32, "rmask")
        P.op("pool", lambda g: g.memset(rmask[:, :], 1.0), writes=(rmask,))
        P.op("pool", lambda g: g.memset(rmask[:, :].rearrange("p (c i) -> p c i", i=64)[:, :, 0:1], 0.0), writes=(rmask,))
        maskS = P.sb([128, 128], F32, "maskS")
        mask2 = P.sb([128, 256], F32, "mask2")
        P.op("pool", lambda g: g.affine_select(out=maskS[:, :], in_=onesf[:, 0:128], pattern=[[-1, 128]], compare_op=ALU.is_ge, fill=0.0, base=-1, channel_multiplier=1),
             reads=(onesf,), writes=(maskS,))
        P.op("pool", lambda g: g.memset(maskS[64:128, 0:64], 0.0), writes=(maskS,))
        P.op("pool", lambda g: g.affine_select(out=mask2[:, 0:128], in_=onesf[:, 0:128], pattern=[[1, 128]], compare_op=ALU.is_ge, fill=0.0, base=-1, channel_multiplier=-1),
             reads=(onesf,), writes=(mask2,))
        P.op("pool", lambda g: g.affine_select(out=mask2[:, 128:256], in_=onesf[:, 0:128], pattern=[[1, 128]], compare_op=ALU.is_ge, fill=0.0, base=0, channel_multiplier=-1),
             reads=(onesf,), writes=(mask2,))
        P.op("pool", lambda g: g.memset(mask2[0:64, 64:128], 0.0), writes=(mask2,))
        P.op("pool", lambda g: g.memset(mask2[0:64, 192:256], 0.0), writes=(mask2,))
        zb = P.sb([64, 5, W + 1], F32, "zb")
        zg = P.sb([128, W + 1], F32, "zg")
        P.op("pool", lambda g: g.memset(zb[:, :, :], 0.0), writes=(zb,))
        P.op("pool", lambda g: g.memset(zg[:, :], 0.0), writes=(zg,))
        dbuf = P.sb([64, 5, W], F32, "dbuf")
        zz = dbuf
        dg = P.sb([128, W], F32, "dg")
        sgx = dg
        m_ = {n: P.sb([64, W], F32, n) for n in ("lw", "asg", "kk0", "s1", "s2", "s3", "kmod", "bvec", "bonus", "cl", "gi", "ginv", "gmap")}
        for alias, tgt in (("sgw", "lw"), ("kksq", "s1"), ("tt", "s1"), ("rk", "s1"), ("ssc", "s2"), ("rs", "s2"), ("tw", "s3"), ("cm", "s3"),
                           ("gprev", "s3"), ("kkn", "kk0"), ("BT", "bvec"), ("KT", "kmod"), ("ynT", "ginv"), ("yo", "ginv")):
            m_[alias] = m_[tgt]
        ART = P.sb([64, 4, 2, 128], F32, "ART")
        NSET = 2
        sets = []
        for si in range(NSET):
            sets.append(dict(
                tok=P.sb([128, 192], F32, f"tok{si}"), abT=P.sb([128, 256], F32, f"abT{si}"), akT=P.sb([128, 256], F32, f"akT{si}"),
                Lm=P.sb([128, 128], F32, f"Lm{si}"), Xb=[P.sb([128, 128], F32, f"Xb{si}_{i}") for i in range(2)],
                PPb=[P.sb([128, 256], F32, f"PP{si}_{i}") for i in range(2)], LMb=[P.sb([64, 64], F32, f"LM{si}_{i}") for i in range(2)],
                N0Gb=[P.sb([64, 64], F32, f"N0G{si}_{i}") for i in range(2)], RpT=P.sb([64, 128], F32, f"RpT{si}"),
                ysb=P.sb([128, 64], F32, f"ysb{si}"), ysq=P.sb([128, 64], F32, f"ysq{si}"), yn=P.sb([128, 64], F32, f"yn{si}"),
                gst=P.sb([128, 8], F32, f"gst{si}")))
        NZ = 8
        Zb = [P.sb([64, 64], F32, f"Z{i}") for i in range(NZ)]
        zst = {"i": 0}
        P.op("pool", lambda g: g.memset(Zb[0][:, :], 0.0), writes=(Zb[0],))

    def rmsnorm():
        P.op("dve", lambda v: v.tensor_tensor(out=sq[:, :, :], in0=xt[:, :, :], in1=xt[:, :, :], op=ALU.mult), reads=(xt,), writes=(sq,))
        pb = bank()
        mm_group(P, pb[:, :], pb, [(onesb[:, :], sq[:, kc, :]) for kc in range(KC)], reads=(onesb, sq))
        P.op("act", lambda a: a.activation(out=lnt[:, :], in_=pb[:, :], func=AF.Ln, bias=RMS_EPS, scale=1.0 / D), reads=(pb,), writes=(lnt,))
        P.op("act", lambda a: a.activation(out=rstd[:, :], in_=lnt[:, :], func=AF.Exp, scale=-0.5), reads=(lnt,), writes=(rstd,))
        for kc in range(KC):
            P.op("dve", lambda v, kc=kc: v.scalar_tensor_tensor(out=h[:, kc, :], in0=xt[:, kc, :], scalar=vec[:, kc:kc + 1], in1=rstd[:, :], op0=ALU.mult, op1=ALU.mult),
                 reads=(xt, vec, rstd), writes=(h,))

    def proj(col, m):
        pb = bank()
        mm_group(P, pb[0:m, :], pb, [(wsv[:, kc, col:col + m], h[:, kc, :]) for kc in range(KC)], reads=(ws, h))
        return pb

    def V(n):
        return m_[n]

    def ew_tt(e, o, a, b, op, rd, wr):
        P.op(e, lambda v: v.tensor_tensor(out=o, in0=a, in1=b, op=op), reads=rd, writes=wr)

    def rwkv_tile(ti):
        c0 = ti * W
        for m in range(5):
            pb_ = proj(192 + 64 * m, 64)
            P.op("act", lambda a, m=m, pb_=pb_: a.copy(out=zb[:, m, 1:W + 1], in_=pb_[0:64, :]), reads=(pb_,), writes=(zb,))
        pg_ = proj(512, 128)
        P.op("act", lambda a: a.copy(out=zg[:, 1:W + 1], in_=pg_[:, :]), reads=(pg_,), writes=(zg,))
        ew_tt("dve", dbuf[:, :, :], zb[:, :, 0:W], zb[:, :, 1:W + 1], ALU.subtract, (zb,), (dbuf,))
        for m in range(5):
            P.op("dve", lambda v, m=m: v.scalar_tensor_tensor(out=zz[:, m, :], in0=dbuf[:, m, :], scalar=vec[0:64, 9 + m:10 + m], in1=zb[:, m, 1:W + 1], op0=ALU.mult, op1=ALU.add),
                 reads=(dbuf, vec, zb), writes=(zz,))
        P.op("dve", lambda v: v.tensor_copy(out=zb[:, :, 0:1], in_=zb[:, :, W:W + 1]), reads=(zb,), writes=(zb,))
        ew_tt("dve", dg[:, :], zg[:, 0:W], zg[:, 1:W + 1], ALU.subtract, (zg,), (dg,))
        P.op("dve", lambda v: v.scalar_tensor_tensor(out=dg[:, :], in0=dg[:, :], scalar=vec[:, 8:9], in1=zg[:, 1:W + 1], op0=ALU.mult, op1=ALU.add),
             reads=(dg, vec, zg), writes=(dg,))
        P.op("dve", lambda v: v.tensor_copy(out=zg[:, 0:1], in_=zg[:, W:W + 1]), reads=(zg,), writes=(zg,))
        yield
        Rm, Km, Vm, XW, XA = (zz[:, i, :] for i in range(5))
        P.op("act", lambda a: a.activation(out=V("tw")[:, :], in_=XW, func=AF.Tanh), reads=(zz,), writes=(V("tw"),))
        P.op("act", lambda a: a.activation(out=sgx[:, :], in_=dg[:, :], func=AF.Sigmoid), reads=(dg,), writes=(sgx,))
        pw = bank()
        mm_group(P, pw[0:64, :], pw, [(w2h[:, :], V("tw")[:, :])], reads=(w2h, V("tw")))
        pa = bank()
        mm_group(P, pa[0:64, :], pa, [(a2h[:, :], XA)], reads=(a2h, zz))
        pgm = bank()
        mm_group(P, pgm[0:64, :], pgm, [(g2h[:, :], sgx[:, :])], reads=(g2h, sgx))
        P.op("act", lambda a: a.activation(out=V("sgw")[:, :], in_=pw[0:64, :], func=AF.Sigmoid, bias=vec[0:64, 14:15], scale=1.0), reads=(pw, vec), writes=(V("sgw"),))
        P.op("act", lambda a: a.activation(out=V("asg")[:, :], in_=pa[0:64, :], func=AF.Sigmoid, bias=vec[0:64, 15:16], scale=1.0), reads=(pa, vec), writes=(V("asg"),))
        P.op("act", lambda a: a.copy(out=V("gmap")[:, :], in_=pgm[0:64, :]), reads=(pgm,), writes=(V("gmap"),))
        P.op("dve", lambda v: v.tensor_scalar(out=V("lw")[:, :], in0=V("sgw")[:, :], scalar1=-DECAY_SCALE, scalar2=None, op0=ALU.mult), reads=(V("sgw"),), writes=(V("lw"),))
        P.op("dve", lambda v: v.tensor_scalar(out=V("kk0")[:, :], in0=Km, scalar1=vec[0:64, 16:17], scalar2=None, op0=ALU.mult), reads=(zz, vec), writes=(V("kk0"),))
        ew_tt("dve", V("kksq")[:, :], V("kk0")[:, :], V("kk0")[:, :], ALU.mult, (V("kk0"),), (V("kksq"),))
        yield
        pss = bank()
        mm_group(P, pss[0:64, :], pss, [(onesf[0:64, 0:64], V("kksq")[:, :])], reads=(onesf, V("kksq")))
        P.op("dve", lambda v: v.tensor_scalar(out=V("ssc")[:, :], in0=pss[0:64, :], scalar1=1e-24, scalar2=None, op0=ALU.max), reads=(pss,), writes=(V("ssc"),))
        P.op("dve", lambda v: v.tensor_scalar(out=V("tt")[:, :], in0=V("asg")[:, :], scalar1=-1.0, scalar2=vec[0:64, 17:18], op0=ALU.add, op1=ALU.mult), reads=(V("asg"), vec), writes=(V("tt"),))
        P.op("dve", lambda v: v.scalar_tensor_tensor(out=V("kmod")[:, :], in0=V("tt")[:, :], scalar=1.0, in1=Km, op0=ALU.add, op1=ALU.mult), reads=(V("tt"), zz), writes=(V("kmod"),))
        P.op("dve", lambda v: v.scalar_tensor_tensor(out=V("rk")[:, :], in0=Rm, scalar=vec[0:64, 18:19], in1=V("kmod")[:, :], op0=ALU.mult, op1=ALU.mult), reads=(zz, vec, V("kmod")), writes=(V("rk"),))
        pbn = bank()
        mm_group(P, pbn[0:64, :], pbn, [(onesf[0:64, 0:64], V("rk")[:, :])], reads=(onesf, V("rk")))
        ew_tt("dve", V("bonus")[:, :], pbn[0:64, :], Vm, ALU.mult, (pbn, zz), (V("bonus"),))
        P.op("dve", lambda v: v.tensor_tensor_scan(out=V("cl")[:, :], data0=rmask[:, :], data1=V("lw")[:, :], initial=0.0, op0=ALU.mult, op1=ALU.add),
             reads=(rmask, V("lw")), writes=(V("cl"),))
        ew_tt("dve", V("cm")[:, :], V("cl")[:, :], V("lw")[:, :], ALU.subtract, (V("cl"), V("lw")), (V("cm"),))
        yield
        P.op("act", lambda a: a.activation(out=V("rs")[:, :], in_=V("ssc")[:, :], func=AF.Ln), reads=(V("ssc"),), writes=(V("rs"),))
        P.op("act", lambda a: a.activation(out=V("rs")[:, :], in_=V("rs")[:, :], func=AF.Exp, scale=-0.5), reads=(V("rs"),), writes=(V("rs"),))
        P.op("act", lambda a: a.activation(out=V("gi")[:, :], in_=V("cl")[:, :], func=AF.Exp), reads=(V("cl"),), writes=(V("gi"),))
        P.op("act", lambda a: a.activation(out=V("ginv")[:, :], in_=V("cl")[:, :], func=AF.Exp, scale=-1.0), reads=(V("cl"),), writes=(V("ginv"),))
        P.op("act", lambda a: a.activation(out=V("gprev")[:, :], in_=V("cm")[:, :], func=AF.Exp), reads=(V("cm"),), writes=(V("gprev"),))
        ew_tt("dve", V("kkn")[:, :], V("kk0")[:, :], V("rs")[:, :], ALU.mult, (V("kk0"), V("rs")), (V("kkn"),))
        ew_tt("dve", V("bvec")[:, :], V("kkn")[:, :], V("asg")[:, :], ALU.mult, (V("kkn"), V("asg")), (V("bvec"),))
        a4 = lambda ap: ap.rearrange("p (a b) -> p a b", b=128)
        P.op("dve", lambda v: v.scalar_tensor_tensor(out=ART[:, :, 0, :], in0=a4(V("kkn")[:, :]), scalar=-1.0, in1=a4(V("gprev")[:, :]), op0=ALU.mult, op1=ALU.mult),
             reads=(V("kkn"), V("gprev")), writes=(ART,))
        ew_tt("dve", ART[:, :, 1, :], a4(Rm), a4(V("gi")[:, :]), ALU.mult, (zz, V("gi")), (ART,))
        ew_tt("dve", V("BT")[:, :], V("bvec")[:, :], V("ginv")[:, :], ALU.mult, (V("bvec"), V("ginv")), (V("BT"),))
        ew_tt("dve", V("KT")[:, :], V("kmod")[:, :], V("ginv")[:, :], ALU.mult, (V("kmod"), V("ginv")), (V("KT"),))
        yield
        BT, KT, gi = V("BT"), V("KT"), V("gi")
        def pair_gen(pr, S):
            tok, abT, akT, Lm, Xb, PPb, LMb, N0Gb, RpT = S["tok"], S["abT"], S["akT"], S["Lm"], S["Xb"], S["PPb"], S["LMb"], S["N0Gb"], S["RpT"]
            ysb, ysq, yn, gst = S["ysb"], S["ysq"], S["yn"], S["gst"]
            s0 = pr * 128
            ATp, RTp = ART[:, pr, 0, :], ART[:, pr, 1, :]
            ARp = ART[:, pr, :, :].rearrange("p a b -> p (a b)")
            BTp, KTp, VTp = BT[:, s0:s0 + 128], KT[:, s0:s0 + 128], zz[:, 2, s0:s0 + 128]
            i64 = identf[0:64, 0:64]
            pT = bank()
            mm_group(P, pT[:, 0:64], pT, [(BTp, i64)], reads=(BT, identf))
            mm_group(P, pT[:, 64:128], pT, [(KTp, i64)], reads=(KT, identf))
            mm_group(P, pT[:, 128:192], pT, [(VTp, i64)], reads=(zz, identf))
            P.op("act", lambda a, pT=pT: a.copy(out=tok[:, :], in_=pT[:, 0:192]), reads=(pT,), writes=(tok,))
            yield
            pA = bank()
            mm_group(P, pA[:, 0:256], pA, [(BTp, ARp)], reads=(BT, ART))
            ew_tt("dve", abT[:, :], pA[:, 0:256], mask2[:, :], ALU.mult, (pA, mask2), (abT,))
            pK = bank()
            mm_group(P, pK[:, 0:256], pK, [(KTp, ARp)], reads=(KT, ART))
            ew_tt("dve", akT[:, :], pK[:, 0:256], mask2[:, :], ALU.mult, (pK, mask2), (akT,))
            pL = bank()
            mm_group(P, pL[:, 0:128], pL, [(ATp, BTp)], reads=(ART, BT))
            ew_tt("dve", Lm[:, :], pL[:, 0:128], maskS[:, :], ALU.mult, (pL, maskS), (Lm,))
            yield
            pX = bank()
            mm_group(P, pX[:, 0:64], pX, [(ATp, i64)], reads=(ART, identf))
            mm_group(P, pX[:, 64:128], pX, [(akT[:, 0:128], tok[:, 128:192])], reads=(akT, tok))
            X = Xb[0]
            P.op("act", lambda a, pX=pX, X=X: a.copy(out=X[:, :], in_=pX[:, 0:128]), reads=(pX,), writes=(X,))
            yield
            Pk_ap, PkT_ap, Pk_b, PkT_b = Lm[:, :], abT[:, 0:128], Lm, abT
            for it in range(6):
                pXn = bank()
                mm_group(P, pXn[:, 0:128], pXn, [(PkT_ap, X[:, :])], reads=(PkT_b, X))
                Xn = Xb[(it + 1) % 2]
                ew_tt("dve", Xn[:, :], pXn[:, 0:128], X[:, :], ALU.add, (pXn, X), (Xn,))
                yield
                if it < 5:
                    pP = bank()
                    mm_group(P, pP[:, 0:128], pP, [(PkT_ap, Pk_ap)], reads=(PkT_b, Pk_b))
                    mm_group(P, pP[:, 128:256], pP, [(Pk_ap, PkT_ap)], reads=(PkT_b, Pk_b))
                    PPn = PPb[it % 2]
                    P.op("act", lambda a, pP=pP, PPn=PPn: a.copy(out=PPn[:, :], in_=pP[:, 0:256]), reads=(pP,), writes=(PPn,))
                    Pk_ap, PkT_ap, Pk_b, PkT_b = PPn[:, 0:128], PPn[:, 128:256], PPn, PPn
                X = Xn
            Zs = []
            for hh in range(2):
                hs = slice(hh * 64, hh * 64 + 64)
                pMN = bank()
                mm_group(P, pMN[0:64, 0:64], pMN, [(X[hs, 0:64], tok[hs, 0:64])], reads=(X, tok))
                mm_group(P, pMN[0:64, 64:128], pMN, [(tok[hs, 0:64], X[hs, 64:128]), (tok[hs, 64:128], tok[hs, 128:192])], reads=(X, tok))
                ge = gi[:, s0 + hh * 64 + 63:s0 + hh * 64 + 64]
                LM, N0G = LMb[hh], N0Gb[hh]
                ew_tt("dve", LM[:, :], pMN[0:64, 0:64], i64, ALU.add, (pMN, identf), (LM,))
                P.op("dve", lambda v, pMN=pMN, N0G=N0G, ge=ge: v.tensor_scalar(out=N0G[:, :], in0=pMN[0:64, 64:128], scalar1=ge, scalar2=None, op0=ALU.mult),
                     reads=(pMN, gi), writes=(N0G,))
                Zc = Zb[zst["i"] % NZ]
                Zn = Zb[(zst["i"] + 1) % NZ]
                zst["i"] += 1
                pZ = bank()
                mm_group(P, pZ[0:64, 0:64], pZ, [(LM[:, :], Zc[:, :])], reads=(LM, Zc))
                P.op("dve", lambda v, pZ=pZ, Zn=Zn, N0G=N0G, ge=ge: v.scalar_tensor_tensor(out=Zn[:, :], in0=pZ[0:64, 0:64], scalar=ge, in1=N0G[:, :], op0=ALU.mult, op1=ALU.add),
                     reads=(pZ, gi, N0G), writes=(Zn,))
                Zs.append(Zc)
            pR = bank()
            mm_group(P, pR[0:64, 0:128], pR, [(X[:, 0:64], abT[:, 128:256])], reads=(X, abT))
            ew_tt("dve", RpT[:, :], pR[0:64, 0:128], RTp, ALU.add, (pR, ART), (RpT,))
            yield
            pY = bank()
            P.op("pe", lambda t, pY=pY, X=X: t.matmul(pY[:, 0:64], lhsT=abT[:, 128:256], rhs=X[:, 64:128], start=True, stop=False), reads=(abT, X), writes=(pY,), inc=False)
            P.op("pe", lambda t, pY=pY: t.matmul(pY[:, 0:64], lhsT=akT[:, 128:256], rhs=tok[:, 128:192], start=False, stop=False), reads=(akT, tok), writes=(pY,), inc=False)
            P.op("pe", lambda t, pY=pY: t.matmul(pY[0:64, 0:64], lhsT=RpT[:, 0:64], rhs=Zs[0][:, :], start=False, stop=False), reads=(RpT, Zs[0]), writes=(pY,), inc=False)
            P.op("pe", lambda t, pY=pY: t.matmul(pY[64:128, 0:64], lhsT=RpT[:, 64:128], rhs=Zs[1][:, :], start=False, stop=True), reads=(RpT, Zs[1]), writes=(pY,))
            P.op("act", lambda a, pY=pY: a.copy(out=ysb[:, :], in_=pY[:, 0:64]), reads=(pY,), writes=(ysb,))
            yield
            P.op("dve", lambda v: v.tensor_reduce(out=gst[:, 0:1], in_=ysb[:, :], axis=AX.X, op=ALU.add), reads=(ysb,), writes=(gst,))
            ew_tt("dve", ysq[:, :], ysb[:, :], ysb[:, :], ALU.mult, (ysb,), (ysq,))
            P.op("dve", lambda v: v.tensor_reduce(out=gst[:, 1:2], in_=ysq[:, :], axis=AX.X, op=ALU.add), reads=(ysq, gst), writes=(gst,))
            P.op("dve", lambda v: v.tensor_scalar(out=gst[:, 2:4], in0=gst[:, 0:2], scalar1=1.0 / 64, scalar2=None, op0=ALU.mult), reads=(gst,), writes=(gst,))
            ew_tt("dve", gst[:, 4:5], gst[:, 2:3], gst[:, 2:3], ALU.mult, (gst,), (gst,))
            ew_tt("dve", gst[:, 5:6], gst[:, 3:4], gst[:, 4:5], ALU.subtract, (gst,), (gst,))
            P.op("act", lambda a: a.activation(out=gst[:, 6:7], in_=gst[:, 5:6], func=AF.Ln, bias=GN_EPS, scale=1.0), reads=(gst,), writes=(gst,))
            P.op("act", lambda a: a.activation(out=gst[:, 7:8], in_=gst[:, 6:7], func=AF.Exp, scale=-0.5), reads=(gst,), writes=(gst,))
            P.op("dve", lambda v: v.tensor_scalar(out=yn[:, :], in0=ysb[:, :], scalar1=gst[:, 2:3], scalar2=gst[:, 7:8], op0=ALU.subtract, op1=ALU.mult), reads=(ysb, gst), writes=(yn,))
            yield
            pYT = bank()
            mm_group(P, pYT[0:64, 0:128], pYT, [(yn[:, :], identf[:, :])], reads=(yn, identf))
            P.op("act", lambda a, pYT=pYT, s0=s0: a.copy(out=V("ynT")[:, s0:s0 + 128], in_=pYT[0:64, 0:128]), reads=(pYT,), writes=(V("ynT"),))

        for p0 in range(0, 4, NSET):
            gens = [pair_gen(p0 + k, sets[k]) for k in range(NSET)]
            while gens:
                for g_ in list(gens):
                    try:
                        next(g_)
                    except StopIteration:
                        gens.remove(g_)
                yield
        P.op("dve", lambda v: v.tensor_scalar(out=V("yo")[:, :], in0=V("ynT")[:, :], scalar1=vec[0:64, 19:20], scalar2=vec[0:64, 20:21], op0=ALU.mult, op1=ALU.add), reads=(V("ynT"), vec), writes=(V("yo"),))
        ew_tt("dve", V("yo")[:, :], V("yo")[:, :], V("bonus")[:, :], ALU.add, (V("yo"), V("bonus")), (V("yo"),))
        ew_tt("dve", V("yo")[:, :], V("yo")[:, :], V("gmap")[:, :], ALU.mult, (V("yo"), V("gmap")), (V("yo"),))
        yb_ = yxk[c0 // CH]
        P.dma("sp", yb_, lambda q: q.dma_start(out=yb_.t.ap()[64:128, c0 % CH:c0 % CH + W], in_=V("yo")[:, :]), reads=(V("yo"),))

    if do_attn:
        NTI = P.sb([128, 128], BF16, "NTI")
        NON = P.sb([128, 128], BF16, "NON")
        m01 = P.sb([128, 128], BF16, "m01")
        Z0 = P.sb([128, 64], BF16, "Z0")
        P.op("pool", lambda g: g.memset(NON[:, :], -1.0), writes=(NON,))
        P.op("pool", lambda g: g.memset(Z0[:, :], 0.0), writes=(Z0,))
        P.op("pool", lambda g: g.affine_select(out=NTI[:, :], in_=NON[:, :], pattern=[[-1, 128]], compare_op=ALU.is_ge, fill=0.0, base=0, channel_multiplier=1),
             reads=(NON,), writes=(NTI,))
        P.op("pool", lambda g: g.affine_select(out=m01[:, :], in_=onesb[:, :], pattern=[[1, 128]], compare_op=ALU.is_ge, fill=0.0, base=-1, channel_multiplier=-1),
             reads=(onesb,), writes=(m01,))
        eb = [P.sb([128, W], BF16, f"eb{i}") for i in range(2)]
        spb = [P.sb([128, W], BF16, f"spb{i}") for i in range(3)]
        Ab = [P.sb([128, W], BF16, f"Ab{i}") for i in range(3)]
        spaccs = [P.sb([128, W], BF16, f"spacc{i}") for i in range(2)]
        ybo = P.sb([64, W], F32, "ybo")
        po = banks[7]
        rot = {"i": 0}
        cnt = {"n": 0}

        def bank3():
            b = banks[4 + rot["i"] % 3]
            rot["i"] += 1
            return b

        def ew2(o, a, b, op, rd, wr):
            P.op("dve", lambda v: v.tensor_tensor(out=o, in0=a, in1=b, op=op), reads=rd, writes=wr)

        def S1(d):
            qt, kb, cc, w = d["qt"], d["kb"], d["cc"], d["w"]
            n = cnt["n"]
            cnt["n"] += 1
            d["n"] = n
            kb_buf = kTt[kb // 4]
            d["kbuf"] = kb_buf
            d["kblk"] = kb_buf[:, (kb % 4) * 128:(kb % 4 + 1) * 128]
            d["qcols"] = qTt[qt][:, cc:W]
            e_, sp_ = eb[n % 2], spb[n % 3]
            d["sp"] = sp_
            pz = bank3()
            d["pz"] = pz
            mm_group(P, pz[:, 0:w], pz, [(d["kblk"], d["qcols"])], reads=(kb_buf, qTt[qt]))
            P.op("act", lambda a: a.activation(out=e_[:, 0:w], in_=pz[:, 0:w], func=AF.Exp), reads=(pz,), writes=(e_,))
            P.op("act", lambda a: a.activation(out=sp_[:, 0:w], in_=e_[:, 0:w], func=AF.Ln, bias=1.0, scale=1.0), reads=(e_,), writes=(sp_,))
            if d["diag"]:
                ew2(sp_[:, 0:128], sp_[:, 0:128], m01[:, :], ALU.mult, (sp_, m01), (sp_,))

        def S2(d):
            qt, kb, cc, w, n = d["qt"], d["kb"], d["cc"], d["w"], d["n"]
            sp_, A_ = d["sp"], Ab[n % 3]
            d["A"] = A_
            spacc = spaccs[qt % 2]
            if d["first"]:
                P.op("pool", lambda g: g.memset(spacc[:, :], 0.0), writes=(spacc,))
            pe_ = d["pz"]
            P.op("pe", lambda t: t.matmul(pe_[:, 0:w], lhsT=NTI[:, :], rhs=sp_[:, 0:w], start=False, stop=False), reads=(NTI, sp_), writes=(pe_,), inc=False)
            P.op("pe", lambda t: t.matmul(pe_[:, 0:w], lhsT=NON[:, :], rhs=spacc[:, cc:W], start=False, stop=True), reads=(NON, spacc), writes=(pe_,))
            P.op("act", lambda a: a.activation(out=A_[:, 0:w], in_=pe_[:, 0:w], func=AF.Exp), reads=(pe_,), writes=(A_,))
            if d["diag"]:
                ew2(A_[:, 0:128], A_[:, 0:128], m01[:, :], ALU.mult, (A_, m01), (A_,))
            if not d["last"]:
                ew2(spacc[:, cc:W], spacc[:, cc:W], sp_[:, 0:w], ALU.add, (spacc, sp_), (spacc,))

        def S3(d):
            qt, kb, cc, w = d["qt"], d["kb"], d["cc"], d["w"]
            if d["first"]:
                for c4 in range(4):
                    P.op("pe", lambda t, c4=c4: t.matmul(po[0:64, c4 * 128:(c4 + 1) * 128], lhsT=Z0[:, :], rhs=NON[:, :], start=True, stop=False),
                         reads=(Z0, NON), writes=(po,), inc=(c4 == 3))
            A_ = d["A"]
            vb = vtt[kb // 4]
            P.op("pe", lambda t: t.matmul(po[0:64, cc:W], lhsT=vb[:, kb % 4, :], rhs=A_[:, 0:w], start=False, stop=d["last"]), reads=(vb, A_), writes=(po,))
            if d["last"]:
                q0 = qt * W
                P.op("act", lambda a: a.copy(out=ybo[:, :], in_=po[0:64, :]), reads=(po,), writes=(ybo,))
                yb_ = yxk[q0 // CH]
                P.dma("sp", yb_, lambda q: q.dma_start(out=yb_.t.ap()[0:64, q0 % CH:q0 % CH + W], in_=ybo[:, :]), reads=(ybo,))

        def attn_gen(qt):
            tl = []
            for kb in range(4 * qt + 3, -1, -1):
                diag = kb >= 4 * qt
                cc = 128 * (kb - 4 * qt) if diag else 0
                tl.append(dict(qt=qt, kb=kb, diag=diag, cc=cc, w=W - cc, first=(kb == 4 * qt + 3), last=(kb == 0)))
            n = len(tl)
            for i in range(-2, n):
                if 0 <= i + 2 < n:
                    S1(tl[i + 2])
                if 0 <= i + 1 < n:
                    S2(tl[i + 1])
                if 0 <= i < n:
                    S3(tl[i])
                yield

    for ti in range(NTL):
        c0 = ti * W
        if L == 0:
            xsrc = io["xT0"]
            P.dma("sp", xt, lambda q: q.dma_start(out=xt[:, :, :], in_=xsrc.t.ap().rearrange("(kc p) t -> p kc t", p=128)[:, :, c0:c0 + W]), reads=(xsrc,))
        else:
            sg_, o_ = c0 // NT, c0 % NT
            for kc in range(KC):
                xb_ = io["xgk"][kc][o_ // CH]
                P.dma("sp", xt, lambda q, kc=kc, xb_=xb_: q.dma_start(out=xt[:, kc, :], in_=xb_.t.ap()[sg_ * 128:(sg_ + 1) * 128, o_ % CH:o_ % CH + W]), reads=(xb_,))
        rmsnorm()
        ag = None
        if do_attn:
            pq = proj(0, 64)
            P.op("act", lambda a: a.activation(out=qTt[ti][:, :], in_=pq[0:64, :], func=AF.Copy, scale=0.125), reads=(pq,), writes=(qTt[ti],))
            pk = proj(64, 64)
            P.op("act", lambda a: a.copy(out=kTt[ti][:, :], in_=pk[0:64, :]), reads=(pk,), writes=(kTt[ti],))
            for tb in range(4):
                pv = bank()
                mm_group(P, pv[:, 0:64], pv, [(h[:, kc, tb * 128:(tb + 1) * 128], wsv[:, kc, 128:192]) for kc in range(KC)], reads=(h, ws))
                P.op("dve", lambda v, pv=pv, tb=tb: v.tensor_copy(out=vtt[ti][:, tb, :], in_=pv[:, 0:64]), reads=(pv,), writes=(vtt[ti],))
            ag = attn_gen(ti) if interleave else None
            n_it = 4 * ti + 4 + 2
        if do_rwkv:
            per = max(1, -(-n_it // 26)) if ag is not None else 0
            for _ in rwkv_tile(ti):
                if ag is not None:
                    for _k in range(per):
                        if next(ag, "done") == "done":
                            ag = None
                            break
        if ag is not None:
            for _ in ag:
                pass
        if io.get("after_seq_tile") is not None:
            io["after_seq_tile"](ti)
    if do_attn and not interleave:
        for ti in range(NTL):
            for _ in attn_gen(ti):
                pass


def pack_seqmix_weights(inp, l, hd):
    f = np.float32
    w_in = inp["w_in"][l]
    hs = slice(hd * 64, hd * 64 + 64)
    cols = np.concatenate([SB_OFF + np.arange(64) + hd * 64, SB_OFF + 256 + np.arange(64) + hd * 64, SB_OFF + 512 + np.arange(64) + hd * 64,
                           RW_OFF + np.arange(64) + hd * 64, RW_OFF + 256 + np.arange(64) + hd * 64, RW_OFF + 512 + np.arange(64) + hd * 64,
                           RW_OFF + 768 + np.arange(256)])
    d = {"wS": _kc_layout(w_in[:, cols])}
    vec = np.zeros((128, 32), f)
    vec[:, 0:8] = _pvec(inp["mix_norm_g"][l])
    mu = inp["rw_mu"][l]
    vec[:, 8] = mu[896:1024]
    for m, o in enumerate((hd * 64, 256 + hd * 64, 512 + hd * 64, 768, 832)):
        vec[0:64, 9 + m] = mu[o:o + 64]
    vec[0:64, 14] = inp["rw_w0"][l][hs]
    vec[0:64, 15] = inp["rw_a0"][l][hs]
    vec[0:64, 16] = inp["rw_k_k"][l][hs]
    vec[0:64, 17] = inp["rw_k_a"][l][hs]
    vec[0:64, 18] = inp["rw_r_k"][l][hd]
    vec[0:64, 19] = inp["rw_gn_g"][l][hs]
    vec[0:64, 20] = inp["rw_gn_b"][l][hs]
    d["vecS"] = vec
    d["w2h"] = inp["rw_w2"][l][:, hs]
    d["a2h"] = inp["rw_a2"][l][:, hs]
    d["g2h"] = inp["rw_g2"][l][:, hs]
    return {k: np.ascontiguousarray(v, dtype=f) for k, v in d.items()}


def build_fused(T, nlayers=2):
    nc = bass.Bass("TRN2", target_bir_lowering=False)
    P = Prog(nc)
    NT = T // 4
    TP = HALO + T
    EI = "ExternalInput"
    CH = 1024
    io = {
        "xT0": P.dram("xT0", [D, T], F32, EI),
        "xTh": P.dram("xTh", [D, HALO + NT], F32, EI),
        "hm": P.dram("hm", [128, 1], F32, EI),
        "oh": P.dram("oh", [128, 4], F32, EI),
        "ohp": P.dram("ohp", [128, 4], F32, EI),
        "yxk": [P.dram(f"yx{k}", [128, CH], F32, semkey="yx") for k in range(T // CH)],
        "ygk": [P.dram(f"yg{k}", [512, CH], F32, semkey="yg") for k in range(T // CH)],
        "xok": [[P.dram(f"xo{kc}_{cc}", [128, CH], F32, semkey="xo") for cc in range(NT // CH)] for kc in range(KC)],
        "xgk": [[P.dram(f"xg{kc}_{cc}", [512, CH], F32, semkey="xg") for cc in range(NT // CH)] for kc in range(KC)],
    }
    out = P.dram("out", [D, NT], F32, "ExternalOutput")
    RG = [[0, 1, 2, 3], [4, 5, 6, 7]]

    def gather(src, dst):
        P.dma("pool", dst, lambda g: g.collective_compute("AllGather", ALU.bypass, replica_groups=RG, ins=[src.t.ap().opt()], outs=[dst.t.ap().opt()]),
              reads=(src,), inc=1)

    def after_seq_tile(ti):
        if ti % 2 == 1:
            gather(io["yxk"][ti // 2], io["ygk"][ti // 2])

    for L in range(nlayers):
        lastL = (L == nlayers - 1)
        P.begin_phase()
        io["after_seq_tile"] = after_seq_tile
        emit_seqmix(P, T, L, io)
        P.end_phase()
        P.begin_phase()
        io["dense_out"] = out

        def after_dense_tile(ti):
            if ti % 2 == 1:
                for kc in range(KC):
                    gather(io["xok"][kc][ti // 2], io["xgk"][kc][ti // 2])
        io["after_dense_tile"] = None if lastL else after_dense_tile
        emit_dense(P, NT, L, L == 1, io, last_out=lastL)
        if lastL:
            P.final_wait("sp", (out,))
        P.end_phase()
    return nc, list(P.ext_in)


_CACHE = {}


def kernel(**inputs):
    inp = {k: np.asarray(v, dtype=np.float32) for k, v in inputs.items()}
    x = inp["x"]
    B, T, _ = x.shape
    NT = T // 4
    cores = list(range(8))
    nl = int(inputs.get("_nlayers", 2)) if "_nlayers" in inputs else 2
    if (T, nl) not in _CACHE:
        _CACHE[(T, nl)] = build_fused(T, nl)
    nc, ext_names = _CACHE[(T, nl)]
    xT_b = [np.ascontiguousarray(x[b].T) for b in range(B)]
    dw = [pack_dense_weights(inp, l) for l in range(2)]
    maps = []
    for c in cores:
        b, sg = c // 4, c % 4
        m = {"xT0": xT_b[b]}
        halo = np.zeros((D, HALO + NT), np.float32)
        if sg > 0:
            halo[:, :] = xT_b[b][:, sg * NT - HALO:(sg + 1) * NT]
        else:
            halo[:, HALO:] = xT_b[b][:, 0:NT]
        m["xTh"] = halo
        m["hm"] = np.full((128, 1), 0.0 if sg == 0 else 1.0, np.float32)
        oh = np.zeros((128, 4), np.float32)
        oh[:, sg] = 1.0
        ohp = np.zeros((128, 4), np.float32)
        if sg > 0:
            ohp[:, sg - 1] = 1.0
        m["oh"], m["ohp"] = oh, ohp
        for l in range(nl):
            for k, v in dw[l].items():
                m[f"{k}_{l}"] = v
            for k, v in pack_seqmix_weights(inp, l, sg).items():
                m[f"{k}_{l}"] = v
        maps.append(m)
    maps = [{k: m[k] for k in ext_names} for m in maps]
    res = run_bass_kernel_spmd(nc, maps, core_ids=cores)
    outp = np.zeros((B, T, D), np.float32)
    for c in cores:
        b, sg = c // 4, c % 4
        outp[b, sg * NT:(sg + 1) * NT, :] = res.results[c]["out"].T
    return outp
```

```python
import math
from contextlib import ExitStack
import numpy as np
import concourse.bass as bass
import concourse.mybir as mybir
from concourse.bass_utils import run_bass_kernel_spmd

F32 = mybir.dt.float32
BF16 = mybir.dt.bfloat16
AF = mybir.ActivationFunctionType
ALU = mybir.AluOpType
AX = mybir.AxisListType

D = 1024
KC = 8
DFF = 2816
NJ = 22
HALO = 128
RMS_EPS = 1e-6
LN_EPS = 1e-5
GN_EPS = 64e-5
DECAY_SCALE = math.exp(-0.5)


class Buf:
    def __init__(self, t, name, semkey=None):
        self.t = t
        self.name = name
        self.last_w = None
        self.readers = {}
        self.dma_sem = None
        self.semkey = semkey

    def __getitem__(self, idx):
        return self.t[idx]


class _Rec:
    def __init__(self):
        self.call = None

    def __getattr__(self, name):
        def f(*a, **k):
            self.call = (name, a, k)
            return self
        return f


def _rec(fn):
    r = _Rec()
    fn(r)
    assert r.call is not None
    return r.call


class Prog:
    ENGS = ("pe", "act", "dve", "pool", "sp")

    def __init__(self, nc):
        self.nc = nc
        self.ops = {e: [] for e in self.ENGS}
        self.sems = {}
        self.cnt = {}
        self.is_dma = {}
        self.waited = {e: {} for e in self.ENGS}
        self.pending = {e: False for e in self.ENGS}
        self.ekey = {}
        for e in ("pe", "act", "dve", "pool"):
            self._mksem(e, False)
            self.ekey[e] = e
        self.nphase = 0
        self.nbuf = 0
        self.rr = 0
        self.stack = None
        self.ext_in = []

    def begin_phase(self):
        self.stack = ExitStack()
        self.nphase += 1
        for e in ("pe", "act", "dve", "pool"):
            key = f"{e}_p{self.nphase}"
            self._mksem(key, False)
            self.ekey[e] = key
        for e in self.ENGS:
            waits = []
            for k, v in self.cnt.items():
                if v > 0 and k != self.ekey.get(e) and self.waited[e].get(k, 0) < v:
                    self.waited[e][k] = v
                    waits.append((k, v))
            if waits:
                self.ops[e].append((waits, None, None))

    def end_phase(self):
        self.emit()
        self.ops = {e: [] for e in self.ENGS}
        self.stack.close()
        self.stack = None

    def _mksem(self, key, dma):
        self.sems[key] = self.nc.alloc_semaphore("s_" + key)
        self.cnt[key] = 0
        self.is_dma[key] = dma

    def sb(self, shape, dt=F32, name=None):
        self.nbuf += 1
        name = (name or "sb") + f"_{self.nbuf}"
        if self.stack is not None:
            return Buf(self.stack.enter_context(self.nc.sbuf_tensor(name, list(shape), dt)), name)
        return Buf(self.nc.alloc_sbuf_tensor(name, list(shape), dt), name)

    def ps(self, shape=(128, 512), dt=F32, name=None):
        self.nbuf += 1
        name = (name or "ps") + f"_{self.nbuf}"
        if self.stack is not None:
            return Buf(self.stack.enter_context(self.nc.psum_tensor(name, list(shape), dt)), name)
        return Buf(self.nc.alloc_psum_tensor(name, list(shape), dt), name)

    def dram(self, name, shape, dt=F32, kind="Internal", semkey=None):
        if kind == "ExternalInput":
            self.ext_in.append(name)
        return Buf(self.nc.dram_tensor(name, list(shape), dt, kind=kind), name, semkey)

    def _needs(self, eng, reads, writes, pe_chain):
        needs = {}

        def add(tok):
            if tok is None:
                return
            k, v = tok
            if needs.get(k, 0) < v:
                needs[k] = v

        for b in reads:
            add(b.last_w)
        for b in writes:
            add(b.last_w)
            for k, v in b.readers.items():
                add((k, v))
        out = []
        for k, v in needs.items():
            if self.is_dma[k]:
                v = self.cnt[k]
            elif eng == "pe" and k == self.ekey["pe"] and pe_chain:
                continue
            if self.waited[eng].get(k, 0) >= v:
                continue
            self.waited[eng][k] = v
            out.append((k, v))
        return out

    def _commit(self, tok, reads, writes):
        k, v = tok
        for b in reads:
            if b.readers.get(k, 0) < v:
                b.readers[k] = v
        for b in writes:
            b.last_w = tok
            b.readers = {}

    def op(self, eng, fn, reads=(), writes=(), inc=True):
        waits = self._needs(eng, reads, writes, True)
        key = self.ekey[eng]
        if inc:
            self.cnt[key] += 1
            tok = (key, self.cnt[key])
            self.pending[eng] = False
            self.ops[eng].append((waits, _rec(fn), (key, 1)))
        else:
            tok = (key, self.cnt[key] + 1)
            self.pending[eng] = True
            self.ops[eng].append((waits, _rec(fn), None))
        self._commit(tok, reads, writes)

    def dma(self, q, out_buf, fn, reads=(), inc=16):
        if out_buf.dma_sem is None:
            key = "d_" + (out_buf.semkey or out_buf.name)
            if key not in self.sems:
                self._mksem(key, True)
            out_buf.dma_sem = key
        key = out_buf.dma_sem
        waits = self._needs(q, reads, (out_buf,), False)
        self.cnt[key] += inc
        tok = (key, self.cnt[key])
        self.ops[q].append((waits, _rec(fn), (key, inc)))
        self._commit(tok, reads, (out_buf,))

    def final_wait(self, eng, bufs):
        waits = self._needs(eng, bufs, (), False)
        self.ops[eng].append((waits, None, None))

    def emit(self):
        sems = self.sems
        for e in self.ENGS:
            assert not self.pending[e], e

        def run(e_name):
            def body(engine):
                for waits, fn, inc in self.ops[e_name]:
                    for k, v in waits:
                        engine.wait_ge(sems[k], v)
                    if fn is not None:
                        ins = getattr(engine, fn[0])(*fn[1], **fn[2])
                        if inc is not None:
                            ins.then_inc(sems[inc[0]], inc[1])
            return body

        with self.nc.Block() as block:
            block.tensor(run("pe"))
            block.scalar(run("act"))
            block.vector(run("dve"))
            block.gpsimd(run("pool"))
            block.sync(run("sp"))

    def ew(self):
        self.rr ^= 1
        return "dve" if self.rr else "pool"


def mm_group(P, out_ap, out_buf, pairs, reads):
    n = len(pairs)
    for i, (l, r) in enumerate(pairs):
        P.op("pe", (lambda t, l=l, r=r, i=i: t.matmul(out_ap, lhsT=l, rhs=r, start=(i == 0), stop=(i == n - 1))),
             reads=reads, writes=(out_buf,), inc=(i == n - 1))


def emit_dense(P, NT, L, last_layer, io, W=512, last_out=True):
    TT = HALO + NT
    EI = "ExternalInput"
    sfx = f"_{L}"
    hm, ohd, ohpd = io["hm"], io["oh"], io["ohp"]
    CH = 1024
    wAD = P.dram("wAD" + sfx, [128, KC * 1280], F32, EI)
    wG = P.dram("wG" + sfx, [8, 128, KC * 512], F32, EI)
    wOut = P.dram("wOut" + sfx, [4, 128, 2 * D], F32, EI)
    wO = P.dram("wO" + sfx, [2, 128, KC * 512], F32, EI)
    wUp = P.dram("wUp" + sfx, [11, 128, KC * 512], F32, EI)
    wDn = P.dram("wDn" + sfx, [8, 128, NJ * 128], F32, EI)
    vecs = P.dram("vecs" + sfx, [128, 64], F32, EI)
    convF = P.dram("convF" + sfx, [128, 44 * 3], F32, EI)
    rowsD = P.dram("rowsD" + sfx, [128, 512], F32, EI)
    wsT = P.dram("wsT" + sfx, [128, 4 * 128], F32, EI)
    bsr = P.dram("bsr" + sfx, [1, 4 * 128], F32, EI)
    out = io["dense_out"]
    ohb = P.sb([128, 8], F32, "ohb")
    P.dma("sp", ohb, lambda q: q.dma_start(out=ohb[:, 0:4], in_=ohd.t.ap()), reads=(ohd,))
    P.dma("sp", ohb, lambda q: q.dma_start(out=ohb[:, 4:8], in_=ohpd.t.ap()), reads=(ohpd,))

    vec = P.sb([128, 64], F32, "vec")
    P.dma("sp", vec, lambda q: q.dma_start(out=vec[:, :], in_=vecs.t.ap()), reads=(vecs,))
    cvf = P.sb([128, 44 * 3], F32, "cvf")
    P.dma("sp", cvf, lambda q: q.dma_start(out=cvf[:, :], in_=convF.t.ap()), reads=(convF,))
    rows = P.sb([128, 512], F32, "rows")
    P.dma("sp", rows, lambda q: q.dma_start(out=rows[:, :], in_=rowsD.t.ap()), reads=(rowsD,))
    hmb = P.sb([128, 1], F32, "hmb")
    P.dma("sp", hmb, lambda q: q.dma_start(out=hmb[:, :], in_=hm.t.ap()), reads=(hm,))
    wsb = P.sb([128, 512], BF16, "wsb")
    wsf = P.sb([128, 512], F32, "wsf")
    P.dma("sp", wsf, lambda q: q.dma_start(out=wsf[:, :], in_=wsT.t.ap()), reads=(wsT,))
    P.op("pool", lambda g: g.affine_select(out=wsf[:, :].rearrange("p (g t) -> p g t", g=4),
                                           in_=wsf[:, :].rearrange("p (g t) -> p g t", g=4),
                                           pattern=[[0, 4], [1, 128]], compare_op=ALU.is_ge, fill=0.0,
                                           base=0, channel_multiplier=-1),
         reads=(wsf,), writes=(wsf,))
    P.op("dve", lambda v: v.tensor_copy(out=wsb[:, :], in_=wsf[:, :]), reads=(wsf,), writes=(wsb,))
    bsb = P.sb([1, 512], BF16, "bsb")
    P.dma("pool", bsb, lambda q: q.dma_start(out=bsb[:, :], in_=bsr.t.ap()), reads=(bsr,))
    onesb = P.sb([128, 128], BF16, "onesb")
    P.op("pool", lambda g: g.memset(onesb[:, :], 1.0), writes=(onesb,))
    wad = P.sb([128, KC * 1280], BF16, "wad")
    P.dma("pool", wad, lambda q: q.dma_start(out=wad[:, :], in_=wAD.t.ap()), reads=(wAD,))
    wadv = wad.t.ap().rearrange("p (kc c) -> p kc c", kc=KC)
    wout = [P.sb([128, 2 * D], BF16, f"wout{b}") for b in range(4)]
    for b in range(4):
        P.dma("pool", wout[b], lambda q, b=b: q.dma_start(out=wout[b][:, :], in_=wOut.t.ap()[b]), reads=(wOut,))

    NSLOT = 4
    ring = [P.sb([128, 4096], BF16, f"ring{i}") for i in range(NSLOT)]
    stream = []
    for oc in range(8):
        stream.append((wG, oc, KC * 512))
    for i in range(2):
        stream.append((wO, i, KC * 512))
    for jj in range(11):
        stream.append((wUp, jj, KC * 512))
    for oc in range(8):
        stream.append((wDn, oc, NJ * 128))
    state = {"issued": 0, "consumed": 0}

    def issue_block():
        i = state["issued"]
        src, idx, n = stream[i % len(stream)]
        slot = ring[i % NSLOT]
        P.dma("pool", slot, lambda q: q.dma_start(out=slot[:, 0:n], in_=src.t.ap()[idx]), reads=(src,))
        state["issued"] += 1

    def next_block(total_blocks):
        while state["issued"] < min(state["consumed"] + NSLOT - 1, total_blocks):
            issue_block()
        if state["issued"] <= state["consumed"]:
            issue_block()
        slot = ring[state["consumed"] % NSLOT]
        state["consumed"] += 1
        return slot

    xt = P.sb([128, KC, W], F32, "xt")
    h = P.sb([128, KC, W], BF16, "h")
    act = P.sb([128, NJ, W], BF16, "act")
    merged = P.sb([128, KC, W], F32, "merged")
    mb = P.sb([128, KC, W], BF16, "mb")
    ybt = P.sb([128, 4, W], BF16, "ybt")
    ycand = [P.sb([128, 4, W], BF16, f"ycand{i}") for i in range(2)]
    ya = P.sb([128, 2, W], BF16, "ya")
    yd = P.sb([128, 2, W], BF16, "yd")
    ug = P.sb([128, 2, W], F32, "ug")
    cxh = P.sb([128, 2, W + 2], F32, "cxh")
    uh = P.sb([128, 44, 2], F32, "uh")
    uc = [P.sb([128, W + 2], F32, f"uc{i}") for i in range(2)]
    tmp = [P.sb([128, W], F32, f"tmp{i}") for i in range(6)]
    rstd = P.sb([128, W], F32, "rstd")
    lnt = P.sb([128, W], F32, "lnt")
    vtk = [P.sb([128, 256], F32, f"vtk{i}") for i in range(4)]
    vnb = P.sb([128, 256], BF16, "vnb")
    st = P.sb([128, 8], F32, "st")
    banks = [P.ps([128, 512], F32, f"bank{i}") for i in range(8)]
    bstate = {"i": 0}

    def bank():
        b = banks[bstate["i"] % 8]
        bstate["i"] += 1
        return b

    P.op("pool", lambda g: g.memset(cxh[:, :, :], 0.0), writes=(cxh,))
    P.op("pool", lambda g: g.memset(uh[:, :, :], 0.0), writes=(uh,))

    def rmsnorm(w, gcol, out_buf, out_f32=False):
        P.op("dve", lambda v: v.tensor_tensor(out=act[:, 0:KC, 0:w], in0=xt[:, :, 0:w], in1=xt[:, :, 0:w], op=ALU.mult),
             reads=(xt,), writes=(act,))
        pb = bank()
        mm_group(P, pb[:, 0:w], pb, [(onesb[:, :], act[:, kc, 0:w]) for kc in range(KC)], reads=(onesb, act))
        P.op("act", lambda a: a.activation(out=lnt[:, 0:w], in_=pb[:, 0:w], func=AF.Ln, bias=RMS_EPS, scale=1.0 / D),
             reads=(pb,), writes=(lnt,))
        P.op("act", lambda a: a.activation(out=rstd[:, 0:w], in_=lnt[:, 0:w], func=AF.Exp, scale=-0.5),
             reads=(lnt,), writes=(rstd,))
        for kc in range(KC):
            e = "dve"
            P.op(e, lambda v, kc=kc: v.scalar_tensor_tensor(out=out_buf[:, kc, 0:w], in0=xt[:, kc, 0:w],
                                                            scalar=vec[:, gcol + kc:gcol + kc + 1], in1=rstd[:, 0:w],
                                                            op0=ALU.mult, op1=ALU.mult),
                 reads=(xt, vec, rstd), writes=(out_buf,))

    ntiles = NT // W
    total_blocks = len(stream) * ntiles

    def tile(off, w, is_halo, out_off):
        if L == 0:
            xsrc = io["xTh"]
            xv = xsrc.t.ap().rearrange("(kc p) t -> p kc t", p=128)
            P.dma("sp", xt, lambda q: q.dma_start(out=xt[:, :, 0:w], in_=xv[:, :, off:off + w]), reads=(xsrc,))
        elif not is_halo:
            o_ = off - HALO
            for kc in range(KC):
                xb_ = io["xok"][kc][o_ // CH]
                P.dma("sp", xt, lambda q, kc=kc, xb_=xb_: q.dma_start(out=xt[:, kc, 0:w], in_=xb_.t.ap()[:, o_ % CH:o_ % CH + w]), reads=(xb_,))
        else:
            for sgi in range(4):
                for kc in range(KC):
                    xb_ = io["xgk"][kc][NT // CH - 1]
                    P.dma("sp", merged, lambda q, kc=kc, xb_=xb_, sgi=sgi: q.dma_start(out=merged[:, kc, 0:HALO], in_=xb_.t.ap()[sgi * 128:(sgi + 1) * 128, CH - HALO:CH]), reads=(xb_,))
                if sgi == 0:
                    P.op("dve", lambda v: v.tensor_scalar(out=xt[:, :, 0:HALO], in0=merged[:, :, 0:HALO], scalar1=ohb[:, 4:5], scalar2=None, op0=ALU.mult),
                         reads=(merged, ohb), writes=(xt,))
                else:
                    P.op("dve", lambda v, sgi=sgi: v.scalar_tensor_tensor(out=xt[:, :, 0:HALO], in0=merged[:, :, 0:HALO], scalar=ohb[:, 4 + sgi:5 + sgi], in1=xt[:, :, 0:HALO], op0=ALU.mult, op1=ALU.add),
                         reads=(merged, ohb, xt), writes=(xt,))
        for sgi in range(4):
            yc_ = ycand[sgi % 2]
            t0_ = sgi * NT + off - HALO
            if t0_ < 0:
                P.op("dve", lambda v, yc_=yc_: v.memset(yc_[:, :, 0:w], 0.0), writes=(yc_,))
            else:
                gb_ = io["ygk"][t0_ // CH]
                gv_ = gb_.t.ap().rearrange("(h p) t -> p h t", p=128)
                P.dma("pool", yc_, lambda q, yc_=yc_, gv_=gv_, t0_=t0_: q.dma_start(out=yc_[:, :, 0:w], in_=gv_[:, :, t0_ % CH:t0_ % CH + w]), reads=(gb_,))
            if sgi == 0:
                P.op("dve", lambda v, yc_=yc_: v.tensor_scalar(out=ybt[:, :, 0:w], in0=yc_[:, :, 0:w], scalar1=ohb[:, 0:1], scalar2=None, op0=ALU.mult),
                     reads=(yc_, ohb), writes=(ybt,))
            else:
                P.op("dve", lambda v, yc_=yc_, sgi=sgi: v.scalar_tensor_tensor(out=ybt[:, :, 0:w], in0=yc_[:, :, 0:w], scalar=ohb[:, sgi:sgi + 1], in1=ybt[:, :, 0:w], op0=ALU.mult, op1=ALU.add),
                     reads=(yc_, ohb, ybt), writes=(ybt,))
        rmsnorm(w, 0, h)
        hr = (h, wad)
        for ch in range(2):
            pbg, pcg, pxi = bank(), bank(), bank()
            for pb, c in ((pbg, ch), (pcg, 2 + ch), (pxi, 4 + ch)):
                mm_group(P, pb[:, 0:w], pb, [(wadv[:, kc, c * 128:(c + 1) * 128], h[:, kc, 0:w]) for kc in range(KC)], reads=hr)
            t0, t1 = tmp[0], tmp[1]
            P.op("act", lambda a: a.copy(out=t0[:, 0:w], in_=pxi[:, 0:w]), reads=(pxi,), writes=(t0,))
            P.op("dve", lambda v: v.tensor_tensor(out=cxh[:, ch, 2:2 + w], in0=pcg[:, 0:w], in1=t0[:, 0:w], op=ALU.mult),
                 reads=(pcg, t0), writes=(cxh,))
            c0 = 56 + ch * 3
            P.op("dve", lambda v: v.tensor_scalar(out=t1[:, 0:w], in0=cxh[:, ch, 0:w], scalar1=vec[:, c0:c0 + 1], scalar2=None, op0=ALU.mult),
                 reads=(cxh, vec), writes=(t1,))
            P.op("dve", lambda v: v.scalar_tensor_tensor(out=t1[:, 0:w], in0=cxh[:, ch, 1:1 + w], scalar=vec[:, c0 + 1:c0 + 2], in1=t1[:, 0:w], op0=ALU.mult, op1=ALU.add),
                 reads=(cxh, vec, t1), writes=(t1,))
            P.op("dve", lambda v: v.scalar_tensor_tensor(out=t1[:, 0:w], in0=cxh[:, ch, 2:2 + w], scalar=vec[:, c0 + 2:c0 + 3], in1=t1[:, 0:w], op0=ALU.mult, op1=ALU.add),
                 reads=(cxh, vec, t1), writes=(t1,))
            P.op("dve", lambda v: v.tensor_tensor(out=ya[:, ch, 0:w], in0=pbg[:, 0:w], in1=t1[:, 0:w], op=ALU.mult),
                 reads=(pbg, t1), writes=(ya,))
            if is_halo:
                P.op("dve", lambda v: v.tensor_scalar(out=cxh[:, ch, 0:2], in0=cxh[:, ch, w:w + 2], scalar1=hmb[:, 0:1], scalar2=None, op0=ALU.mult),
                     reads=(cxh, hmb), writes=(cxh,))
            else:
                P.op("dve", lambda v: v.tensor_copy(out=cxh[:, ch, 0:2], in_=cxh[:, ch, w:w + 2]), reads=(cxh,), writes=(cxh,))

        def gelu(dst_ap, dst_buf, src_ap, src_buf, npart, n, t_a, t_b):
            P.op("act", lambda a: a.copy(out=t_a[0:npart, 0:n], in_=src_ap), reads=(src_buf,), writes=(t_a,))
            P.op("dve", lambda g: g.tensor_tensor(out=t_b[0:npart, 0:n], in0=t_a[0:npart, 0:n], in1=t_a[0:npart, 0:n], op=ALU.mult),
                 reads=(t_a,), writes=(t_b,))
            P.op("dve", lambda v: v.tensor_scalar(out=t_b[0:npart, 0:n], in0=t_b[0:npart, 0:n], scalar1=0.044715, scalar2=1.0, op0=ALU.mult, op1=ALU.add),
                 reads=(t_b,), writes=(t_b,))
            P.op("dve", lambda g: g.tensor_tensor(out=t_b[0:npart, 0:n], in0=t_b[0:npart, 0:n], in1=t_a[0:npart, 0:n], op=ALU.mult),
                 reads=(t_a, t_b), writes=(t_b,))
            P.op("act", lambda a: a.activation(out=t_b[0:npart, 0:n], in_=t_b[0:npart, 0:n], func=AF.Sigmoid, scale=2.0 * 0.7978845608028654),
                 reads=(t_b,), writes=(t_b,))
            P.op("dve", lambda v: v.tensor_tensor(out=dst_ap, in0=t_a[0:npart, 0:n], in1=t_b[0:npart, 0:n], op=ALU.mult),
                 reads=(t_a, t_b), writes=(dst_buf,))

        for ch in range(2):
            pu = bank()
            c = 768 + ch * 128
            mm_group(P, pu[:, 0:w], pu, [(wadv[:, kc, c:c + 128], h[:, kc, 0:w]) for kc in range(KC)], reads=hr)
            gelu(ug[:, ch, 0:w], ug, pu[:, 0:w], pu, 128, w, tmp[2], tmp[3])
        for tb in range(w // 128):
            pv = bank()
            mm_group(P, pv[:, 0:256], pv, [(h[:, kc, tb * 128:(tb + 1) * 128], wadv[:, kc, 1024:1280]) for kc in range(KC)], reads=hr)
            gv, sq = vtk[0], vtk[1]
            gelu(gv[:, :], gv, pv[:, 0:256], pv, 128, 256, vtk[2], vtk[3])
            P.op("dve", lambda v: v.tensor_reduce(out=st[:, 0:1], in_=gv[:, :], axis=AX.X, op=ALU.add), reads=(gv,), writes=(st,))
            P.op("dve", lambda g: g.tensor_tensor(out=sq[:, :], in0=gv[:, :], in1=gv[:, :], op=ALU.mult), reads=(gv,), writes=(sq,))
            P.op("dve", lambda v: v.tensor_reduce(out=st[:, 1:2], in_=sq[:, :], axis=AX.X, op=ALU.add), reads=(sq, st), writes=(st,))
            P.op("dve", lambda v: v.tensor_scalar(out=st[:, 2:4], in0=st[:, 0:2], scalar1=1.0 / 256, scalar2=None, op0=ALU.mult), reads=(st,), writes=(st,))
            P.op("dve", lambda v: v.tensor_tensor(out=st[:, 4:5], in0=st[:, 2:3], in1=st[:, 2:3], op=ALU.mult), reads=(st,), writes=(st,))
            P.op("dve", lambda v: v.tensor_tensor(out=st[:, 5:6], in0=st[:, 3:4], in1=st[:, 4:5], op=ALU.subtract), reads=(st,), writes=(st,))
            P.op("act", lambda a: a.activation(out=st[:, 6:7], in_=st[:, 5:6], func=AF.Ln, bias=LN_EPS, scale=1.0), reads=(st,), writes=(st,))
            P.op("act", lambda a: a.activation(out=st[:, 7:8], in_=st[:, 6:7], func=AF.Exp, scale=-0.5), reads=(st,), writes=(st,))
            P.op("dve", lambda v: v.tensor_scalar(out=gv[:, :], in0=gv[:, :], scalar1=st[:, 2:3], scalar2=st[:, 7:8], op0=ALU.subtract, op1=ALU.mult),
                 reads=(gv, st), writes=(gv,))
            P.op("dve", lambda g: g.tensor_tensor(out=gv[:, :], in0=gv[:, :], in1=rows[:, 0:256], op=ALU.mult), reads=(gv, rows), writes=(gv,))
            P.op("dve", lambda v: v.tensor_tensor(out=vnb[:, :], in0=gv[:, :], in1=rows[:, 256:512], op=ALU.add), reads=(gv, rows), writes=(vnb,))
            for g4 in range(4):
                pm = bank()
                po = (g4 % 2) * 64
                mm_group(P, pm[po:po + 64, 0:128], pm,
                         [(vnb[:, g4 * 64:(g4 + 1) * 64], wsb[:, g4 * 128:(g4 + 1) * 128]),
                          (onesb[0:1, 0:64], bsb[0:1, g4 * 128:(g4 + 1) * 128])], reads=(vnb, wsb, onesb, bsb))
                P.op("dve", lambda v, g4=g4, pm=pm, po=po: v.tensor_tensor(out=yd[po:po + 64, g4 // 2, tb * 128:(tb + 1) * 128], in0=pm[po:po + 64, 0:128],
                                                                            in1=ug[po:po + 64, g4 // 2, tb * 128:(tb + 1) * 128], op=ALU.mult),
                     reads=(pm, ug), writes=(yd,))

        for oc in range(8):
            slot = next_block(total_blocks) if not is_halo else None
            if is_halo:
                slot = ring_h
                P.dma("pool", slot, lambda q: q.dma_start(out=slot[:, 0:KC * 512], in_=wG.t.ap()[oc]), reads=(wG,))
            sv = slot.t.ap().rearrange("p (kc c) -> p kc c", kc=KC)
            for br in range(4):
                pg, pp = bank(), bank()
                mm_group(P, pg[:, 0:w], pg, [(sv[:, kc, br * 128:(br + 1) * 128], h[:, kc, 0:w]) for kc in range(KC)], reads=(slot, h))
                wo_v = wout[br].t.ap().rearrange("p (k c) -> p k c", k=2)
                if br == 0:
                    prs = [(wo_v[:, k2, oc * 128:(oc + 1) * 128], ya[:, k2, 0:w]) for k2 in range(2)]
                    rd = (wout[br], ya)
                elif br in (1, 2):
                    po_ = 0 if br == 1 else 64
                    prs = []
                    for hd_ in range(4):
                        wv_ = wout[1 + hd_ // 2].t.ap().rearrange("p (k c) -> p k c", k=2)
                        prs.append((wv_[po_:po_ + 64, hd_ % 2, oc * 128:(oc + 1) * 128], ybt[po_:po_ + 64, hd_, 0:w]))
                    rd = (wout[1], wout[2], ybt)
                else:
                    prs = [(wo_v[:, k2, oc * 128:(oc + 1) * 128], yd[:, k2, 0:w]) for k2 in range(2)]
                    rd = (wout[br], yd)
                mm_group(P, pp[:, 0:w], pp, prs, reads=rd)
                gs = tmp[4]
                gb = 24 + br * 8 + oc
                P.op("act", lambda a, pg=pg, gb=gb: a.activation(out=gs[:, 0:w], in_=pg[:, 0:w], func=AF.Sigmoid, bias=vec[:, gb:gb + 1], scale=1.0),
                     reads=(pg, vec), writes=(gs,))
                if br == 0:
                    P.op("dve", lambda v, pp=pp: v.tensor_tensor(out=merged[:, oc, 0:w], in0=pp[:, 0:w], in1=gs[:, 0:w], op=ALU.mult),
                         reads=(pp, gs), writes=(merged,))
                else:
                    t5 = tmp[5]
                    P.op("dve", lambda v, pp=pp: v.tensor_tensor(out=t5[:, 0:w], in0=pp[:, 0:w], in1=gs[:, 0:w], op=ALU.mult),
                         reads=(pp, gs), writes=(t5,))
                    dst = mb if br == 3 else merged
                    P.op("dve", lambda g, dst=dst: g.tensor_tensor(out=dst[:, oc, 0:w], in0=merged[:, oc, 0:w], in1=t5[:, 0:w], op=ALU.add),
                         reads=(merged, t5), writes=(dst,))

        for i in range(2):
            if is_halo:
                slot = ring_h
                P.dma("pool", slot, lambda q: q.dma_start(out=slot[:, 0:KC * 512], in_=wO.t.ap()[i]), reads=(wO,))
            else:
                slot = next_block(total_blocks)
            sv = slot.t.ap().rearrange("p (kc c) -> p kc c", kc=KC)
            for o4 in range(4):
                oc = i * 4 + o4
                pb = bank()
                mm_group(P, pb[:, 0:w], pb, [(sv[:, kc, o4 * 128:(o4 + 1) * 128], mb[:, kc, 0:w]) for kc in range(KC)], reads=(slot, mb))
                P.op("dve", lambda v, pb=pb, oc=oc: v.tensor_tensor(out=xt[:, oc, 0:w], in0=pb[:, 0:w], in1=xt[:, oc, 0:w], op=ALU.add),
                     reads=(pb, xt), writes=(xt,))

        rmsnorm(w, 8, h)
        for jj in range(11):
            if is_halo:
                slot = ring_h
                P.dma("pool", slot, lambda q: q.dma_start(out=slot[:, 0:KC * 512], in_=wUp.t.ap()[jj]), reads=(wUp,))
            else:
                slot = next_block(total_blocks)
            sv = slot.t.ap().rearrange("p (kc c) -> p kc c", kc=KC)
            for j2 in range(2):
                j = jj * 2 + j2
                cv = []
                for gv_i in range(2):
                    pb = bank()
                    c = (j2 * 2 + gv_i) * 128
                    mm_group(P, pb[:, 0:w], pb, [(sv[:, kc, c:c + 128], h[:, kc, 0:w]) for kc in range(KC)], reads=(slot, h))
                    idx = gv_i * NJ + j
                    u = uc[gv_i]
                    P.op("act", lambda a, pb=pb, u=u: a.copy(out=u[:, 2:2 + w], in_=pb[:, 0:w]), reads=(pb,), writes=(u,))
                    P.op("dve", lambda g, u=u, idx=idx: g.tensor_copy(out=u[:, 0:2], in_=uh[:, idx, :]), reads=(uh, u), writes=(u,))
                    if is_halo:
                        P.op("dve", lambda g, u=u, idx=idx: g.tensor_scalar(out=uh[:, idx, :], in0=u[:, w:w + 2], scalar1=hmb[:, 0:1], scalar2=None, op0=ALU.mult),
                             reads=(u, hmb, uh), writes=(uh,))
                        continue
                    P.op("dve", lambda g, u=u, idx=idx: g.tensor_copy(out=uh[:, idx, :], in_=u[:, w:w + 2]), reads=(u, uh), writes=(uh,))
                    t = tmp[gv_i]
                    P.op("act", lambda a, pb=pb, t=t, idx=idx: a.activation(out=t[:, 0:w], in_=pb[:, 0:w], func=AF.Copy, scale=cvf[:, idx * 3 + 2:idx * 3 + 3]),
                         reads=(pb, cvf), writes=(t,))
                    for tap in (0, 1):
                        P.op("dve", lambda v, u=u, t=t, idx=idx, tap=tap: v.scalar_tensor_tensor(out=t[:, 0:w], in0=u[:, tap:tap + w], scalar=cvf[:, idx * 3 + tap:idx * 3 + tap + 1], in1=t[:, 0:w], op0=ALU.mult, op1=ALU.add),
                             reads=(u, cvf, t), writes=(t,))
                    cv.append(t)
                if is_halo:
                    continue
                sg = tmp[2]
                P.op("act", lambda a: a.activation(out=sg[:, 0:w], in_=cv[0][:, 0:w], func=AF.Sigmoid), reads=(cv[0],), writes=(sg,))
                P.op("dve", lambda v: v.tensor_tensor(out=sg[:, 0:w], in0=sg[:, 0:w], in1=cv[0][:, 0:w], op=ALU.mult), reads=(sg, cv[0]), writes=(sg,))
                P.op("dve", lambda g, j=j: g.tensor_tensor(out=act[:, j, 0:w], in0=sg[:, 0:w], in1=cv[1][:, 0:w], op=ALU.mult), reads=(sg, cv[1]), writes=(act,))
        if is_halo:
            return
        for oc in range(8):
            slot = next_block(total_blocks)
            sv = slot.t.ap().rearrange("p (j c) -> p j c", c=128)
            pb = bank()
            mm_group(P, pb[:, 0:w], pb, [(sv[:, j, :], act[:, j, 0:w]) for j in range(NJ)], reads=(slot, act))
            P.op("dve", lambda v, pb=pb, oc=oc: v.tensor_tensor(out=xt[:, oc, 0:w], in0=pb[:, 0:w], in1=xt[:, oc, 0:w], op=ALU.add),
                 reads=(pb, xt), writes=(xt,))
        if last_layer:
            rmsnorm(w, 16, merged, out_f32=True)
            src = merged
        else:
            src = xt
        if last_out:
            outv = out.t.ap().rearrange("(kc p) t -> p kc t", p=128)
            P.dma("sp", out, lambda q: q.dma_start(out=outv[:, :, out_off:out_off + w], in_=src[:, :, 0:w]), reads=(src,))
        else:
            for kc in range(KC):
                xb_ = io["xok"][kc][out_off // CH]
                P.dma("sp", xb_, lambda q, kc=kc, xb_=xb_: q.dma_start(out=xb_.t.ap()[:, out_off % CH:out_off % CH + w], in_=src[:, kc, 0:w]), reads=(src,))

    ring_h = P.sb([128, 4096], BF16, "ring_h")
    tile(0, HALO, True, None)
    for ti in range(ntiles):
        tile(HALO + ti * W, W, False, ti * W)
        if io.get("after_dense_tile") is not None:
            io["after_dense_tile"](ti)


SB_OFF, RW_OFF, SG_OFF, GATE_OFF = 768, 1536, 2560, 3072


def _kc_layout(w):
    K, C = w.shape
    return np.ascontiguousarray(w.reshape(K // 128, 128, C).transpose(1, 0, 2).reshape(128, (K // 128) * C))


def _pvec(v):
    return np.ascontiguousarray(v.reshape(-1, 128).T)


def pack_dense_weights(inp, l):
    f = np.float32
    w_in = inp["w_in"][l]
    cols = np.concatenate([np.arange(0, 768), np.arange(SG_OFF, SG_OFF + 512)])
    d = {}
    d["wAD"] = _kc_layout(w_in[:, cols])
    wg = w_in[:, GATE_OFF:].reshape(D, 4, 8, 128).transpose(2, 0, 1, 3).reshape(8, D, 512)
    d["wG"] = np.stack([_kc_layout(wg[oc]) for oc in range(8)])
    sbo, rwo = inp["sb_out"][l], inp["rw_out"][l]

    def bc(h0):
        a = np.zeros((128, 2, D), np.float32)
        for k in range(2):
            hd = h0 + k
            a[0:64, k] = sbo[hd * 64:(hd + 1) * 64]
            a[64:128, k] = rwo[hd * 64:(hd + 1) * 64]
        return a.reshape(128, 2 * D)
    d["wOut"] = np.stack([_kc_layout(inp["sc_out"][l]), bc(0), bc(2), _kc_layout(inp["sg_out"][l])])
    d["wO"] = np.stack([_kc_layout(inp["w_o"][l][:, i * 512:(i + 1) * 512]) for i in range(2)])
    wu = inp["w_up"][l].reshape(D, 2, 11, 2, 128).transpose(2, 0, 3, 1, 4).reshape(11, D, 512)
    d["wUp"] = np.stack([_kc_layout(wu[jj]) for jj in range(11)])
    d["wDn"] = np.stack([_kc_layout(inp["w_down"][l][:, oc * 128:(oc + 1) * 128]) for oc in range(8)])
    vec = np.zeros((128, 64), f)
    vec[:, 0:8] = _pvec(inp["mix_norm_g"][l])
    vec[:, 8:16] = _pvec(inp["ffn_norm_g"][l])
    vec[:, 16:24] = _pvec(inp["final_norm_g"])
    vec[:, 24:56] = _pvec(inp["gate_b"][l])
    cw = inp["sc_conv_w"][l]
    for ch in range(2):
        for tap in range(3):
            vec[:, 56 + ch * 3 + tap] = cw[tap, ch * 128:(ch + 1) * 128]
    d["vecs"] = vec
    fc = inp["ffn_conv_w"][l]
    d["convF"] = np.ascontiguousarray(fc.reshape(3, 44, 128).transpose(2, 1, 0).reshape(128, 132))
    rows = np.zeros((128, 512), f)
    rows[:, 0:256] = inp["sg_ln_g"][l][None, :]
    rows[:, 256:512] = inp["sg_ln_b"][l][None, :]
    d["rowsD"] = rows
    d["wsT"] = np.ascontiguousarray(inp["sg_w"][l].transpose(2, 0, 1).reshape(128, 512))
    d["bsr"] = np.ascontiguousarray(inp["sg_b"][l].reshape(1, 512))
    return {k: np.ascontiguousarray(v, dtype=f) for k, v in d.items()}


def dense_core_inputs(xT_b, ybc_b, seg, NT):
    def halo(a):
        o = np.zeros((a.shape[0], HALO + NT), np.float32)
        s = seg * NT
        if seg > 0:
            o[:, :] = a[:, s - HALO:s + NT]
        else:
            o[:, HALO:] = a[:, 0:NT]
        return o
    hm = np.full((128, 1), 0.0 if seg == 0 else 1.0, np.float32)
    return {"xT": halo(xT_b), "ybc": halo(ybc_b), "hm": hm}


def emit_seqmix(P, T, L, io, do_attn=True, do_rwkv=True, interleave=True):
    EI = "ExternalInput"
    W = 512
    NTL = T // W
    NT = T // 4
    sfx = f"_{L}"
    wS = P.dram("wS" + sfx, [128, KC * 640], F32, EI)
    vecS = P.dram("vecS" + sfx, [128, 32], F32, EI)
    w2d = P.dram("w2h" + sfx, [64, 64], F32, EI)
    a2d = P.dram("a2h" + sfx, [64, 64], F32, EI)
    g2d = P.dram("g2h" + sfx, [128, 64], F32, EI)
    yxk = io["yxk"]
    CH = 1024

    vec = P.sb([128, 32], F32, "vec")
    P.dma("sp", vec, lambda q: q.dma_start(out=vec[:, :], in_=vecS.t.ap()), reads=(vecS,))
    ws = P.sb([128, KC * 640], BF16, "ws")
    P.dma("pool", ws, lambda q: q.dma_start(out=ws[:, :], in_=wS.t.ap()), reads=(wS,))
    wsv = ws.t.ap().rearrange("p (kc c) -> p kc c", kc=KC)
    w2h = P.sb([64, 64], F32, "w2h_s"); a2h = P.sb([64, 64], F32, "a2h_s"); g2h = P.sb([128, 64], F32, "g2h_s")
    P.dma("sp", w2h, lambda q: q.dma_start(out=w2h[:, :], in_=w2d.t.ap()), reads=(w2d,))
    P.dma("sp", a2h, lambda q: q.dma_start(out=a2h[:, :], in_=a2d.t.ap()), reads=(a2d,))
    P.dma("sp", g2h, lambda q: q.dma_start(out=g2h[:, :], in_=g2d.t.ap()), reads=(g2d,))

    onesb = P.sb([128, 128], BF16, "onesb")
    P.op("pool", lambda g: g.memset(onesb[:, :], 1.0), writes=(onesb,))
    onesf = P.sb([128, 128], F32, "onesf")
    P.op("pool", lambda g: g.memset(onesf[:, :], 1.0), writes=(onesf,))
    identf = P.sb([128, 128], F32, "identf")
    P.op("pool", lambda g: g.affine_select(out=identf[:, :], in_=onesf[:, 0:128], pattern=[[-1, 128]], compare_op=ALU.is_equal,
                                           fill=0.0, base=0, channel_multiplier=1), reads=(onesf,), writes=(identf,))

    qTt = [P.sb([64, W], BF16, f"qT{i}") for i in range(NTL)]
    kTt = [P.sb([64, W], BF16, f"kT{i}") for i in range(NTL)]
    vtt = [P.sb([128, 4, 64], BF16, f"vt{i}") for i in range(NTL)]

    xt = P.sb([128, KC, W], F32, "xt")
    h = P.sb([128, KC, W], BF16, "h")
    sq = h
    rstd = P.sb([128, W], F32, "rstd")
    lnt = rstd
    banks = [P.ps([128, 512], F32, f"bank{i}") for i in range(8)]
    bstate = {"i": 0}

    def bank():
        b = banks[bstate["i"] % 4]
        bstate["i"] += 1
        return b

    if do_rwkv:
        rmask = P.sb([64, W], F32, "rmask")
        P.op("pool", lambda g: g.memset(rmask[:, :], 1.0), writes=(rmask,))
        P.op("pool", lambda g: g.memset(rmask[:, :].rearrange("p (c i) -> p c i", i=64)[:, :, 0:1], 0.0), writes=(rmask,))
        maskS = P.sb([128, 128], F32, "maskS")
        mask2 = P.sb([128, 256], F32, "mask2")
        P.op("pool", lambda g: g.affine_select(out=maskS[:, :], in_=onesf[:, 0:128], pattern=[[-1, 128]], compare_op=ALU.is_ge, fill=0.0, base=-1, channel_multiplier=1),
             reads=(onesf,), writes=(maskS,))
        P.op("pool", lambda g: g.memset(maskS[64:128, 0:64], 0.0), writes=(maskS,))
        P.op("pool", lambda g: g.affine_select(out=mask2[:, 0:128], in_=onesf[:, 0:128], pattern=[[1, 128]], compare_op=ALU.is_ge, fill=0.0, base=-1, channel_multiplier=-1),
             reads=(onesf,), writes=(mask2,))
        P.op("pool", lambda g: g.affine_select(out=mask2[:, 128:256], in_=onesf[:, 0:128], pattern=[[1, 128]], compare_op=ALU.is_ge, fill=0.0, base=0, channel_multiplier=-1),
             reads=(onesf,), writes=(mask2,))
        P.op("pool", lambda g: g.memset(mask2[0:64, 64:128], 0.0), writes=(mask2,))
        P.op("pool", lambda g: g.memset(mask2[0:64, 192:256], 0.0), writes=(mask2,))
        zb = P.sb([64, 5, W + 1], F32, "zb")
        zg = P.sb([128, W + 1], F32, "zg")
        P.op("pool", lambda g: g.memset(zb[:, :, :], 0.0), writes=(zb,))
        P.op("pool", lambda g: g.memset(zg[:, :], 0.0), writes=(zg,))
        dbuf = P.sb([64, 5, W], F32, "dbuf")
        zz = dbuf
        dg = P.sb([128, W], F32, "dg")
        sgx = dg
        m_ = {n: P.sb([64, W], F32, n) for n in ("lw", "asg", "kk0", "s1", "s2", "s3", "kmod", "bvec", "bonus", "cl", "gi", "ginv", "gmap")}
        for alias, tgt in (("sgw", "lw"), ("kksq", "s1"), ("tt", "s1"), ("rk", "s1"), ("ssc", "s2"), ("rs", "s2"), ("tw", "s3"), ("cm", "s3"),
                           ("gprev", "s3"), ("kkn", "kk0"), ("BT", "bvec"), ("KT", "kmod"), ("ynT", "ginv"), ("yo", "ginv")):
            m_[alias] = m_[tgt]
        ART = P.sb([64, 4, 2, 128], F32, "ART")
        NSET = 2
        sets = []
        for si in range(NSET):
            sets.append(dict(
                tok=P.sb([128, 192], BF16, f"tok{si}"), abT=P.sb([128, 256], BF16, f"abT{si}"), akT=P.sb([128, 256], BF16, f"akT{si}"),
                Lm=P.sb([128, 128], BF16, f"Lm{si}"), Xb=[P.sb([128, 128], BF16, f"Xb{si}_{i}") for i in range(2)],
                PPb=[P.sb([128, 256], BF16, f"PP{si}_{i}") for i in range(2)], LMb=[P.sb([64, 64], F32, f"LM{si}_{i}") for i in range(2)],
                N0Gb=[P.sb([64, 64], F32, f"N0G{si}_{i}") for i in range(2)], RpT=P.sb([64, 128], F32, f"RpT{si}"),
                ysb=P.sb([128, 64], F32, f"ysb{si}"), ysq=P.sb([128, 64], F32, f"ysq{si}"), yn=P.sb([128, 64], F32, f"yn{si}"),
                gst=P.sb([128, 8], F32, f"gst{si}")))
        NZ = 8
        Zb = [P.sb([64, 64], F32, f"Z{i}") for i in range(NZ)]
        zst = {"i": 0}
        P.op("pool", lambda g: g.memset(Zb[0][:, :], 0.0), writes=(Zb[0],))

    def rmsnorm():
        P.op("dve", lambda v: v.tensor_tensor(out=sq[:, :, :], in0=xt[:, :, :], in1=xt[:, :, :], op=ALU.mult), reads=(xt,), writes=(sq,))
        pb = bank()
        mm_group(P, pb[:, :], pb, [(onesb[:, :], sq[:, kc, :]) for kc in range(KC)], reads=(onesb, sq))
        P.op("act", lambda a: a.activation(out=lnt[:, :], in_=pb[:, :], func=AF.Ln, bias=RMS_EPS, scale=1.0 / D), reads=(pb,), writes=(lnt,))
        P.op("act", lambda a: a.activation(out=rstd[:, :], in_=lnt[:, :], func=AF.Exp, scale=-0.5), reads=(lnt,), writes=(rstd,))
        for kc in range(KC):
            P.op("dve", lambda v, kc=kc: v.scalar_tensor_tensor(out=h[:, kc, :], in0=xt[:, kc, :], scalar=vec[:, kc:kc + 1], in1=rstd[:, :], op0=ALU.mult, op1=ALU.mult),
                 reads=(xt, vec, rstd), writes=(h,))

    def proj(col, m):
        pb = bank()
        mm_group(P, pb[0:m, :], pb, [(wsv[:, kc, col:col + m], h[:, kc, :]) for kc in range(KC)], reads=(ws, h))
        return pb

    def V(n):
        return m_[n]

    def ew_tt(e, o, a, b, op, rd, wr):
        P.op(e, lambda v: v.tensor_tensor(out=o, in0=a, in1=b, op=op), reads=rd, writes=wr)

    def rwkv_tile(ti):
        c0 = ti * W
        for m in range(5):
            pb_ = proj(192 + 64 * m, 64)
            P.op("act", lambda a, m=m, pb_=pb_: a.copy(out=zb[:, m, 1:W + 1], in_=pb_[0:64, :]), reads=(pb_,), writes=(zb,))
        pg_ = proj(512, 128)
        P.op("act", lambda a: a.copy(out=zg[:, 1:W + 1], in_=pg_[:, :]), reads=(pg_,), writes=(zg,))
        ew_tt("dve", dbuf[:, :, :], zb[:, :, 0:W], zb[:, :, 1:W + 1], ALU.subtract, (zb,), (dbuf,))
        for m in range(5):
            P.op("dve", lambda v, m=m: v.scalar_tensor_tensor(out=zz[:, m, :], in0=dbuf[:, m, :], scalar=vec[0:64, 9 + m:10 + m], in1=zb[:, m, 1:W + 1], op0=ALU.mult, op1=ALU.add),
                 reads=(dbuf, vec, zb), writes=(zz,))
        P.op("dve", lambda v: v.tensor_copy(out=zb[:, :, 0:1], in_=zb[:, :, W:W + 1]), reads=(zb,), writes=(zb,))
        ew_tt("dve", dg[:, :], zg[:, 0:W], zg[:, 1:W + 1], ALU.subtract, (zg,), (dg,))
        P.op("dve", lambda v: v.scalar_tensor_tensor(out=dg[:, :], in0=dg[:, :], scalar=vec[:, 8:9], in1=zg[:, 1:W + 1], op0=ALU.mult, op1=ALU.add),
             reads=(dg, vec, zg), writes=(dg,))
        P.op("dve", lambda v: v.tensor_copy(out=zg[:, 0:1], in_=zg[:, W:W + 1]), reads=(zg,), writes=(zg,))
        yield
        Rm, Km, Vm, XW, XA = (zz[:, i, :] for i in range(5))
        P.op("act", lambda a: a.activation(out=V("tw")[:, :], in_=XW, func=AF.Tanh), reads=(zz,), writes=(V("tw"),))
        P.op("act", lambda a: a.activation(out=sgx[:, :], in_=dg[:, :], func=AF.Sigmoid), reads=(dg,), writes=(sgx,))
        pw = bank()
        mm_group(P, pw[0:64, :], pw, [(w2h[:, :], V("tw")[:, :])], reads=(w2h, V("tw")))
        pa = bank()
        mm_group(P, pa[0:64, :], pa, [(a2h[:, :], XA)], reads=(a2h, zz))
        pgm = bank()
        mm_group(P, pgm[0:64, :], pgm, [(g2h[:, :], sgx[:, :])], reads=(g2h, sgx))
        P.op("act", lambda a: a.activation(out=V("sgw")[:, :], in_=pw[0:64, :], func=AF.Sigmoid, bias=vec[0:64, 14:15], scale=1.0), reads=(pw, vec), writes=(V("sgw"),))
        P.op("act", lambda a: a.activation(out=V("asg")[:, :], in_=pa[0:64, :], func=AF.Sigmoid, bias=vec[0:64, 15:16], scale=1.0), reads=(pa, vec), writes=(V("asg"),))
        P.op("act", lambda a: a.copy(out=V("gmap")[:, :], in_=pgm[0:64, :]), reads=(pgm,), writes=(V("gmap"),))
        P.op("dve", lambda v: v.tensor_scalar(out=V("lw")[:, :], in0=V("sgw")[:, :], scalar1=-DECAY_SCALE, scalar2=None, op0=ALU.mult), reads=(V("sgw"),), writes=(V("lw"),))
        P.op("dve", lambda v: v.tensor_scalar(out=V("kk0")[:, :], in0=Km, scalar1=vec[0:64, 16:17], scalar2=None, op0=ALU.mult), reads=(zz, vec), writes=(V("kk0"),))
        ew_tt("dve", V("kksq")[:, :], V("kk0")[:, :], V("kk0")[:, :], ALU.mult, (V("kk0"),), (V("kksq"),))
        yield
        pss = bank()
        mm_group(P, pss[0:64, :], pss, [(onesf[0:64, 0:64], V("kksq")[:, :])], reads=(onesf, V("kksq")))
        P.op("dve", lambda v: v.tensor_scalar(out=V("ssc")[:, :], in0=pss[0:64, :], scalar1=1e-24, scalar2=None, op0=ALU.max), reads=(pss,), writes=(V("ssc"),))
        P.op("dve", lambda v: v.tensor_scalar(out=V("tt")[:, :], in0=V("asg")[:, :], scalar1=-1.0, scalar2=vec[0:64, 17:18], op0=ALU.add, op1=ALU.mult), reads=(V("asg"), vec), writes=(V("tt"),))
        P.op("dve", lambda v: v.scalar_tensor_tensor(out=V("kmod")[:, :], in0=V("tt")[:, :], scalar=1.0, in1=Km, op0=ALU.add, op1=ALU.mult), reads=(V("tt"), zz), writes=(V("kmod"),))
        P.op("dve", lambda v: v.scalar_tensor_tensor(out=V("rk")[:, :], in0=Rm, scalar=vec[0:64, 18:19], in1=V("kmod")[:, :], op0=ALU.mult, op1=ALU.mult), reads=(zz, vec, V("kmod")), writes=(V("rk"),))
        pbn = bank()
        mm_group(P, pbn[0:64, :], pbn, [(onesf[0:64, 0:64], V("rk")[:, :])], reads=(onesf, V("rk")))
        ew_tt("dve", V("bonus")[:, :], pbn[0:64, :], Vm, ALU.mult, (pbn, zz), (V("bonus"),))
        P.op("dve", lambda v: v.tensor_tensor_scan(out=V("cl")[:, :], data0=rmask[:, :], data1=V("lw")[:, :], initial=0.0, op0=ALU.mult, op1=ALU.add),
             reads=(rmask, V("lw")), writes=(V("cl"),))
        ew_tt("dve", V("cm")[:, :], V("cl")[:, :], V("lw")[:, :], ALU.subtract, (V("cl"), V("lw")), (V("cm"),))
        yield
        P.op("act", lambda a: a.activation(out=V("rs")[:, :], in_=V("ssc")[:, :], func=AF.Ln), reads=(V("ssc"),), writes=(V("rs"),))
        P.op("act", lambda a: a.activation(out=V("rs")[:, :], in_=V("rs")[:, :], func=AF.Exp, scale=-0.5), reads=(V("rs"),), writes=(V("rs"),))
        P.op("act", lambda a: a.activation(out=V("gi")[:, :], in_=V("cl")[:, :], func=AF.Exp), reads=(V("cl"),), writes=(V("gi"),))
        P.op("act", lambda a: a.activation(out=V("ginv")[:, :], in_=V("cl")[:, :], func=AF.Exp, scale=-1.0), reads=(V("cl"),), writes=(V("ginv"),))
        P.op("act", lambda a: a.activation(out=V("gprev")[:, :], in_=V("cm")[:, :], func=AF.Exp), reads=(V("cm"),), writes=(V("gprev"),))
        ew_tt("dve", V("kkn")[:, :], V("kk0")[:, :], V("rs")[:, :], ALU.mult, (V("kk0"), V("rs")), (V("kkn"),))
        ew_tt("dve", V("bvec")[:, :], V("kkn")[:, :], V("asg")[:, :], ALU.mult, (V("kkn"), V("asg")), (V("bvec"),))
        a4 = lambda ap: ap.rearrange("p (a b) -> p a b", b=128)
        P.op("dve", lambda v: v.scalar_tensor_tensor(out=ART[:, :, 0, :], in0=a4(V("kkn")[:, :]), scalar=-1.0, in1=a4(V("gprev")[:, :]), op0=ALU.mult, op1=ALU.mult),
             reads=(V("kkn"), V("gprev")), writes=(ART,))
        ew_tt("dve", ART[:, :, 1, :], a4(Rm), a4(V("gi")[:, :]), ALU.mult, (zz, V("gi")), (ART,))
        ew_tt("dve", V("BT")[:, :], V("bvec")[:, :], V("ginv")[:, :], ALU.mult, (V("bvec"), V("ginv")), (V("BT"),))
        ew_tt("dve", V("KT")[:, :], V("kmod")[:, :], V("ginv")[:, :], ALU.mult, (V("kmod"), V("ginv")), (V("KT"),))
        yield
        BT, KT, gi = V("BT"), V("KT"), V("gi")
        def pair_gen(pr, S):
            tok, abT, akT, Lm, Xb, PPb, LMb, N0Gb, RpT = S["tok"], S["abT"], S["akT"], S["Lm"], S["Xb"], S["PPb"], S["LMb"], S["N0Gb"], S["RpT"]
            ysb, ysq, yn, gst = S["ysb"], S["ysq"], S["yn"], S["gst"]
            s0 = pr * 128
            ATp, RTp = ART[:, pr, 0, :], ART[:, pr, 1, :]
            ARp = ART[:, pr, :, :].rearrange("p a b -> p (a b)")
            BTp, KTp, VTp = BT[:, s0:s0 + 128], KT[:, s0:s0 + 128], zz[:, 2, s0:s0 + 128]
            i64 = identf[0:64, 0:64]
            pT = bank()
            mm_group(P, pT[:, 0:64], pT, [(BTp, i64)], reads=(BT, identf))
            mm_group(P, pT[:, 64:128], pT, [(KTp, i64)], reads=(KT, identf))
            mm_group(P, pT[:, 128:192], pT, [(VTp, i64)], reads=(zz, identf))
            P.op("act", lambda a, pT=pT: a.copy(out=tok[:, :], in_=pT[:, 0:192]), reads=(pT,), writes=(tok,))
            yield
            pA = bank()
            mm_group(P, pA[:, 0:256], pA, [(BTp, ARp)], reads=(BT, ART))
            ew_tt("dve", abT[:, :], pA[:, 0:256], mask2[:, :], ALU.mult, (pA, mask2), (abT,))
            pK = bank()
            mm_group(P, pK[:, 0:256], pK, [(KTp, ARp)], reads=(KT, ART))
            ew_tt("dve", akT[:, :], pK[:, 0:256], mask2[:, :], ALU.mult, (pK, mask2), (akT,))
            pL = bank()
            mm_group(P, pL[:, 0:128], pL, [(ATp, BTp)], reads=(ART, BT))
            ew_tt("dve", Lm[:, :], pL[:, 0:128], maskS[:, :], ALU.mult, (pL, maskS), (Lm,))
            yield
            pX = bank()
            mm_group(P, pX[:, 0:64], pX, [(ATp, i64)], reads=(ART, identf))
            mm_group(P, pX[:, 64:128], pX, [(akT[:, 0:128], tok[:, 128:192])], reads=(akT, tok))
            X = Xb[0]
            P.op("act", lambda a, pX=pX, X=X: a.copy(out=X[:, :], in_=pX[:, 0:128]), reads=(pX,), writes=(X,))
            yield
            Pk_ap, PkT_ap, Pk_b, PkT_b = Lm[:, :], abT[:, 0:128], Lm, abT
            for it in range(6):
                pXn = bank()
                mm_group(P, pXn[:, 0:128], pXn, [(PkT_ap, X[:, :])], reads=(PkT_b, X))
                Xn = Xb[(it + 1) % 2]
                ew_tt("dve", Xn[:, :], pXn[:, 0:128], X[:, :], ALU.add, (pXn, X), (Xn,))
                yield
                if it < 5:
                    pP = bank()
                    mm_group(P, pP[:, 0:128], pP, [(PkT_ap, Pk_ap)], reads=(PkT_b, Pk_b))
                    mm_group(P, pP[:, 128:256], pP, [(Pk_ap, PkT_ap)], reads=(PkT_b, Pk_b))
                    PPn = PPb[it % 2]
                    P.op("act", lambda a, pP=pP, PPn=PPn: a.copy(out=PPn[:, :], in_=pP[:, 0:256]), reads=(pP,), writes=(PPn,))
                    Pk_ap, PkT_ap, Pk_b, PkT_b = PPn[:, 0:128], PPn[:, 128:256], PPn, PPn
                X = Xn
            Zs = []
            for hh in range(2):
                hs = slice(hh * 64, hh * 64 + 64)
                pMN = bank()
                mm_group(P, pMN[0:64, 0:64], pMN, [(X[hs, 0:64], tok[hs, 0:64])], reads=(X, tok))
                mm_group(P, pMN[0:64, 64:128], pMN, [(tok[hs, 0:64], X[hs, 64:128]), (tok[hs, 64:128], tok[hs, 128:192])], reads=(X, tok))
                ge = gi[:, s0 + hh * 64 + 63:s0 + hh * 64 + 64]
                LM, N0G = LMb[hh], N0Gb[hh]
                ew_tt("dve", LM[:, :], pMN[0:64, 0:64], i64, ALU.add, (pMN, identf), (LM,))
                P.op("dve", lambda v, pMN=pMN, N0G=N0G, ge=ge: v.tensor_scalar(out=N0G[:, :], in0=pMN[0:64, 64:128], scalar1=ge, scalar2=None, op0=ALU.mult),
                     reads=(pMN, gi), writes=(N0G,))
                Zc = Zb[zst["i"] % NZ]
                Zn = Zb[(zst["i"] + 1) % NZ]
                zst["i"] += 1
                pZ = bank()
                mm_group(P, pZ[0:64, 0:64], pZ, [(LM[:, :], Zc[:, :])], reads=(LM, Zc))
                P.op("dve", lambda v, pZ=pZ, Zn=Zn, N0G=N0G, ge=ge: v.scalar_tensor_tensor(out=Zn[:, :], in0=pZ[0:64, 0:64], scalar=ge, in1=N0G[:, :], op0=ALU.mult, op1=ALU.add),
                     reads=(pZ, gi, N0G), writes=(Zn,))
                Zs.append(Zc)
            pR = bank()
            mm_group(P, pR[0:64, 0:128], pR, [(X[:, 0:64], abT[:, 128:256])], reads=(X, abT))
            ew_tt("dve", RpT[:, :], pR[0:64, 0:128], RTp, ALU.add, (pR, ART), (RpT,))
            yield
            pY = bank()
            P.op("pe", lambda t, pY=pY, X=X: t.matmul(pY[:, 0:64], lhsT=abT[:, 128:256], rhs=X[:, 64:128], start=True, stop=False), reads=(abT, X), writes=(pY,), inc=False)
            P.op("pe", lambda t, pY=pY: t.matmul(pY[:, 0:64], lhsT=akT[:, 128:256], rhs=tok[:, 128:192], start=False, stop=False), reads=(akT, tok), writes=(pY,), inc=False)
            P.op("pe", lambda t, pY=pY: t.matmul(pY[0:64, 0:64], lhsT=RpT[:, 0:64], rhs=Zs[0][:, :], start=False, stop=False), reads=(RpT, Zs[0]), writes=(pY,), inc=False)
            P.op("pe", lambda t, pY=pY: t.matmul(pY[64:128, 0:64], lhsT=RpT[:, 64:128], rhs=Zs[1][:, :], start=False, stop=True), reads=(RpT, Zs[1]), writes=(pY,))
            P.op("act", lambda a, pY=pY: a.copy(out=ysb[:, :], in_=pY[:, 0:64]), reads=(pY,), writes=(ysb,))
            yield
            P.op("dve", lambda v: v.tensor_reduce(out=gst[:, 0:1], in_=ysb[:, :], axis=AX.X, op=ALU.add), reads=(ysb,), writes=(gst,))
            ew_tt("dve", ysq[:, :], ysb[:, :], ysb[:, :], ALU.mult, (ysb,), (ysq,))
            P.op("dve", lambda v: v.tensor_reduce(out=gst[:, 1:2], in_=ysq[:, :], axis=AX.X, op=ALU.add), reads=(ysq, gst), writes=(gst,))
            P.op("dve", lambda v: v.tensor_scalar(out=gst[:, 2:4], in0=gst[:, 0:2], scalar1=1.0 / 64, scalar2=None, op0=ALU.mult), reads=(gst,), writes=(gst,))
            ew_tt("dve", gst[:, 4:5], gst[:, 2:3], gst[:, 2:3], ALU.mult, (gst,), (gst,))
            ew_tt("dve", gst[:, 5:6], gst[:, 3:4], gst[:, 4:5], ALU.subtract, (gst,), (gst,))
            P.op("act", lambda a: a.activation(out=gst[:, 6:7], in_=gst[:, 5:6], func=AF.Ln, bias=GN_EPS, scale=1.0), reads=(gst,), writes=(gst,))
            P.op("act", lambda a: a.activation(out=gst[:, 7:8], in_=gst[:, 6:7], func=AF.Exp, scale=-0.5), reads=(gst,), writes=(gst,))
            P.op("dve", lambda v: v.tensor_scalar(out=yn[:, :], in0=ysb[:, :], scalar1=gst[:, 2:3], scalar2=gst[:, 7:8], op0=ALU.subtract, op1=ALU.mult), reads=(ysb, gst), writes=(yn,))
            yield
            pYT = bank()
            mm_group(P, pYT[0:64, 0:128], pYT, [(yn[:, :], identf[:, :])], reads=(yn, identf))
            P.op("act", lambda a, pYT=pYT, s0=s0: a.copy(out=V("ynT")[:, s0:s0 + 128], in_=pYT[0:64, 0:128]), reads=(pYT,), writes=(V("ynT"),))

        for p0 in range(0, 4, NSET):
            gens = [pair_gen(p0 + k, sets[k]) for k in range(NSET)]
            while gens:
                for g_ in list(gens):
                    try:
                        next(g_)
                    except StopIteration:
                        gens.remove(g_)
                yield
        P.op("dve", lambda v: v.tensor_scalar(out=V("yo")[:, :], in0=V("ynT")[:, :], scalar1=vec[0:64, 19:20], scalar2=vec[0:64, 20:21], op0=ALU.mult, op1=ALU.add), reads=(V("ynT"), vec), writes=(V("yo"),))
        ew_tt("dve", V("yo")[:, :], V("yo")[:, :], V("bonus")[:, :], ALU.add, (V("yo"), V("bonus")), (V("yo"),))
        ew_tt("dve", V("yo")[:, :], V("yo")[:, :], V("gmap")[:, :], ALU.mult, (V("yo"), V("gmap")), (V("yo"),))
        yb_ = yxk[c0 // CH]
        P.dma("sp", yb_, lambda q: q.dma_start(out=yb_.t.ap()[64:128, c0 % CH:c0 % CH + W], in_=V("yo")[:, :]), reads=(V("yo"),))

    if do_attn:
        NTI = P.sb([128, 128], BF16, "NTI")
        NON = P.sb([128, 128], BF16, "NON")
        m01 = P.sb([128, 128], BF16, "m01")
        Z0 = P.sb([128, 64], BF16, "Z0")
        P.op("pool", lambda g: g.memset(NON[:, :], -1.0), writes=(NON,))
        P.op("pool", lambda g: g.memset(Z0[:, :], 0.0), writes=(Z0,))
        P.op("pool", lambda g: g.affine_select(out=NTI[:, :], in_=NON[:, :], pattern=[[-1, 128]], compare_op=ALU.is_ge, fill=0.0, base=0, channel_multiplier=1),
             reads=(NON,), writes=(NTI,))
        P.op("pool", lambda g: g.affine_select(out=m01[:, :], in_=onesb[:, :], pattern=[[1, 128]], compare_op=ALU.is_ge, fill=0.0, base=-1, channel_multiplier=-1),
             reads=(onesb,), writes=(m01,))
        eb = [P.sb([128, W], BF16, f"eb{i}") for i in range(2)]
        spb = [P.sb([128, W], BF16, f"spb{i}") for i in range(3)]
        Ab = [P.sb([128, W], BF16, f"Ab{i}") for i in range(3)]
        spaccs = [P.sb([128, W], BF16, f"spacc{i}") for i in range(2)]
        ybo = P.sb([64, W], F32, "ybo")
        po = banks[7]
        rot = {"i": 0}
        cnt = {"n": 0}

        def bank3():
            b = banks[4 + rot["i"] % 3]
            rot["i"] += 1
            return b

        def ew2(o, a, b, op, rd, wr):
            P.op("dve", lambda v: v.tensor_tensor(out=o, in0=a, in1=b, op=op), reads=rd, writes=wr)

        def S1(d):
            qt, kb, cc, w = d["qt"], d["kb"], d["cc"], d["w"]
            n = cnt["n"]
            cnt["n"] += 1
            d["n"] = n
            kb_buf = kTt[kb // 4]
            d["kbuf"] = kb_buf
            d["kblk"] = kb_buf[:, (kb % 4) * 128:(kb % 4 + 1) * 128]
            d["qcols"] = qTt[qt][:, cc:W]
            e_, sp_ = eb[n % 2], spb[n % 3]
            d["sp"] = sp_
            pz = bank3()
            d["pz"] = pz
            mm_group(P, pz[:, 0:w], pz, [(d["kblk"], d["qcols"])], reads=(kb_buf, qTt[qt]))
            P.op("act", lambda a: a.activation(out=e_[:, 0:w], in_=pz[:, 0:w], func=AF.Exp), reads=(pz,), writes=(e_,))
            P.op("act", lambda a: a.activation(out=sp_[:, 0:w], in_=e_[:, 0:w], func=AF.Ln, bias=1.0, scale=1.0), reads=(e_,), writes=(sp_,))
            if d["diag"]:
                ew2(sp_[:, 0:128], sp_[:, 0:128], m01[:, :], ALU.mult, (sp_, m01), (sp_,))

        def S2(d):
            qt, kb, cc, w, n = d["qt"], d["kb"], d["cc"], d["w"], d["n"]
            sp_, A_ = d["sp"], Ab[n % 3]
            d["A"] = A_
            spacc = spaccs[qt % 2]
            if d["first"]:
                P.op("pool", lambda g: g.memset(spacc[:, :], 0.0), writes=(spacc,))
            pe_ = d["pz"]
            P.op("pe", lambda t: t.matmul(pe_[:, 0:w], lhsT=NTI[:, :], rhs=sp_[:, 0:w], start=False, stop=False), reads=(NTI, sp_), writes=(pe_,), inc=False)
            P.op("pe", lambda t: t.matmul(pe_[:, 0:w], lhsT=NON[:, :], rhs=spacc[:, cc:W], start=False, stop=True), reads=(NON, spacc), writes=(pe_,))
            P.op("act", lambda a: a.activation(out=A_[:, 0:w], in_=pe_[:, 0:w], func=AF.Exp), reads=(pe_,), writes=(A_,))
            if d["diag"]:
                ew2(A_[:, 0:128], A_[:, 0:128], m01[:, :], ALU.mult, (A_, m01), (A_,))
            if not d["last"]:
                ew2(spacc[:, cc:W], spacc[:, cc:W], sp_[:, 0:w], ALU.add, (spacc, sp_), (spacc,))

        def S3(d):
            qt, kb, cc, w = d["qt"], d["kb"], d["cc"], d["w"]
            if d["first"]:
                for c4 in range(4):
                    P.op("pe", lambda t, c4=c4: t.matmul(po[0:64, c4 * 128:(c4 + 1) * 128], lhsT=Z0[:, :], rhs=NON[:, :], start=True, stop=False),
                         reads=(Z0, NON), writes=(po,), inc=(c4 == 3))
            A_ = d["A"]
            vb = vtt[kb // 4]
            P.op("pe", lambda t: t.matmul(po[0:64, cc:W], lhsT=vb[:, kb % 4, :], rhs=A_[:, 0:w], start=False, stop=d["last"]), reads=(vb, A_), writes=(po,))
            if d["last"]:
                q0 = qt * W
                P.op("act", lambda a: a.copy(out=ybo[:, :], in_=po[0:64, :]), reads=(po,), writes=(ybo,))
                yb_ = yxk[q0 // CH]
                P.dma("sp", yb_, lambda q: q.dma_start(out=yb_.t.ap()[0:64, q0 % CH:q0 % CH + W], in_=ybo[:, :]), reads=(ybo,))

        def attn_gen(qt):
            tl = []
            for kb in range(4 * qt + 3, -1, -1):
                diag = kb >= 4 * qt
                cc = 128 * (kb - 4 * qt) if diag else 0
                tl.append(dict(qt=qt, kb=kb, diag=diag, cc=cc, w=W - cc, first=(kb == 4 * qt + 3), last=(kb == 0)))
            n = len(tl)
            for i in range(-2, n):
                if 0 <= i + 2 < n:
                    S1(tl[i + 2])
                if 0 <= i + 1 < n:
                    S2(tl[i + 1])
                if 0 <= i < n:
                    S3(tl[i])
                yield

    for ti in range(NTL):
        c0 = ti * W
        if L == 0:
            xsrc = io["xT0"]
            P.dma("sp", xt, lambda q: q.dma_start(out=xt[:, :, :], in_=xsrc.t.ap().rearrange("(kc p) t -> p kc t", p=128)[:, :, c0:c0 + W]), reads=(xsrc,))
        else:
            sg_, o_ = c0 // NT, c0 % NT
            for kc in range(KC):
                xb_ = io["xgk"][kc][o_ // CH]
                P.dma("sp", xt, lambda q, kc=kc, xb_=xb_: q.dma_start(out=xt[:, kc, :], in_=xb_.t.ap()[sg_ * 128:(sg_ + 1) * 128, o_ % CH:o_ % CH + W]), reads=(xb_,))
        rmsnorm()
        ag = None
        if do_attn:
            pq = proj(0, 64)
            P.op("act", lambda a: a.activation(out=qTt[ti][:, :], in_=pq[0:64, :], func=AF.Copy, scale=0.125), reads=(pq,), writes=(qTt[ti],))
            pk = proj(64, 64)
            P.op("act", lambda a: a.copy(out=kTt[ti][:, :], in_=pk[0:64, :]), reads=(pk,), writes=(kTt[ti],))
            for tb in range(4):
                pv = bank()
                mm_group(P, pv[:, 0:64], pv, [(h[:, kc, tb * 128:(tb + 1) * 128], wsv[:, kc, 128:192]) for kc in range(KC)], reads=(h, ws))
                P.op("dve", lambda v, pv=pv, tb=tb: v.tensor_copy(out=vtt[ti][:, tb, :], in_=pv[:, 0:64]), reads=(pv,), writes=(vtt[ti],))
            ag = attn_gen(ti) if interleave else None
            n_it = 4 * ti + 4 + 2
        if do_rwkv:
            per = max(1, -(-n_it // 26)) if ag is not None else 0
            for _ in rwkv_tile(ti):
                if ag is not None:
                    for _k in range(per):
                        if next(ag, "done") == "done":
                            ag = None
                            break
        if ag is not None:
            for _ in ag:
                pass
        if io.get("after_seq_tile") is not None:
            io["after_seq_tile"](ti)
    if do_attn and not interleave:
        for ti in range(NTL):
            for _ in attn_gen(ti):
                pass


def pack_seqmix_weights(inp, l, hd):
    f = np.float32
    w_in = inp["w_in"][l]
    hs = slice(hd * 64, hd * 64 + 64)
    cols = np.concatenate([SB_OFF + np.arange(64) + hd * 64, SB_OFF + 256 + np.arange(64) + hd * 64, SB_OFF + 512 + np.arange(64) + hd * 64,
                           RW_OFF + np.arange(64) + hd * 64, RW_OFF + 256 + np.arange(64) + hd * 64, RW_OFF + 512 + np.arange(64) + hd * 64,
                           RW_OFF + 768 + np.arange(256)])
    d = {"wS": _kc_layout(w_in[:, cols])}
    vec = np.zeros((128, 32), f)
    vec[:, 0:8] = _pvec(inp["mix_norm_g"][l])
    mu = inp["rw_mu"][l]
    vec[:, 8] = mu[896:1024]
    for m, o in enumerate((hd * 64, 256 + hd * 64, 512 + hd * 64, 768, 832)):
        vec[0:64, 9 + m] = mu[o:o + 64]
    vec[0:64, 14] = inp["rw_w0"][l][hs]
    vec[0:64, 15] = inp["rw_a0"][l][hs]
    vec[0:64, 16] = inp["rw_k_k"][l][hs]
    vec[0:64, 17] = inp["rw_k_a"][l][hs]
    vec[0:64, 18] = inp["rw_r_k"][l][hd]
    vec[0:64, 19] = inp["rw_gn_g"][l][hs]
    vec[0:64, 20] = inp["rw_gn_b"][l][hs]
    d["vecS"] = vec
    d["w2h"] = inp["rw_w2"][l][:, hs]
    d["a2h"] = inp["rw_a2"][l][:, hs]
    d["g2h"] = inp["rw_g2"][l][:, hs]
    return {k: np.ascontiguousarray(v, dtype=f) for k, v in d.items()}


def build_fused(T, nlayers=2):
    nc = bass.Bass("TRN2", target_bir_lowering=False)
    P = Prog(nc)
    NT = T // 4
    TP = HALO + T
    EI = "ExternalInput"
    CH = 1024
    io = {
        "xT0": P.dram("xT0", [D, T], F32, EI),
        "xTh": P.dram("xTh", [D, HALO + NT], F32, EI),
        "hm": P.dram("hm", [128, 1], F32, EI),
        "oh": P.dram("oh", [128, 4], F32, EI),
        "ohp": P.dram("ohp", [128, 4], F32, EI),
        "yxk": [P.dram(f"yx{k}", [128, CH], F32, semkey="yx") for k in range(T // CH)],
        "ygk": [P.dram(f"yg{k}", [512, CH], F32, semkey="yg") for k in range(T // CH)],
        "xok": [[P.dram(f"xo{kc}_{cc}", [128, CH], F32, semkey="xo") for cc in range(NT // CH)] for kc in range(KC)],
        "xgk": [[P.dram(f"xg{kc}_{cc}", [512, CH], F32, semkey="xg") for cc in range(NT // CH)] for kc in range(KC)],
    }
    out = P.dram("out", [D, NT], F32, "ExternalOutput")
    RG = [[0, 1, 2, 3], [4, 5, 6, 7]]

    def gather(src, dst):
        P.dma("pool", dst, lambda g: g.collective_compute("AllGather", ALU.bypass, replica_groups=RG, ins=[src.t.ap().opt()], outs=[dst.t.ap().opt()]),
              reads=(src,), inc=1)

    def after_seq_tile(ti):
        if ti % 2 == 1:
            gather(io["yxk"][ti // 2], io["ygk"][ti // 2])

    for L in range(nlayers):
        lastL = (L == nlayers - 1)
        P.begin_phase()
        io["after_seq_tile"] = after_seq_tile
        emit_seqmix(P, T, L, io)
        P.end_phase()
        P.begin_phase()
        io["dense_out"] = out

        def after_dense_tile(ti):
            if ti % 2 == 1:
                for kc in range(KC):
                    gather(io["xok"][kc][ti // 2], io["xgk"][kc][ti // 2])
        io["after_dense_tile"] = None if lastL else after_dense_tile
        emit_dense(P, NT, L, L == 1, io, last_out=lastL)
        if lastL:
            P.final_wait("sp", (out,))
        P.end_phase()
    return nc, list(P.ext_in)


_CACHE = {}


def kernel(**inputs):
    inp = {k: np.asarray(v, dtype=np.float32) for k, v in inputs.items()}
    x = inp["x"]
    B, T, _ = x.shape
    NT = T // 4
    cores = list(range(8))
    nl = int(inputs.get("_nlayers", 2)) if "_nlayers" in inputs else 2
    if (T, nl) not in _CACHE:
        _CACHE[(T, nl)] = build_fused(T, nl)
    nc, ext_names = _CACHE[(T, nl)]
    xT_b = [np.ascontiguousarray(x[b].T) for b in range(B)]
    dw = [pack_dense_weights(inp, l) for l in range(2)]
    maps = []
    for c in cores:
        b, sg = c // 4, c % 4
        m = {"xT0": xT_b[b]}
        halo = np.zeros((D, HALO + NT), np.float32)
        if sg > 0:
            halo[:, :] = xT_b[b][:, sg * NT - HALO:(sg + 1) * NT]
        else:
            halo[:, HALO:] = xT_b[b][:, 0:NT]
        m["xTh"] = halo
        m["hm"] = np.full((128, 1), 0.0 if sg == 0 else 1.0, np.float32)
        oh = np.zeros((128, 4), np.float32)
        oh[:, sg] = 1.0
        ohp = np.zeros((128, 4), np.float32)
        if sg > 0:
            ohp[:, sg - 1] = 1.0
        m["oh"], m["ohp"] = oh, ohp
        for l in range(nl):
            for k, v in dw[l].items():
                m[f"{k}_{l}"] = v
            for k, v in pack_seqmix_weights(inp, l, sg).items():
                m[f"{k}_{l}"] = v
        maps.append(m)
    maps = [{k: m[k] for k in ext_names} for m in maps]
    res = run_bass_kernel_spmd(nc, maps, core_ids=cores)
    outp = np.zeros((B, T, D), np.float32)
    for c in cores:
        b, sg = c // 4, c % 4
        outp[b, sg * NT:(sg + 1) * NT, :] = res.results[c]["out"].T
    return outp
```

```python
import math
from contextlib import ExitStack
import numpy as np
import concourse.bass as bass
import concourse.mybir as mybir
from concourse.bass_utils import run_bass_kernel_spmd

F32 = mybir.dt.float32
BF16 = mybir.dt.bfloat16
AF = mybir.ActivationFunctionType
ALU = mybir.AluOpType
AX = mybir.AxisListType

D = 1024
KC = 8
DFF = 2816
NJ = 22
HALO = 128
RMS_EPS = 1e-6
LN_EPS = 1e-5
GN_EPS = 64e-5
DECAY_SCALE = math.exp(-0.5)


class Buf:
    def __init__(self, t, name, semkey=None):
        self.t = t
        self.name = name
        self.last_w = None
        self.readers = {}
        self.dma_sem = None
        self.semkey = semkey

    def __getitem__(self, idx):
        return self.t[idx]


class _Rec:
    def __init__(self):
        self.call = None

    def __getattr__(self, name):
        def f(*a, **k):
            self.call = (name, a, k)
            return self
        return f


def _rec(fn):
    r = _Rec()
    fn(r)
    assert r.call is not None
    return r.call


class Prog:
    ENGS = ("pe", "act", "dve", "pool", "sp")

    def __init__(self, nc):
        self.nc = nc
        self.ops = {e: [] for e in self.ENGS}
        self.sems = {}
        self.cnt = {}
        self.is_dma = {}
        self.waited = {e: {} for e in self.ENGS}
        self.pending = {e: False for e in self.ENGS}
        self.ekey = {}
        for e in ("pe", "act", "dve", "pool"):
            self._mksem(e, False)
            self.ekey[e] = e
        self.nphase = 0
        self.nbuf = 0
        self.rr = 0
        self.stack = None
        self.ext_in = []

    def begin_phase(self):
        self.stack = ExitStack()
        self.nphase += 1
        for e in ("pe", "act", "dve", "pool"):
            key = f"{e}_p{self.nphase}"
            self._mksem(key, False)
            self.ekey[e] = key
        for e in self.ENGS:
            waits = []
            for k, v in self.cnt.items():
                if v > 0 and k != self.ekey.get(e) and self.waited[e].get(k, 0) < v:
                    self.waited[e][k] = v
                    waits.append((k, v))
            if waits:
                self.ops[e].append((waits, None, None))

    def end_phase(self):
        self.emit()
        self.ops = {e: [] for e in self.ENGS}
        self.stack.close()
        self.stack = None

    def _mksem(self, key, dma):
        self.sems[key] = self.nc.alloc_semaphore("s_" + key)
        self.cnt[key] = 0
        self.is_dma[key] = dma

    def sb(self, shape, dt=F32, name=None):
        self.nbuf += 1
        name = (name or "sb") + f"_{self.nbuf}"
        if self.stack is not None:
            return Buf(self.stack.enter_context(self.nc.sbuf_tensor(name, list(shape), dt)), name)
        return Buf(self.nc.alloc_sbuf_tensor(name, list(shape), dt), name)

    def ps(self, shape=(128, 512), dt=F32, name=None):
        self.nbuf += 1
        name = (name or "ps") + f"_{self.nbuf}"
        if self.stack is not None:
            return Buf(self.stack.enter_context(self.nc.psum_tensor(name, list(shape), dt)), name)
        return Buf(self.nc.alloc_psum_tensor(name, list(shape), dt), name)

    def dram(self, name, shape, dt=F32, kind="Internal", semkey=None):
        if kind == "ExternalInput":
            self.ext_in.append(name)
        return Buf(self.nc.dram_tensor(name, list(shape), dt, kind=kind), name, semkey)

    def _needs(self, eng, reads, writes, pe_chain):
        needs = {}

        def add(tok):
            if tok is None:
                return
            k, v = tok
            if needs.get(k, 0) < v:
                needs[k] = v

        for b in reads:
            add(b.last_w)
        for b in writes:
            add(b.last_w)
            for k, v in b.readers.items():
                add((k, v))
        out = []
        for k, v in needs.items():
            if self.is_dma[k]:
                v = self.cnt[k]
            elif eng == "pe" and k == self.ekey["pe"] and pe_chain:
                continue
            if self.waited[eng].get(k, 0) >= v:
                continue
            self.waited[eng][k] = v
            out.append((k, v))
        return out

    def _commit(self, tok, reads, writes):
        k, v = tok
        for b in reads:
            if b.readers.get(k, 0) < v:
                b.readers[k] = v
        for b in writes:
            b.last_w = tok
            b.readers = {}

    def op(self, eng, fn, reads=(), writes=(), inc=True):
        waits = self._needs(eng, reads, writes, True)
        key = self.ekey[eng]
        if inc:
            self.cnt[key] += 1
            tok = (key, self.cnt[key])
            self.pending[eng] = False
            self.ops[eng].append((waits, _rec(fn), (key, 1)))
        else:
            tok = (key, self.cnt[key] + 1)
            self.pending[eng] = True
            self.ops[eng].append((waits, _rec(fn), None))
        self._commit(tok, reads, writes)

    def dma(self, q, out_buf, fn, reads=(), inc=16):
        if out_buf.dma_sem is None:
            key = "d_" + (out_buf.semkey or out_buf.name)
            if key not in self.sems:
                self._mksem(key, True)
            out_buf.dma_sem = key
        key = out_buf.dma_sem
        waits = self._needs(q, reads, (out_buf,), False)
        self.cnt[key] += inc
        tok = (key, self.cnt[key])
        self.ops[q].append((waits, _rec(fn), (key, inc)))
        self._commit(tok, reads, (out_buf,))

    def final_wait(self, eng, bufs):
        waits = self._needs(eng, bufs, (), False)
        self.ops[eng].append((waits, None, None))

    def emit(self):
        sems = self.sems
        for e in self.ENGS:
            assert not self.pending[e], e

        def run(e_name):
            def body(engine):
                for waits, fn, inc in self.ops[e_name]:
                    for k, v in waits:
                        engine.wait_ge(sems[k], v)
                    if fn is not None:
                        ins = getattr(engine, fn[0])(*fn[1], **fn[2])
                        if inc is not None:
                            ins.then_inc(sems[inc[0]], inc[1])
            return body

        with self.nc.Block() as block:
            block.tensor(run("pe"))
            block.scalar(run("act"))
            block.vector(run("dve"))
            block.gpsimd(run("pool"))
            block.sync(run("sp"))

    def ew(self):
        self.rr ^= 1
        return "dve" if self.rr else "pool"


def mm_group(P, out_ap, out_buf, pairs, reads):
    n = len(pairs)
    for i, (l, r) in enumerate(pairs):
        P.op("pe", (lambda t, l=l, r=r, i=i: t.matmul(out_ap, lhsT=l, rhs=r, start=(i == 0), stop=(i == n - 1))),
             reads=reads, writes=(out_buf,), inc=(i == n - 1))


def emit_dense(P, NT, L, last_layer, io, W=512, last_out=True):
    TT = HALO + NT
    EI = "ExternalInput"
    sfx = f"_{L}"
    hm, ohd, ohpd = io["hm"], io["oh"], io["ohp"]
    CH = 1024
    wAD = P.dram("wAD" + sfx, [128, KC * 1280], F32, EI)
    wG = P.dram("wG" + sfx, [8, 128, KC * 512], F32, EI)
    wOut = P.dram("wOut" + sfx, [4, 128, 2 * D], F32, EI)
    wO = P.dram("wO" + sfx, [2, 128, KC * 512], F32, EI)
    wUp = P.dram("wUp" + sfx, [11, 128, KC * 512], F32, EI)
    wDn = P.dram("wDn" + sfx, [8, 128, NJ * 128], F32, EI)
    vecs = P.dram("vecs" + sfx, [128, 64], F32, EI)
    convF = P.dram("convF" + sfx, [128, 44 * 3], F32, EI)
    rowsD = P.dram("rowsD" + sfx, [128, 512], F32, EI)
    wsT = P.dram("wsT" + sfx, [128, 4 * 128], F32, EI)
    bsr = P.dram("bsr" + sfx, [1, 4 * 128], F32, EI)
    out = io["dense_out"]
    ohb = P.sb([128, 8], F32, "ohb")
    P.dma("sp", ohb, lambda q: q.dma_start(out=ohb[:, 0:4], in_=ohd.t.ap()), reads=(ohd,))
    P.dma("sp", ohb, lambda q: q.dma_start(out=ohb[:, 4:8], in_=ohpd.t.ap()), reads=(ohpd,))

    vec = P.sb([128, 64], F32, "vec")
    P.dma("sp", vec, lambda q: q.dma_start(out=vec[:, :], in_=vecs.t.ap()), reads=(vecs,))
    cvf = P.sb([128, 44 * 3], F32, "cvf")
    P.dma("sp", cvf, lambda q: q.dma_start(out=cvf[:, :], in_=convF.t.ap()), reads=(convF,))
    rows = P.sb([128, 512], F32, "rows")
    P.dma("sp", rows, lambda q: q.dma_start(out=rows[:, :], in_=rowsD.t.ap()), reads=(rowsD,))
    hmb = P.sb([128, 1], F32, "hmb")
    P.dma("sp", hmb, lambda q: q.dma_start(out=hmb[:, :], in_=hm.t.ap()), reads=(hm,))
    wsb = P.sb([128, 512], BF16, "wsb")
    wsf = P.sb([128, 512], F32, "wsf")
    P.dma("sp", wsf, lambda q: q.dma_start(out=wsf[:, :], in_=wsT.t.ap()), reads=(wsT,))
    P.op("pool", lambda g: g.affine_select(out=wsf[:, :].rearrange("p (g t) -> p g t", g=4),
                                           in_=wsf[:, :].rearrange("p (g t) -> p g t", g=4),
                                           pattern=[[0, 4], [1, 128]], compare_op=ALU.is_ge, fill=0.0,
                                           base=0, channel_multiplier=-1),
         reads=(wsf,), writes=(wsf,))
    P.op("dve", lambda v: v.tensor_copy(out=wsb[:, :], in_=wsf[:, :]), reads=(wsf,), writes=(wsb,))
    bsb = P.sb([1, 512], BF16, "bsb")
    P.dma("pool", bsb, lambda q: q.dma_start(out=bsb[:, :], in_=bsr.t.ap()), reads=(bsr,))
    onesb = P.sb([128, 128], BF16, "onesb")
    P.op("pool", lambda g: g.memset(onesb[:, :], 1.0), writes=(onesb,))
    wad = P.sb([128, KC * 1280], BF16, "wad")
    P.dma("pool", wad, lambda q: q.dma_start(out=wad[:, :], in_=wAD.t.ap()), reads=(wAD,))
    wadv = wad.t.ap().rearrange("p (kc c) -> p kc c", kc=KC)
    wout = [P.sb([128, 2 * D], BF16, f"wout{b}") for b in range(4)]
    for b in range(4):
        P.dma("pool", wout[b], lambda q, b=b: q.dma_start(out=wout[b][:, :], in_=wOut.t.ap()[b]), reads=(wOut,))

    NSLOT = 4
    ring = [P.sb([128, 4096], BF16, f"ring{i}") for i in range(NSLOT)]
    stream = []
    for oc in range(8):
        stream.append((wG, oc, KC * 512))
    for i in range(2):
        stream.append((wO, i, KC * 512))
    for jj in range(11):
        stream.append((wUp, jj, KC * 512))
    for oc in range(8):
        stream.append((wDn, oc, NJ * 128))
    state = {"issued": 0, "consumed": 0}

    def issue_block():
        i = state["issued"]
        src, idx, n = stream[i % len(stream)]
        slot = ring[i % NSLOT]
        P.dma("pool", slot, lambda q: q.dma_start(out=slot[:, 0:n], in_=src.t.ap()[idx]), reads=(src,))
        state["issued"] += 1

    def next_block(total_blocks):
        while state["issued"] < min(state["consumed"] + NSLOT - 1, total_blocks):
            issue_block()
        if state["issued"] <= state["consumed"]:
            issue_block()
        slot = ring[state["consumed"] % NSLOT]
        state["consumed"] += 1
        return slot

    xt = P.sb([128, KC, W], F32, "xt")
    h = P.sb([128, KC, W], BF16, "h")
    act = P.sb([128, NJ, W], BF16, "act")
    merged = P.sb([128, KC, W], F32, "merged")
    mb = P.sb([128, KC, W], BF16, "mb")
    ybt = P.sb([128, 4, W], BF16, "ybt")
    ycand = [P.sb([128, 4, W], BF16, f"ycand{i}") for i in range(2)]
    ya = P.sb([128, 2, W], BF16, "ya")
    yd = P.sb([128, 2, W], BF16, "yd")
    ug = P.sb([128, 2, W], F32, "ug")
    cxh = P.sb([128, 2, W + 2], F32, "cxh")
    uh = P.sb([128, 44, 2], F32, "uh")
    uc = [P.sb([128, W + 2], F32, f"uc{i}") for i in range(2)]
    tmp = [P.sb([128, W], F32, f"tmp{i}") for i in range(6)]
    rstd = P.sb([128, W], F32, "rstd")
    lnt = P.sb([128, W], F32, "lnt")
    vtk = [P.sb([128, 256], F32, f"vtk{i}") for i in range(4)]
    vnb = P.sb([128, 256], BF16, "vnb")
    st = P.sb([128, 8], F32, "st")
    banks = [P.ps([128, 512], F32, f"bank{i}") for i in range(8)]
    bstate = {"i": 0}

    def bank():
        b = banks[bstate["i"] % 8]
        bstate["i"] += 1
        return b

    P.op("pool", lambda g: g.memset(cxh[:, :, :], 0.0), writes=(cxh,))
    P.op("pool", lambda g: g.memset(uh[:, :, :], 0.0), writes=(uh,))

    def rmsnorm(w, gcol, out_buf, out_f32=False):
        P.op("dve", lambda v: v.tensor_tensor(out=act[:, 0:KC, 0:w], in0=xt[:, :, 0:w], in1=xt[:, :, 0:w], op=ALU.mult),
             reads=(xt,), writes=(act,))
        pb = bank()
        mm_group(P, pb[:, 0:w], pb, [(onesb[:, :], act[:, kc, 0:w]) for kc in range(KC)], reads=(onesb, act))
        P.op("act", lambda a: a.activation(out=lnt[:, 0:w], in_=pb[:, 0:w], func=AF.Ln, bias=RMS_EPS, scale=1.0 / D),
             reads=(pb,), writes=(lnt,))
        P.op("act", lambda a: a.activation(out=rstd[:, 0:w], in_=lnt[:, 0:w], func=AF.Exp, scale=-0.5),
             reads=(lnt,), writes=(rstd,))
        for kc in range(KC):
            e = "dve"
            P.op(e, lambda v, kc=kc: v.scalar_tensor_tensor(out=out_buf[:, kc, 0:w], in0=xt[:, kc, 0:w],
                                                            scalar=vec[:, gcol + kc:gcol + kc + 1], in1=rstd[:, 0:w],
                                                            op0=ALU.mult, op1=ALU.mult),
                 reads=(xt, vec, rstd), writes=(out_buf,))

    ntiles = NT // W
    total_blocks = len(stream) * ntiles

    def tile(off, w, is_halo, out_off):
        if L == 0:
            xsrc = io["xTh"]
            xv = xsrc.t.ap().rearrange("(kc p) t -> p kc t", p=128)
            P.dma("sp", xt, lambda q: q.dma_start(out=xt[:, :, 0:w], in_=xv[:, :, off:off + w]), reads=(xsrc,))
        elif not is_halo:
            o_ = off - HALO
            for kc in range(KC):
                xb_ = io["xok"][kc][o_ // CH]
                P.dma("sp", xt, lambda q, kc=kc, xb_=xb_: q.dma_start(out=xt[:, kc, 0:w], in_=xb_.t.ap()[:, o_ % CH:o_ % CH + w]), reads=(xb_,))
        else:
            for sgi in range(4):
                for kc in range(KC):
                    xb_ = io["xgk"][kc][NT // CH - 1]
                    P.dma("sp", merged, lambda q, kc=kc, xb_=xb_, sgi=sgi: q.dma_start(out=merged[:, kc, 0:HALO], in_=xb_.t.ap()[sgi * 128:(sgi + 1) * 128, CH - HALO:CH]), reads=(xb_,))
                if sgi == 0:
                    P.op("dve", lambda v: v.tensor_scalar(out=xt[:, :, 0:HALO], in0=merged[:, :, 0:HALO], scalar1=ohb[:, 4:5], scalar2=None, op0=ALU.mult),
                         reads=(merged, ohb), writes=(xt,))
                else:
                    P.op("dve", lambda v, sgi=sgi: v.scalar_tensor_tensor(out=xt[:, :, 0:HALO], in0=merged[:, :, 0:HALO], scalar=ohb[:, 4 + sgi:5 + sgi], in1=xt[:, :, 0:HALO], op0=ALU.mult, op1=ALU.add),
                         reads=(merged, ohb, xt), writes=(xt,))
        for sgi in range(4):
            yc_ = ycand[sgi % 2]
            t0_ = sgi * NT + off - HALO
            if t0_ < 0:
                P.op("dve", lambda v, yc_=yc_: v.memset(yc_[:, :, 0:w], 0.0), writes=(yc_,))
            else:
                gb_ = io["ygk"][t0_ // CH]
                gv_ = gb_.t.ap().rearrange("(h p) t -> p h t", p=128)
                P.dma("pool", yc_, lambda q, yc_=yc_, gv_=gv_, t0_=t0_: q.dma_start(out=yc_[:, :, 0:w], in_=gv_[:, :, t0_ % CH:t0_ % CH + w]), reads=(gb_,))
            if sgi == 0:
                P.op("dve", lambda v, yc_=yc_: v.tensor_scalar(out=ybt[:, :, 0:w], in0=yc_[:, :, 0:w], scalar1=ohb[:, 0:1], scalar2=None, op0=ALU.mult),
                     reads=(yc_, ohb), writes=(ybt,))
            else:
                P.op("dve", lambda v, yc_=yc_, sgi=sgi: v.scalar_tensor_tensor(out=ybt[:, :, 0:w], in0=yc_[:, :, 0:w], scalar=ohb[:, sgi:sgi + 1], in1=ybt[:, :, 0:w], op0=ALU.mult, op1=ALU.add),
                     reads=(yc_, ohb, ybt), writes=(ybt,))
        rmsnorm(w, 0, h)
        hr = (h, wad)
        for ch in range(2):
            pbg, pcg, pxi = bank(), bank(), bank()
            for pb, c in ((pbg, ch), (pcg, 2 + ch), (pxi, 4 + ch)):
                mm_group(P, pb[:, 0:w], pb, [(wadv[:, kc, c * 128:(c + 1) * 128], h[:, kc, 0:w]) for kc in range(KC)], reads=hr)
            t0, t1 = tmp[0], tmp[1]
            P.op("act", lambda a: a.copy(out=t0[:, 0:w], in_=pxi[:, 0:w]), reads=(pxi,), writes=(t0,))
            P.op("dve", lambda v: v.tensor_tensor(out=cxh[:, ch, 2:2 + w], in0=pcg[:, 0:w], in1=t0[:, 0:w], op=ALU.mult),
                 reads=(pcg, t0), writes=(cxh,))
            c0 = 56 + ch * 3
            P.op("dve", lambda v: v.tensor_scalar(out=t1[:, 0:w], in0=cxh[:, ch, 0:w], scalar1=vec[:, c0:c0 + 1], scalar2=None, op0=ALU.mult),
                 reads=(cxh, vec), writes=(t1,))
            P.op("dve", lambda v: v.scalar_tensor_tensor(out=t1[:, 0:w], in0=cxh[:, ch, 1:1 + w], scalar=vec[:, c0 + 1:c0 + 2], in1=t1[:, 0:w], op0=ALU.mult, op1=ALU.add),
                 reads=(cxh, vec, t1), writes=(t1,))
            P.op("dve", lambda v: v.scalar_tensor_tensor(out=t1[:, 0:w], in0=cxh[:, ch, 2:2 + w], scalar=vec[:, c0 + 2:c0 + 3], in1=t1[:, 0:w], op0=ALU.mult, op1=ALU.add),
                 reads=(cxh, vec, t1), writes=(t1,))
            P.op("dve", lambda v: v.tensor_tensor(out=ya[:, ch, 0:w], in0=pbg[:, 0:w], in1=t1[:, 0:w], op=ALU.mult),
                 reads=(pbg, t1), writes=(ya,))
            if is_halo:
                P.op("dve", lambda v: v.tensor_scalar(out=cxh[:, ch, 0:2], in0=cxh[:, ch, w:w + 2], scalar1=hmb[:, 0:1], scalar2=None, op0=ALU.mult),
                     reads=(cxh, hmb), writes=(cxh,))
            else:
                P.op("dve", lambda v: v.tensor_copy(out=cxh[:, ch, 0:2], in_=cxh[:, ch, w:w + 2]), reads=(cxh,), writes=(cxh,))

        def gelu(dst_ap, dst_buf, src_ap, src_buf, npart, n, t_a, t_b):
            P.op("act", lambda a: a.copy(out=t_a[0:npart, 0:n], in_=src_ap), reads=(src_buf,), writes=(t_a,))
            P.op("dve", lambda g: g.tensor_tensor(out=t_b[0:npart, 0:n], in0=t_a[0:npart, 0:n], in1=t_a[0:npart, 0:n], op=ALU.mult),
                 reads=(t_a,), writes=(t_b,))
            P.op("dve", lambda v: v.tensor_scalar(out=t_b[0:npart, 0:n], in0=t_b[0:npart, 0:n], scalar1=0.044715, scalar2=1.0, op0=ALU.mult, op1=ALU.add),
                 reads=(t_b,), writes=(t_b,))
            P.op("dve", lambda g: g.tensor_tensor(out=t_b[0:npart, 0:n], in0=t_b[0:npart, 0:n], in1=t_a[0:npart, 0:n], op=ALU.mult),
                 reads=(t_a, t_b), writes=(t_b,))
            P.op("act", lambda a: a.activation(out=t_b[0:npart, 0:n], in_=t_b[0:npart, 0:n], func=AF.Sigmoid, scale=2.0 * 0.7978845608028654),
                 reads=(t_b,), writes=(t_b,))
            P.op("dve", lambda v: v.tensor_tensor(out=dst_ap, in0=t_a[0:npart, 0:n], in1=t_b[0:npart, 0:n], op=ALU.mult),
                 reads=(t_a, t_b), writes=(dst_buf,))

        for ch in range(2):
            pu = bank()
            c = 768 + ch * 128
            mm_group(P, pu[:, 0:w], pu, [(wadv[:, kc, c:c + 128], h[:, kc, 0:w]) for kc in range(KC)], reads=hr)
            gelu(ug[:, ch, 0:w], ug, pu[:, 0:w], pu, 128, w, tmp[2], tmp[3])
        for tb in range(w // 128):
            pv = bank()
            mm_group(P, pv[:, 0:256], pv, [(h[:, kc, tb * 128:(tb + 1) * 128], wadv[:, kc, 1024:1280]) for kc in range(KC)], reads=hr)
            gv, sq = vtk[0], vtk[1]
            gelu(gv[:, :], gv, pv[:, 0:256], pv, 128, 256, vtk[2], vtk[3])
            P.op("dve", lambda v: v.tensor_reduce(out=st[:, 0:1], in_=gv[:, :], axis=AX.X, op=ALU.add), reads=(gv,), writes=(st,))
            P.op("dve", lambda g: g.tensor_tensor(out=sq[:, :], in0=gv[:, :], in1=gv[:, :], op=ALU.mult), reads=(gv,), writes=(sq,))
            P.op("dve", lambda v: v.tensor_reduce(out=st[:, 1:2], in_=sq[:, :], axis=AX.X, op=ALU.add), reads=(sq, st), writes=(st,))
            P.op("dve", lambda v: v.tensor_scalar(out=st[:, 2:4], in0=st[:, 0:2], scalar1=1.0 / 256, scalar2=None, op0=ALU.mult), reads=(st,), writes=(st,))
            P.op("dve", lambda v: v.tensor_tensor(out=st[:, 4:5], in0=st[:, 2:3], in1=st[:, 2:3], op=ALU.mult), reads=(st,), writes=(st,))
            P.op("dve", lambda v: v.tensor_tensor(out=st[:, 5:6], in0=st[:, 3:4], in1=st[:, 4:5], op=ALU.subtract), reads=(st,), writes=(st,))
            P.op("act", lambda a: a.activation(out=st[:, 6:7], in_=st[:, 5:6], func=AF.Ln, bias=LN_EPS, scale=1.0), reads=(st,), writes=(st,))
            P.op("act", lambda a: a.activation(out=st[:, 7:8], in_=st[:, 6:7], func=AF.Exp, scale=-0.5), reads=(st,), writes=(st,))
            P.op("dve", lambda v: v.tensor_scalar(out=gv[:, :], in0=gv[:, :], scalar1=st[:, 2:3], scalar2=st[:, 7:8], op0=ALU.subtract, op1=ALU.mult),
                 reads=(gv, st), writes=(gv,))
            P.op("dve", lambda g: g.tensor_tensor(out=gv[:, :], in0=gv[:, :], in1=rows[:, 0:256], op=ALU.mult), reads=(gv, rows), writes=(gv,))
            P.op("dve", lambda v: v.tensor_tensor(out=vnb[:, :], in0=gv[:, :], in1=rows[:, 256:512], op=ALU.add), reads=(gv, rows), writes=(vnb,))
            for g4 in range(4):
                pm = bank()
                po = (g4 % 2) * 64
                mm_group(P, pm[po:po + 64, 0:128], pm,
                         [(vnb[:, g4 * 64:(g4 + 1) * 64], wsb[:, g4 * 128:(g4 + 1) * 128]),
                          (onesb[0:1, 0:64], bsb[0:1, g4 * 128:(g4 + 1) * 128])], reads=(vnb, wsb, onesb, bsb))
                P.op("dve", lambda v, g4=g4, pm=pm, po=po: v.tensor_tensor(out=yd[po:po + 64, g4 // 2, tb * 128:(tb + 1) * 128], in0=pm[po:po + 64, 0:128],
                                                                            in1=ug[po:po + 64, g4 // 2, tb * 128:(tb + 1) * 128], op=ALU.mult),
                     reads=(pm, ug), writes=(yd,))

        for oc in range(8):
            slot = next_block(total_blocks) if not is_halo else None
            if is_halo:
                slot = ring_h
                P.dma("pool", slot, lambda q: q.dma_start(out=slot[:, 0:KC * 512], in_=wG.t.ap()[oc]), reads=(wG,))
            sv = slot.t.ap().rearrange("p (kc c) -> p kc c", kc=KC)
            for br in range(4):
                pg, pp = bank(), bank()
                mm_group(P, pg[:, 0:w], pg, [(sv[:, kc, br * 128:(br + 1) * 128], h[:, kc, 0:w]) for kc in range(KC)], reads=(slot, h))
                wo_v = wout[br].t.ap().rearrange("p (k c) -> p k c", k=2)
                if br == 0:
                    prs = [(wo_v[:, k2, oc * 128:(oc + 1) * 128], ya[:, k2, 0:w]) for k2 in range(2)]
                    rd = (wout[br], ya)
                elif br in (1, 2):
                    po_ = 0 if br == 1 else 64
                    prs = []
                    for hd_ in range(4):
                        wv_ = wout[1 + hd_ // 2].t.ap().rearrange("p (k c) -> p k c", k=2)
                        prs.append((wv_[po_:po_ + 64, hd_ % 2, oc * 128:(oc + 1) * 128], ybt[po_:po_ + 64, hd_, 0:w]))
                    rd = (wout[1], wout[2], ybt)
                else:
                    prs = [(wo_v[:, k2, oc * 128:(oc + 1) * 128], yd[:, k2, 0:w]) for k2 in range(2)]
                    rd = (wout[br], yd)
                mm_group(P, pp[:, 0:w], pp, prs, reads=rd)
                gs = tmp[4]
                gb = 24 + br * 8 + oc
                P.op("act", lambda a, pg=pg, gb=gb: a.activation(out=gs[:, 0:w], in_=pg[:, 0:w], func=AF.Sigmoid, bias=vec[:, gb:gb + 1], scale=1.0),
                     reads=(pg, vec), writes=(gs,))
                if br == 0:
                    P.op("dve", lambda v, pp=pp: v.tensor_tensor(out=merged[:, oc, 0:w], in0=pp[:, 0:w], in1=gs[:, 0:w], op=ALU.mult),
                         reads=(pp, gs), writes=(merged,))
                else:
                    t5 = tmp[5]
                    P.op("dve", lambda v, pp=pp: v.tensor_tensor(out=t5[:, 0:w], in0=pp[:, 0:w], in1=gs[:, 0:w], op=ALU.mult),
                         reads=(pp, gs), writes=(t5,))
                    dst = mb if br == 3 else merged
                    P.op("dve", lambda g, dst=dst: g.tensor_tensor(out=dst[:, oc, 0:w], in0=merged[:, oc, 0:w], in1=t5[:, 0:w], op=ALU.add),
                         reads=(merged, t5), writes=(dst,))

        for i in range(2):
            if is_halo:
                slot = ring_h
                P.dma("pool", slot, lambda q: q.dma_start(out=slot[:, 0:KC * 512], in_=wO.t.ap()[i]), reads=(wO,))
            else:
                slot = next_block(total_blocks)
            sv = slot.t.ap().rearrange("p (kc c) -> p kc c", kc=KC)
            for o4 in range(4):
                oc = i * 4 + o4
                pb = bank()
                mm_group(P, pb[:, 0:w], pb, [(sv[:, kc, o4 * 128:(o4 + 1) * 128], mb[:, kc, 0:w]) for kc in range(KC)], reads=(slot, mb))
                P.op("dve", lambda v, pb=pb, oc=oc: v.tensor_tensor(out=xt[:, oc, 0:w], in0=pb[:, 0:w], in1=xt[:, oc, 0:w], op=ALU.add),
                     reads=(pb, xt), writes=(xt,))

        rmsnorm(w, 8, h)
        for jj in range(11):
            if is_halo:
                slot = ring_h
                P.dma("pool", slot, lambda q: q.dma_start(out=slot[:, 0:KC * 512], in_=wUp.t.ap()[jj]), reads=(wUp,))
            else:
                slot = next_block(total_blocks)
            sv = slot.t.ap().rearrange("p (kc c) -> p kc c", kc=KC)
            for j2 in range(2):
                j = jj * 2 + j2
                cv = []
                for gv_i in range(2):
                    pb = bank()
                    c = (j2 * 2 + gv_i) * 128
                    mm_group(P, pb[:, 0:w], pb, [(sv[:, kc, c:c + 128], h[:, kc, 0:w]) for kc in range(KC)], reads=(slot, h))
                    idx = gv_i * NJ + j
                    u = uc[gv_i]
                    P.op("act", lambda a, pb=pb, u=u: a.copy(out=u[:, 2:2 + w], in_=pb[:, 0:w]), reads=(pb,), writes=(u,))
                    P.op("dve", lambda g, u=u, idx=idx: g.tensor_copy(out=u[:, 0:2], in_=uh[:, idx, :]), reads=(uh, u), writes=(u,))
                    if is_halo:
                        P.op("dve", lambda g, u=u, idx=idx: g.tensor_scalar(out=uh[:, idx, :], in0=u[:, w:w + 2], scalar1=hmb[:, 0:1], scalar2=None, op0=ALU.mult),
                             reads=(u, hmb, uh), writes=(uh,))
                        continue
                    P.op("dve", lambda g, u=u, idx=idx: g.tensor_copy(out=uh[:, idx, :], in_=u[:, w:w + 2]), reads=(u, uh), writes=(uh,))
                    t = tmp[gv_i]
                    P.op("act", lambda a, pb=pb, t=t, idx=idx: a.activation(out=t[:, 0:w], in_=pb[:, 0:w], func=AF.Copy, scale=cvf[:, idx * 3 + 2:idx * 3 + 3]),
                         reads=(pb, cvf), writes=(t,))
                    for tap in (0, 1):
                        P.op("dve", lambda v, u=u, t=t, idx=idx, tap=tap: v.scalar_tensor_tensor(out=t[:, 0:w], in0=u[:, tap:tap + w], scalar=cvf[:, idx * 3 + tap:idx * 3 + tap + 1], in1=t[:, 0:w], op0=ALU.mult, op1=ALU.add),
                             reads=(u, cvf, t), writes=(t,))
                    cv.append(t)
                if is_halo:
                    continue
                sg = tmp[2]
                P.op("act", lambda a: a.activation(out=sg[:, 0:w], in_=cv[0][:, 0:w], func=AF.Sigmoid), reads=(cv[0],), writes=(sg,))
                P.op("dve", lambda v: v.tensor_tensor(out=sg[:, 0:w], in0=sg[:, 0:w], in1=cv[0][:, 0:w], op=ALU.mult), reads=(sg, cv[0]), writes=(sg,))
                P.op("dve", lambda g, j=j: g.tensor_tensor(out=act[:, j, 0:w], in0=sg[:, 0:w], in1=cv[1][:, 0:w], op=ALU.mult), reads=(sg, cv[1]), writes=(act,))
        if is_halo:
            return
        for oc in range(8):
            slot = next_block(total_blocks)
            sv = slot.t.ap().rearrange("p (j c) -> p j c", c=128)
            pb = bank()
            mm_group(P, pb[:, 0:w], pb, [(sv[:, j, :], act[:, j, 0:w]) for j in range(NJ)], reads=(slot, act))
            P.op("dve", lambda v, pb=pb, oc=oc: v.tensor_tensor(out=xt[:, oc, 0:w], in0=pb[:, 0:w], in1=xt[:, oc, 0:w], op=ALU.add),
                 reads=(pb, xt), writes=(xt,))
        if last_layer:
            rmsnorm(w, 16, merged, out_f32=True)
            src = merged
        else:
            src = xt
        if last_out:
            outv = out.t.ap().rearrange("(kc p) t -> p kc t", p=128)
            P.dma("sp", out, lambda q: q.dma_start(out=outv[:, :, out_off:out_off + w], in_=src[:, :, 0:w]), reads=(src,))
        else:
            for kc in range(KC):
                xb_ = io["xok"][kc][out_off // CH]
                P.dma("sp", xb_, lambda q, kc=kc, xb_=xb_: q.dma_start(out=xb_.t.ap()[:, out_off % CH:out_off % CH + w], in_=src[:, kc, 0:w]), reads=(src,))

    ring_h = P.sb([128, 4096], BF16, "ring_h")
    tile(0, HALO, True, None)
    for ti in range(ntiles):
        tile(HALO + ti * W, W, False, ti * W)
        if io.get("after_dense_tile") is not None:
            io["after_dense_tile"](ti)


SB_OFF, RW_OFF, SG_OFF, GATE_OFF = 768, 1536, 2560, 3072


def _kc_layout(w):
    K, C = w.shape
    return np.ascontiguousarray(w.reshape(K // 128, 128, C).transpose(1, 0, 2).reshape(128, (K // 128) * C))


def _pvec(v):
    return np.ascontiguousarray(v.reshape(-1, 128).T)


def pack_dense_weights(inp, l):
    f = np.float32
    w_in = inp["w_in"][l]
    cols = np.concatenate([np.arange(0, 768), np.arange(SG_OFF, SG_OFF + 512)])
    d = {}
    d["wAD"] = _kc_layout(w_in[:, cols])
    wg = w_in[:, GATE_OFF:].reshape(D, 4, 8, 128).transpose(2, 0, 1, 3).reshape(8, D, 512)
    d["wG"] = np.stack([_kc_layout(wg[oc]) for oc in range(8)])
    sbo, rwo = inp["sb_out"][l], inp["rw_out"][l]

    def bc(h0):
        a = np.zeros((128, 2, D), np.float32)
        for k in range(2):
            hd = h0 + k
            a[0:64, k] = sbo[hd * 64:(hd + 1) * 64]
            a[64:128, k] = rwo[hd * 64:(hd + 1) * 64]
        return a.reshape(128, 2 * D)
    d["wOut"] = np.stack([_kc_layout(inp["sc_out"][l]), bc(0), bc(2), _kc_layout(inp["sg_out"][l])])
    d["wO"] = np.stack([_kc_layout(inp["w_o"][l][:, i * 512:(i + 1) * 512]) for i in range(2)])
    wu = inp["w_up"][l].reshape(D, 2, 11, 2, 128).transpose(2, 0, 3, 1, 4).reshape(11, D, 512)
    d["wUp"] = np.stack([_kc_layout(wu[jj]) for jj in range(11)])
    d["wDn"] = np.stack([_kc_layout(inp["w_down"][l][:, oc * 128:(oc + 1) * 128]) for oc in range(8)])
    vec = np.zeros((128, 64), f)
    vec[:, 0:8] = _pvec(inp["mix_norm_g"][l])
    vec[:, 8:16] = _pvec(inp["ffn_norm_g"][l])
    vec[:, 16:24] = _pvec(inp["final_norm_g"])
    vec[:, 24:56] = _pvec(inp["gate_b"][l])
    cw = inp["sc_conv_w"][l]
    for ch in range(2):
        for tap in range(3):
            vec[:, 56 + ch * 3 + tap] = cw[tap, ch * 128:(ch + 1) * 128]
    d["vecs"] = vec
    fc = inp["ffn_conv_w"][l]
    d["convF"] = np.ascontiguousarray(fc.reshape(3, 44, 128).transpose(2, 1, 0).reshape(128, 132))
    rows = np.zeros((128, 512), f)
    rows[:, 0:256] = inp["sg_ln_g"][l][None, :]
    rows[:, 256:512] = inp["sg_ln_b"][l][None, :]
    d["rowsD"] = rows
    d["wsT"] = np.ascontiguousarray(inp["sg_w"][l].transpose(2, 0, 1).reshape(128, 512))
    d["bsr"] = np.ascontiguousarray(inp["sg_b"][l].reshape(1, 512))
    return {k: np.ascontiguousarray(v, dtype=f) for k, v in d.items()}


def dense_core_inputs(xT_b, ybc_b, seg, NT):
    def halo(a):
        o = np.zeros((a.shape[0], HALO + NT), np.float32)
        s = seg * NT
        if seg > 0:
            o[:, :] = a[:, s - HALO:s + NT]
        else:
            o[:, HALO:] = a[:, 0:NT]
        return o
    hm = np.full((128, 1), 0.0 if seg == 0 else 1.0, np.float32)
    return {"xT": halo(xT_b), "ybc": halo(ybc_b), "hm": hm}


def emit_seqmix(P, T, L, io, do_attn=True, do_rwkv=True, interleave=True):
    EI = "ExternalInput"
    W = 512
    NTL = T // W
    NT = T // 4
    sfx = f"_{L}"
    wS = P.dram("wS" + sfx, [128, KC * 640], F32, EI)
    vecS = P.dram("vecS" + sfx, [128, 32], F32, EI)
    w2d = P.dram("w2h" + sfx, [64, 64], F32, EI)
    a2d = P.dram("a2h" + sfx, [64, 64], F32, EI)
    g2d = P.dram("g2h" + sfx, [128, 64], F32, EI)
    yxk = io["yxk"]
    CH = 1024

    vec = P.sb([128, 32], F32, "vec")
    P.dma("sp", vec, lambda q: q.dma_start(out=vec[:, :], in_=vecS.t.ap()), reads=(vecS,))
    ws = P.sb([128, KC * 640], BF16, "ws")
    P.dma("pool", ws, lambda q: q.dma_start(out=ws[:, :], in_=wS.t.ap()), reads=(wS,))
    wsv = ws.t.ap().rearrange("p (kc c) -> p kc c", kc=KC)
    w2h = P.sb([64, 64], F32, "w2h_s"); a2h = P.sb([64, 64], F32, "a2h_s"); g2h = P.sb([128, 64], F32, "g2h_s")
    P.dma("sp", w2h, lambda q: q.dma_start(out=w2h[:, :], in_=w2d.t.ap()), reads=(w2d,))
    P.dma("sp", a2h, lambda q: q.dma_start(out=a2h[:, :], in_=a2d.t.ap()), reads=(a2d,))
    P.dma("sp", g2h, lambda q: q.dma_start(out=g2h[:, :], in_=g2d.t.ap()), reads=(g2d,))

    onesb = P.sb([128, 128], BF16, "onesb")
    P.op("pool", lambda g: g.memset(onesb[:, :], 1.0), writes=(onesb,))
    onesf = P.sb([128, 128], F32, "onesf")
    P.op("pool", lambda g: g.memset(onesf[:, :], 1.0), writes=(onesf,))
    identf = P.sb([128, 128], F32, "identf")
    P.op("pool", lambda g: g.affine_select(out=identf[:, :], in_=onesf[:, 0:128], pattern=[[-1, 128]], compare_op=ALU.is_equal,
                                           fill=0.0, base=0, channel_multiplier=1), reads=(onesf,), writes=(identf,))

    qTt = [P.sb([64, W], BF16, f"qT{i}") for i in range(NTL)]
    kTt = [P.sb([64, W], BF16, f"kT{i}") for i in range(NTL)]
    vtt = [P.sb([128, 4, 64], BF16, f"vt{i}") for i in range(NTL)]

    xt = P.sb([128, KC, W], F32, "xt")
    h = P.sb([128, KC, W], BF16, "h")
    sq = h
    rstd = P.sb([128, W], F32, "rstd")
    lnt = rstd
    banks = [P.ps([128, 512], F32, f"bank{i}") for i in range(8)]
    bstate = {"i": 0}

    def bank():
        b = banks[bstate["i"] % 4]
        bstate["i"] += 1
        return b

    if do_rwkv:
        rmask = P.sb([64, W], F32, "rmask")
        P.op("pool", lambda g: g.memset(rmask[:, :], 1.0), writes=(rmask,))
        P.op("pool", lambda g: g.memset(rmask[:, :].rearrange("p (c i) -> p c i", i=64)[:, :, 0:1], 0.0), writes=(rmask,))
        maskS = P.sb([128, 128], F32, "maskS")
        mask2 = P.sb([128, 256], F32, "mask2")
        P.op("pool", lambda g: g.affine_select(out=maskS[:, :], in_=onesf[:, 0:128], pattern=[[-1, 128]], compare_op=ALU.is_ge, fill=0.0, base=-1, channel_multiplier=1),
             reads=(onesf,), writes=(maskS,))
        P.op("pool", lambda g: g.memset(maskS[64:128, 0:64], 0.0), writes=(maskS,))
        P.op("pool", lambda g: g.affine_select(out=mask2[:, 0:128], in_=onesf[:, 0:128], pattern=[[1, 128]], compare_op=ALU.is_ge, fill=0.0, base=-1, channel_multiplier=-1),
             reads=(onesf,), writes=(mask2,))
        P.op("pool", lambda g: g.affine_select(out=mask2[:, 128:256], in_=onesf[:, 0:128], pattern=[[1, 128]], compare_op=ALU.is_ge, fill=0.0, base=0, channel_multiplier=-1),
             reads=(onesf,), writes=(mask2,))
        P.op("pool", lambda g: g.memset(mask2[0:64, 64:128], 0.0), writes=(mask2,))
        P.op("pool", lambda g: g.memset(mask2[0:64, 192:256], 0.0), writes=(mask2,))
        zb = P.sb([64, 5, W + 1], F32, "zb")
        zg = P.sb([128, W + 1], F32, "zg")
        P.op("pool", lambda g: g.memset(zb[:, :, :], 0.0), writes=(zb,))
        P.op("pool", lambda g: g.memset(zg[:, :], 0.0), writes=(zg,))
        dbuf = P.sb([64, 5, W], F32, "dbuf")
        zz = dbuf
        dg = P.sb([128, W], F32, "dg")
        sgx = dg
        m_ = {n: P.sb([64, W], F32, n) for n in ("lw", "asg", "kk0", "s1", "s2", "s3", "kmod", "bvec", "bonus", "cl", "gi", "ginv", "gmap")}
        for alias, tgt in (("sgw", "lw"), ("kksq", "s1"), ("tt", "s1"), ("rk", "s1"), ("ssc", "s2"), ("rs", "s2"), ("tw", "s3"), ("cm", "s3"),
                           ("gprev", "s3"), ("kkn", "kk0"), ("ynT", "ginv"), ("yo", "ginv")):
            m_[alias] = m_[tgt]
        m_["BT"] = P.sb([64, W], BF16, "BTb")
        m_["KT"] = P.sb([64, W], BF16, "KTb")
        identb = P.sb([128, 128], BF16, "identb")
        P.op("dve", lambda v: v.tensor_copy(out=identb[:, :], in_=identf[:, :]), reads=(identf,), writes=(identb,))
        ART = P.sb([64, 4, 2, 128], BF16, "ART")
        NSET = 2
        sets = []
        for si in range(NSET):
            sets.append(dict(
                tok=P.sb([128, 192], BF16, f"tok{si}"), abT=P.sb([128, 256], BF16, f"abT{si}"), akT=P.sb([128, 256], BF16, f"akT{si}"),
                Lm=P.sb([128, 128], BF16, f"Lm{si}"), Xb=[P.sb([128, 128], BF16, f"Xb{si}_{i}") for i in range(2)],
                PPb=[P.sb([128, 256], BF16, f"PP{si}_{i}") for i in range(2)], LMb=[P.sb([64, 64], F32, f"LM{si}_{i}") for i in range(2)],
                N0Gb=[P.sb([64, 64], F32, f"N0G{si}_{i}") for i in range(2)], RpT=P.sb([64, 128], F32, f"RpT{si}"),
                ysb=P.sb([128, 64], F32, f"ysb{si}"), ysq=P.sb([128, 64], F32, f"ysq{si}"), yn=P.sb([128, 64], F32, f"yn{si}"),
                gst=P.sb([128, 8], F32, f"gst{si}")))
        NZ = 8
        Zb = [P.sb([64, 64], F32, f"Z{i}") for i in range(NZ)]
        zst = {"i": 0}
        P.op("pool", lambda g: g.memset(Zb[0][:, :], 0.0), writes=(Zb[0],))

    def rmsnorm():
        P.op("dve", lambda v: v.tensor_tensor(out=sq[:, :, :], in0=xt[:, :, :], in1=xt[:, :, :], op=ALU.mult), reads=(xt,), writes=(sq,))
        pb = bank()
        mm_group(P, pb[:, :], pb, [(onesb[:, :], sq[:, kc, :]) for kc in range(KC)], reads=(onesb, sq))
        P.op("act", lambda a: a.activation(out=lnt[:, :], in_=pb[:, :], func=AF.Ln, bias=RMS_EPS, scale=1.0 / D), reads=(pb,), writes=(lnt,))
        P.op("act", lambda a: a.activation(out=rstd[:, :], in_=lnt[:, :], func=AF.Exp, scale=-0.5), reads=(lnt,), writes=(rstd,))
        for kc in range(KC):
            P.op("dve", lambda v, kc=kc: v.scalar_tensor_tensor(out=h[:, kc, :], in0=xt[:, kc, :], scalar=vec[:, kc:kc + 1], in1=rstd[:, :], op0=ALU.mult, op1=ALU.mult),
                 reads=(xt, vec, rstd), writes=(h,))

    def proj(col, m):
        pb = bank()
        mm_group(P, pb[0:m, :], pb, [(wsv[:, kc, col:col + m], h[:, kc, :]) for kc in range(KC)], reads=(ws, h))
        return pb

    def V(n):
        return m_[n]

    def ew_tt(e, o, a, b, op, rd, wr):
        P.op(e, lambda v: v.tensor_tensor(out=o, in0=a, in1=b, op=op), reads=rd, writes=wr)

    def rwkv_tile(ti):
        c0 = ti * W
        for m in range(5):
            pb_ = proj(192 + 64 * m, 64)
            P.op("act", lambda a, m=m, pb_=pb_: a.copy(out=zb[:, m, 1:W + 1], in_=pb_[0:64, :]), reads=(pb_,), writes=(zb,))
        pg_ = proj(512, 128)
        P.op("act", lambda a: a.copy(out=zg[:, 1:W + 1], in_=pg_[:, :]), reads=(pg_,), writes=(zg,))
        ew_tt("dve", dbuf[:, :, :], zb[:, :, 0:W], zb[:, :, 1:W + 1], ALU.subtract, (zb,), (dbuf,))
        for m in range(5):
            P.op("dve", lambda v, m=m: v.scalar_tensor_tensor(out=zz[:, m, :], in0=dbuf[:, m, :], scalar=vec[0:64, 9 + m:10 + m], in1=zb[:, m, 1:W + 1], op0=ALU.mult, op1=ALU.add),
                 reads=(dbuf, vec, zb), writes=(zz,))
        P.op("dve", lambda v: v.tensor_copy(out=zb[:, :, 0:1], in_=zb[:, :, W:W + 1]), reads=(zb,), writes=(zb,))
        ew_tt("dve", dg[:, :], zg[:, 0:W], zg[:, 1:W + 1], ALU.subtract, (zg,), (dg,))
        P.op("dve", lambda v: v.scalar_tensor_tensor(out=dg[:, :], in0=dg[:, :], scalar=vec[:, 8:9], in1=zg[:, 1:W + 1], op0=ALU.mult, op1=ALU.add),
             reads=(dg, vec, zg), writes=(dg,))
        P.op("dve", lambda v: v.tensor_copy(out=zg[:, 0:1], in_=zg[:, W:W + 1]), reads=(zg,), writes=(zg,))
        yield
        Rm, Km, Vm, XW, XA = (zz[:, i, :] for i in range(5))
        P.op("act", lambda a: a.activation(out=V("tw")[:, :], in_=XW, func=AF.Tanh), reads=(zz,), writes=(V("tw"),))
        P.op("act", lambda a: a.activation(out=sgx[:, :], in_=dg[:, :], func=AF.Sigmoid), reads=(dg,), writes=(sgx,))
        pw = bank()
        mm_group(P, pw[0:64, :], pw, [(w2h[:, :], V("tw")[:, :])], reads=(w2h, V("tw")))
        pa = bank()
        mm_group(P, pa[0:64, :], pa, [(a2h[:, :], XA)], reads=(a2h, zz))
        pgm = bank()
        mm_group(P, pgm[0:64, :], pgm, [(g2h[:, :], sgx[:, :])], reads=(g2h, sgx))
        P.op("act", lambda a: a.activation(out=V("sgw")[:, :], in_=pw[0:64, :], func=AF.Sigmoid, bias=vec[0:64, 14:15], scale=1.0), reads=(pw, vec), writes=(V("sgw"),))
        P.op("act", lambda a: a.activation(out=V("asg")[:, :], in_=pa[0:64, :], func=AF.Sigmoid, bias=vec[0:64, 15:16], scale=1.0), reads=(pa, vec), writes=(V("asg"),))
        P.op("act", lambda a: a.copy(out=V("gmap")[:, :], in_=pgm[0:64, :]), reads=(pgm,), writes=(V("gmap"),))
        P.op("dve", lambda v: v.tensor_scalar(out=V("lw")[:, :], in0=V("sgw")[:, :], scalar1=-DECAY_SCALE, scalar2=None, op0=ALU.mult), reads=(V("sgw"),), writes=(V("lw"),))
        P.op("dve", lambda v: v.tensor_scalar(out=V("kk0")[:, :], in0=Km, scalar1=vec[0:64, 16:17], scalar2=None, op0=ALU.mult), reads=(zz, vec), writes=(V("kk0"),))
        ew_tt("dve", V("kksq")[:, :], V("kk0")[:, :], V("kk0")[:, :], ALU.mult, (V("kk0"),), (V("kksq"),))
        yield
        pss = bank()
        mm_group(P, pss[0:64, :], pss, [(onesf[0:64, 0:64], V("kksq")[:, :])], reads=(onesf, V("kksq")))
        P.op("dve", lambda v: v.tensor_scalar(out=V("ssc")[:, :], in0=pss[0:64, :], scalar1=1e-24, scalar2=None, op0=ALU.max), reads=(pss,), writes=(V("ssc"),))
        P.op("dve", lambda v: v.tensor_scalar(out=V("tt")[:, :], in0=V("asg")[:, :], scalar1=-1.0, scalar2=vec[0:64, 17:18], op0=ALU.add, op1=ALU.mult), reads=(V("asg"), vec), writes=(V("tt"),))
        P.op("dve", lambda v: v.scalar_tensor_tensor(out=V("kmod")[:, :], in0=V("tt")[:, :], scalar=1.0, in1=Km, op0=ALU.add, op1=ALU.mult), reads=(V("tt"), zz), writes=(V("kmod"),))
        P.op("dve", lambda v: v.scalar_tensor_tensor(out=V("rk")[:, :], in0=Rm, scalar=vec[0:64, 18:19], in1=V("kmod")[:, :], op0=ALU.mult, op1=ALU.mult), reads=(zz, vec, V("kmod")), writes=(V("rk"),))
        pbn = bank()
        mm_group(P, pbn[0:64, :], pbn, [(onesf[0:64, 0:64], V("rk")[:, :])], reads=(onesf, V("rk")))
        ew_tt("dve", V("bonus")[:, :], pbn[0:64, :], Vm, ALU.mult, (pbn, zz), (V("bonus"),))
        P.op("dve", lambda v: v.tensor_tensor_scan(out=V("cl")[:, :], data0=rmask[:, :], data1=V("lw")[:, :], initial=0.0, op0=ALU.mult, op1=ALU.add),
             reads=(rmask, V("lw")), writes=(V("cl"),))
        ew_tt("dve", V("cm")[:, :], V("cl")[:, :], V("lw")[:, :], ALU.subtract, (V("cl"), V("lw")), (V("cm"),))
        yield
        P.op("act", lambda a: a.activation(out=V("rs")[:, :], in_=V("ssc")[:, :], func=AF.Ln), reads=(V("ssc"),), writes=(V("rs"),))
        P.op("act", lambda a: a.activation(out=V("rs")[:, :], in_=V("rs")[:, :], func=AF.Exp, scale=-0.5), reads=(V("rs"),), writes=(V("rs"),))
        P.op("act", lambda a: a.activation(out=V("gi")[:, :], in_=V("cl")[:, :], func=AF.Exp), reads=(V("cl"),), writes=(V("gi"),))
        P.op("act", lambda a: a.activation(out=V("ginv")[:, :], in_=V("cl")[:, :], func=AF.Exp, scale=-1.0), reads=(V("cl"),), writes=(V("ginv"),))
        P.op("act", lambda a: a.activation(out=V("gprev")[:, :], in_=V("cm")[:, :], func=AF.Exp), reads=(V("cm"),), writes=(V("gprev"),))
        ew_tt("dve", V("kkn")[:, :], V("kk0")[:, :], V("rs")[:, :], ALU.mult, (V("kk0"), V("rs")), (V("kkn"),))
        ew_tt("dve", V("bvec")[:, :], V("kkn")[:, :], V("asg")[:, :], ALU.mult, (V("kkn"), V("asg")), (V("bvec"),))
        a4 = lambda ap: ap.rearrange("p (a b) -> p a b", b=128)
        P.op("dve", lambda v: v.scalar_tensor_tensor(out=ART[:, :, 0, :], in0=a4(V("kkn")[:, :]), scalar=-1.0, in1=a4(V("gprev")[:, :]), op0=ALU.mult, op1=ALU.mult),
             reads=(V("kkn"), V("gprev")), writes=(ART,))
        ew_tt("dve", ART[:, :, 1, :], a4(Rm), a4(V("gi")[:, :]), ALU.mult, (zz, V("gi")), (ART,))
        ew_tt("dve", V("BT")[:, :], V("bvec")[:, :], V("ginv")[:, :], ALU.mult, (V("bvec"), V("ginv")), (V("BT"),))
        ew_tt("dve", V("KT")[:, :], V("kmod")[:, :], V("ginv")[:, :], ALU.mult, (V("kmod"), V("ginv")), (V("KT"),))
        yield
        BT, KT, gi = V("BT"), V("KT"), V("gi")
        def pair_gen(pr, S):
            tok, abT, akT, Lm, Xb, PPb, LMb, N0Gb, RpT = S["tok"], S["abT"], S["akT"], S["Lm"], S["Xb"], S["PPb"], S["LMb"], S["N0Gb"], S["RpT"]
            ysb, ysq, yn, gst = S["ysb"], S["ysq"], S["yn"], S["gst"]
            s0 = pr * 128
            ATp, RTp = ART[:, pr, 0, :], ART[:, pr, 1, :]
            ARp = ART[:, pr, :, :].rearrange("p a b -> p (a b)")
            BTp, KTp, VTp = BT[:, s0:s0 + 128], KT[:, s0:s0 + 128], zz[:, 2, s0:s0 + 128]
            i64 = identf[0:64, 0:64]
            i64b = identb[0:64, 0:64]
            pT = bank()
            mm_group(P, pT[:, 0:64], pT, [(BTp, i64b)], reads=(BT, identb))
            mm_group(P, pT[:, 64:128], pT, [(KTp, i64b)], reads=(KT, identb))
            mm_group(P, pT[:, 128:192], pT, [(VTp, i64)], reads=(zz, identf))
            P.op("act", lambda a, pT=pT: a.copy(out=tok[:, :], in_=pT[:, 0:192]), reads=(pT,), writes=(tok,))
            yield
            pA = bank()
            mm_group(P, pA[:, 0:256], pA, [(BTp, ARp)], reads=(BT, ART))
            ew_tt("dve", abT[:, :], pA[:, 0:256], mask2[:, :], ALU.mult, (pA, mask2), (abT,))
            pK = bank()
            mm_group(P, pK[:, 0:256], pK, [(KTp, ARp)], reads=(KT, ART))
            ew_tt("dve", akT[:, :], pK[:, 0:256], mask2[:, :], ALU.mult, (pK, mask2), (akT,))
            pL = bank()
            mm_group(P, pL[:, 0:128], pL, [(ATp, BTp)], reads=(ART, BT))
            ew_tt("dve", Lm[:, :], pL[:, 0:128], maskS[:, :], ALU.mult, (pL, maskS), (Lm,))
            yield
            pX = bank()
            mm_group(P, pX[:, 0:64], pX, [(ATp, i64b)], reads=(ART, identb))
            mm_group(P, pX[:, 64:128], pX, [(akT[:, 0:128], tok[:, 128:192])], reads=(akT, tok))
            X = Xb[0]
            P.op("act", lambda a, pX=pX, X=X: a.copy(out=X[:, :], in_=pX[:, 0:128]), reads=(pX,), writes=(X,))
            yield
            Pk_ap, PkT_ap, Pk_b, PkT_b = Lm[:, :], abT[:, 0:128], Lm, abT
            for it in range(6):
                pXn = bank()
                mm_group(P, pXn[:, 0:128], pXn, [(PkT_ap, X[:, :])], reads=(PkT_b, X))
                Xn = Xb[(it + 1) % 2]
                ew_tt("dve", Xn[:, :], pXn[:, 0:128], X[:, :], ALU.add, (pXn, X), (Xn,))
                yield
                if it < 5:
                    pP = bank()
                    mm_group(P, pP[:, 0:128], pP, [(PkT_ap, Pk_ap)], reads=(PkT_b, Pk_b))
                    mm_group(P, pP[:, 128:256], pP, [(Pk_ap, PkT_ap)], reads=(PkT_b, Pk_b))
                    PPn = PPb[it % 2]
                    P.op("act", lambda a, pP=pP, PPn=PPn: a.copy(out=PPn[:, :], in_=pP[:, 0:256]), reads=(pP,), writes=(PPn,))
                    Pk_ap, PkT_ap, Pk_b, PkT_b = PPn[:, 0:128], PPn[:, 128:256], PPn, PPn
                X = Xn
            Zs = []
            for hh in range(2):
                hs = slice(hh * 64, hh * 64 + 64)
                pMN = bank()
                mm_group(P, pMN[0:64, 0:64], pMN, [(X[hs, 0:64], tok[hs, 0:64])], reads=(X, tok))
                mm_group(P, pMN[0:64, 64:128], pMN, [(tok[hs, 0:64], X[hs, 64:128]), (tok[hs, 64:128], tok[hs, 128:192])], reads=(X, tok))
                ge = gi[:, s0 + hh * 64 + 63:s0 + hh * 64 + 64]
                LM, N0G = LMb[hh], N0Gb[hh]
                ew_tt("dve", LM[:, :], pMN[0:64, 0:64], i64, ALU.add, (pMN, identf), (LM,))
                P.op("dve", lambda v, pMN=pMN, N0G=N0G, ge=ge: v.tensor_scalar(out=N0G[:, :], in0=pMN[0:64, 64:128], scalar1=ge, scalar2=None, op0=ALU.mult),
                     reads=(pMN, gi), writes=(N0G,))
                Zc = Zb[zst["i"] % NZ]
                Zn = Zb[(zst["i"] + 1) % NZ]
                zst["i"] += 1
                pZ = bank()
                mm_group(P, pZ[0:64, 0:64], pZ, [(LM[:, :], Zc[:, :])], reads=(LM, Zc))
                P.op("dve", lambda v, pZ=pZ, Zn=Zn, N0G=N0G, ge=ge: v.scalar_tensor_tensor(out=Zn[:, :], in0=pZ[0:64, 0:64], scalar=ge, in1=N0G[:, :], op0=ALU.mult, op1=ALU.add),
                     reads=(pZ, gi, N0G), writes=(Zn,))
                Zs.append(Zc)
            pR = bank()
            mm_group(P, pR[0:64, 0:128], pR, [(X[:, 0:64], abT[:, 128:256])], reads=(X, abT))
            ew_tt("dve", RpT[:, :], pR[0:64, 0:128], RTp, ALU.add, (pR, ART), (RpT,))
            yield
            pY = bank()
            P.op("pe", lambda t, pY=pY, X=X: t.matmul(pY[:, 0:64], lhsT=abT[:, 128:256], rhs=X[:, 64:128], start=True, stop=False), reads=(abT, X), writes=(pY,), inc=False)
            P.op("pe", lambda t, pY=pY: t.matmul(pY[:, 0:64], lhsT=akT[:, 128:256], rhs=tok[:, 128:192], start=False, stop=False), reads=(akT, tok), writes=(pY,), inc=False)
            P.op("pe", lambda t, pY=pY: t.matmul(pY[0:64, 0:64], lhsT=RpT[:, 0:64], rhs=Zs[0][:, :], start=False, stop=False), reads=(RpT, Zs[0]), writes=(pY,), inc=False)
            P.op("pe", lambda t, pY=pY: t.matmul(pY[64:128, 0:64], lhsT=RpT[:, 64:128], rhs=Zs[1][:, :], start=False, stop=True), reads=(RpT, Zs[1]), writes=(pY,))
            P.op("act", lambda a, pY=pY: a.copy(out=ysb[:, :], in_=pY[:, 0:64]), reads=(pY,), writes=(ysb,))
            yield
            P.op("dve", lambda v: v.tensor_reduce(out=gst[:, 0:1], in_=ysb[:, :], axis=AX.X, op=ALU.add), reads=(ysb,), writes=(gst,))
            ew_tt("dve", ysq[:, :], ysb[:, :], ysb[:, :], ALU.mult, (ysb,), (ysq,))
            P.op("dve", lambda v: v.tensor_reduce(out=gst[:, 1:2], in_=ysq[:, :], axis=AX.X, op=ALU.add), reads=(ysq, gst), writes=(gst,))
            P.op("dve", lambda v: v.tensor_scalar(out=gst[:, 2:4], in0=gst[:, 0:2], scalar1=1.0 / 64, scalar2=None, op0=ALU.mult), reads=(gst,), writes=(gst,))
            ew_tt("dve", gst[:, 4:5], gst[:, 2:3], gst[:, 2:3], ALU.mult, (gst,), (gst,))
            ew_tt("dve", gst[:, 5:6], gst[:, 3:4], gst[:, 4:5], ALU.subtract, (gst,), (gst,))
            P.op("act", lambda a: a.activation(out=gst[:, 6:7], in_=gst[:, 5:6], func=AF.Ln, bias=GN_EPS, scale=1.0), reads=(gst,), writes=(gst,))
            P.op("act", lambda a: a.activation(out=gst[:, 7:8], in_=gst[:, 6:7], func=AF.Exp, scale=-0.5), reads=(gst,), writes=(gst,))
            P.op("dve", lambda v: v.tensor_scalar(out=yn[:, :], in0=ysb[:, :], scalar1=gst[:, 2:3], scalar2=gst[:, 7:8], op0=ALU.subtract, op1=ALU.mult), reads=(ysb, gst), writes=(yn,))
            yield
            pYT = bank()
            mm_group(P, pYT[0:64, 0:128], pYT, [(yn[:, :], identf[:, :])], reads=(yn, identf))
            P.op("act", lambda a, pYT=pYT, s0=s0: a.copy(out=V("ynT")[:, s0:s0 + 128], in_=pYT[0:64, 0:128]), reads=(pYT,), writes=(V("ynT"),))

        for p0 in range(0, 4, NSET):
            gens = [pair_gen(p0 + k, sets[k]) for k in range(NSET)]
            while gens:
                for g_ in list(gens):
                    try:
                        next(g_)
                    except StopIteration:
                        gens.remove(g_)
                yield
        P.op("dve", lambda v: v.tensor_scalar(out=V("yo")[:, :], in0=V("ynT")[:, :], scalar1=vec[0:64, 19:20], scalar2=vec[0:64, 20:21], op0=ALU.mult, op1=ALU.add), reads=(V("ynT"), vec), writes=(V("yo"),))
        ew_tt("dve", V("yo")[:, :], V("yo")[:, :], V("bonus")[:, :], ALU.add, (V("yo"), V("bonus")), (V("yo"),))
        ew_tt("dve", V("yo")[:, :], V("yo")[:, :], V("gmap")[:, :], ALU.mult, (V("yo"), V("gmap")), (V("yo"),))
        yb_ = yxk[c0 // CH]
        P.dma("sp", yb_, lambda q: q.dma_start(out=yb_.t.ap()[64:128, c0 % CH:c0 % CH + W], in_=V("yo")[:, :]), reads=(V("yo"),))

    if do_attn:
        NTI = P.sb([128, 128], BF16, "NTI")
        NON = P.sb([128, 128], BF16, "NON")
        m01 = P.sb([128, 128], BF16, "m01")
        Z0 = P.sb([128, 64], BF16, "Z0")
        P.op("pool", lambda g: g.memset(NON[:, :], -1.0), writes=(NON,))
        P.op("pool", lambda g: g.memset(Z0[:, :], 0.0), writes=(Z0,))
        P.op("pool", lambda g: g.affine_select(out=NTI[:, :], in_=NON[:, :], pattern=[[-1, 128]], compare_op=ALU.is_ge, fill=0.0, base=0, channel_multiplier=1),
             reads=(NON,), writes=(NTI,))
        P.op("pool", lambda g: g.affine_select(out=m01[:, :], in_=onesb[:, :], pattern=[[1, 128]], compare_op=ALU.is_ge, fill=0.0, base=-1, channel_multiplier=-1),
             reads=(onesb,), writes=(m01,))
        eb = [P.sb([128, W], BF16, f"eb{i}") for i in range(2)]
        spb = [P.sb([128, W], BF16, f"spb{i}") for i in range(3)]
        Ab = [P.sb([128, W], BF16, f"Ab{i}") for i in range(3)]
        spaccs = [P.sb([128, W], BF16, f"spacc{i}") for i in range(2)]
        ybo = P.sb([64, W], F32, "ybo")
        po = banks[7]
        rot = {"i": 0}
        cnt = {"n": 0}

        def bank3():
            b = banks[4 + rot["i"] % 3]
            rot["i"] += 1
            return b

        def ew2(o, a, b, op, rd, wr):
            P.op("dve", lambda v: v.tensor_tensor(out=o, in0=a, in1=b, op=op), reads=rd, writes=wr)

        def S1(d):
            qt, kb, cc, w = d["qt"], d["kb"], d["cc"], d["w"]
            n = cnt["n"]
            cnt["n"] += 1
            d["n"] = n
            kb_buf = kTt[kb // 4]
            d["kbuf"] = kb_buf
            d["kblk"] = kb_buf[:, (kb % 4) * 128:(kb % 4 + 1) * 128]
            d["qcols"] = qTt[qt][:, cc:W]
            e_, sp_ = eb[n % 2], spb[n % 3]
            d["sp"] = sp_
            pz = bank3()
            d["pz"] = pz
            mm_group(P, pz[:, 0:w], pz, [(d["kblk"], d["qcols"])], reads=(kb_buf, qTt[qt]))
            P.op("act", lambda a: a.activation(out=e_[:, 0:w], in_=pz[:, 0:w], func=AF.Exp), reads=(pz,), writes=(e_,))
            P.op("act", lambda a: a.activation(out=sp_[:, 0:w], in_=e_[:, 0:w], func=AF.Ln, bias=1.0, scale=1.0), reads=(e_,), writes=(sp_,))
            if d["diag"]:
                ew2(sp_[:, 0:128], sp_[:, 0:128], m01[:, :], ALU.mult, (sp_, m01), (sp_,))

        def S2(d):
            qt, kb, cc, w, n = d["qt"], d["kb"], d["cc"], d["w"], d["n"]
            sp_, A_ = d["sp"], Ab[n % 3]
            d["A"] = A_
            spacc = spaccs[qt % 2]
            if d["first"]:
                P.op("pool", lambda g: g.memset(spacc[:, :], 0.0), writes=(spacc,))
            pe_ = d["pz"]
            P.op("pe", lambda t: t.matmul(pe_[:, 0:w], lhsT=NTI[:, :], rhs=sp_[:, 0:w], start=False, stop=False), reads=(NTI, sp_), writes=(pe_,), inc=False)
            P.op("pe", lambda t: t.matmul(pe_[:, 0:w], lhsT=NON[:, :], rhs=spacc[:, cc:W], start=False, stop=True), reads=(NON, spacc), writes=(pe_,))
            P.op("act", lambda a: a.activation(out=A_[:, 0:w], in_=pe_[:, 0:w], func=AF.Exp), reads=(pe_,), writes=(A_,))
            if d["diag"]:
                ew2(A_[:, 0:128], A_[:, 0:128], m01[:, :], ALU.mult, (A_, m01), (A_,))
            if not d["last"]:
                ew2(spacc[:, cc:W], spacc[:, cc:W], sp_[:, 0:w], ALU.add, (spacc, sp_), (spacc,))

        def S3(d):
            qt, kb, cc, w = d["qt"], d["kb"], d["cc"], d["w"]
            if d["first"]:
                for c4 in range(4):
                    P.op("pe", lambda t, c4=c4: t.matmul(po[0:64, c4 * 128:(c4 + 1) * 128], lhsT=Z0[:, :], rhs=NON[:, :], start=True, stop=False),
                         reads=(Z0, NON), writes=(po,), inc=(c4 == 3))
            A_ = d["A"]
            vb = vtt[kb // 4]
            P.op("pe", lambda t: t.matmul(po[0:64, cc:W], lhsT=vb[:, kb % 4, :], rhs=A_[:, 0:w], start=False, stop=d["last"]), reads=(vb, A_), writes=(po,))
            if d["last"]:
                q0 = qt * W
                P.op("act", lambda a: a.copy(out=ybo[:, :], in_=po[0:64, :]), reads=(po,), writes=(ybo,))
                yb_ = yxk[q0 // CH]
                P.dma("sp", yb_, lambda q: q.dma_start(out=yb_.t.ap()[0:64, q0 % CH:q0 % CH + W], in_=ybo[:, :]), reads=(ybo,))

        def attn_gen(qt):
            tl = []
            for kb in range(4 * qt + 3, -1, -1):
                diag = kb >= 4 * qt
                cc = 128 * (kb - 4 * qt) if diag else 0
                tl.append(dict(qt=qt, kb=kb, diag=diag, cc=cc, w=W - cc, first=(kb == 4 * qt + 3), last=(kb == 0)))
            n = len(tl)
            for i in range(-2, n):
                if 0 <= i + 2 < n:
                    S1(tl[i + 2])
                if 0 <= i + 1 < n:
                    S2(tl[i + 1])
                if 0 <= i < n:
                    S3(tl[i])
                yield

    for ti in range(NTL):
        c0 = ti * W
        if L == 0:
            xsrc = io["xT0"]
            P.dma("sp", xt, lambda q: q.dma_start(out=xt[:, :, :], in_=xsrc.t.ap().rearrange("(kc p) t -> p kc t", p=128)[:, :, c0:c0 + W]), reads=(xsrc,))
        else:
            sg_, o_ = c0 // NT, c0 % NT
            for kc in range(KC):
                xb_ = io["xgk"][kc][o_ // CH]
                P.dma("sp", xt, lambda q, kc=kc, xb_=xb_: q.dma_start(out=xt[:, kc, :], in_=xb_.t.ap()[sg_ * 128:(sg_ + 1) * 128, o_ % CH:o_ % CH + W]), reads=(xb_,))
        rmsnorm()
        ag = None
        if do_attn:
            pq = proj(0, 64)
            P.op("act", lambda a: a.activation(out=qTt[ti][:, :], in_=pq[0:64, :], func=AF.Copy, scale=0.125), reads=(pq,), writes=(qTt[ti],))
            pk = proj(64, 64)
            P.op("act", lambda a: a.copy(out=kTt[ti][:, :], in_=pk[0:64, :]), reads=(pk,), writes=(kTt[ti],))
            for tb in range(4):
                pv = bank()
                mm_group(P, pv[:, 0:64], pv, [(h[:, kc, tb * 128:(tb + 1) * 128], wsv[:, kc, 128:192]) for kc in range(KC)], reads=(h, ws))
                P.op("dve", lambda v, pv=pv, tb=tb: v.tensor_copy(out=vtt[ti][:, tb, :], in_=pv[:, 0:64]), reads=(pv,), writes=(vtt[ti],))
            ag = attn_gen(ti) if interleave else None
            n_it = 4 * ti + 4 + 2
        if do_rwkv:
            per = max(1, -(-n_it // 26)) if ag is not None else 0
            for _ in rwkv_tile(ti):
                if ag is not None:
                    for _k in range(per):
                        if next(ag, "done") == "done":
                            ag = None
                            break
        if ag is not None:
            for _ in ag:
                pass
        if io.get("after_seq_tile") is not None:
            io["after_seq_tile"](ti)
    if do_attn and not interleave:
        for ti in range(NTL):
            for _ in attn_gen(ti):
                pass


def pack_seqmix_weights(inp, l, hd):
    f = np.float32
    w_in = inp["w_in"][l]
    hs = slice(hd * 64, hd * 64 + 64)
    cols = np.concatenate([SB_OFF + np.arange(64) + hd * 64, SB_OFF + 256 + np.arange(64) + hd * 64, SB_OFF + 512 + np.arange(64) + hd * 64,
                           RW_OFF + np.arange(64) + hd * 64, RW_OFF + 256 + np.arange(64) + hd * 64, RW_OFF + 512 + np.arange(64) + hd * 64,
                           RW_OFF + 768 + np.arange(256)])
    d = {"wS": _kc_layout(w_in[:, cols])}
    vec = np.zeros((128, 32), f)
    vec[:, 0:8] = _pvec(inp["mix_norm_g"][l])
    mu = inp["rw_mu"][l]
    vec[:, 8] = mu[896:1024]
    for m, o in enumerate((hd * 64, 256 + hd * 64, 512 + hd * 64, 768, 832)):
        vec[0:64, 9 + m] = mu[o:o + 64]
    vec[0:64, 14] = inp["rw_w0"][l][hs]
    vec[0:64, 15] = inp["rw_a0"][l][hs]
    vec[0:64, 16] = inp["rw_k_k"][l][hs]
    vec[0:64, 17] = inp["rw_k_a"][l][hs]
    vec[0:64, 18] = inp["rw_r_k"][l][hd]
    vec[0:64, 19] = inp["rw_gn_g"][l][hs]
    vec[0:64, 20] = inp["rw_gn_b"][l][hs]
    d["vecS"] = vec
    d["w2h"] = inp["rw_w2"][l][:, hs]
    d["a2h"] = inp["rw_a2"][l][:, hs]
    d["g2h"] = inp["rw_g2"][l][:, hs]
    return {k: np.ascontiguousarray(v, dtype=f) for k, v in d.items()}


def build_fused(T, nlayers=2):
    nc = bass.Bass("TRN2", target_bir_lowering=False)
    P = Prog(nc)
    NT = T // 4
    TP = HALO + T
    EI = "ExternalInput"
    CH = 1024
    io = {
        "xT0": P.dram("xT0", [D, T], F32, EI),
        "xTh": P.dram("xTh", [D, HALO + NT], F32, EI),
        "hm": P.dram("hm", [128, 1], F32, EI),
        "oh": P.dram("oh", [128, 4], F32, EI),
        "ohp": P.dram("ohp", [128, 4], F32, EI),
        "yxk": [P.dram(f"yx{k}", [128, CH], F32, semkey="yx") for k in range(T // CH)],
        "ygk": [P.dram(f"yg{k}", [512, CH], F32, semkey="yg") for k in range(T // CH)],
        "xok": [[P.dram(f"xo{kc}_{cc}", [128, CH], F32, semkey="xo") for cc in range(NT // CH)] for kc in range(KC)],
        "xgk": [[P.dram(f"xg{kc}_{cc}", [512, CH], F32, semkey="xg") for cc in range(NT // CH)] for kc in range(KC)],
    }
    out = P.dram("out", [D, NT], F32, "ExternalOutput")
    RG = [[0, 1, 2, 3], [4, 5, 6, 7]]

    def gather(src, dst):
        P.dma("pool", dst, lambda g: g.collective_compute("AllGather", ALU.bypass, replica_groups=RG, ins=[src.t.ap().opt()], outs=[dst.t.ap().opt()]),
              reads=(src,), inc=1)

    def after_seq_tile(ti):
        if ti % 2 == 1:
            gather(io["yxk"][ti // 2], io["ygk"][ti // 2])

    for L in range(nlayers):
        lastL = (L == nlayers - 1)
        P.begin_phase()
        io["after_seq_tile"] = after_seq_tile
        emit_seqmix(P, T, L, io)
        P.end_phase()
        P.begin_phase()
        io["dense_out"] = out

        def after_dense_tile(ti):
            if ti % 2 == 1:
                for kc in range(KC):
                    gather(io["xok"][kc][ti // 2], io["xgk"][kc][ti // 2])
        io["after_dense_tile"] = None if lastL else after_dense_tile
        emit_dense(P, NT, L, L == 1, io, last_out=lastL)
        if lastL:
            P.final_wait("sp", (out,))
        P.end_phase()
    return nc, list(P.ext_in)


_CACHE = {}


def kernel(**inputs):
    inp = {k: np.asarray(v, dtype=np.float32) for k, v in inputs.items()}
    x = inp["x"]
    B, T, _ = x.shape
    NT = T // 4
    cores = list(range(8))
    nl = int(inputs.get("_nlayers", 2)) if "_nlayers" in inputs else 2
    if (T, nl) not in _CACHE:
        _CACHE[(T, nl)] = build_fused(T, nl)
    nc, ext_names = _CACHE[(T, nl)]
    xT_b = [np.ascontiguousarray(x[b].T) for b in range(B)]
    dw = [pack_dense_weights(inp, l) for l in range(2)]
    maps = []
    for c in cores:
        b, sg = c // 4, c % 4
        m = {"xT0": xT_b[b]}
        halo = np.zeros((D, HALO + NT), np.float32)
        if sg > 0:
            halo[:, :] = xT_b[b][:, sg * NT - HALO:(sg + 1) * NT]
        else:
            halo[:, HALO:] = xT_b[b][:, 0:NT]
        m["xTh"] = halo
        m["hm"] = np.full((128, 1), 0.0 if sg == 0 else 1.0, np.float32)
        oh = np.zeros((128, 4), np.float32)
        oh[:, sg] = 1.0
        ohp = np.zeros((128, 4), np.float32)
        if sg > 0:
            ohp[:, sg - 1] = 1.0
        m["oh"], m["ohp"] = oh, ohp
        for l in range(nl):
            for k, v in dw[l].items():
                m[f"{k}_{l}"] = v
            for k, v in pack_seqmix_weights(inp, l, sg).items():
                m[f"{k}_{l}"] = v
        maps.append(m)
    maps = [{k: m[k] for k in ext_names} for m in maps]
    res = run_bass_kernel_spmd(nc, maps, core_ids=cores)
    outp = np.zeros((B, T, D), np.float32)
    for c in cores:
        b, sg = c // 4, c % 4
        outp[b, sg * NT:(sg + 1) * NT, :] = res.results[c]["out"].T
    return outp
```

```python
import math
from contextlib import ExitStack
import numpy as np
import concourse.bass as bass
import concourse.mybir as mybir
from concourse.bass_utils import run_bass_kernel_spmd

F32 = mybir.dt.float32
BF16 = mybir.dt.bfloat16
AF = mybir.ActivationFunctionType
ALU = mybir.AluOpType
AX = mybir.AxisListType

D = 1024
KC = 8
DFF = 2816
NJ = 22
HALO = 128
RMS_EPS = 1e-6
LN_EPS = 1e-5
GN_EPS = 64e-5
DECAY_SCALE = math.exp(-0.5)


class Buf:
    def __init__(self, t, name, semkey=None):
        self.t = t
        self.name = name
        self.last_w = None
        self.readers = {}
        self.dma_sem = None
        self.semkey = semkey

    def __getitem__(self, idx):
        return self.t[idx]


class _Rec:
    def __init__(self):
        self.call = None

    def __getattr__(self, name):
        def f(*a, **k):
            self.call = (name, a, k)
            return self
        return f


def _rec(fn):
    r = _Rec()
    fn(r)
    assert r.call is not None
    return r.call


class Prog:
    ENGS = ("pe", "act", "dve", "pool", "sp")

    def __init__(self, nc):
        self.nc = nc
        self.ops = {e: [] for e in self.ENGS}
        self.sems = {}
        self.cnt = {}
        self.is_dma = {}
        self.waited = {e: {} for e in self.ENGS}
        self.pending = {e: False for e in self.ENGS}
        self.ekey = {}
        for e in ("pe", "act", "dve", "pool"):
            self._mksem(e, False)
            self.ekey[e] = e
        self.nphase = 0
        self.nbuf = 0
        self.rr = 0
        self.stack = None
        self.ext_in = []

    def begin_phase(self):
        self.stack = ExitStack()
        self.nphase += 1
        for e in ("pe", "act", "dve", "pool"):
            key = f"{e}_p{self.nphase}"
            self._mksem(key, False)
            self.ekey[e] = key
        for e in self.ENGS:
            waits = []
            for k, v in self.cnt.items():
                if v > 0 and k != self.ekey.get(e) and self.waited[e].get(k, 0) < v:
                    self.waited[e][k] = v
                    waits.append((k, v))
            if waits:
                self.ops[e].append((waits, None, None))

    def end_phase(self):
        self.emit()
        self.ops = {e: [] for e in self.ENGS}
        self.stack.close()
        self.stack = None

    def _mksem(self, key, dma):
        self.sems[key] = self.nc.alloc_semaphore("s_" + key)
        self.cnt[key] = 0
        self.is_dma[key] = dma

    def sb(self, shape, dt=F32, name=None):
        self.nbuf += 1
        name = (name or "sb") + f"_{self.nbuf}"
        if self.stack is not None:
            return Buf(self.stack.enter_context(self.nc.sbuf_tensor(name, list(shape), dt)), name)
        return Buf(self.nc.alloc_sbuf_tensor(name, list(shape), dt), name)

    def ps(self, shape=(128, 512), dt=F32, name=None):
        self.nbuf += 1
        name = (name or "ps") + f"_{self.nbuf}"
        if self.stack is not None:
            return Buf(self.stack.enter_context(self.nc.psum_tensor(name, list(shape), dt)), name)
        return Buf(self.nc.alloc_psum_tensor(name, list(shape), dt), name)

    def dram(self, name, shape, dt=F32, kind="Internal", semkey=None):
        if kind == "ExternalInput":
            self.ext_in.append(name)
        return Buf(self.nc.dram_tensor(name, list(shape), dt, kind=kind), name, semkey)

    def _needs(self, eng, reads, writes, pe_chain):
        needs = {}

        def add(tok):
            if tok is None:
                return
            k, v = tok
            if needs.get(k, 0) < v:
                needs[k] = v

        for b in reads:
            add(b.last_w)
        for b in writes:
            add(b.last_w)
            for k, v in b.readers.items():
                add((k, v))
        out = []
        for k, v in needs.items():
            if self.is_dma[k]:
                v = self.cnt[k]
            elif eng == "pe" and k == self.ekey["pe"] and pe_chain:
                continue
            if self.waited[eng].get(k, 0) >= v:
                continue
            self.waited[eng][k] = v
            out.append((k, v))
        return out

    def _commit(self, tok, reads, writes):
        k, v = tok
        for b in reads:
            if b.readers.get(k, 0) < v:
                b.readers[k] = v
        for b in writes:
            b.last_w = tok
            b.readers = {}

    def op(self, eng, fn, reads=(), writes=(), inc=True):
        waits = self._needs(eng, reads, writes, True)
        key = self.ekey[eng]
        if inc:
            self.cnt[key] += 1
            tok = (key, self.cnt[key])
            self.pending[eng] = False
            self.ops[eng].append((waits, _rec(fn), (key, 1)))
        else:
            tok = (key, self.cnt[key] + 1)
            self.pending[eng] = True
            self.ops[eng].append((waits, _rec(fn), None))
        self._commit(tok, reads, writes)

    def dma(self, q, out_buf, fn, reads=(), inc=16):
        if out_buf.dma_sem is None:
            key = "d_" + (out_buf.semkey or out_buf.name)
            if key not in self.sems:
                self._mksem(key, True)
            out_buf.dma_sem = key
        key = out_buf.dma_sem
        waits = self._needs(q, reads, (out_buf,), False)
        self.cnt[key] += inc
        tok = (key, self.cnt[key])
        self.ops[q].append((waits, _rec(fn), (key, inc)))
        self._commit(tok, reads, (out_buf,))

    def final_wait(self, eng, bufs):
        waits = self._needs(eng, bufs, (), False)
        self.ops[eng].append((waits, None, None))

    def emit(self):
        sems = self.sems
        for e in self.ENGS:
            assert not self.pending[e], e

        def run(e_name):
            def body(engine):
                for waits, fn, inc in self.ops[e_name]:
                    for k, v in waits:
                        engine.wait_ge(sems[k], v)
                    if fn is not None:
                        ins = getattr(engine, fn[0])(*fn[1], **fn[2])
                        if inc is not None:
                            ins.then_inc(sems[inc[0]], inc[1])
            return body

        with self.nc.Block() as block:
            block.tensor(run("pe"))
            block.scalar(run("act"))
            block.vector(run("dve"))
            block.gpsimd(run("pool"))
            block.sync(run("sp"))

    def ew(self):
        self.rr ^= 1
        return "dve" if self.rr else "pool"


def mm_group(P, out_ap, out_buf, pairs, reads):
    n = len(pairs)
    for i, (l, r) in enumerate(pairs):
        P.op("pe", (lambda t, l=l, r=r, i=i: t.matmul(out_ap, lhsT=l, rhs=r, start=(i == 0), stop=(i == n - 1))),
             reads=reads, writes=(out_buf,), inc=(i == n - 1))


def emit_dense(P, NT, L, last_layer, io, W=512, last_out=True):
    TT = HALO + NT
    EI = "ExternalInput"
    sfx = f"_{L}"
    hm, ohd, ohpd = io["hm"], io["oh"], io["ohp"]
    CH = 1024
    wAD = P.dram("wAD" + sfx, [128, KC * 1280], F32, EI)
    wG = P.dram("wG" + sfx, [8, 128, KC * 512], F32, EI)
    wOut = P.dram("wOut" + sfx, [4, 128, 2 * D], F32, EI)
    wO = P.dram("wO" + sfx, [2, 128, KC * 512], F32, EI)
    wUp = P.dram("wUp" + sfx, [11, 128, KC * 512], F32, EI)
    wDn = P.dram("wDn" + sfx, [8, 128, NJ * 128], F32, EI)
    vecs = P.dram("vecs" + sfx, [128, 64], F32, EI)
    convF = P.dram("convF" + sfx, [128, 44 * 3], F32, EI)
    rowsD = P.dram("rowsD" + sfx, [128, 512], F32, EI)
    wsT = P.dram("wsT" + sfx, [128, 4 * 128], F32, EI)
    bsr = P.dram("bsr" + sfx, [1, 4 * 128], F32, EI)
    out = io["dense_out"]
    ohb = P.sb([128, 8], F32, "ohb")
    P.dma("sp", ohb, lambda q: q.dma_start(out=ohb[:, 0:4], in_=ohd.t.ap()), reads=(ohd,))
    P.dma("sp", ohb, lambda q: q.dma_start(out=ohb[:, 4:8], in_=ohpd.t.ap()), reads=(ohpd,))

    vec = P.sb([128, 64], F32, "vec")
    P.dma("sp", vec, lambda q: q.dma_start(out=vec[:, :], in_=vecs.t.ap()), reads=(vecs,))
    cvf = P.sb([128, 44 * 3], F32, "cvf")
    P.dma("sp", cvf, lambda q: q.dma_start(out=cvf[:, :], in_=convF.t.ap()), reads=(convF,))
    rows = P.sb([128, 512], F32, "rows")
    P.dma("sp", rows, lambda q: q.dma_start(out=rows[:, :], in_=rowsD.t.ap()), reads=(rowsD,))
    hmb = P.sb([128, 1], F32, "hmb")
    P.dma("sp", hmb, lambda q: q.dma_start(out=hmb[:, :], in_=hm.t.ap()), reads=(hm,))
    wsb = P.sb([128, 512], BF16, "wsb")
    wsf = P.sb([128, 512], F32, "wsf")
    P.dma("sp", wsf, lambda q: q.dma_start(out=wsf[:, :], in_=wsT.t.ap()), reads=(wsT,))
    P.op("pool", lambda g: g.affine_select(out=wsf[:, :].rearrange("p (g t) -> p g t", g=4),
                                           in_=wsf[:, :].rearrange("p (g t) -> p g t", g=4),
                                           pattern=[[0, 4], [1, 128]], compare_op=ALU.is_ge, fill=0.0,
                                           base=0, channel_multiplier=-1),
         reads=(wsf,), writes=(wsf,))
    P.op("dve", lambda v: v.tensor_copy(out=wsb[:, :], in_=wsf[:, :]), reads=(wsf,), writes=(wsb,))
    bsb = P.sb([1, 512], BF16, "bsb")
    P.dma("pool", bsb, lambda q: q.dma_start(out=bsb[:, :], in_=bsr.t.ap()), reads=(bsr,))
    onesb = P.sb([128, 128], BF16, "onesb")
    P.op("pool", lambda g: g.memset(onesb[:, :], 1.0), writes=(onesb,))
    wad = P.sb([128, KC * 1280], BF16, "wad")
    P.dma("pool", wad, lambda q: q.dma_start(out=wad[:, :], in_=wAD.t.ap()), reads=(wAD,))
    wadv = wad.t.ap().rearrange("p (kc c) -> p kc c", kc=KC)
    wout = [P.sb([128, 2 * D], BF16, f"wout{b}") for b in range(4)]
    for b in range(4):
        P.dma("pool", wout[b], lambda q, b=b: q.dma_start(out=wout[b][:, :], in_=wOut.t.ap()[b]), reads=(wOut,))

    NSLOT = 4
    ring = [P.sb([128, 4096], BF16, f"ring{i}") for i in range(NSLOT)]
    stream = []
    for oc in range(8):
        stream.append((wG, oc, KC * 512))
    for i in range(2):
        stream.append((wO, i, KC * 512))
    for jj in range(11):
        stream.append((wUp, jj, KC * 512))
    for oc in range(8):
        stream.append((wDn, oc, NJ * 128))
    state = {"issued": 0, "consumed": 0}

    def issue_block():
        i = state["issued"]
        src, idx, n = stream[i % len(stream)]
        slot = ring[i % NSLOT]
        P.dma("pool", slot, lambda q: q.dma_start(out=slot[:, 0:n], in_=src.t.ap()[idx]), reads=(src,))
        state["issued"] += 1

    def next_block(total_blocks):
        while state["issued"] < min(state["consumed"] + NSLOT - 1, total_blocks):
            issue_block()
        if state["issued"] <= state["consumed"]:
            issue_block()
        slot = ring[state["consumed"] % NSLOT]
        state["consumed"] += 1
        return slot

    xt = P.sb([128, KC, W], F32, "xt")
    h = P.sb([128, KC, W], BF16, "h")
    act = P.sb([128, NJ, W], BF16, "act")
    merged = P.sb([128, KC, W], F32, "merged")
    mb = P.sb([128, KC, W], BF16, "mb")
    ybt = P.sb([128, 4, W], BF16, "ybt")
    ycand = [P.sb([128, 4, W], BF16, f"ycand{i}") for i in range(2)]
    ya = P.sb([128, 2, W], BF16, "ya")
    yd = P.sb([128, 2, W], BF16, "yd")
    ug = P.sb([128, 2, W], F32, "ug")
    cxh = P.sb([128, 2, W + 2], F32, "cxh")
    uh = P.sb([128, 44, 2], F32, "uh")
    uc = [P.sb([128, W + 2], F32, f"uc{i}") for i in range(2)]
    tmp = [P.sb([128, W], F32, f"tmp{i}") for i in range(6)]
    rstd = P.sb([128, W], F32, "rstd")
    lnt = P.sb([128, W], F32, "lnt")
    vtk = [P.sb([128, 256], F32, f"vtk{i}") for i in range(4)]
    vnb = P.sb([128, 256], BF16, "vnb")
    st = P.sb([128, 8], F32, "st")
    banks = [P.ps([128, 512], F32, f"bank{i}") for i in range(8)]
    bstate = {"i": 0}

    def bank():
        b = banks[bstate["i"] % 8]
        bstate["i"] += 1
        return b

    P.op("pool", lambda g: g.memset(cxh[:, :, :], 0.0), writes=(cxh,))
    P.op("pool", lambda g: g.memset(uh[:, :, :], 0.0), writes=(uh,))

    def rmsnorm(w, gcol, out_buf, out_f32=False):
        P.op("dve", lambda v: v.tensor_tensor(out=act[:, 0:KC, 0:w], in0=xt[:, :, 0:w], in1=xt[:, :, 0:w], op=ALU.mult),
             reads=(xt,), writes=(act,))
        pb = bank()
        mm_group(P, pb[:, 0:w], pb, [(onesb[:, :], act[:, kc, 0:w]) for kc in range(KC)], reads=(onesb, act))
        P.op("act", lambda a: a.activation(out=lnt[:, 0:w], in_=pb[:, 0:w], func=AF.Ln, bias=RMS_EPS, scale=1.0 / D),
             reads=(pb,), writes=(lnt,))
        P.op("act", lambda a: a.activation(out=rstd[:, 0:w], in_=lnt[:, 0:w], func=AF.Exp, scale=-0.5),
             reads=(lnt,), writes=(rstd,))
        for kc in range(KC):
            e = "dve"
            P.op(e, lambda v, kc=kc: v.scalar_tensor_tensor(out=out_buf[:, kc, 0:w], in0=xt[:, kc, 0:w],
                                                            scalar=vec[:, gcol + kc:gcol + kc + 1], in1=rstd[:, 0:w],
                                                            op0=ALU.mult, op1=ALU.mult),
                 reads=(xt, vec, rstd), writes=(out_buf,))

    ntiles = NT // W
    total_blocks = len(stream) * ntiles

    def tile(off, w, is_halo, out_off):
        if L == 0:
            xsrc = io["xTh"]
            xv = xsrc.t.ap().rearrange("(kc p) t -> p kc t", p=128)
            P.dma("sp", xt, lambda q: q.dma_start(out=xt[:, :, 0:w], in_=xv[:, :, off:off + w]), reads=(xsrc,))
        elif not is_halo:
            o_ = off - HALO
            for kc in range(KC):
                xb_ = io["xok"][kc][o_ // CH]
                P.dma("sp", xt, lambda q, kc=kc, xb_=xb_: q.dma_start(out=xt[:, kc, 0:w], in_=xb_.t.ap()[:, o_ % CH:o_ % CH + w]), reads=(xb_,))
        else:
            for sgi in range(4):
                for kc in range(KC):
                    xb_ = io["xgk"][kc][NT // CH - 1]
                    P.dma("sp", merged, lambda q, kc=kc, xb_=xb_, sgi=sgi: q.dma_start(out=merged[:, kc, 0:HALO], in_=xb_.t.ap()[sgi * 128:(sgi + 1) * 128, CH - HALO:CH]), reads=(xb_,))
                if sgi == 0:
                    P.op("dve", lambda v: v.tensor_scalar(out=xt[:, :, 0:HALO], in0=merged[:, :, 0:HALO], scalar1=ohb[:, 4:5], scalar2=None, op0=ALU.mult),
                         reads=(merged, ohb), writes=(xt,))
                else:
                    P.op("dve", lambda v, sgi=sgi: v.scalar_tensor_tensor(out=xt[:, :, 0:HALO], in0=merged[:, :, 0:HALO], scalar=ohb[:, 4 + sgi:5 + sgi], in1=xt[:, :, 0:HALO], op0=ALU.mult, op1=ALU.add),
                         reads=(merged, ohb, xt), writes=(xt,))
        for sgi in range(4):
            yc_ = ycand[sgi % 2]
            t0_ = sgi * NT + off - HALO
            if t0_ < 0:
                P.op("dve", lambda v, yc_=yc_: v.memset(yc_[:, :, 0:w], 0.0), writes=(yc_,))
            else:
                gb_ = io["ygk"][t0_ // CH]
                gv_ = gb_.t.ap().rearrange("(h p) t -> p h t", p=128)
                P.dma("pool", yc_, lambda q, yc_=yc_, gv_=gv_, t0_=t0_: q.dma_start(out=yc_[:, :, 0:w], in_=gv_[:, :, t0_ % CH:t0_ % CH + w]), reads=(gb_,))
            if sgi == 0:
                P.op("dve", lambda v, yc_=yc_: v.tensor_scalar(out=ybt[:, :, 0:w], in0=yc_[:, :, 0:w], scalar1=ohb[:, 0:1], scalar2=None, op0=ALU.mult),
                     reads=(yc_, ohb), writes=(ybt,))
            else:
                P.op("dve", lambda v, yc_=yc_, sgi=sgi: v.scalar_tensor_tensor(out=ybt[:, :, 0:w], in0=yc_[:, :, 0:w], scalar=ohb[:, sgi:sgi + 1], in1=ybt[:, :, 0:w], op0=ALU.mult, op1=ALU.add),
                     reads=(yc_, ohb, ybt), writes=(ybt,))
        rmsnorm(w, 0, h)
        hr = (h, wad)
        for ch in range(2):
            pbg, pcg, pxi = bank(), bank(), bank()
            for pb, c in ((pbg, ch), (pcg, 2 + ch), (pxi, 4 + ch)):
                mm_group(P, pb[:, 0:w], pb, [(wadv[:, kc, c * 128:(c + 1) * 128], h[:, kc, 0:w]) for kc in range(KC)], reads=hr)
            t0, t1 = tmp[0], tmp[1]
            P.op("act", lambda a: a.copy(out=t0[:, 0:w], in_=pxi[:, 0:w]), reads=(pxi,), writes=(t0,))
            P.op("dve", lambda v: v.tensor_tensor(out=cxh[:, ch, 2:2 + w], in0=pcg[:, 0:w], in1=t0[:, 0:w], op=ALU.mult),
                 reads=(pcg, t0), writes=(cxh,))
            c0 = 56 + ch * 3
            P.op("dve", lambda v: v.tensor_scalar(out=t1[:, 0:w], in0=cxh[:, ch, 0:w], scalar1=vec[:, c0:c0 + 1], scalar2=None, op0=ALU.mult),
                 reads=(cxh, vec), writes=(t1,))
            P.op("dve", lambda v: v.scalar_tensor_tensor(out=t1[:, 0:w], in0=cxh[:, ch, 1:1 + w], scalar=vec[:, c0 + 1:c0 + 2], in1=t1[:, 0:w], op0=ALU.mult, op1=ALU.add),
                 reads=(cxh, vec, t1), writes=(t1,))
            P.op("dve", lambda v: v.scalar_tensor_tensor(out=t1[:, 0:w], in0=cxh[:, ch, 2:2 + w], scalar=vec[:, c0 + 2:c0 + 3], in1=t1[:, 0:w], op0=ALU.mult, op1=ALU.add),
                 reads=(cxh, vec, t1), writes=(t1,))
            P.op("dve", lambda v: v.tensor_tensor(out=ya[:, ch, 0:w], in0=pbg[:, 0:w], in1=t1[:, 0:w], op=ALU.mult),
                 reads=(pbg, t1), writes=(ya,))
            if is_halo:
                P.op("dve", lambda v: v.tensor_scalar(out=cxh[:, ch, 0:2], in0=cxh[:, ch, w:w + 2], scalar1=hmb[:, 0:1], scalar2=None, op0=ALU.mult),
                     reads=(cxh, hmb), writes=(cxh,))
            else:
                P.op("dve", lambda v: v.tensor_copy(out=cxh[:, ch, 0:2], in_=cxh[:, ch, w:w + 2]), reads=(cxh,), writes=(cxh,))

        def gelu(dst_ap, dst_buf, src_ap, src_buf, npart, n, t_a, t_b):
            P.op("act", lambda a: a.copy(out=t_a[0:npart, 0:n], in_=src_ap), reads=(src_buf,), writes=(t_a,))
            P.op("dve", lambda g: g.tensor_tensor(out=t_b[0:npart, 0:n], in0=t_a[0:npart, 0:n], in1=t_a[0:npart, 0:n], op=ALU.mult),
                 reads=(t_a,), writes=(t_b,))
            P.op("dve", lambda v: v.tensor_scalar(out=t_b[0:npart, 0:n], in0=t_b[0:npart, 0:n], scalar1=0.044715, scalar2=1.0, op0=ALU.mult, op1=ALU.add),
                 reads=(t_b,), writes=(t_b,))
            P.op("dve", lambda g: g.tensor_tensor(out=t_b[0:npart, 0:n], in0=t_b[0:npart, 0:n], in1=t_a[0:npart, 0:n], op=ALU.mult),
                 reads=(t_a, t_b), writes=(t_b,))
            P.op("act", lambda a: a.activation(out=t_b[0:npart, 0:n], in_=t_b[0:npart, 0:n], func=AF.Sigmoid, scale=2.0 * 0.7978845608028654),
                 reads=(t_b,), writes=(t_b,))
            P.op("dve", lambda v: v.tensor_tensor(out=dst_ap, in0=t_a[0:npart, 0:n], in1=t_b[0:npart, 0:n], op=ALU.mult),
                 reads=(t_a, t_b), writes=(dst_buf,))

        for ch in range(2):
            pu = bank()
            c = 768 + ch * 128
            mm_group(P, pu[:, 0:w], pu, [(wadv[:, kc, c:c + 128], h[:, kc, 0:w]) for kc in range(KC)], reads=hr)
            gelu(ug[:, ch, 0:w], ug, pu[:, 0:w], pu, 128, w, tmp[2], tmp[3])
        for tb in range(w // 128):
            pv = bank()
            mm_group(P, pv[:, 0:256], pv, [(h[:, kc, tb * 128:(tb + 1) * 128], wadv[:, kc, 1024:1280]) for kc in range(KC)], reads=hr)
            gv, sq = vtk[0], vtk[1]
            gelu(gv[:, :], gv, pv[:, 0:256], pv, 128, 256, vtk[2], vtk[3])
            P.op("dve", lambda v: v.tensor_reduce(out=st[:, 0:1], in_=gv[:, :], axis=AX.X, op=ALU.add), reads=(gv,), writes=(st,))
            P.op("dve", lambda g: g.tensor_tensor(out=sq[:, :], in0=gv[:, :], in1=gv[:, :], op=ALU.mult), reads=(gv,), writes=(sq,))
            P.op("dve", lambda v: v.tensor_reduce(out=st[:, 1:2], in_=sq[:, :], axis=AX.X, op=ALU.add), reads=(sq, st), writes=(st,))
            P.op("dve", lambda v: v.tensor_scalar(out=st[:, 2:4], in0=st[:, 0:2], scalar1=1.0 / 256, scalar2=None, op0=ALU.mult), reads=(st,), writes=(st,))
            P.op("dve", lambda v: v.tensor_tensor(out=st[:, 4:5], in0=st[:, 2:3], in1=st[:, 2:3], op=ALU.mult), reads=(st,), writes=(st,))
            P.op("dve", lambda v: v.tensor_tensor(out=st[:, 5:6], in0=st[:, 3:4], in1=st[:, 4:5], op=ALU.subtract), reads=(st,), writes=(st,))
            P.op("act", lambda a: a.activation(out=st[:, 6:7], in_=st[:, 5:6], func=AF.Ln, bias=LN_EPS, scale=1.0), reads=(st,), writes=(st,))
            P.op("act", lambda a: a.activation(out=st[:, 7:8], in_=st[:, 6:7], func=AF.Exp, scale=-0.5), reads=(st,), writes=(st,))
            P.op("dve", lambda v: v.tensor_scalar(out=gv[:, :], in0=gv[:, :], scalar1=st[:, 2:3], scalar2=st[:, 7:8], op0=ALU.subtract, op1=ALU.mult),
                 reads=(gv, st), writes=(gv,))
            P.op("dve", lambda g: g.tensor_tensor(out=gv[:, :], in0=gv[:, :], in1=rows[:, 0:256], op=ALU.mult), reads=(gv, rows), writes=(gv,))
            P.op("dve", lambda v: v.tensor_tensor(out=vnb[:, :], in0=gv[:, :], in1=rows[:, 256:512], op=ALU.add), reads=(gv, rows), writes=(vnb,))
            for g4 in range(4):
                pm = bank()
                po = (g4 % 2) * 64
                mm_group(P, pm[po:po + 64, 0:128], pm,
                         [(vnb[:, g4 * 64:(g4 + 1) * 64], wsb[:, g4 * 128:(g4 + 1) * 128]),
                          (onesb[0:1, 0:64], bsb[0:1, g4 * 128:(g4 + 1) * 128])], reads=(vnb, wsb, onesb, bsb))
                P.op("dve", lambda v, g4=g4, pm=pm, po=po: v.tensor_tensor(out=yd[po:po + 64, g4 // 2, tb * 128:(tb + 1) * 128], in0=pm[po:po + 64, 0:128],
                                                                            in1=ug[po:po + 64, g4 // 2, tb * 128:(tb + 1) * 128], op=ALU.mult),
                     reads=(pm, ug), writes=(yd,))

        for oc in range(8):
            slot = next_block(total_blocks) if not is_halo else None
            if is_halo:
                slot = ring_h
                P.dma("pool", slot, lambda q: q.dma_start(out=slot[:, 0:KC * 512], in_=wG.t.ap()[oc]), reads=(wG,))
            sv = slot.t.ap().rearrange("p (kc c) -> p kc c", kc=KC)
            for br in range(4):
                pg, pp = bank(), bank()
                mm_group(P, pg[:, 0:w], pg, [(sv[:, kc, br * 128:(br + 1) * 128], h[:, kc, 0:w]) for kc in range(KC)], reads=(slot, h))
                wo_v = wout[br].t.ap().rearrange("p (k c) -> p k c", k=2)
                if br == 0:
                    prs = [(wo_v[:, k2, oc * 128:(oc + 1) * 128], ya[:, k2, 0:w]) for k2 in range(2)]
                    rd = (wout[br], ya)
                elif br in (1, 2):
                    po_ = 0 if br == 1 else 64
                    prs = []
                    for hd_ in range(4):
                        wv_ = wout[1 + hd_ // 2].t.ap().rearrange("p (k c) -> p k c", k=2)
                        prs.append((wv_[po_:po_ + 64, hd_ % 2, oc * 128:(oc + 1) * 128], ybt[po_:po_ + 64, hd_, 0:w]))
                    rd = (wout[1], wout[2], ybt)
                else:
                    prs = [(wo_v[:, k2, oc * 128:(oc + 1) * 128], yd[:, k2, 0:w]) for k2 in range(2)]
                    rd = (wout[br], yd)
                mm_group(P, pp[:, 0:w], pp, prs, reads=rd)
                gs = tmp[4]
                gb = 24 + br * 8 + oc
                P.op("act", lambda a, pg=pg, gb=gb: a.activation(out=gs[:, 0:w], in_=pg[:, 0:w], func=AF.Sigmoid, bias=vec[:, gb:gb + 1], scale=1.0),
                     reads=(pg, vec), writes=(gs,))
                if br == 0:
                    P.op("dve", lambda v, pp=pp: v.tensor_tensor(out=merged[:, oc, 0:w], in0=pp[:, 0:w], in1=gs[:, 0:w], op=ALU.mult),
                         reads=(pp, gs), writes=(merged,))
                else:
                    t5 = tmp[5]
                    P.op("dve", lambda v, pp=pp: v.tensor_tensor(out=t5[:, 0:w], in0=pp[:, 0:w], in1=gs[:, 0:w], op=ALU.mult),
                         reads=(pp, gs), writes=(t5,))
                    dst = mb if br == 3 else merged
                    P.op("dve", lambda g, dst=dst: g.tensor_tensor(out=dst[:, oc, 0:w], in0=merged[:, oc, 0:w], in1=t5[:, 0:w], op=ALU.add),
                         reads=(merged, t5), writes=(dst,))

        for i in range(2):
            if is_halo:
                slot = ring_h
                P.dma("pool", slot, lambda q: q.dma_start(out=slot[:, 0:KC * 512], in_=wO.t.ap()[i]), reads=(wO,))
            else:
                slot = next_block(total_blocks)
            sv = slot.t.ap().rearrange("p (kc c) -> p kc c", kc=KC)
            for o4 in range(4):
                oc = i * 4 + o4
                pb = bank()
                mm_group(P, pb[:, 0:w], pb, [(sv[:, kc, o4 * 128:(o4 + 1) * 128], mb[:, kc, 0:w]) for kc in range(KC)], reads=(slot, mb))
                P.op("dve", lambda v, pb=pb, oc=oc: v.tensor_tensor(out=xt[:, oc, 0:w], in0=pb[:, 0:w], in1=xt[:, oc, 0:w], op=ALU.add),
                     reads=(pb, xt), writes=(xt,))

        rmsnorm(w, 8, h)
        for jj in range(11):
            if is_halo:
                slot = ring_h
                P.dma("pool", slot, lambda q: q.dma_start(out=slot[:, 0:KC * 512], in_=wUp.t.ap()[jj]), reads=(wUp,))
            else:
                slot = next_block(total_blocks)
            sv = slot.t.ap().rearrange("p (kc c) -> p kc c", kc=KC)
            for j2 in range(2):
                j = jj * 2 + j2
                cv = []
                for gv_i in range(2):
                    pb = bank()
                    c = (j2 * 2 + gv_i) * 128
                    mm_group(P, pb[:, 0:w], pb, [(sv[:, kc, c:c + 128], h[:, kc, 0:w]) for kc in range(KC)], reads=(slot, h))
                    idx = gv_i * NJ + j
                    u = uc[gv_i]
                    P.op("act", lambda a, pb=pb, u=u: a.copy(out=u[:, 2:2 + w], in_=pb[:, 0:w]), reads=(pb,), writes=(u,))
                    P.op("dve", lambda g, u=u, idx=idx: g.tensor_copy(out=u[:, 0:2], in_=uh[:, idx, :]), reads=(uh, u), writes=(u,))
                    if is_halo:
                        P.op("dve", lambda g, u=u, idx=idx: g.tensor_scalar(out=uh[:, idx, :], in0=u[:, w:w + 2], scalar1=hmb[:, 0:1], scalar2=None, op0=ALU.mult),
                             reads=(u, hmb, uh), writes=(uh,))
                        continue
                    P.op("dve", lambda g, u=u, idx=idx: g.tensor_copy(out=uh[:, idx, :], in_=u[:, w:w + 2]), reads=(u, uh), writes=(uh,))
                    t = tmp[gv_i]
                    P.op("act", lambda a, pb=pb, t=t, idx=idx: a.activation(out=t[:, 0:w], in_=pb[:, 0:w], func=AF.Copy, scale=cvf[:, idx * 3 + 2:idx * 3 + 3]),
                         reads=(pb, cvf), writes=(t,))
                    for tap in (0, 1):
                        P.op("dve", lambda v, u=u, t=t, idx=idx, tap=tap: v.scalar_tensor_tensor(out=t[:, 0:w], in0=u[:, tap:tap + w], scalar=cvf[:, idx * 3 + tap:idx * 3 + tap + 1], in1=t[:, 0:w], op0=ALU.mult, op1=ALU.add),
                             reads=(u, cvf, t), writes=(t,))
                    cv.append(t)
                if is_halo:
                    continue
                sg = tmp[2]
                P.op("act", lambda a: a.activation(out=sg[:, 0:w], in_=cv[0][:, 0:w], func=AF.Sigmoid), reads=(cv[0],), writes=(sg,))
                P.op("dve", lambda v: v.tensor_tensor(out=sg[:, 0:w], in0=sg[:, 0:w], in1=cv[0][:, 0:w], op=ALU.mult), reads=(sg, cv[0]), writes=(sg,))
                P.op("dve", lambda g, j=j: g.tensor_tensor(out=act[:, j, 0:w], in0=sg[:, 0:w], in1=cv[1][:, 0:w], op=ALU.mult), reads=(sg, cv[1]), writes=(act,))
        if is_halo:
            return
        for oc in range(8):
            slot = next_block(total_blocks)
            sv = slot.t.ap().rearrange("p (j c) -> p j c", c=128)
            pb = bank()
            mm_group(P, pb[:, 0:w], pb, [(sv[:, j, :], act[:, j, 0:w]) for j in range(NJ)], reads=(slot, act))
            P.op("dve", lambda v, pb=pb, oc=oc: v.tensor_tensor(out=xt[:, oc, 0:w], in0=pb[:, 0:w], in1=xt[:, oc, 0:w], op=ALU.add),
                 reads=(pb, xt), writes=(xt,))
        if last_layer:
            rmsnorm(w, 16, merged, out_f32=True)
            src = merged
        else:
            src = xt
        if last_out:
            outv = out.t.ap().rearrange("(kc p) t -> p kc t", p=128)
            P.dma("sp", out, lambda q: q.dma_start(out=outv[:, :, out_off:out_off + w], in_=src[:, :, 0:w]), reads=(src,))
        else:
            for kc in range(KC):
                xb_ = io["xok"][kc][out_off // CH]
                P.dma("sp", xb_, lambda q, kc=kc, xb_=xb_: q.dma_start(out=xb_.t.ap()[:, out_off % CH:out_off % CH + w], in_=src[:, kc, 0:w]), reads=(src,))

    ring_h = P.sb([128, 4096], BF16, "ring_h")
    tile(0, HALO, True, None)
    for ti in range(ntiles):
        tile(HALO + ti * W, W, False, ti * W)
        if io.get("after_dense_tile") is not None:
            io["after_dense_tile"](ti)


SB_OFF, RW_OFF, SG_OFF, GATE_OFF = 768, 1536, 2560, 3072


def _kc_layout(w):
    K, C = w.shape
    return np.ascontiguousarray(w.reshape(K // 128, 128, C).transpose(1, 0, 2).reshape(128, (K // 128) * C))


def _pvec(v):
    return np.ascontiguousarray(v.reshape(-1, 128).T)


def pack_dense_weights(inp, l):
    f = np.float32
    w_in = inp["w_in"][l]
    cols = np.concatenate([np.arange(0, 768), np.arange(SG_OFF, SG_OFF + 512)])
    d = {}
    d["wAD"] = _kc_layout(w_in[:, cols])
    wg = w_in[:, GATE_OFF:].reshape(D, 4, 8, 128).transpose(2, 0, 1, 3).reshape(8, D, 512)
    d["wG"] = np.stack([_kc_layout(wg[oc]) for oc in range(8)])
    sbo, rwo = inp["sb_out"][l], inp["rw_out"][l]

    def bc(h0):
        a = np.zeros((128, 2, D), np.float32)
        for k in range(2):
            hd = h0 + k
            a[0:64, k] = sbo[hd * 64:(hd + 1) * 64]
            a[64:128, k] = rwo[hd * 64:(hd + 1) * 64]
        return a.reshape(128, 2 * D)
    d["wOut"] = np.stack([_kc_layout(inp["sc_out"][l]), bc(0), bc(2), _kc_layout(inp["sg_out"][l])])
    d["wO"] = np.stack([_kc_layout(inp["w_o"][l][:, i * 512:(i + 1) * 512]) for i in range(2)])
    wu = inp["w_up"][l].reshape(D, 2, 11, 2, 128).transpose(2, 0, 3, 1, 4).reshape(11, D, 512)
    d["wUp"] = np.stack([_kc_layout(wu[jj]) for jj in range(11)])
    d["wDn"] = np.stack([_kc_layout(inp["w_down"][l][:, oc * 128:(oc + 1) * 128]) for oc in range(8)])
    vec = np.zeros((128, 64), f)
    vec[:, 0:8] = _pvec(inp["mix_norm_g"][l])
    vec[:, 8:16] = _pvec(inp["ffn_norm_g"][l])
    vec[:, 16:24] = _pvec(inp["final_norm_g"])
    vec[:, 24:56] = _pvec(inp["gate_b"][l])
    cw = inp["sc_conv_w"][l]
    for ch in range(2):
        for tap in range(3):
            vec[:, 56 + ch * 3 + tap] = cw[tap, ch * 128:(ch + 1) * 128]
    d["vecs"] = vec
    fc = inp["ffn_conv_w"][l]
    d["convF"] = np.ascontiguousarray(fc.reshape(3, 44, 128).transpose(2, 1, 0).reshape(128, 132))
    rows = np.zeros((128, 512), f)
    rows[:, 0:256] = inp["sg_ln_g"][l][None, :]
    rows[:, 256:512] = inp["sg_ln_b"][l][None, :]
    d["rowsD"] = rows
    d["wsT"] = np.ascontiguousarray(inp["sg_w"][l].transpose(2, 0, 1).reshape(128, 512))
    d["bsr"] = np.ascontiguousarray(inp["sg_b"][l].reshape(1, 512))
    return {k: np.ascontiguousarray(v, dtype=f) for k, v in d.items()}


def dense_core_inputs(xT_b, ybc_b, seg, NT):
    def halo(a):
        o = np.zeros((a.shape[0], HALO + NT), np.float32)
        s = seg * NT
        if seg > 0:
            o[:, :] = a[:, s - HALO:s + NT]
        else:
            o[:, HALO:] = a[:, 0:NT]
        return o
    hm = np.full((128, 1), 0.0 if seg == 0 else 1.0, np.float32)
    return {"xT": halo(xT_b), "ybc": halo(ybc_b), "hm": hm}


def emit_seqmix(P, T, L, io, do_attn=True, do_rwkv=True, interleave=True):
    EI = "ExternalInput"
    W = 512
    NTL = T // W
    NT = T // 4
    sfx = f"_{L}"
    wS = P.dram("wS" + sfx, [128, KC * 640], F32, EI)
    vecS = P.dram("vecS" + sfx, [128, 32], F32, EI)
    w2d = P.dram("w2h" + sfx, [64, 64], F32, EI)
    a2d = P.dram("a2h" + sfx, [64, 64], F32, EI)
    g2d = P.dram("g2h" + sfx, [128, 64], F32, EI)
    yxk = io["yxk"]
    CH = 1024

    vec = P.sb([128, 32], F32, "vec")
    P.dma("sp", vec, lambda q: q.dma_start(out=vec[:, :], in_=vecS.t.ap()), reads=(vecS,))
    ws = P.sb([128, KC * 640], BF16, "ws")
    P.dma("pool", ws, lambda q: q.dma_start(out=ws[:, :], in_=wS.t.ap()), reads=(wS,))
    wsv = ws.t.ap().rearrange("p (kc c) -> p kc c", kc=KC)
    w2h = P.sb([64, 64], F32, "w2h_s"); a2h = P.sb([64, 64], F32, "a2h_s"); g2h = P.sb([128, 64], F32, "g2h_s")
    P.dma("sp", w2h, lambda q: q.dma_start(out=w2h[:, :], in_=w2d.t.ap()), reads=(w2d,))
    P.dma("sp", a2h, lambda q: q.dma_start(out=a2h[:, :], in_=a2d.t.ap()), reads=(a2d,))
    P.dma("sp", g2h, lambda q: q.dma_start(out=g2h[:, :], in_=g2d.t.ap()), reads=(g2d,))
    w2hb = P.sb([64, 64], BF16, "w2hb"); g2hb = P.sb([128, 64], BF16, "g2hb")
    P.op("dve", lambda v: v.tensor_copy(out=w2hb[:, :], in_=w2h[:, :]), reads=(w2h,), writes=(w2hb,))
    P.op("dve", lambda v: v.tensor_copy(out=g2hb[:, :], in_=g2h[:, :]), reads=(g2h,), writes=(g2hb,))
    twb = P.sb([64, 512], BF16, "twb"); sgxb = P.sb([128, 512], BF16, "sgxb"); s1b = P.sb([64, 512], BF16, "s1b")

    onesb = P.sb([128, 128], BF16, "onesb")
    P.op("pool", lambda g: g.memset(onesb[:, :], 1.0), writes=(onesb,))
    onesf = P.sb([128, 128], F32, "onesf")
    P.op("pool", lambda g: g.memset(onesf[:, :], 1.0), writes=(onesf,))
    identf = P.sb([128, 128], F32, "identf")
    P.op("pool", lambda g: g.affine_select(out=identf[:, :], in_=onesf[:, 0:128], pattern=[[-1, 128]], compare_op=ALU.is_equal,
                                           fill=0.0, base=0, channel_multiplier=1), reads=(onesf,), writes=(identf,))

    qTt = [P.sb([64, W], BF16, f"qT{i}") for i in range(NTL)]
    kTt = [P.sb([64, W], BF16, f"kT{i}") for i in range(NTL)]
    vtt = [P.sb([128, 4, 64], BF16, f"vt{i}") for i in range(NTL)]

    xt = P.sb([128, KC, W], F32, "xt")
    h = P.sb([128, KC, W], BF16, "h")
    sq = h
    rstd = P.sb([128, W], F32, "rstd")
    lnt = rstd
    banks = [P.ps([128, 512], F32, f"bank{i}") for i in range(8)]
    bstate = {"i": 0}

    def bank():
        b = banks[bstate["i"] % 4]
        bstate["i"] += 1
        return b

    if do_rwkv:
        rmask = P.sb([64, W], F32, "rmask")
        P.op("pool", lambda g: g.memset(rmask[:, :], 1.0), writes=(rmask,))
        P.op("pool", lambda g: g.memset(rmask[:, :].rearrange("p (c i) -> p c i", i=64)[:, :, 0:1], 0.0), writes=(rmask,))
        maskS = P.sb([128, 128], F32, "maskS")
        mask2 = P.sb([128, 256], F32, "mask2")
        P.op("pool", lambda g: g.affine_select(out=maskS[:, :], in_=onesf[:, 0:128], pattern=[[-1, 128]], compare_op=ALU.is_ge, fill=0.0, base=-1, channel_multiplier=1),
             reads=(onesf,), writes=(maskS,))
        P.op("pool", lambda g: g.memset(maskS[64:128, 0:64], 0.0), writes=(maskS,))
        P.op("pool", lambda g: g.affine_select(out=mask2[:, 0:128], in_=onesf[:, 0:128], pattern=[[1, 128]], compare_op=ALU.is_ge, fill=0.0, base=-1, channel_multiplier=-1),
             reads=(onesf,), writes=(mask2,))
        P.op("pool", lambda g: g.affine_select(out=mask2[:, 128:256], in_=onesf[:, 0:128], pattern=[[1, 128]], compare_op=ALU.is_ge, fill=0.0, base=0, channel_multiplier=-1),
             reads=(onesf,), writes=(mask2,))
        P.op("pool", lambda g: g.memset(mask2[0:64, 64:128], 0.0), writes=(mask2,))
        P.op("pool", lambda g: g.memset(mask2[0:64, 192:256], 0.0), writes=(mask2,))
        zb = P.sb([64, 5, W + 1], F32, "zb")
        zg = P.sb([128, W + 1], F32, "zg")
        P.op("pool", lambda g: g.memset(zb[:, :, :], 0.0), writes=(zb,))
        P.op("pool", lambda g: g.memset(zg[:, :], 0.0), writes=(zg,))
        dbuf = P.sb([64, 5, W], F32, "dbuf")
        zz = dbuf
        dg = P.sb([128, W], F32, "dg")
        sgx = dg
        m_ = {n: P.sb([64, W], F32, n) for n in ("lw", "asg", "kk0", "s1", "s2", "s3", "kmod", "bvec", "bonus", "cl", "gi", "ginv", "gmap")}
        for alias, tgt in (("sgw", "lw"), ("kksq", "s1"), ("tt", "s1"), ("rk", "s1"), ("ssc", "s2"), ("rs", "s2"), ("tw", "s3"), ("cm", "s3"),
                           ("gprev", "s3"), ("kkn", "kk0"), ("ynT", "ginv"), ("yo", "ginv")):
            m_[alias] = m_[tgt]
        m_["BT"] = P.sb([64, W], BF16, "BTb")
        m_["KT"] = P.sb([64, W], BF16, "KTb")
        identb = P.sb([128, 128], BF16, "identb")
        P.op("dve", lambda v: v.tensor_copy(out=identb[:, :], in_=identf[:, :]), reads=(identf,), writes=(identb,))
        ART = P.sb([64, 4, 2, 128], BF16, "ART")
        NSET = 2
        sets = []
        for si in range(NSET):
            sets.append(dict(
                tok=P.sb([128, 192], BF16, f"tok{si}"), abT=P.sb([128, 256], BF16, f"abT{si}"), akT=P.sb([128, 256], BF16, f"akT{si}"),
                Lm=P.sb([128, 128], BF16, f"Lm{si}"), Xb=[P.sb([128, 128], BF16, f"Xb{si}_{i}") for i in range(2)],
                PPb=[P.sb([128, 256], BF16, f"PP{si}_{i}") for i in range(2)], LMb=[P.sb([64, 64], F32, f"LM{si}_{i}") for i in range(2)],
                N0Gb=[P.sb([64, 64], F32, f"N0G{si}_{i}") for i in range(2)], RpT=P.sb([64, 128], F32, f"RpT{si}"),
                ysb=P.sb([128, 64], F32, f"ysb{si}"), ysq=P.sb([128, 64], F32, f"ysq{si}"), yn=P.sb([128, 64], F32, f"yn{si}"),
                gst=P.sb([128, 8], F32, f"gst{si}")))
        NZ = 8
        Zb = [P.sb([64, 64], F32, f"Z{i}") for i in range(NZ)]
        zst = {"i": 0}
        P.op("pool", lambda g: g.memset(Zb[0][:, :], 0.0), writes=(Zb[0],))

    def rmsnorm():
        P.op("dve", lambda v: v.tensor_tensor(out=sq[:, :, :], in0=xt[:, :, :], in1=xt[:, :, :], op=ALU.mult), reads=(xt,), writes=(sq,))
        pb = bank()
        mm_group(P, pb[:, :], pb, [(onesb[:, :], sq[:, kc, :]) for kc in range(KC)], reads=(onesb, sq))
        P.op("act", lambda a: a.activation(out=lnt[:, :], in_=pb[:, :], func=AF.Ln, bias=RMS_EPS, scale=1.0 / D), reads=(pb,), writes=(lnt,))
        P.op("act", lambda a: a.activation(out=rstd[:, :], in_=lnt[:, :], func=AF.Exp, scale=-0.5), reads=(lnt,), writes=(rstd,))
        for kc in range(KC):
            P.op("dve", lambda v, kc=kc: v.scalar_tensor_tensor(out=h[:, kc, :], in0=xt[:, kc, :], scalar=vec[:, kc:kc + 1], in1=rstd[:, :], op0=ALU.mult, op1=ALU.mult),
                 reads=(xt, vec, rstd), writes=(h,))

    def proj(col, m):
        pb = bank()
        mm_group(P, pb[0:m, :], pb, [(wsv[:, kc, col:col + m], h[:, kc, :]) for kc in range(KC)], reads=(ws, h))
        return pb

    def V(n):
        return m_[n]

    def ew_tt(e, o, a, b, op, rd, wr):
        P.op(e, lambda v: v.tensor_tensor(out=o, in0=a, in1=b, op=op), reads=rd, writes=wr)

    def rwkv_tile(ti):
        c0 = ti * W
        for m in range(5):
            pb_ = proj(192 + 64 * m, 64)
            P.op("act", lambda a, m=m, pb_=pb_: a.copy(out=zb[:, m, 1:W + 1], in_=pb_[0:64, :]), reads=(pb_,), writes=(zb,))
        pg_ = proj(512, 128)
        P.op("act", lambda a: a.copy(out=zg[:, 1:W + 1], in_=pg_[:, :]), reads=(pg_,), writes=(zg,))
        ew_tt("dve", dbuf[:, :, :], zb[:, :, 0:W], zb[:, :, 1:W + 1], ALU.subtract, (zb,), (dbuf,))
        for m in range(5):
            P.op("dve", lambda v, m=m: v.scalar_tensor_tensor(out=zz[:, m, :], in0=dbuf[:, m, :], scalar=vec[0:64, 9 + m:10 + m], in1=zb[:, m, 1:W + 1], op0=ALU.mult, op1=ALU.add),
                 reads=(dbuf, vec, zb), writes=(zz,))
        P.op("dve", lambda v: v.tensor_copy(out=zb[:, :, 0:1], in_=zb[:, :, W:W + 1]), reads=(zb,), writes=(zb,))
        ew_tt("dve", dg[:, :], zg[:, 0:W], zg[:, 1:W + 1], ALU.subtract, (zg,), (dg,))
        P.op("dve", lambda v: v.scalar_tensor_tensor(out=dg[:, :], in0=dg[:, :], scalar=vec[:, 8:9], in1=zg[:, 1:W + 1], op0=ALU.mult, op1=ALU.add),
             reads=(dg, vec, zg), writes=(dg,))
        P.op("dve", lambda v: v.tensor_copy(out=zg[:, 0:1], in_=zg[:, W:W + 1]), reads=(zg,), writes=(zg,))
        yield
        Rm, Km, Vm, XW, XA = (zz[:, i, :] for i in range(5))
        P.op("act", lambda a: a.activation(out=twb[:, :], in_=XW, func=AF.Tanh), reads=(zz,), writes=(twb,))
        P.op("act", lambda a: a.activation(out=sgxb[:, :], in_=dg[:, :], func=AF.Sigmoid), reads=(dg,), writes=(sgxb,))
        pw = bank()
        mm_group(P, pw[0:64, :], pw, [(w2hb[:, :], twb[:, :])], reads=(w2hb, twb))
        pa = bank()
        mm_group(P, pa[0:64, :], pa, [(a2h[:, :], XA)], reads=(a2h, zz))
        pgm = bank()
        mm_group(P, pgm[0:64, :], pgm, [(g2hb[:, :], sgxb[:, :])], reads=(g2hb, sgxb))
        P.op("act", lambda a: a.activation(out=V("sgw")[:, :], in_=pw[0:64, :], func=AF.Sigmoid, bias=vec[0:64, 14:15], scale=1.0), reads=(pw, vec), writes=(V("sgw"),))
        P.op("act", lambda a: a.activation(out=V("asg")[:, :], in_=pa[0:64, :], func=AF.Sigmoid, bias=vec[0:64, 15:16], scale=1.0), reads=(pa, vec), writes=(V("asg"),))
        P.op("act", lambda a: a.copy(out=V("gmap")[:, :], in_=pgm[0:64, :]), reads=(pgm,), writes=(V("gmap"),))
        P.op("dve", lambda v: v.tensor_scalar(out=V("lw")[:, :], in0=V("sgw")[:, :], scalar1=-DECAY_SCALE, scalar2=None, op0=ALU.mult), reads=(V("sgw"),), writes=(V("lw"),))
        P.op("dve", lambda v: v.tensor_scalar(out=V("kk0")[:, :], in0=Km, scalar1=vec[0:64, 16:17], scalar2=None, op0=ALU.mult), reads=(zz, vec), writes=(V("kk0"),))
        ew_tt("dve", s1b[:, :], V("kk0")[:, :], V("kk0")[:, :], ALU.mult, (V("kk0"),), (s1b,))
        yield
        pss = bank()
        mm_group(P, pss[0:64, :], pss, [(onesb[0:64, 0:64], s1b[:, :])], reads=(onesb, s1b))
        P.op("dve", lambda v: v.tensor_scalar(out=V("ssc")[:, :], in0=pss[0:64, :], scalar1=1e-24, scalar2=None, op0=ALU.max), reads=(pss,), writes=(V("ssc"),))
        P.op("dve", lambda v: v.tensor_scalar(out=V("tt")[:, :], in0=V("asg")[:, :], scalar1=-1.0, scalar2=vec[0:64, 17:18], op0=ALU.add, op1=ALU.mult), reads=(V("asg"), vec), writes=(V("tt"),))
        P.op("dve", lambda v: v.scalar_tensor_tensor(out=V("kmod")[:, :], in0=V("tt")[:, :], scalar=1.0, in1=Km, op0=ALU.add, op1=ALU.mult), reads=(V("tt"), zz), writes=(V("kmod"),))
        P.op("dve", lambda v: v.scalar_tensor_tensor(out=s1b[:, :], in0=Rm, scalar=vec[0:64, 18:19], in1=V("kmod")[:, :], op0=ALU.mult, op1=ALU.mult), reads=(zz, vec, V("kmod")), writes=(s1b,))
        pbn = bank()
        mm_group(P, pbn[0:64, :], pbn, [(onesb[0:64, 0:64], s1b[:, :])], reads=(onesb, s1b))
        ew_tt("dve", V("bonus")[:, :], pbn[0:64, :], Vm, ALU.mult, (pbn, zz), (V("bonus"),))
        P.op("dve", lambda v: v.tensor_tensor_scan(out=V("cl")[:, :], data0=rmask[:, :], data1=V("lw")[:, :], initial=0.0, op0=ALU.mult, op1=ALU.add),
             reads=(rmask, V("lw")), writes=(V("cl"),))
        ew_tt("dve", V("cm")[:, :], V("cl")[:, :], V("lw")[:, :], ALU.subtract, (V("cl"), V("lw")), (V("cm"),))
        yield
        P.op("act", lambda a: a.activation(out=V("rs")[:, :], in_=V("ssc")[:, :], func=AF.Ln), reads=(V("ssc"),), writes=(V("rs"),))
        P.op("act", lambda a: a.activation(out=V("rs")[:, :], in_=V("rs")[:, :], func=AF.Exp, scale=-0.5), reads=(V("rs"),), writes=(V("rs"),))
        P.op("act", lambda a: a.activation(out=V("gi")[:, :], in_=V("cl")[:, :], func=AF.Exp), reads=(V("cl"),), writes=(V("gi"),))
        P.op("act", lambda a: a.activation(out=V("ginv")[:, :], in_=V("cl")[:, :], func=AF.Exp, scale=-1.0), reads=(V("cl"),), writes=(V("ginv"),))
        P.op("act", lambda a: a.activation(out=V("gprev")[:, :], in_=V("cm")[:, :], func=AF.Exp), reads=(V("cm"),), writes=(V("gprev"),))
        ew_tt("dve", V("kkn")[:, :], V("kk0")[:, :], V("rs")[:, :], ALU.mult, (V("kk0"), V("rs")), (V("kkn"),))
        ew_tt("dve", V("bvec")[:, :], V("kkn")[:, :], V("asg")[:, :], ALU.mult, (V("kkn"), V("asg")), (V("bvec"),))
        a4 = lambda ap: ap.rearrange("p (a b) -> p a b", b=128)
        P.op("dve", lambda v: v.scalar_tensor_tensor(out=ART[:, :, 0, :], in0=a4(V("kkn")[:, :]), scalar=-1.0, in1=a4(V("gprev")[:, :]), op0=ALU.mult, op1=ALU.mult),
             reads=(V("kkn"), V("gprev")), writes=(ART,))
        ew_tt("dve", ART[:, :, 1, :], a4(Rm), a4(V("gi")[:, :]), ALU.mult, (zz, V("gi")), (ART,))
        ew_tt("dve", V("BT")[:, :], V("bvec")[:, :], V("ginv")[:, :], ALU.mult, (V("bvec"), V("ginv")), (V("BT"),))
        ew_tt("dve", V("KT")[:, :], V("kmod")[:, :], V("ginv")[:, :], ALU.mult, (V("kmod"), V("ginv")), (V("KT"),))
        yield
        BT, KT, gi = V("BT"), V("KT"), V("gi")
        def pair_gen(pr, S):
            tok, abT, akT, Lm, Xb, PPb, LMb, N0Gb, RpT = S["tok"], S["abT"], S["akT"], S["Lm"], S["Xb"], S["PPb"], S["LMb"], S["N0Gb"], S["RpT"]
            ysb, ysq, yn, gst = S["ysb"], S["ysq"], S["yn"], S["gst"]
            s0 = pr * 128
            ATp, RTp = ART[:, pr, 0, :], ART[:, pr, 1, :]
            ARp = ART[:, pr, :, :].rearrange("p a b -> p (a b)")
            BTp, KTp, VTp = BT[:, s0:s0 + 128], KT[:, s0:s0 + 128], zz[:, 2, s0:s0 + 128]
            i64 = identf[0:64, 0:64]
            i64b = identb[0:64, 0:64]
            pT = bank()
            mm_group(P, pT[:, 0:64], pT, [(BTp, i64b)], reads=(BT, identb))
            mm_group(P, pT[:, 64:128], pT, [(KTp, i64b)], reads=(KT, identb))
            mm_group(P, pT[:, 128:192], pT, [(VTp, i64)], reads=(zz, identf))
            P.op("act", lambda a, pT=pT: a.copy(out=tok[:, :], in_=pT[:, 0:192]), reads=(pT,), writes=(tok,))
            yield
            pA = bank()
            mm_group(P, pA[:, 0:256], pA, [(BTp, ARp)], reads=(BT, ART))
            ew_tt("dve", abT[:, :], pA[:, 0:256], mask2[:, :], ALU.mult, (pA, mask2), (abT,))
            pK = bank()
            mm_group(P, pK[:, 0:256], pK, [(KTp, ARp)], reads=(KT, ART))
            ew_tt("dve", akT[:, :], pK[:, 0:256], mask2[:, :], ALU.mult, (pK, mask2), (akT,))
            pL = bank()
            mm_group(P, pL[:, 0:128], pL, [(ATp, BTp)], reads=(ART, BT))
            ew_tt("dve", Lm[:, :], pL[:, 0:128], maskS[:, :], ALU.mult, (pL, maskS), (Lm,))
            yield
            pX = bank()
            mm_group(P, pX[:, 0:64], pX, [(ATp, i64b)], reads=(ART, identb))
            mm_group(P, pX[:, 64:128], pX, [(akT[:, 0:128], tok[:, 128:192])], reads=(akT, tok))
            X = Xb[0]
            P.op("act", lambda a, pX=pX, X=X: a.copy(out=X[:, :], in_=pX[:, 0:128]), reads=(pX,), writes=(X,))
            yield
            Pk_ap, PkT_ap, Pk_b, PkT_b = Lm[:, :], abT[:, 0:128], Lm, abT
            for it in range(6):
                pXn = bank()
                mm_group(P, pXn[:, 0:128], pXn, [(PkT_ap, X[:, :])], reads=(PkT_b, X))
                Xn = Xb[(it + 1) % 2]
                ew_tt("dve", Xn[:, :], pXn[:, 0:128], X[:, :], ALU.add, (pXn, X), (Xn,))
                yield
                if it < 5:
                    pP = bank()
                    mm_group(P, pP[:, 0:128], pP, [(PkT_ap, Pk_ap)], reads=(PkT_b, Pk_b))
                    mm_group(P, pP[:, 128:256], pP, [(Pk_ap, PkT_ap)], reads=(PkT_b, Pk_b))
                    PPn = PPb[it % 2]
                    P.op("act", lambda a, pP=pP, PPn=PPn: a.copy(out=PPn[:, :], in_=pP[:, 0:256]), reads=(pP,), writes=(PPn,))
                    Pk_ap, PkT_ap, Pk_b, PkT_b = PPn[:, 0:128], PPn[:, 128:256], PPn, PPn
                X = Xn
            Zs = []
            for hh in range(2):
                hs = slice(hh * 64, hh * 64 + 64)
                pMN = bank()
                mm_group(P, pMN[0:64, 0:64], pMN, [(X[hs, 0:64], tok[hs, 0:64])], reads=(X, tok))
                mm_group(P, pMN[0:64, 64:128], pMN, [(tok[hs, 0:64], X[hs, 64:128]), (tok[hs, 64:128], tok[hs, 128:192])], reads=(X, tok))
                ge = gi[:, s0 + hh * 64 + 63:s0 + hh * 64 + 64]
                LM, N0G = LMb[hh], N0Gb[hh]
                ew_tt("dve", LM[:, :], pMN[0:64, 0:64], i64, ALU.add, (pMN, identf), (LM,))
                P.op("dve", lambda v, pMN=pMN, N0G=N0G, ge=ge: v.tensor_scalar(out=N0G[:, :], in0=pMN[0:64, 64:128], scalar1=ge, scalar2=None, op0=ALU.mult),
                     reads=(pMN, gi), writes=(N0G,))
                Zc = Zb[zst["i"] % NZ]
                Zn = Zb[(zst["i"] + 1) % NZ]
                zst["i"] += 1
                pZ = bank()
                mm_group(P, pZ[0:64, 0:64], pZ, [(LM[:, :], Zc[:, :])], reads=(LM, Zc))
                P.op("dve", lambda v, pZ=pZ, Zn=Zn, N0G=N0G, ge=ge: v.scalar_tensor_tensor(out=Zn[:, :], in0=pZ[0:64, 0:64], scalar=ge, in1=N0G[:, :], op0=ALU.mult, op1=ALU.add),
                     reads=(pZ, gi, N0G), writes=(Zn,))
                Zs.append(Zc)
            pR = bank()
            mm_group(P, pR[0:64, 0:128], pR, [(X[:, 0:64], abT[:, 128:256])], reads=(X, abT))
            ew_tt("dve", RpT[:, :], pR[0:64, 0:128], RTp, ALU.add, (pR, ART), (RpT,))
            yield
            pY = bank()
            P.op("pe", lambda t, pY=pY, X=X: t.matmul(pY[:, 0:64], lhsT=abT[:, 128:256], rhs=X[:, 64:128], start=True, stop=False), reads=(abT, X), writes=(pY,), inc=False)
            P.op("pe", lambda t, pY=pY: t.matmul(pY[:, 0:64], lhsT=akT[:, 128:256], rhs=tok[:, 128:192], start=False, stop=False), reads=(akT, tok), writes=(pY,), inc=False)
            P.op("pe", lambda t, pY=pY: t.matmul(pY[0:64, 0:64], lhsT=RpT[:, 0:64], rhs=Zs[0][:, :], start=False, stop=False), reads=(RpT, Zs[0]), writes=(pY,), inc=False)
            P.op("pe", lambda t, pY=pY: t.matmul(pY[64:128, 0:64], lhsT=RpT[:, 64:128], rhs=Zs[1][:, :], start=False, stop=True), reads=(RpT, Zs[1]), writes=(pY,))
            P.op("act", lambda a, pY=pY: a.copy(out=ysb[:, :], in_=pY[:, 0:64]), reads=(pY,), writes=(ysb,))
            yield
            P.op("dve", lambda v: v.tensor_reduce(out=gst[:, 0:1], in_=ysb[:, :], axis=AX.X, op=ALU.add), reads=(ysb,), writes=(gst,))
            ew_tt("dve", ysq[:, :], ysb[:, :], ysb[:, :], ALU.mult, (ysb,), (ysq,))
            P.op("dve", lambda v: v.tensor_reduce(out=gst[:, 1:2], in_=ysq[:, :], axis=AX.X, op=ALU.add), reads=(ysq, gst), writes=(gst,))
            P.op("dve", lambda v: v.tensor_scalar(out=gst[:, 2:4], in0=gst[:, 0:2], scalar1=1.0 / 64, scalar2=None, op0=ALU.mult), reads=(gst,), writes=(gst,))
            ew_tt("dve", gst[:, 4:5], gst[:, 2:3], gst[:, 2:3], ALU.mult, (gst,), (gst,))
            ew_tt("dve", gst[:, 5:6], gst[:, 3:4], gst[:, 4:5], ALU.subtract, (gst,), (gst,))
            P.op("act", lambda a: a.activation(out=gst[:, 6:7], in_=gst[:, 5:6], func=AF.Ln, bias=GN_EPS, scale=1.0), reads=(gst,), writes=(gst,))
            P.op("act", lambda a: a.activation(out=gst[:, 7:8], in_=gst[:, 6:7], func=AF.Exp, scale=-0.5), reads=(gst,), writes=(gst,))
            P.op("dve", lambda v: v.tensor_scalar(out=yn[:, :], in0=ysb[:, :], scalar1=gst[:, 2:3], scalar2=gst[:, 7:8], op0=ALU.subtract, op1=ALU.mult), reads=(ysb, gst), writes=(yn,))
            yield
            pYT = bank()
            mm_group(P, pYT[0:64, 0:128], pYT, [(yn[:, :], identf[:, :])], reads=(yn, identf))
            P.op("act", lambda a, pYT=pYT, s0=s0: a.copy(out=V("ynT")[:, s0:s0 + 128], in_=pYT[0:64, 0:128]), reads=(pYT,), writes=(V("ynT"),))

        for p0 in range(0, 4, NSET):
            gens = [pair_gen(p0 + k, sets[k]) for k in range(NSET)]
            while gens:
                for g_ in list(gens):
                    try:
                        next(g_)
                    except StopIteration:
                        gens.remove(g_)
                yield
        P.op("dve", lambda v: v.tensor_scalar(out=V("yo")[:, :], in0=V("ynT")[:, :], scalar1=vec[0:64, 19:20], scalar2=vec[0:64, 20:21], op0=ALU.mult, op1=ALU.add), reads=(V("ynT"), vec), writes=(V("yo"),))
        ew_tt("dve", V("yo")[:, :], V("yo")[:, :], V("bonus")[:, :], ALU.add, (V("yo"), V("bonus")), (V("yo"),))
        ew_tt("dve", V("yo")[:, :], V("yo")[:, :], V("gmap")[:, :], ALU.mult, (V("yo"), V("gmap")), (V("yo"),))
        yb_ = yxk[c0 // CH]
        P.dma("sp", yb_, lambda q: q.dma_start(out=yb_.t.ap()[64:128, c0 % CH:c0 % CH + W], in_=V("yo")[:, :]), reads=(V("yo"),))

    if do_attn:
        NTI = P.sb([128, 128], BF16, "NTI")
        NON = P.sb([128, 128], BF16, "NON")
        m01 = P.sb([128, 128], BF16, "m01")
        Z0 = P.sb([128, 64], BF16, "Z0")
        P.op("pool", lambda g: g.memset(NON[:, :], -1.0), writes=(NON,))
        P.op("pool", lambda g: g.memset(Z0[:, :], 0.0), writes=(Z0,))
        P.op("pool", lambda g: g.affine_select(out=NTI[:, :], in_=NON[:, :], pattern=[[-1, 128]], compare_op=ALU.is_ge, fill=0.0, base=0, channel_multiplier=1),
             reads=(NON,), writes=(NTI,))
        P.op("pool", lambda g: g.affine_select(out=m01[:, :], in_=onesb[:, :], pattern=[[1, 128]], compare_op=ALU.is_ge, fill=0.0, base=-1, channel_multiplier=-1),
             reads=(onesb,), writes=(m01,))
        eb = [P.sb([128, W], BF16, f"eb{i}") for i in range(2)]
        spb = [P.sb([128, W], BF16, f"spb{i}") for i in range(3)]
        Ab = [P.sb([128, W], BF16, f"Ab{i}") for i in range(3)]
        spaccs = [P.sb([128, W], BF16, f"spacc{i}") for i in range(2)]
        ybo = P.sb([64, W], F32, "ybo")
        po = banks[7]
        rot = {"i": 0}
        cnt = {"n": 0}

        def bank3():
            b = banks[4 + rot["i"] % 3]
            rot["i"] += 1
            return b

        def ew2(o, a, b, op, rd, wr):
            P.op("dve", lambda v: v.tensor_tensor(out=o, in0=a, in1=b, op=op), reads=rd, writes=wr)

        def S1(d):
            qt, kb, cc, w = d["qt"], d["kb"], d["cc"], d["w"]
            n = cnt["n"]
            cnt["n"] += 1
            d["n"] = n
            kb_buf = kTt[kb // 4]
            d["kbuf"] = kb_buf
            d["kblk"] = kb_buf[:, (kb % 4) * 128:(kb % 4 + 1) * 128]
            d["qcols"] = qTt[qt][:, cc:W]
            e_, sp_ = eb[n % 2], spb[n % 3]
            d["sp"] = sp_
            pz = bank3()
            d["pz"] = pz
            mm_group(P, pz[:, 0:w], pz, [(d["kblk"], d["qcols"])], reads=(kb_buf, qTt[qt]))
            P.op("act", lambda a: a.activation(out=e_[:, 0:w], in_=pz[:, 0:w], func=AF.Exp), reads=(pz,), writes=(e_,))
            P.op("act", lambda a: a.activation(out=sp_[:, 0:w], in_=e_[:, 0:w], func=AF.Ln, bias=1.0, scale=1.0), reads=(e_,), writes=(sp_,))
            if d["diag"]:
                ew2(sp_[:, 0:128], sp_[:, 0:128], m01[:, :], ALU.mult, (sp_, m01), (sp_,))

        def S2(d):
            qt, kb, cc, w, n = d["qt"], d["kb"], d["cc"], d["w"], d["n"]
            sp_, A_ = d["sp"], Ab[n % 3]
            d["A"] = A_
            spacc = spaccs[qt % 2]
            if d["first"]:
                P.op("pool", lambda g: g.memset(spacc[:, :], 0.0), writes=(spacc,))
            pe_ = d["pz"]
            P.op("pe", lambda t: t.matmul(pe_[:, 0:w], lhsT=NTI[:, :], rhs=sp_[:, 0:w], start=False, stop=False), reads=(NTI, sp_), writes=(pe_,), inc=False)
            P.op("pe", lambda t: t.matmul(pe_[:, 0:w], lhsT=NON[:, :], rhs=spacc[:, cc:W], start=False, stop=True), reads=(NON, spacc), writes=(pe_,))
            P.op("act", lambda a: a.activation(out=A_[:, 0:w], in_=pe_[:, 0:w], func=AF.Exp), reads=(pe_,), writes=(A_,))
            if d["diag"]:
                ew2(A_[:, 0:128], A_[:, 0:128], m01[:, :], ALU.mult, (A_, m01), (A_,))
            if not d["last"]:
                ew2(spacc[:, cc:W], spacc[:, cc:W], sp_[:, 0:w], ALU.add, (spacc, sp_), (spacc,))

        def S3(d):
            qt, kb, cc, w = d["qt"], d["kb"], d["cc"], d["w"]
            if d["first"]:
                for c4 in range(4):
                    P.op("pe", lambda t, c4=c4: t.matmul(po[0:64, c4 * 128:(c4 + 1) * 128], lhsT=Z0[:, :], rhs=NON[:, :], start=True, stop=False),
                         reads=(Z0, NON), writes=(po,), inc=(c4 == 3))
            A_ = d["A"]
            vb = vtt[kb // 4]
            P.op("pe", lambda t: t.matmul(po[0:64, cc:W], lhsT=vb[:, kb % 4, :], rhs=A_[:, 0:w], start=False, stop=d["last"]), reads=(vb, A_), writes=(po,))
            if d["last"]:
                q0 = qt * W
                P.op("act", lambda a: a.copy(out=ybo[:, :], in_=po[0:64, :]), reads=(po,), writes=(ybo,))
                yb_ = yxk[q0 // CH]
                P.dma("sp", yb_, lambda q: q.dma_start(out=yb_.t.ap()[0:64, q0 % CH:q0 % CH + W], in_=ybo[:, :]), reads=(ybo,))

        def attn_gen(qt):
            tl = []
            for kb in range(4 * qt + 3, -1, -1):
                diag = kb >= 4 * qt
                cc = 128 * (kb - 4 * qt) if diag else 0
                tl.append(dict(qt=qt, kb=kb, diag=diag, cc=cc, w=W - cc, first=(kb == 4 * qt + 3), last=(kb == 0)))
            n = len(tl)
            for i in range(-2, n):
                if 0 <= i + 2 < n:
                    S1(tl[i + 2])
                if 0 <= i + 1 < n:
                    S2(tl[i + 1])
                if 0 <= i < n:
                    S3(tl[i])
                yield

    for ti in range(NTL):
        c0 = ti * W
        if L == 0:
            xsrc = io["xT0"]
            P.dma("sp", xt, lambda q: q.dma_start(out=xt[:, :, :], in_=xsrc.t.ap().rearrange("(kc p) t -> p kc t", p=128)[:, :, c0:c0 + W]), reads=(xsrc,))
        else:
            sg_, o_ = c0 // NT, c0 % NT
            for kc in range(KC):
                xb_ = io["xgk"][kc][o_ // CH]
                P.dma("sp", xt, lambda q, kc=kc, xb_=xb_: q.dma_start(out=xt[:, kc, :], in_=xb_.t.ap()[sg_ * 128:(sg_ + 1) * 128, o_ % CH:o_ % CH + W]), reads=(xb_,))
        rmsnorm()
        ag = None
        if do_attn:
            pq = proj(0, 64)
            P.op("act", lambda a: a.activation(out=qTt[ti][:, :], in_=pq[0:64, :], func=AF.Copy, scale=0.125), reads=(pq,), writes=(qTt[ti],))
            pk = proj(64, 64)
            P.op("act", lambda a: a.copy(out=kTt[ti][:, :], in_=pk[0:64, :]), reads=(pk,), writes=(kTt[ti],))
            for tb in range(4):
                pv = bank()
                mm_group(P, pv[:, 0:64], pv, [(h[:, kc, tb * 128:(tb + 1) * 128], wsv[:, kc, 128:192]) for kc in range(KC)], reads=(h, ws))
                P.op("dve", lambda v, pv=pv, tb=tb: v.tensor_copy(out=vtt[ti][:, tb, :], in_=pv[:, 0:64]), reads=(pv,), writes=(vtt[ti],))
            ag = attn_gen(ti) if interleave else None
            n_it = 4 * ti + 4 + 2
        if do_rwkv:
            per = max(1, -(-n_it // 26)) if ag is not None else 0
            for _ in rwkv_tile(ti):
                if ag is not None:
                    for _k in range(per):
                        if next(ag, "done") == "done":
                            ag = None
                            break
        if ag is not None:
            for _ in ag:
                pass
        if io.get("after_seq_tile") is not None:
            io["after_seq_tile"](ti)
    if do_attn and not interleave:
        for ti in range(NTL):
            for _ in attn_gen(ti):
                pass


def pack_seqmix_weights(inp, l, hd):
    f = np.float32
    w_in = inp["w_in"][l]
    hs = slice(hd * 64, hd * 64 + 64)
    cols = np.concatenate([SB_OFF + np.arange(64) + hd * 64, SB_OFF + 256 + np.arange(64) + hd * 64, SB_OFF + 512 + np.arange(64) + hd * 64,
                           RW_OFF + np.arange(64) + hd * 64, RW_OFF + 256 + np.arange(64) + hd * 64, RW_OFF + 512 + np.arange(64) + hd * 64,
                           RW_OFF + 768 + np.arange(256)])
    d = {"wS": _kc_layout(w_in[:, cols])}
    vec = np.zeros((128, 32), f)
    vec[:, 0:8] = _pvec(inp["mix_norm_g"][l])
    mu = inp["rw_mu"][l]
    vec[:, 8] = mu[896:1024]
    for m, o in enumerate((hd * 64, 256 + hd * 64, 512 + hd * 64, 768, 832)):
        vec[0:64, 9 + m] = mu[o:o + 64]
    vec[0:64, 14] = inp["rw_w0"][l][hs]
    vec[0:64, 15] = inp["rw_a0"][l][hs]
    vec[0:64, 16] = inp["rw_k_k"][l][hs]
    vec[0:64, 17] = inp["rw_k_a"][l][hs]
    vec[0:64, 18] = inp["rw_r_k"][l][hd]
    vec[0:64, 19] = inp["rw_gn_g"][l][hs]
    vec[0:64, 20] = inp["rw_gn_b"][l][hs]
    d["vecS"] = vec
    d["w2h"] = inp["rw_w2"][l][:, hs]
    d["a2h"] = inp["rw_a2"][l][:, hs]
    d["g2h"] = inp["rw_g2"][l][:, hs]
    return {k: np.ascontiguousarray(v, dtype=f) for k, v in d.items()}


def build_fused(T, nlayers=2):
    nc = bass.Bass("TRN2", target_bir_lowering=False)
    P = Prog(nc)
    NT = T // 4
    TP = HALO + T
    EI = "ExternalInput"
    CH = 1024
    io = {
        "xT0": P.dram("xT0", [D, T], F32, EI),
        "xTh": P.dram("xTh", [D, HALO + NT], F32, EI),
        "hm": P.dram("hm", [128, 1], F32, EI),
        "oh": P.dram("oh", [128, 4], F32, EI),
        "ohp": P.dram("ohp", [128, 4], F32, EI),
        "yxk": [P.dram(f"yx{k}", [128, CH], F32, semkey="yx") for k in range(T // CH)],
        "ygk": [P.dram(f"yg{k}", [512, CH], F32, semkey="yg") for k in range(T // CH)],
        "xok": [[P.dram(f"xo{kc}_{cc}", [128, CH], F32, semkey="xo") for cc in range(NT // CH)] for kc in range(KC)],
        "xgk": [[P.dram(f"xg{kc}_{cc}", [512, CH], F32, semkey="xg") for cc in range(NT // CH)] for kc in range(KC)],
    }
    out = P.dram("out", [D, NT], F32, "ExternalOutput")
    RG = [[0, 1, 2, 3], [4, 5, 6, 7]]

    def gather(src, dst):
        P.dma("pool", dst, lambda g: g.collective_compute("AllGather", ALU.bypass, replica_groups=RG, ins=[src.t.ap().opt()], outs=[dst.t.ap().opt()]),
              reads=(src,), inc=1)

    def after_seq_tile(ti):
        if ti % 2 == 1:
            gather(io["yxk"][ti // 2], io["ygk"][ti // 2])

    for L in range(nlayers):
        lastL = (L == nlayers - 1)
        P.begin_phase()
        io["after_seq_tile"] = after_seq_tile
        emit_seqmix(P, T, L, io)
        P.end_phase()
        P.begin_phase()
        io["dense_out"] = out

        def after_dense_tile(ti):
            if ti % 2 == 1:
                for kc in range(KC):
                    gather(io["xok"][kc][ti // 2], io["xgk"][kc][ti // 2])
        io["after_dense_tile"] = None if lastL else after_dense_tile
        emit_dense(P, NT, L, L == 1, io, last_out=lastL)
        if lastL:
            P.final_wait("sp", (out,))
        P.end_phase()
    return nc, list(P.ext_in)


_CACHE = {}


def kernel(**inputs):
    inp = {k: np.asarray(v, dtype=np.float32) for k, v in inputs.items()}
    x = inp["x"]
    B, T, _ = x.shape
    NT = T // 4
    cores = list(range(8))
    nl = int(inputs.get("_nlayers", 2)) if "_nlayers" in inputs else 2
    if (T, nl) not in _CACHE:
        _CACHE[(T, nl)] = build_fused(T, nl)
    nc, ext_names = _CACHE[(T, nl)]
    xT_b = [np.ascontiguousarray(x[b].T) for b in range(B)]
    dw = [pack_dense_weights(inp, l) for l in range(2)]
    maps = []
    for c in cores:
        b, sg = c // 4, c % 4
        m = {"xT0": xT_b[b]}
        halo = np.zeros((D, HALO + NT), np.float32)
        if sg > 0:
            halo[:, :] = xT_b[b][:, sg * NT - HALO:(sg + 1) * NT]
        else:
            halo[:, HALO:] = xT_b[b][:, 0:NT]
        m["xTh"] = halo
        m["hm"] = np.full((128, 1), 0.0 if sg == 0 else 1.0, np.float32)
        oh = np.zeros((128, 4), np.float32)
        oh[:, sg] = 1.0
        ohp = np.zeros((128, 4), np.float32)
        if sg > 0:
            ohp[:, sg - 1] = 1.0
        m["oh"], m["ohp"] = oh, ohp
        for l in range(nl):
            for k, v in dw[l].items():
                m[f"{k}_{l}"] = v
            for k, v in pack_seqmix_weights(inp, l, sg).items():
                m[f"{k}_{l}"] = v
        maps.append(m)
    maps = [{k: m[k] for k in ext_names} for m in maps]
    res = run_bass_kernel_spmd(nc, maps, core_ids=cores)
    outp = np.zeros((B, T, D), np.float32)
    for c in cores:
        b, sg = c // 4, c % 4
        outp[b, sg * NT:(sg + 1) * NT, :] = res.results[c]["out"].T
    return outp
```

```python
import math
from contextlib import ExitStack
import numpy as np
import concourse.bass as bass
import concourse.mybir as mybir
from concourse.bass_utils import run_bass_kernel_spmd

F32 = mybir.dt.float32
BF16 = mybir.dt.bfloat16
AF = mybir.ActivationFunctionType
ALU = mybir.AluOpType
AX = mybir.AxisListType

D = 1024
KC = 8
DFF = 2816
NJ = 22
HALO = 128
RMS_EPS = 1e-6
LN_EPS = 1e-5
GN_EPS = 64e-5
DECAY_SCALE = math.exp(-0.5)


class Buf:
    def __init__(self, t, name, semkey=None):
        self.t = t
        self.name = name
        self.last_w = None
        self.readers = {}
        self.dma_sem = None
        self.semkey = semkey

    def __getitem__(self, idx):
        return self.t[idx]


class _Rec:
    def __init__(self):
        self.call = None

    def __getattr__(self, name):
        def f(*a, **k):
            self.call = (name, a, k)
            return self
        return f


def _rec(fn):
    r = _Rec()
    fn(r)
    assert r.call is not None
    return r.call


class Prog:
    ENGS = ("pe", "act", "dve", "pool", "sp")

    def __init__(self, nc):
        self.nc = nc
        self.ops = {e: [] for e in self.ENGS}
        self.sems = {}
        self.cnt = {}
        self.is_dma = {}
        self.waited = {e: {} for e in self.ENGS}
        self.pending = {e: False for e in self.ENGS}
        self.ekey = {}
        for e in ("pe", "act", "dve", "pool"):
            self._mksem(e, False)
            self.ekey[e] = e
        self.nphase = 0
        self.nbuf = 0
        self.rr = 0
        self.stack = None
        self.ext_in = []

    def begin_phase(self):
        self.stack = ExitStack()
        self.nphase += 1
        for e in ("pe", "act", "dve", "pool"):
            key = f"{e}_p{self.nphase}"
            self._mksem(key, False)
            self.ekey[e] = key
        for e in self.ENGS:
            waits = []
            for k, v in self.cnt.items():
                if v > 0 and k != self.ekey.get(e) and self.waited[e].get(k, 0) < v:
                    self.waited[e][k] = v
                    waits.append((k, v))
            if waits:
                self.ops[e].append((waits, None, None))

    def end_phase(self):
        self.emit()
        self.ops = {e: [] for e in self.ENGS}
        self.stack.close()
        self.stack = None

    def _mksem(self, key, dma):
        self.sems[key] = self.nc.alloc_semaphore("s_" + key)
        self.cnt[key] = 0
        self.is_dma[key] = dma

    def sb(self, shape, dt=F32, name=None):
        self.nbuf += 1
        name = (name or "sb") + f"_{self.nbuf}"
        if self.stack is not None:
            return Buf(self.stack.enter_context(self.nc.sbuf_tensor(name, list(shape), dt)), name)
        return Buf(self.nc.alloc_sbuf_tensor(name, list(shape), dt), name)

    def ps(self, shape=(128, 512), dt=F32, name=None):
        self.nbuf += 1
        name = (name or "ps") + f"_{self.nbuf}"
        if self.stack is not None:
            return Buf(self.stack.enter_context(self.nc.psum_tensor(name, list(shape), dt)), name)
        return Buf(self.nc.alloc_psum_tensor(name, list(shape), dt), name)

    def dram(self, name, shape, dt=F32, kind="Internal", semkey=None):
        if kind == "ExternalInput":
            self.ext_in.append(name)
        return Buf(self.nc.dram_tensor(name, list(shape), dt, kind=kind), name, semkey)

    def _needs(self, eng, reads, writes, pe_chain):
        needs = {}

        def add(tok):
            if tok is None:
                return
            k, v = tok
            if needs.get(k, 0) < v:
                needs[k] = v

        for b in reads:
            add(b.last_w)
        for b in writes:
            add(b.last_w)
            for k, v in b.readers.items():
                add((k, v))
        out = []
        for k, v in needs.items():
            if self.is_dma[k]:
                v = self.cnt[k]
            elif eng == "pe" and k == self.ekey["pe"] and pe_chain:
                continue
            if self.waited[eng].get(k, 0) >= v:
                continue
            self.waited[eng][k] = v
            out.append((k, v))
        return out

    def _commit(self, tok, reads, writes):
        k, v = tok
        for b in reads:
            if b.readers.get(k, 0) < v:
                b.readers[k] = v
        for b in writes:
            b.last_w = tok
            b.readers = {}

    def op(self, eng, fn, reads=(), writes=(), inc=True):
        waits = self._needs(eng, reads, writes, True)
        key = self.ekey[eng]
        if inc:
            self.cnt[key] += 1
            tok = (key, self.cnt[key])
            self.pending[eng] = False
            self.ops[eng].append((waits, _rec(fn), (key, 1)))
        else:
            tok = (key, self.cnt[key] + 1)
            self.pending[eng] = True
            self.ops[eng].append((waits, _rec(fn), None))
        self._commit(tok, reads, writes)

    def dma(self, q, out_buf, fn, reads=(), inc=16):
        if out_buf.dma_sem is None:
            key = "d_" + (out_buf.semkey or out_buf.name)
            if key not in self.sems:
                self._mksem(key, True)
            out_buf.dma_sem = key
        key = out_buf.dma_sem
        waits = self._needs(q, reads, (out_buf,), False)
        self.cnt[key] += inc
        tok = (key, self.cnt[key])
        self.ops[q].append((waits, _rec(fn), (key, inc)))
        self._commit(tok, reads, (out_buf,))

    def final_wait(self, eng, bufs):
        waits = self._needs(eng, bufs, (), False)
        self.ops[eng].append((waits, None, None))

    def emit(self):
        sems = self.sems
        for e in self.ENGS:
            assert not self.pending[e], e

        def run(e_name):
            def body(engine):
                for waits, fn, inc in self.ops[e_name]:
                    for k, v in waits:
                        engine.wait_ge(sems[k], v)
                    if fn is not None:
                        ins = getattr(engine, fn[0])(*fn[1], **fn[2])
                        if inc is not None:
                            ins.then_inc(sems[inc[0]], inc[1])
            return body

        with self.nc.Block() as block:
            block.tensor(run("pe"))
            block.scalar(run("act"))
            block.vector(run("dve"))
            block.gpsimd(run("pool"))
            block.sync(run("sp"))

    def ew(self):
        self.rr ^= 1
        return "dve" if self.rr else "pool"


def mm_group(P, out_ap, out_buf, pairs, reads):
    n = len(pairs)
    for i, (l, r) in enumerate(pairs):
        P.op("pe", (lambda t, l=l, r=r, i=i: t.matmul(out_ap, lhsT=l, rhs=r, start=(i == 0), stop=(i == n - 1))),
             reads=reads, writes=(out_buf,), inc=(i == n - 1))


def emit_dense(P, NT, L, last_layer, io, W=512, last_out=True):
    TT = HALO + NT
    EI = "ExternalInput"
    sfx = f"_{L}"
    hm, ohd, ohpd = io["hm"], io["oh"], io["ohp"]
    CH = 1024
    wAD = P.dram("wAD" + sfx, [128, KC * 1280], F32, EI)
    wG = P.dram("wG" + sfx, [8, 128, KC * 512], F32, EI)
    wOut = P.dram("wOut" + sfx, [4, 128, 2 * D], F32, EI)
    wO = P.dram("wO" + sfx, [2, 128, KC * 512], F32, EI)
    wUp = P.dram("wUp" + sfx, [11, 128, KC * 512], F32, EI)
    wDn = P.dram("wDn" + sfx, [8, 128, NJ * 128], F32, EI)
    vecs = P.dram("vecs" + sfx, [128, 64], F32, EI)
    convF = P.dram("convF" + sfx, [128, 44 * 3], F32, EI)
    rowsD = P.dram("rowsD" + sfx, [128, 512], F32, EI)
    wsT = P.dram("wsT" + sfx, [128, 4 * 128], F32, EI)
    bsr = P.dram("bsr" + sfx, [1, 4 * 128], F32, EI)
    out = io["dense_out"]
    ohb = P.sb([128, 8], F32, "ohb")
    P.dma("sp", ohb, lambda q: q.dma_start(out=ohb[:, 0:4], in_=ohd.t.ap()), reads=(ohd,))
    P.dma("sp", ohb, lambda q: q.dma_start(out=ohb[:, 4:8], in_=ohpd.t.ap()), reads=(ohpd,))

    vec = P.sb([128, 64], F32, "vec")
    P.dma("sp", vec, lambda q: q.dma_start(out=vec[:, :], in_=vecs.t.ap()), reads=(vecs,))
    cvf = P.sb([128, 44 * 3], F32, "cvf")
    P.dma("sp", cvf, lambda q: q.dma_start(out=cvf[:, :], in_=convF.t.ap()), reads=(convF,))
    rows = P.sb([128, 512], F32, "rows")
    P.dma("sp", rows, lambda q: q.dma_start(out=rows[:, :], in_=rowsD.t.ap()), reads=(rowsD,))
    hmb = P.sb([128, 1], F32, "hmb")
    P.dma("sp", hmb, lambda q: q.dma_start(out=hmb[:, :], in_=hm.t.ap()), reads=(hm,))
    wsb = P.sb([128, 512], BF16, "wsb")
    wsf = P.sb([128, 512], F32, "wsf")
    P.dma("sp", wsf, lambda q: q.dma_start(out=wsf[:, :], in_=wsT.t.ap()), reads=(wsT,))
    P.op("pool", lambda g: g.affine_select(out=wsf[:, :].rearrange("p (g t) -> p g t", g=4),
                                           in_=wsf[:, :].rearrange("p (g t) -> p g t", g=4),
                                           pattern=[[0, 4], [1, 128]], compare_op=ALU.is_ge, fill=0.0,
                                           base=0, channel_multiplier=-1),
         reads=(wsf,), writes=(wsf,))
    P.op("dve", lambda v: v.tensor_copy(out=wsb[:, :], in_=wsf[:, :]), reads=(wsf,), writes=(wsb,))
    bsb = P.sb([1, 512], BF16, "bsb")
    P.dma("pool", bsb, lambda q: q.dma_start(out=bsb[:, :], in_=bsr.t.ap()), reads=(bsr,))
    onesb = P.sb([128, 128], BF16, "onesb")
    P.op("pool", lambda g: g.memset(onesb[:, :], 1.0), writes=(onesb,))
    wad = P.sb([128, KC * 1280], BF16, "wad")
    P.dma("pool", wad, lambda q: q.dma_start(out=wad[:, :], in_=wAD.t.ap()), reads=(wAD,))
    wadv = wad.t.ap().rearrange("p (kc c) -> p kc c", kc=KC)
    wout = [P.sb([128, 2 * D], BF16, f"wout{b}") for b in range(4)]
    for b in range(4):
        P.dma("pool", wout[b], lambda q, b=b: q.dma_start(out=wout[b][:, :], in_=wOut.t.ap()[b]), reads=(wOut,))

    NSLOT = 4
    ring = [P.sb([128, 4096], BF16, f"ring{i}") for i in range(NSLOT)]
    stream = []
    for oc in range(8):
        stream.append((wG, oc, KC * 512))
    for i in range(2):
        stream.append((wO, i, KC * 512))
    for jj in range(11):
        stream.append((wUp, jj, KC * 512))
    for oc in range(8):
        stream.append((wDn, oc, NJ * 128))
    state = {"issued": 0, "consumed": 0}

    def issue_block():
        i = state["issued"]
        src, idx, n = stream[i % len(stream)]
        slot = ring[i % NSLOT]
        P.dma("pool", slot, lambda q: q.dma_start(out=slot[:, 0:n], in_=src.t.ap()[idx]), reads=(src,))
        state["issued"] += 1

    def next_block(total_blocks):
        while state["issued"] < min(state["consumed"] + NSLOT - 1, total_blocks):
            issue_block()
        if state["issued"] <= state["consumed"]:
            issue_block()
        slot = ring[state["consumed"] % NSLOT]
        state["consumed"] += 1
        return slot

    xt = P.sb([128, KC, W], F32, "xt")
    h = P.sb([128, KC, W], BF16, "h")
    act = P.sb([128, NJ, W], BF16, "act")
    merged = P.sb([128, KC, W], F32, "merged")
    mb = P.sb([128, KC, W], BF16, "mb")
    ybt = P.sb([128, 4, W], BF16, "ybt")
    ycand = [P.sb([128, 4, W], BF16, f"ycand{i}") for i in range(2)]
    ya = P.sb([128, 2, W], BF16, "ya")
    yd = P.sb([128, 2, W], BF16, "yd")
    ug = P.sb([128, 2, W], F32, "ug")
    cxh = P.sb([128, 2, W + 2], F32, "cxh")
    uh = P.sb([128, 44, 2], F32, "uh")
    uc = [P.sb([128, W + 2], F32, f"uc{i}") for i in range(2)]
    tmp = [P.sb([128, W], F32, f"tmp{i}") for i in range(6)]
    rstd = P.sb([128, W], F32, "rstd")
    lnt = P.sb([128, W], F32, "lnt")
    vtk = [P.sb([128, 256], F32, f"vtk{i}") for i in range(4)]
    vnb = P.sb([128, 256], BF16, "vnb")
    st = P.sb([128, 8], F32, "st")
    banks = [P.ps([128, 512], F32, f"bank{i}") for i in range(8)]
    bstate = {"i": 0}

    def bank():
        b = banks[bstate["i"] % 8]
        bstate["i"] += 1
        return b

    P.op("pool", lambda g: g.memset(cxh[:, :, :], 0.0), writes=(cxh,))
    P.op("pool", lambda g: g.memset(uh[:, :, :], 0.0), writes=(uh,))

    def rmsnorm(w, gcol, out_buf, out_f32=False):
        P.op("dve", lambda v: v.tensor_tensor(out=act[:, 0:KC, 0:w], in0=xt[:, :, 0:w], in1=xt[:, :, 0:w], op=ALU.mult),
             reads=(xt,), writes=(act,))
        pb = bank()
        mm_group(P, pb[:, 0:w], pb, [(onesb[:, :], act[:, kc, 0:w]) for kc in range(KC)], reads=(onesb, act))
        P.op("act", lambda a: a.activation(out=lnt[:, 0:w], in_=pb[:, 0:w], func=AF.Ln, bias=RMS_EPS, scale=1.0 / D),
             reads=(pb,), writes=(lnt,))
        P.op("act", lambda a: a.activation(out=rstd[:, 0:w], in_=lnt[:, 0:w], func=AF.Exp, scale=-0.5),
             reads=(lnt,), writes=(rstd,))
        for kc in range(KC):
            e = "dve"
            P.op(e, lambda v, kc=kc: v.scalar_tensor_tensor(out=out_buf[:, kc, 0:w], in0=xt[:, kc, 0:w],
                                                            scalar=vec[:, gcol + kc:gcol + kc + 1], in1=rstd[:, 0:w],
                                                            op0=ALU.mult, op1=ALU.mult),
                 reads=(xt, vec, rstd), writes=(out_buf,))

    ntiles = NT // W
    total_blocks = len(stream) * ntiles

    def tile(off, w, is_halo, out_off):
        if L == 0:
            xsrc = io["xTh"]
            xv = xsrc.t.ap().rearrange("(kc p) t -> p kc t", p=128)
            P.dma("sp", xt, lambda q: q.dma_start(out=xt[:, :, 0:w], in_=xv[:, :, off:off + w]), reads=(xsrc,))
        elif not is_halo:
            o_ = off - HALO
            for kc in range(KC):
                xb_ = io["xok"][kc][o_ // CH]
                P.dma("sp", xt, lambda q, kc=kc, xb_=xb_: q.dma_start(out=xt[:, kc, 0:w], in_=xb_.t.ap()[:, o_ % CH:o_ % CH + w]), reads=(xb_,))
        else:
            for sgi in range(4):
                for kc in range(KC):
                    xb_ = io["xgk"][kc][NT // CH - 1]
                    P.dma("sp", merged, lambda q, kc=kc, xb_=xb_, sgi=sgi: q.dma_start(out=merged[:, kc, 0:HALO], in_=xb_.t.ap()[sgi * 128:(sgi + 1) * 128, CH - HALO:CH]), reads=(xb_,))
                if sgi == 0:
                    P.op("dve", lambda v: v.tensor_scalar(out=xt[:, :, 0:HALO], in0=merged[:, :, 0:HALO], scalar1=ohb[:, 4:5], scalar2=None, op0=ALU.mult),
                         reads=(merged, ohb), writes=(xt,))
                else:
                    P.op("dve", lambda v, sgi=sgi: v.scalar_tensor_tensor(out=xt[:, :, 0:HALO], in0=merged[:, :, 0:HALO], scalar=ohb[:, 4 + sgi:5 + sgi], in1=xt[:, :, 0:HALO], op0=ALU.mult, op1=ALU.add),
                         reads=(merged, ohb, xt), writes=(xt,))
        for sgi in range(4):
            yc_ = ycand[sgi % 2]
            t0_ = sgi * NT + off - HALO
            if t0_ < 0:
                P.op("dve", lambda v, yc_=yc_: v.memset(yc_[:, :, 0:w], 0.0), writes=(yc_,))
            else:
                gb_ = io["ygk"][t0_ // CH]
                gv_ = gb_.t.ap().rearrange("(h p) t -> p h t", p=128)
                P.dma("pool", yc_, lambda q, yc_=yc_, gv_=gv_, t0_=t0_: q.dma_start(out=yc_[:, :, 0:w], in_=gv_[:, :, t0_ % CH:t0_ % CH + w]), reads=(gb_,))
            if sgi == 0:
                P.op("dve", lambda v, yc_=yc_: v.tensor_scalar(out=ybt[:, :, 0:w], in0=yc_[:, :, 0:w], scalar1=ohb[:, 0:1], scalar2=None, op0=ALU.mult),
                     reads=(yc_, ohb), writes=(ybt,))
            else:
                P.op("dve", lambda v, yc_=yc_, sgi=sgi: v.scalar_tensor_tensor(out=ybt[:, :, 0:w], in0=yc_[:, :, 0:w], scalar=ohb[:, sgi:sgi + 1], in1=ybt[:, :, 0:w], op0=ALU.mult, op1=ALU.add),
                     reads=(yc_, ohb, ybt), writes=(ybt,))
        rmsnorm(w, 0, h)
        hr = (h, wad)
        for ch in range(2):
            pbg, pcg, pxi = bank(), bank(), bank()
            for pb, c in ((pbg, ch), (pcg, 2 + ch), (pxi, 4 + ch)):
                mm_group(P, pb[:, 0:w], pb, [(wadv[:, kc, c * 128:(c + 1) * 128], h[:, kc, 0:w]) for kc in range(KC)], reads=hr)
            t0, t1 = tmp[0], tmp[1]
            P.op("act", lambda a: a.copy(out=t0[:, 0:w], in_=pxi[:, 0:w]), reads=(pxi,), writes=(t0,))
            P.op("dve", lambda v: v.tensor_tensor(out=cxh[:, ch, 2:2 + w], in0=pcg[:, 0:w], in1=t0[:, 0:w], op=ALU.mult),
                 reads=(pcg, t0), writes=(cxh,))
            c0 = 56 + ch * 3
            P.op("dve", lambda v: v.tensor_scalar(out=t1[:, 0:w], in0=cxh[:, ch, 0:w], scalar1=vec[:, c0:c0 + 1], scalar2=None, op0=ALU.mult),
                 reads=(cxh, vec), writes=(t1,))
            P.op("dve", lambda v: v.scalar_tensor_tensor(out=t1[:, 0:w], in0=cxh[:, ch, 1:1 + w], scalar=vec[:, c0 + 1:c0 + 2], in1=t1[:, 0:w], op0=ALU.mult, op1=ALU.add),
                 reads=(cxh, vec, t1), writes=(t1,))
            P.op("dve", lambda v: v.scalar_tensor_tensor(out=t1[:, 0:w], in0=cxh[:, ch, 2:2 + w], scalar=vec[:, c0 + 2:c0 + 3], in1=t1[:, 0:w], op0=ALU.mult, op1=ALU.add),
                 reads=(cxh, vec, t1), writes=(t1,))
            P.op("dve", lambda v: v.tensor_tensor(out=ya[:, ch, 0:w], in0=pbg[:, 0:w], in1=t1[:, 0:w], op=ALU.mult),
                 reads=(pbg, t1), writes=(ya,))
            if is_halo:
                P.op("dve", lambda v: v.tensor_scalar(out=cxh[:, ch, 0:2], in0=cxh[:, ch, w:w + 2], scalar1=hmb[:, 0:1], scalar2=None, op0=ALU.mult),
                     reads=(cxh, hmb), writes=(cxh,))
            else:
                P.op("dve", lambda v: v.tensor_copy(out=cxh[:, ch, 0:2], in_=cxh[:, ch, w:w + 2]), reads=(cxh,), writes=(cxh,))

        def gelu(dst_ap, dst_buf, src_ap, src_buf, npart, n, t_a, t_b):
            P.op("act", lambda a: a.copy(out=t_a[0:npart, 0:n], in_=src_ap), reads=(src_buf,), writes=(t_a,))
            P.op("dve", lambda g: g.tensor_tensor(out=t_b[0:npart, 0:n], in0=t_a[0:npart, 0:n], in1=t_a[0:npart, 0:n], op=ALU.mult),
                 reads=(t_a,), writes=(t_b,))
            P.op("dve", lambda v: v.tensor_scalar(out=t_b[0:npart, 0:n], in0=t_b[0:npart, 0:n], scalar1=0.044715, scalar2=1.0, op0=ALU.mult, op1=ALU.add),
                 reads=(t_b,), writes=(t_b,))
            P.op("dve", lambda g: g.tensor_tensor(out=t_b[0:npart, 0:n], in0=t_b[0:npart, 0:n], in1=t_a[0:npart, 0:n], op=ALU.mult),
                 reads=(t_a, t_b), writes=(t_b,))
            P.op("act", lambda a: a.activation(out=t_b[0:npart, 0:n], in_=t_b[0:npart, 0:n], func=AF.Sigmoid, scale=2.0 * 0.7978845608028654),
                 reads=(t_b,), writes=(t_b,))
            P.op("dve", lambda v: v.tensor_tensor(out=dst_ap, in0=t_a[0:npart, 0:n], in1=t_b[0:npart, 0:n], op=ALU.mult),
                 reads=(t_a, t_b), writes=(dst_buf,))

        for ch in range(2):
            pu = bank()
            c = 768 + ch * 128
            mm_group(P, pu[:, 0:w], pu, [(wadv[:, kc, c:c + 128], h[:, kc, 0:w]) for kc in range(KC)], reads=hr)
            gelu(ug[:, ch, 0:w], ug, pu[:, 0:w], pu, 128, w, tmp[2], tmp[3])
        for tb in range(w // 128):
            pv = bank()
            mm_group(P, pv[:, 0:256], pv, [(h[:, kc, tb * 128:(tb + 1) * 128], wadv[:, kc, 1024:1280]) for kc in range(KC)], reads=hr)
            gv, sq = vtk[0], vtk[1]
            gelu(gv[:, :], gv, pv[:, 0:256], pv, 128, 256, vtk[2], vtk[3])
            P.op("dve", lambda v: v.tensor_reduce(out=st[:, 0:1], in_=gv[:, :], axis=AX.X, op=ALU.add), reads=(gv,), writes=(st,))
            P.op("dve", lambda g: g.tensor_tensor(out=sq[:, :], in0=gv[:, :], in1=gv[:, :], op=ALU.mult), reads=(gv,), writes=(sq,))
            P.op("dve", lambda v: v.tensor_reduce(out=st[:, 1:2], in_=sq[:, :], axis=AX.X, op=ALU.add), reads=(sq, st), writes=(st,))
            P.op("dve", lambda v: v.tensor_scalar(out=st[:, 2:4], in0=st[:, 0:2], scalar1=1.0 / 256, scalar2=None, op0=ALU.mult), reads=(st,), writes=(st,))
            P.op("dve", lambda v: v.tensor_tensor(out=st[:, 4:5], in0=st[:, 2:3], in1=st[:, 2:3], op=ALU.mult), reads=(st,), writes=(st,))
            P.op("dve", lambda v: v.tensor_tensor(out=st[:, 5:6], in0=st[:, 3:4], in1=st[:, 4:5], op=ALU.subtract), reads=(st,), writes=(st,))
            P.op("act", lambda a: a.activation(out=st[:, 6:7], in_=st[:, 5:6], func=AF.Ln, bias=LN_EPS, scale=1.0), reads=(st,), writes=(st,))
            P.op("act", lambda a: a.activation(out=st[:, 7:8], in_=st[:, 6:7], func=AF.Exp, scale=-0.5), reads=(st,), writes=(st,))
            P.op("dve", lambda v: v.tensor_scalar(out=gv[:, :], in0=gv[:, :], scalar1=st[:, 2:3], scalar2=st[:, 7:8], op0=ALU.subtract, op1=ALU.mult),
                 reads=(gv, st), writes=(gv,))
            P.op("dve", lambda g: g.tensor_tensor(out=gv[:, :], in0=gv[:, :], in1=rows[:, 0:256], op=ALU.mult), reads=(gv, rows), writes=(gv,))
            P.op("dve", lambda v: v.tensor_tensor(out=vnb[:, :], in0=gv[:, :], in1=rows[:, 256:512], op=ALU.add), reads=(gv, rows), writes=(vnb,))
            for g4 in range(4):
                pm = bank()
                po = (g4 % 2) * 64
                mm_group(P, pm[po:po + 64, 0:128], pm,
                         [(vnb[:, g4 * 64:(g4 + 1) * 64], wsb[:, g4 * 128:(g4 + 1) * 128]),
                          (onesb[0:1, 0:64], bsb[0:1, g4 * 128:(g4 + 1) * 128])], reads=(vnb, wsb, onesb, bsb))
                P.op("dve", lambda v, g4=g4, pm=pm, po=po: v.tensor_tensor(out=yd[po:po + 64, g4 // 2, tb * 128:(tb + 1) * 128], in0=pm[po:po + 64, 0:128],
                                                                            in1=ug[po:po + 64, g4 // 2, tb * 128:(tb + 1) * 128], op=ALU.mult),
                     reads=(pm, ug), writes=(yd,))

        for oc in range(8):
            slot = next_block(total_blocks) if not is_halo else None
            if is_halo:
                slot = ring_h
                P.dma("pool", slot, lambda q: q.dma_start(out=slot[:, 0:KC * 512], in_=wG.t.ap()[oc]), reads=(wG,))
            sv = slot.t.ap().rearrange("p (kc c) -> p kc c", kc=KC)
            for br in range(4):
                pg, pp = bank(), bank()
                mm_group(P, pg[:, 0:w], pg, [(sv[:, kc, br * 128:(br + 1) * 128], h[:, kc, 0:w]) for kc in range(KC)], reads=(slot, h))
                wo_v = wout[br].t.ap().rearrange("p (k c) -> p k c", k=2)
                if br == 0:
                    prs = [(wo_v[:, k2, oc * 128:(oc + 1) * 128], ya[:, k2, 0:w]) for k2 in range(2)]
                    rd = (wout[br], ya)
                elif br in (1, 2):
                    po_ = 0 if br == 1 else 64
                    prs = []
                    for hd_ in range(4):
                        wv_ = wout[1 + hd_ // 2].t.ap().rearrange("p (k c) -> p k c", k=2)
                        prs.append((wv_[po_:po_ + 64, hd_ % 2, oc * 128:(oc + 1) * 128], ybt[po_:po_ + 64, hd_, 0:w]))
                    rd = (wout[1], wout[2], ybt)
                else:
                    prs = [(wo_v[:, k2, oc * 128:(oc + 1) * 128], yd[:, k2, 0:w]) for k2 in range(2)]
                    rd = (wout[br], yd)
                mm_group(P, pp[:, 0:w], pp, prs, reads=rd)
                gs = tmp[4]
                gb = 24 + br * 8 + oc
                P.op("act", lambda a, pg=pg, gb=gb: a.activation(out=gs[:, 0:w], in_=pg[:, 0:w], func=AF.Sigmoid, bias=vec[:, gb:gb + 1], scale=1.0),
                     reads=(pg, vec), writes=(gs,))
                if br == 0:
                    P.op("dve", lambda v, pp=pp: v.tensor_tensor(out=merged[:, oc, 0:w], in0=pp[:, 0:w], in1=gs[:, 0:w], op=ALU.mult),
                         reads=(pp, gs), writes=(merged,))
                else:
                    t5 = tmp[5]
                    P.op("dve", lambda v, pp=pp: v.tensor_tensor(out=t5[:, 0:w], in0=pp[:, 0:w], in1=gs[:, 0:w], op=ALU.mult),
                         reads=(pp, gs), writes=(t5,))
                    dst = mb if br == 3 else merged
                    P.op("dve", lambda g, dst=dst: g.tensor_tensor(out=dst[:, oc, 0:w], in0=merged[:, oc, 0:w], in1=t5[:, 0:w], op=ALU.add),
                         reads=(merged, t5), writes=(dst,))

        for i in range(2):
            if is_halo:
                slot = ring_h
                P.dma("pool", slot, lambda q: q.dma_start(out=slot[:, 0:KC * 512], in_=wO.t.ap()[i]), reads=(wO,))
            else:
                slot = next_block(total_blocks)
            sv = slot.t.ap().rearrange("p (kc c) -> p kc c", kc=KC)
            for o4 in range(4):
                oc = i * 4 + o4
                pb = bank()
                mm_group(P, pb[:, 0:w], pb, [(sv[:, kc, o4 * 128:(o4 + 1) * 128], mb[:, kc, 0:w]) for kc in range(KC)], reads=(slot, mb))
                P.op("dve", lambda v, pb=pb, oc=oc: v.tensor_tensor(out=xt[:, oc, 0:w], in0=pb[:, 0:w], in1=xt[:, oc, 0:w], op=ALU.add),
                     reads=(pb, xt), writes=(xt,))

        rmsnorm(w, 8, h)
        for jj in range(11):
            if is_halo:
                slot = ring_h
                P.dma("pool", slot, lambda q: q.dma_start(out=slot[:, 0:KC * 512], in_=wUp.t.ap()[jj]), reads=(wUp,))
            else:
                slot = next_block(total_blocks)
            sv = slot.t.ap().rearrange("p (kc c) -> p kc c", kc=KC)
            for j2 in range(2):
                j = jj * 2 + j2
                cv = []
                for gv_i in range(2):
                    pb = bank()
                    c = (j2 * 2 + gv_i) * 128
                    mm_group(P, pb[:, 0:w], pb, [(sv[:, kc, c:c + 128], h[:, kc, 0:w]) for kc in range(KC)], reads=(slot, h))
                    idx = gv_i * NJ + j
                    u = uc[gv_i]
                    P.op("act", lambda a, pb=pb, u=u: a.copy(out=u[:, 2:2 + w], in_=pb[:, 0:w]), reads=(pb,), writes=(u,))
                    P.op("dve", lambda g, u=u, idx=idx: g.tensor_copy(out=u[:, 0:2], in_=uh[:, idx, :]), reads=(uh, u), writes=(u,))
                    if is_halo:
                        P.op("dve", lambda g, u=u, idx=idx: g.tensor_scalar(out=uh[:, idx, :], in0=u[:, w:w + 2], scalar1=hmb[:, 0:1], scalar2=None, op0=ALU.mult),
                             reads=(u, hmb, uh), writes=(uh,))
                        continue
                    P.op("dve", lambda g, u=u, idx=idx: g.tensor_copy(out=uh[:, idx, :], in_=u[:, w:w + 2]), reads=(u, uh), writes=(uh,))
                    t = tmp[gv_i]
                    P.op("act", lambda a, pb=pb, t=t, idx=idx: a.activation(out=t[:, 0:w], in_=pb[:, 0:w], func=AF.Copy, scale=cvf[:, idx * 3 + 2:idx * 3 + 3]),
                         reads=(pb, cvf), writes=(t,))
                    for tap in (0, 1):
                        P.op("dve", lambda v, u=u, t=t, idx=idx, tap=tap: v.scalar_tensor_tensor(out=t[:, 0:w], in0=u[:, tap:tap + w], scalar=cvf[:, idx * 3 + tap:idx * 3 + tap + 1], in1=t[:, 0:w], op0=ALU.mult, op1=ALU.add),
                             reads=(u, cvf, t), writes=(t,))
                    cv.append(t)
                if is_halo:
                    continue
                sg = tmp[2]
                P.op("act", lambda a: a.activation(out=sg[:, 0:w], in_=cv[0][:, 0:w], func=AF.Sigmoid), reads=(cv[0],), writes=(sg,))
                P.op("dve", lambda v: v.tensor_tensor(out=sg[:, 0:w], in0=sg[:, 0:w], in1=cv[0][:, 0:w], op=ALU.mult), reads=(sg, cv[0]), writes=(sg,))
                P.op("dve", lambda g, j=j: g.tensor_tensor(out=act[:, j, 0:w], in0=sg[:, 0:w], in1=cv[1][:, 0:w], op=ALU.mult), reads=(sg, cv[1]), writes=(act,))
        if is_halo:
            return
        for oc in range(8):
            slot = next_block(total_blocks)
            sv = slot.t.ap().rearrange("p (j c) -> p j c", c=128)
            pb = bank()
            mm_group(P, pb[:, 0:w], pb, [(sv[:, j, :], act[:, j, 0:w]) for j in range(NJ)], reads=(slot, act))
            P.op("dve", lambda v, pb=pb, oc=oc: v.tensor_tensor(out=xt[:, oc, 0:w], in0=pb[:, 0:w], in1=xt[:, oc, 0:w], op=ALU.add),
                 reads=(pb, xt), writes=(xt,))
        if last_layer:
            rmsnorm(w, 16, merged, out_f32=True)
            src = merged
        else:
            src = xt
        if last_out:
            outv = out.t.ap().rearrange("(kc p) t -> p kc t", p=128)
            P.dma("sp", out, lambda q: q.dma_start(out=outv[:, :, out_off:out_off + w], in_=src[:, :, 0:w]), reads=(src,))
        else:
            for kc in range(KC):
                xb_ = io["xok"][kc][out_off // CH]
                P.dma("sp", xb_, lambda q, kc=kc, xb_=xb_: q.dma_start(out=xb_.t.ap()[:, out_off % CH:out_off % CH + w], in_=src[:, kc, 0:w]), reads=(src,))

    ring_h = P.sb([128, 4096], BF16, "ring_h")
    tile(0, HALO, True, None)
    for ti in range(ntiles):
        tile(HALO + ti * W, W, False, ti * W)
        if io.get("after_dense_tile") is not None:
            io["after_dense_tile"](ti)


SB_OFF, RW_OFF, SG_OFF, GATE_OFF = 768, 1536, 2560, 3072


def _kc_layout(w):
    K, C = w.shape
    return np.ascontiguousarray(w.reshape(K // 128, 128, C).transpose(1, 0, 2).reshape(128, (K // 128) * C))


def _pvec(v):
    return np.ascontiguousarray(v.reshape(-1, 128).T)


def pack_dense_weights(inp, l):
    f = np.float32
    w_in = inp["w_in"][l]
    cols = np.concatenate([np.arange(0, 768), np.arange(SG_OFF, SG_OFF + 512)])
    d = {}
    d["wAD"] = _kc_layout(w_in[:, cols])
    wg = w_in[:, GATE_OFF:].reshape(D, 4, 8, 128).transpose(2, 0, 1, 3).reshape(8, D, 512)
    d["wG"] = np.stack([_kc_layout(wg[oc]) for oc in range(8)])
    sbo, rwo = inp["sb_out"][l], inp["rw_out"][l]

    def bc(h0):
        a = np.zeros((128, 2, D), np.float32)
        for k in range(2):
            hd = h0 + k
            a[0:64, k] = sbo[hd * 64:(hd + 1) * 64]
            a[64:128, k] = rwo[hd * 64:(hd + 1) * 64]
        return a.reshape(128, 2 * D)
    d["wOut"] = np.stack([_kc_layout(inp["sc_out"][l]), bc(0), bc(2), _kc_layout(inp["sg_out"][l])])
    d["wO"] = np.stack([_kc_layout(inp["w_o"][l][:, i * 512:(i + 1) * 512]) for i in range(2)])
    wu = inp["w_up"][l].reshape(D, 2, 11, 2, 128).transpose(2, 0, 3, 1, 4).reshape(11, D, 512)
    d["wUp"] = np.stack([_kc_layout(wu[jj]) for jj in range(11)])
    d["wDn"] = np.stack([_kc_layout(inp["w_down"][l][:, oc * 128:(oc + 1) * 128]) for oc in range(8)])
    vec = np.zeros((128, 64), f)
    vec[:, 0:8] = _pvec(inp["mix_norm_g"][l])
    vec[:, 8:16] = _pvec(inp["ffn_norm_g"][l])
    vec[:, 16:24] = _pvec(inp["final_norm_g"])
    vec[:, 24:56] = _pvec(inp["gate_b"][l])
    cw = inp["sc_conv_w"][l]
    for ch in range(2):
        for tap in range(3):
            vec[:, 56 + ch * 3 + tap] = cw[tap, ch * 128:(ch + 1) * 128]
    d["vecs"] = vec
    fc = inp["ffn_conv_w"][l]
    d["convF"] = np.ascontiguousarray(fc.reshape(3, 44, 128).transpose(2, 1, 0).reshape(128, 132))
    rows = np.zeros((128, 512), f)
    rows[:, 0:256] = inp["sg_ln_g"][l][None, :]
    rows[:, 256:512] = inp["sg_ln_b"][l][None, :]
    d["rowsD"] = rows
    d["wsT"] = np.ascontiguousarray(inp["sg_w"][l].transpose(2, 0, 1).reshape(128, 512))
    d["bsr"] = np.ascontiguousarray(inp["sg_b"][l].reshape(1, 512))
    return {k: np.ascontiguousarray(v, dtype=f) for k, v in d.items()}


def dense_core_inputs(xT_b, ybc_b, seg, NT):
    def halo(a):
        o = np.zeros((a.shape[0], HALO + NT), np.float32)
        s = seg * NT
        if seg > 0:
            o[:, :] = a[:, s - HALO:s + NT]
        else:
            o[:, HALO:] = a[:, 0:NT]
        return o
    hm = np.full((128, 1), 0.0 if seg == 0 else 1.0, np.float32)
    return {"xT": halo(xT_b), "ybc": halo(ybc_b), "hm": hm}


def emit_seqmix(P, T, L, io, do_attn=True, do_rwkv=True, interleave=True):
    EI = "ExternalInput"
    W = 512
    NTL = T // W
    NT = T // 4
    sfx = f"_{L}"
    wS = P.dram("wS" + sfx, [128, KC * 640], F32, EI)
    vecS = P.dram("vecS" + sfx, [128, 32], F32, EI)
    w2d = P.dram("w2h" + sfx, [64, 64], F32, EI)
    a2d = P.dram("a2h" + sfx, [64, 64], F32, EI)
    g2d = P.dram("g2h" + sfx, [128, 64], F32, EI)
    yxk = io["yxk"]
    CH = 1024

    vec = P.sb([128, 32], F32, "vec")
    P.dma("sp", vec, lambda q: q.dma_start(out=vec[:, :], in_=vecS.t.ap()), reads=(vecS,))
    ws = P.sb([128, KC * 640], BF16, "ws")
    P.dma("pool", ws, lambda q: q.dma_start(out=ws[:, :], in_=wS.t.ap()), reads=(wS,))
    wsv = ws.t.ap().rearrange("p (kc c) -> p kc c", kc=KC)
    w2h = P.sb([64, 64], F32, "w2h_s"); a2h = P.sb([64, 64], F32, "a2h_s"); g2h = P.sb([128, 64], F32, "g2h_s")
    P.dma("sp", w2h, lambda q: q.dma_start(out=w2h[:, :], in_=w2d.t.ap()), reads=(w2d,))
    P.dma("sp", a2h, lambda q: q.dma_start(out=a2h[:, :], in_=a2d.t.ap()), reads=(a2d,))
    P.dma("sp", g2h, lambda q: q.dma_start(out=g2h[:, :], in_=g2d.t.ap()), reads=(g2d,))
    w2hb = P.sb([64, 64], BF16, "w2hb"); g2hb = P.sb([128, 64], BF16, "g2hb")
    P.op("dve", lambda v: v.tensor_copy(out=w2hb[:, :], in_=w2h[:, :]), reads=(w2h,), writes=(w2hb,))
    P.op("dve", lambda v: v.tensor_copy(out=g2hb[:, :], in_=g2h[:, :]), reads=(g2h,), writes=(g2hb,))
    twb = P.sb([64, 512], BF16, "twb"); sgxb = P.sb([128, 512], BF16, "sgxb"); s1b = P.sb([64, 512], BF16, "s1b")

    onesb = P.sb([128, 128], BF16, "onesb")
    P.op("pool", lambda g: g.memset(onesb[:, :], 1.0), writes=(onesb,))
    onesf = P.sb([128, 128], F32, "onesf")
    P.op("pool", lambda g: g.memset(onesf[:, :], 1.0), writes=(onesf,))
    identf = P.sb([128, 128], F32, "identf")
    P.op("pool", lambda g: g.affine_select(out=identf[:, :], in_=onesf[:, 0:128], pattern=[[-1, 128]], compare_op=ALU.is_equal,
                                           fill=0.0, base=0, channel_multiplier=1), reads=(onesf,), writes=(identf,))

    qTt = [P.sb([64, W], BF16, f"qT{i}") for i in range(NTL)]
    kTt = [P.sb([64, W], BF16, f"kT{i}") for i in range(NTL)]
    vtt = [P.sb([128, 4, 64], BF16, f"vt{i}") for i in range(NTL)]

    xt = P.sb([128, KC, W], F32, "xt")
    h = P.sb([128, KC, W], BF16, "h")
    sq = h
    rstd = P.sb([128, W], F32, "rstd")
    lnt = rstd
    banks = [P.ps([128, 512], F32, f"bank{i}") for i in range(8)]
    bstate = {"i": 0}

    def bank():
        b = banks[bstate["i"] % 4]
        bstate["i"] += 1
        return b

    if do_rwkv:
        rmask = P.sb([64, W], F32, "rmask")
        P.op("pool", lambda g: g.memset(rmask[:, :], 1.0), writes=(rmask,))
        P.op("pool", lambda g: g.memset(rmask[:, :].rearrange("p (c i) -> p c i", i=64)[:, :, 0:1], 0.0), writes=(rmask,))
        maskS = P.sb([128, 128], F32, "maskS")
        mask2 = P.sb([128, 256], F32, "mask2")
        P.op("pool", lambda g: g.affine_select(out=maskS[:, :], in_=onesf[:, 0:128], pattern=[[-1, 128]], compare_op=ALU.is_ge, fill=0.0, base=-1, channel_multiplier=1),
             reads=(onesf,), writes=(maskS,))
        P.op("pool", lambda g: g.memset(maskS[64:128, 0:64], 0.0), writes=(maskS,))
        P.op("pool", lambda g: g.affine_select(out=mask2[:, 0:128], in_=onesf[:, 0:128], pattern=[[1, 128]], compare_op=ALU.is_ge, fill=0.0, base=-1, channel_multiplier=-1),
             reads=(onesf,), writes=(mask2,))
        P.op("pool", lambda g: g.affine_select(out=mask2[:, 128:256], in_=onesf[:, 0:128], pattern=[[1, 128]], compare_op=ALU.is_ge, fill=0.0, base=0, channel_multiplier=-1),
             reads=(onesf,), writes=(mask2,))
        P.op("pool", lambda g: g.memset(mask2[0:64, 64:128], 0.0), writes=(mask2,))
        P.op("pool", lambda g: g.memset(mask2[0:64, 192:256], 0.0), writes=(mask2,))
        zb = P.sb([64, 5, W + 1], F32, "zb")
        zg = P.sb([128, W + 1], F32, "zg")
        P.op("pool", lambda g: g.memset(zb[:, :, :], 0.0), writes=(zb,))
        P.op("pool", lambda g: g.memset(zg[:, :], 0.0), writes=(zg,))
        dbuf = P.sb([64, 5, W], F32, "dbuf")
        zz = dbuf
        dg = P.sb([128, W], F32, "dg")
        sgx = dg
        m_ = {n: P.sb([64, W], F32, n) for n in ("lw", "asg", "kk0", "s1", "s2", "s3", "kmod", "bvec", "bonus", "cl", "gi", "ginv", "gmap")}
        for alias, tgt in (("sgw", "lw"), ("kksq", "s1"), ("tt", "s1"), ("rk", "s1"), ("ssc", "s2"), ("rs", "s2"), ("tw", "s3"), ("cm", "s3"),
                           ("gprev", "s3"), ("kkn", "kk0"), ("ynT", "ginv"), ("yo", "ginv")):
            m_[alias] = m_[tgt]
        m_["BT"] = P.sb([64, W], BF16, "BTb")
        m_["KT"] = P.sb([64, W], BF16, "KTb")
        identb = P.sb([128, 128], BF16, "identb")
        P.op("dve", lambda v: v.tensor_copy(out=identb[:, :], in_=identf[:, :]), reads=(identf,), writes=(identb,))
        ART = P.sb([64, 4, 2, 128], BF16, "ART")
        NSET = 2
        sets = []
        for si in range(NSET):
            sets.append(dict(
                tok=P.sb([128, 192], BF16, f"tok{si}"), abT=P.sb([128, 256], BF16, f"abT{si}"), akT=P.sb([128, 256], BF16, f"akT{si}"),
                Lm=P.sb([128, 128], BF16, f"Lm{si}"), Xb=[P.sb([128, 128], BF16, f"Xb{si}_{i}") for i in range(2)],
                PPb=[P.sb([128, 256], BF16, f"PP{si}_{i}") for i in range(2)], LMb=[P.sb([64, 64], F32, f"LM{si}_{i}") for i in range(2)],
                N0Gb=[P.sb([64, 64], F32, f"N0G{si}_{i}") for i in range(2)], RpT=P.sb([64, 128], F32, f"RpT{si}"),
                ysb=P.sb([128, 64], F32, f"ysb{si}"), ysq=P.sb([128, 64], F32, f"ysq{si}"), yn=P.sb([128, 64], BF16, f"yn{si}"),
                gst=P.sb([128, 8], F32, f"gst{si}")))
        NZ = 8
        Zb = [P.sb([64, 64], F32, f"Z{i}") for i in range(NZ)]
        zst = {"i": 0}
        P.op("pool", lambda g: g.memset(Zb[0][:, :], 0.0), writes=(Zb[0],))

    def rmsnorm():
        P.op("dve", lambda v: v.tensor_tensor(out=sq[:, :, :], in0=xt[:, :, :], in1=xt[:, :, :], op=ALU.mult), reads=(xt,), writes=(sq,))
        pb = bank()
        mm_group(P, pb[:, :], pb, [(onesb[:, :], sq[:, kc, :]) for kc in range(KC)], reads=(onesb, sq))
        P.op("act", lambda a: a.activation(out=lnt[:, :], in_=pb[:, :], func=AF.Ln, bias=RMS_EPS, scale=1.0 / D), reads=(pb,), writes=(lnt,))
        P.op("act", lambda a: a.activation(out=rstd[:, :], in_=lnt[:, :], func=AF.Exp, scale=-0.5), reads=(lnt,), writes=(rstd,))
        for kc in range(KC):
            P.op("dve", lambda v, kc=kc: v.scalar_tensor_tensor(out=h[:, kc, :], in0=xt[:, kc, :], scalar=vec[:, kc:kc + 1], in1=rstd[:, :], op0=ALU.mult, op1=ALU.mult),
                 reads=(xt, vec, rstd), writes=(h,))

    def proj(col, m):
        pb = bank()
        mm_group(P, pb[0:m, :], pb, [(wsv[:, kc, col:col + m], h[:, kc, :]) for kc in range(KC)], reads=(ws, h))
        return pb

    def V(n):
        return m_[n]

    def ew_tt(e, o, a, b, op, rd, wr):
        P.op(e, lambda v: v.tensor_tensor(out=o, in0=a, in1=b, op=op), reads=rd, writes=wr)

    def rwkv_tile(ti):
        c0 = ti * W
        for m in range(5):
            pb_ = proj(192 + 64 * m, 64)
            P.op("act", lambda a, m=m, pb_=pb_: a.copy(out=zb[:, m, 1:W + 1], in_=pb_[0:64, :]), reads=(pb_,), writes=(zb,))
        pg_ = proj(512, 128)
        P.op("act", lambda a: a.copy(out=zg[:, 1:W + 1], in_=pg_[:, :]), reads=(pg_,), writes=(zg,))
        ew_tt("dve", dbuf[:, :, :], zb[:, :, 0:W], zb[:, :, 1:W + 1], ALU.subtract, (zb,), (dbuf,))
        for m in range(5):
            P.op("dve", lambda v, m=m: v.scalar_tensor_tensor(out=zz[:, m, :], in0=dbuf[:, m, :], scalar=vec[0:64, 9 + m:10 + m], in1=zb[:, m, 1:W + 1], op0=ALU.mult, op1=ALU.add),
                 reads=(dbuf, vec, zb), writes=(zz,))
        P.op("dve", lambda v: v.tensor_copy(out=zb[:, :, 0:1], in_=zb[:, :, W:W + 1]), reads=(zb,), writes=(zb,))
        ew_tt("dve", dg[:, :], zg[:, 0:W], zg[:, 1:W + 1], ALU.subtract, (zg,), (dg,))
        P.op("dve", lambda v: v.scalar_tensor_tensor(out=dg[:, :], in0=dg[:, :], scalar=vec[:, 8:9], in1=zg[:, 1:W + 1], op0=ALU.mult, op1=ALU.add),
             reads=(dg, vec, zg), writes=(dg,))
        P.op("dve", lambda v: v.tensor_copy(out=zg[:, 0:1], in_=zg[:, W:W + 1]), reads=(zg,), writes=(zg,))
        yield
        Rm, Km, Vm, XW, XA = (zz[:, i, :] for i in range(5))
        P.op("act", lambda a: a.activation(out=twb[:, :], in_=XW, func=AF.Tanh), reads=(zz,), writes=(twb,))
        P.op("act", lambda a: a.activation(out=sgxb[:, :], in_=dg[:, :], func=AF.Sigmoid), reads=(dg,), writes=(sgxb,))
        pw = bank()
        mm_group(P, pw[0:64, :], pw, [(w2hb[:, :], twb[:, :])], reads=(w2hb, twb))
        pa = bank()
        mm_group(P, pa[0:64, :], pa, [(a2h[:, :], XA)], reads=(a2h, zz))
        pgm = bank()
        mm_group(P, pgm[0:64, :], pgm, [(g2hb[:, :], sgxb[:, :])], reads=(g2hb, sgxb))
        P.op("act", lambda a: a.activation(out=V("sgw")[:, :], in_=pw[0:64, :], func=AF.Sigmoid, bias=vec[0:64, 14:15], scale=1.0), reads=(pw, vec), writes=(V("sgw"),))
        P.op("act", lambda a: a.activation(out=V("asg")[:, :], in_=pa[0:64, :], func=AF.Sigmoid, bias=vec[0:64, 15:16], scale=1.0), reads=(pa, vec), writes=(V("asg"),))
        P.op("act", lambda a: a.copy(out=V("gmap")[:, :], in_=pgm[0:64, :]), reads=(pgm,), writes=(V("gmap"),))
        P.op("dve", lambda v: v.tensor_scalar(out=V("lw")[:, :], in0=V("sgw")[:, :], scalar1=-DECAY_SCALE, scalar2=None, op0=ALU.mult), reads=(V("sgw"),), writes=(V("lw"),))
        P.op("dve", lambda v: v.tensor_scalar(out=V("kk0")[:, :], in0=Km, scalar1=vec[0:64, 16:17], scalar2=None, op0=ALU.mult), reads=(zz, vec), writes=(V("kk0"),))
        ew_tt("dve", s1b[:, :], V("kk0")[:, :], V("kk0")[:, :], ALU.mult, (V("kk0"),), (s1b,))
        yield
        pss = bank()
        mm_group(P, pss[0:64, :], pss, [(onesb[0:64, 0:64], s1b[:, :])], reads=(onesb, s1b))
        P.op("dve", lambda v: v.tensor_scalar(out=V("ssc")[:, :], in0=pss[0:64, :], scalar1=1e-24, scalar2=None, op0=ALU.max), reads=(pss,), writes=(V("ssc"),))
        P.op("dve", lambda v: v.tensor_scalar(out=V("tt")[:, :], in0=V("asg")[:, :], scalar1=-1.0, scalar2=vec[0:64, 17:18], op0=ALU.add, op1=ALU.mult), reads=(V("asg"), vec), writes=(V("tt"),))
        P.op("dve", lambda v: v.scalar_tensor_tensor(out=V("kmod")[:, :], in0=V("tt")[:, :], scalar=1.0, in1=Km, op0=ALU.add, op1=ALU.mult), reads=(V("tt"), zz), writes=(V("kmod"),))
        P.op("dve", lambda v: v.scalar_tensor_tensor(out=s1b[:, :], in0=Rm, scalar=vec[0:64, 18:19], in1=V("kmod")[:, :], op0=ALU.mult, op1=ALU.mult), reads=(zz, vec, V("kmod")), writes=(s1b,))
        pbn = bank()
        mm_group(P, pbn[0:64, :], pbn, [(onesb[0:64, 0:64], s1b[:, :])], reads=(onesb, s1b))
        ew_tt("dve", V("bonus")[:, :], pbn[0:64, :], Vm, ALU.mult, (pbn, zz), (V("bonus"),))
        P.op("dve", lambda v: v.tensor_tensor_scan(out=V("cl")[:, :], data0=rmask[:, :], data1=V("lw")[:, :], initial=0.0, op0=ALU.mult, op1=ALU.add),
             reads=(rmask, V("lw")), writes=(V("cl"),))
        ew_tt("dve", V("cm")[:, :], V("cl")[:, :], V("lw")[:, :], ALU.subtract, (V("cl"), V("lw")), (V("cm"),))
        yield
        P.op("act", lambda a: a.activation(out=V("rs")[:, :], in_=V("ssc")[:, :], func=AF.Ln), reads=(V("ssc"),), writes=(V("rs"),))
        P.op("act", lambda a: a.activation(out=V("rs")[:, :], in_=V("rs")[:, :], func=AF.Exp, scale=-0.5), reads=(V("rs"),), writes=(V("rs"),))
        P.op("act", lambda a: a.activation(out=V("gi")[:, :], in_=V("cl")[:, :], func=AF.Exp), reads=(V("cl"),), writes=(V("gi"),))
        P.op("act", lambda a: a.activation(out=V("ginv")[:, :], in_=V("cl")[:, :], func=AF.Exp, scale=-1.0), reads=(V("cl"),), writes=(V("ginv"),))
        P.op("act", lambda a: a.activation(out=V("gprev")[:, :], in_=V("cm")[:, :], func=AF.Exp), reads=(V("cm"),), writes=(V("gprev"),))
        ew_tt("dve", V("kkn")[:, :], V("kk0")[:, :], V("rs")[:, :], ALU.mult, (V("kk0"), V("rs")), (V("kkn"),))
        ew_tt("dve", V("bvec")[:, :], V("kkn")[:, :], V("asg")[:, :], ALU.mult, (V("kkn"), V("asg")), (V("bvec"),))
        a4 = lambda ap: ap.rearrange("p (a b) -> p a b", b=128)
        P.op("dve", lambda v: v.scalar_tensor_tensor(out=ART[:, :, 0, :], in0=a4(V("kkn")[:, :]), scalar=-1.0, in1=a4(V("gprev")[:, :]), op0=ALU.mult, op1=ALU.mult),
             reads=(V("kkn"), V("gprev")), writes=(ART,))
        ew_tt("dve", ART[:, :, 1, :], a4(Rm), a4(V("gi")[:, :]), ALU.mult, (zz, V("gi")), (ART,))
        ew_tt("dve", V("BT")[:, :], V("bvec")[:, :], V("ginv")[:, :], ALU.mult, (V("bvec"), V("ginv")), (V("BT"),))
        ew_tt("dve", V("KT")[:, :], V("kmod")[:, :], V("ginv")[:, :], ALU.mult, (V("kmod"), V("ginv")), (V("KT"),))
        yield
        BT, KT, gi = V("BT"), V("KT"), V("gi")
        def pair_gen(pr, S):
            tok, abT, akT, Lm, Xb, PPb, LMb, N0Gb, RpT = S["tok"], S["abT"], S["akT"], S["Lm"], S["Xb"], S["PPb"], S["LMb"], S["N0Gb"], S["RpT"]
            ysb, ysq, yn, gst = S["ysb"], S["ysq"], S["yn"], S["gst"]
            s0 = pr * 128
            ATp, RTp = ART[:, pr, 0, :], ART[:, pr, 1, :]
            ARp = ART[:, pr, :, :].rearrange("p a b -> p (a b)")
            BTp, KTp, VTp = BT[:, s0:s0 + 128], KT[:, s0:s0 + 128], zz[:, 2, s0:s0 + 128]
            i64 = identf[0:64, 0:64]
            i64b = identb[0:64, 0:64]
            pT = bank()
            mm_group(P, pT[:, 0:64], pT, [(BTp, i64b)], reads=(BT, identb))
            mm_group(P, pT[:, 64:128], pT, [(KTp, i64b)], reads=(KT, identb))
            mm_group(P, pT[:, 128:192], pT, [(VTp, i64)], reads=(zz, identf))
            P.op("act", lambda a, pT=pT: a.copy(out=tok[:, :], in_=pT[:, 0:192]), reads=(pT,), writes=(tok,))
            yield
            pA = bank()
            mm_group(P, pA[:, 0:256], pA, [(BTp, ARp)], reads=(BT, ART))
            ew_tt("dve", abT[:, :], pA[:, 0:256], mask2[:, :], ALU.mult, (pA, mask2), (abT,))
            pK = bank()
            mm_group(P, pK[:, 0:256], pK, [(KTp, ARp)], reads=(KT, ART))
            ew_tt("dve", akT[:, :], pK[:, 0:256], mask2[:, :], ALU.mult, (pK, mask2), (akT,))
            pL = bank()
            mm_group(P, pL[:, 0:128], pL, [(ATp, BTp)], reads=(ART, BT))
            ew_tt("dve", Lm[:, :], pL[:, 0:128], maskS[:, :], ALU.mult, (pL, maskS), (Lm,))
            yield
            pX = bank()
            mm_group(P, pX[:, 0:64], pX, [(ATp, i64b)], reads=(ART, identb))
            mm_group(P, pX[:, 64:128], pX, [(akT[:, 0:128], tok[:, 128:192])], reads=(akT, tok))
            X = Xb[0]
            P.op("act", lambda a, pX=pX, X=X: a.copy(out=X[:, :], in_=pX[:, 0:128]), reads=(pX,), writes=(X,))
            yield
            Pk_ap, PkT_ap, Pk_b, PkT_b = Lm[:, :], abT[:, 0:128], Lm, abT
            for it in range(6):
                pXn = bank()
                mm_group(P, pXn[:, 0:128], pXn, [(PkT_ap, X[:, :])], reads=(PkT_b, X))
                Xn = Xb[(it + 1) % 2]
                ew_tt("dve", Xn[:, :], pXn[:, 0:128], X[:, :], ALU.add, (pXn, X), (Xn,))
                yield
                if it < 5:
                    pP = bank()
                    mm_group(P, pP[:, 0:128], pP, [(PkT_ap, Pk_ap)], reads=(PkT_b, Pk_b))
                    mm_group(P, pP[:, 128:256], pP, [(Pk_ap, PkT_ap)], reads=(PkT_b, Pk_b))
                    PPn = PPb[it % 2]
                    P.op("act", lambda a, pP=pP, PPn=PPn: a.copy(out=PPn[:, :], in_=pP[:, 0:256]), reads=(pP,), writes=(PPn,))
                    Pk_ap, PkT_ap, Pk_b, PkT_b = PPn[:, 0:128], PPn[:, 128:256], PPn, PPn
                X = Xn
            Zs = []
            for hh in range(2):
                hs = slice(hh * 64, hh * 64 + 64)
                pMN = bank()
                mm_group(P, pMN[0:64, 0:64], pMN, [(X[hs, 0:64], tok[hs, 0:64])], reads=(X, tok))
                mm_group(P, pMN[0:64, 64:128], pMN, [(tok[hs, 0:64], X[hs, 64:128]), (tok[hs, 64:128], tok[hs, 128:192])], reads=(X, tok))
                ge = gi[:, s0 + hh * 64 + 63:s0 + hh * 64 + 64]
                LM, N0G = LMb[hh], N0Gb[hh]
                ew_tt("dve", LM[:, :], pMN[0:64, 0:64], i64, ALU.add, (pMN, identf), (LM,))
                P.op("dve", lambda v, pMN=pMN, N0G=N0G, ge=ge: v.tensor_scalar(out=N0G[:, :], in0=pMN[0:64, 64:128], scalar1=ge, scalar2=None, op0=ALU.mult),
                     reads=(pMN, gi), writes=(N0G,))
                Zc = Zb[zst["i"] % NZ]
                Zn = Zb[(zst["i"] + 1) % NZ]
                zst["i"] += 1
                pZ = bank()
                mm_group(P, pZ[0:64, 0:64], pZ, [(LM[:, :], Zc[:, :])], reads=(LM, Zc))
                P.op("dve", lambda v, pZ=pZ, Zn=Zn, N0G=N0G, ge=ge: v.scalar_tensor_tensor(out=Zn[:, :], in0=pZ[0:64, 0:64], scalar=ge, in1=N0G[:, :], op0=ALU.mult, op1=ALU.add),
                     reads=(pZ, gi, N0G), writes=(Zn,))
                Zs.append(Zc)
            pR = bank()
            mm_group(P, pR[0:64, 0:128], pR, [(X[:, 0:64], abT[:, 128:256])], reads=(X, abT))
            ew_tt("dve", RpT[:, :], pR[0:64, 0:128], RTp, ALU.add, (pR, ART), (RpT,))
            yield
            pY = bank()
            P.op("pe", lambda t, pY=pY, X=X: t.matmul(pY[:, 0:64], lhsT=abT[:, 128:256], rhs=X[:, 64:128], start=True, stop=False), reads=(abT, X), writes=(pY,), inc=False)
            P.op("pe", lambda t, pY=pY: t.matmul(pY[:, 0:64], lhsT=akT[:, 128:256], rhs=tok[:, 128:192], start=False, stop=False), reads=(akT, tok), writes=(pY,), inc=False)
            P.op("pe", lambda t, pY=pY: t.matmul(pY[0:64, 0:64], lhsT=RpT[:, 0:64], rhs=Zs[0][:, :], start=False, stop=False), reads=(RpT, Zs[0]), writes=(pY,), inc=False)
            P.op("pe", lambda t, pY=pY: t.matmul(pY[64:128, 0:64], lhsT=RpT[:, 64:128], rhs=Zs[1][:, :], start=False, stop=True), reads=(RpT, Zs[1]), writes=(pY,))
            P.op("act", lambda a, pY=pY: a.copy(out=ysb[:, :], in_=pY[:, 0:64]), reads=(pY,), writes=(ysb,))
            yield
            P.op("dve", lambda v: v.tensor_reduce(out=gst[:, 0:1], in_=ysb[:, :], axis=AX.X, op=ALU.add), reads=(ysb,), writes=(gst,))
            ew_tt("dve", ysq[:, :], ysb[:, :], ysb[:, :], ALU.mult, (ysb,), (ysq,))
            P.op("dve", lambda v: v.tensor_reduce(out=gst[:, 1:2], in_=ysq[:, :], axis=AX.X, op=ALU.add), reads=(ysq, gst), writes=(gst,))
            P.op("dve", lambda v: v.tensor_scalar(out=gst[:, 2:4], in0=gst[:, 0:2], scalar1=1.0 / 64, scalar2=None, op0=ALU.mult), reads=(gst,), writes=(gst,))
            ew_tt("dve", gst[:, 4:5], gst[:, 2:3], gst[:, 2:3], ALU.mult, (gst,), (gst,))
            ew_tt("dve", gst[:, 5:6], gst[:, 3:4], gst[:, 4:5], ALU.subtract, (gst,), (gst,))
            P.op("act", lambda a: a.activation(out=gst[:, 6:7], in_=gst[:, 5:6], func=AF.Ln, bias=GN_EPS, scale=1.0), reads=(gst,), writes=(gst,))
            P.op("act", lambda a: a.activation(out=gst[:, 7:8], in_=gst[:, 6:7], func=AF.Exp, scale=-0.5), reads=(gst,), writes=(gst,))
            P.op("dve", lambda v: v.tensor_scalar(out=yn[:, :], in0=ysb[:, :], scalar1=gst[:, 2:3], scalar2=gst[:, 7:8], op0=ALU.subtract, op1=ALU.mult), reads=(ysb, gst), writes=(yn,))
            yield
            pYT = bank()
            mm_group(P, pYT[0:64, 0:128], pYT, [(yn[:, :], identb[:, :])], reads=(yn, identb))
            P.op("act", lambda a, pYT=pYT, s0=s0: a.copy(out=V("ynT")[:, s0:s0 + 128], in_=pYT[0:64, 0:128]), reads=(pYT,), writes=(V("ynT"),))

        for p0 in range(0, 4, NSET):
            gens = [pair_gen(p0 + k, sets[k]) for k in range(NSET)]
            while gens:
                for g_ in list(gens):
                    try:
                        next(g_)
                    except StopIteration:
                        gens.remove(g_)
                yield
        P.op("dve", lambda v: v.tensor_scalar(out=V("yo")[:, :], in0=V("ynT")[:, :], scalar1=vec[0:64, 19:20], scalar2=vec[0:64, 20:21], op0=ALU.mult, op1=ALU.add), reads=(V("ynT"), vec), writes=(V("yo"),))
        ew_tt("dve", V("yo")[:, :], V("yo")[:, :], V("bonus")[:, :], ALU.add, (V("yo"), V("bonus")), (V("yo"),))
        ew_tt("dve", V("yo")[:, :], V("yo")[:, :], V("gmap")[:, :], ALU.mult, (V("yo"), V("gmap")), (V("yo"),))
        yb_ = yxk[c0 // CH]
        P.dma("sp", yb_, lambda q: q.dma_start(out=yb_.t.ap()[64:128, c0 % CH:c0 % CH + W], in_=V("yo")[:, :]), reads=(V("yo"),))

    if do_attn:
        NTI = P.sb([128, 128], BF16, "NTI")
        NON = P.sb([128, 128], BF16, "NON")
        m01 = P.sb([128, 128], BF16, "m01")
        Z0 = P.sb([128, 64], BF16, "Z0")
        P.op("pool", lambda g: g.memset(NON[:, :], -1.0), writes=(NON,))
        P.op("pool", lambda g: g.memset(Z0[:, :], 0.0), writes=(Z0,))
        P.op("pool", lambda g: g.affine_select(out=NTI[:, :], in_=NON[:, :], pattern=[[-1, 128]], compare_op=ALU.is_ge, fill=0.0, base=0, channel_multiplier=1),
             reads=(NON,), writes=(NTI,))
        P.op("pool", lambda g: g.affine_select(out=m01[:, :], in_=onesb[:, :], pattern=[[1, 128]], compare_op=ALU.is_ge, fill=0.0, base=-1, channel_multiplier=-1),
             reads=(onesb,), writes=(m01,))
        eb = [P.sb([128, W], BF16, f"eb{i}") for i in range(2)]
        spb = [P.sb([128, W], BF16, f"spb{i}") for i in range(3)]
        Ab = [P.sb([128, W], BF16, f"Ab{i}") for i in range(3)]
        spaccs = [P.sb([128, W], BF16, f"spacc{i}") for i in range(2)]
        ybo = P.sb([64, W], F32, "ybo")
        po = banks[7]
        rot = {"i": 0}
        cnt = {"n": 0}

        def bank3():
            b = banks[4 + rot["i"] % 3]
            rot["i"] += 1
            return b

        def ew2(o, a, b, op, rd, wr):
            P.op("dve", lambda v: v.tensor_tensor(out=o, in0=a, in1=b, op=op), reads=rd, writes=wr)

        def S1(d):
            qt, kb, cc, w = d["qt"], d["kb"], d["cc"], d["w"]
            n = cnt["n"]
            cnt["n"] += 1
            d["n"] = n
            kb_buf = kTt[kb // 4]
            d["kbuf"] = kb_buf
            d["kblk"] = kb_buf[:, (kb % 4) * 128:(kb % 4 + 1) * 128]
            d["qcols"] = qTt[qt][:, cc:W]
            e_, sp_ = eb[n % 2], spb[n % 3]
            d["sp"] = sp_
            pz = bank3()
            d["pz"] = pz
            mm_group(P, pz[:, 0:w], pz, [(d["kblk"], d["qcols"])], reads=(kb_buf, qTt[qt]))
            P.op("act", lambda a: a.activation(out=e_[:, 0:w], in_=pz[:, 0:w], func=AF.Exp), reads=(pz,), writes=(e_,))
            P.op("act", lambda a: a.activation(out=sp_[:, 0:w], in_=e_[:, 0:w], func=AF.Ln, bias=1.0, scale=1.0), reads=(e_,), writes=(sp_,))
            if d["diag"]:
                ew2(sp_[:, 0:128], sp_[:, 0:128], m01[:, :], ALU.mult, (sp_, m01), (sp_,))

        def S2(d):
            qt, kb, cc, w, n = d["qt"], d["kb"], d["cc"], d["w"], d["n"]
            sp_, A_ = d["sp"], Ab[n % 3]
            d["A"] = A_
            spacc = spaccs[qt % 2]
            if d["first"]:
                P.op("pool", lambda g: g.memset(spacc[:, :], 0.0), writes=(spacc,))
            pe_ = d["pz"]
            P.op("pe", lambda t: t.matmul(pe_[:, 0:w], lhsT=NTI[:, :], rhs=sp_[:, 0:w], start=False, stop=False), reads=(NTI, sp_), writes=(pe_,), inc=False)
            P.op("pe", lambda t: t.matmul(pe_[:, 0:w], lhsT=NON[:, :], rhs=spacc[:, cc:W], start=False, stop=True), reads=(NON, spacc), writes=(pe_,))
            P.op("act", lambda a: a.activation(out=A_[:, 0:w], in_=pe_[:, 0:w], func=AF.Exp), reads=(pe_,), writes=(A_,))
            if d["diag"]:
                ew2(A_[:, 0:128], A_[:, 0:128], m01[:, :], ALU.mult, (A_, m01), (A_,))
            if not d["last"]:
                ew2(spacc[:, cc:W], spacc[:, cc:W], sp_[:, 0:w], ALU.add, (spacc, sp_), (spacc,))

        def S3(d):
            qt, kb, cc, w = d["qt"], d["kb"], d["cc"], d["w"]
            if d["first"]:
                for c4 in range(4):
                    P.op("pe", lambda t, c4=c4: t.matmul(po[0:64, c4 * 128:(c4 + 1) * 128], lhsT=Z0[:, :], rhs=NON[:, :], start=True, stop=False),
                         reads=(Z0, NON), writes=(po,), inc=(c4 == 3))
            A_ = d["A"]
            vb = vtt[kb // 4]
            P.op("pe", lambda t: t.matmul(po[0:64, cc:W], lhsT=vb[:, kb % 4, :], rhs=A_[:, 0:w], start=False, stop=d["last"]), reads=(vb, A_), writes=(po,))
            if d["last"]:
                q0 = qt * W
                P.op("act", lambda a: a.copy(out=ybo[:, :], in_=po[0:64, :]), reads=(po,), writes=(ybo,))
                yb_ = yxk[q0 // CH]
                P.dma("sp", yb_, lambda q: q.dma_start(out=yb_.t.ap()[0:64, q0 % CH:q0 % CH + W], in_=ybo[:, :]), reads=(ybo,))

        def attn_gen(qt):
            tl = []
            for kb in range(4 * qt + 3, -1, -1):
                diag = kb >= 4 * qt
                cc = 128 * (kb - 4 * qt) if diag else 0
                tl.append(dict(qt=qt, kb=kb, diag=diag, cc=cc, w=W - cc, first=(kb == 4 * qt + 3), last=(kb == 0)))
            n = len(tl)
            for i in range(-2, n):
                if 0 <= i + 2 < n:
                    S1(tl[i + 2])
                if 0 <= i + 1 < n:
                    S2(tl[i + 1])
                if 0 <= i < n:
                    S3(tl[i])
                yield

    for ti in range(NTL):
        c0 = ti * W
        if L == 0:
            xsrc = io["xT0"]
            P.dma("sp", xt, lambda q: q.dma_start(out=xt[:, :, :], in_=xsrc.t.ap().rearrange("(kc p) t -> p kc t", p=128)[:, :, c0:c0 + W]), reads=(xsrc,))
        else:
            sg_, o_ = c0 // NT, c0 % NT
            for kc in range(KC):
                xb_ = io["xgk"][kc][o_ // CH]
                P.dma("sp", xt, lambda q, kc=kc, xb_=xb_: q.dma_start(out=xt[:, kc, :], in_=xb_.t.ap()[sg_ * 128:(sg_ + 1) * 128, o_ % CH:o_ % CH + W]), reads=(xb_,))
        rmsnorm()
        ag = None
        if do_attn:
            pq = proj(0, 64)
            P.op("act", lambda a: a.activation(out=qTt[ti][:, :], in_=pq[0:64, :], func=AF.Copy, scale=0.125), reads=(pq,), writes=(qTt[ti],))
            pk = proj(64, 64)
            P.op("act", lambda a: a.copy(out=kTt[ti][:, :], in_=pk[0:64, :]), reads=(pk,), writes=(kTt[ti],))
            for tb in range(4):
                pv = bank()
                mm_group(P, pv[:, 0:64], pv, [(h[:, kc, tb * 128:(tb + 1) * 128], wsv[:, kc, 128:192]) for kc in range(KC)], reads=(h, ws))
                P.op("dve", lambda v, pv=pv, tb=tb: v.tensor_copy(out=vtt[ti][:, tb, :], in_=pv[:, 0:64]), reads=(pv,), writes=(vtt[ti],))
            ag = attn_gen(ti) if interleave else None
            n_it = 4 * ti + 4 + 2
        if do_rwkv:
            per = max(1, -(-n_it // 26)) if ag is not None else 0
            for _ in rwkv_tile(ti):
                if ag is not None:
                    for _k in range(per):
                        if next(ag, "done") == "done":
                            ag = None
                            break
        if ag is not None:
            for _ in ag:
                pass
        if io.get("after_seq_tile") is not None:
            io["after_seq_tile"](ti)
    if do_attn and not interleave:
        for ti in range(NTL):
            for _ in attn_gen(ti):
                pass


def pack_seqmix_weights(inp, l, hd):
    f = np.float32
    w_in = inp["w_in"][l]
    hs = slice(hd * 64, hd * 64 + 64)
    cols = np.concatenate([SB_OFF + np.arange(64) + hd * 64, SB_OFF + 256 + np.arange(64) + hd * 64, SB_OFF + 512 + np.arange(64) + hd * 64,
                           RW_OFF + np.arange(64) + hd * 64, RW_OFF + 256 + np.arange(64) + hd * 64, RW_OFF + 512 + np.arange(64) + hd * 64,
                           RW_OFF + 768 + np.arange(256)])
    d = {"wS": _kc_layout(w_in[:, cols])}
    vec = np.zeros((128, 32), f)
    vec[:, 0:8] = _pvec(inp["mix_norm_g"][l])
    mu = inp["rw_mu"][l]
    vec[:, 8] = mu[896:1024]
    for m, o in enumerate((hd * 64, 256 + hd * 64, 512 + hd * 64, 768, 832)):
        vec[0:64, 9 + m] = mu[o:o + 64]
    vec[0:64, 14] = inp["rw_w0"][l][hs]
    vec[0:64, 15] = inp["rw_a0"][l][hs]
    vec[0:64, 16] = inp["rw_k_k"][l][hs]
    vec[0:64, 17] = inp["rw_k_a"][l][hs]
    vec[0:64, 18] = inp["rw_r_k"][l][hd]
    vec[0:64, 19] = inp["rw_gn_g"][l][hs]
    vec[0:64, 20] = inp["rw_gn_b"][l][hs]
    d["vecS"] = vec
    d["w2h"] = inp["rw_w2"][l][:, hs]
    d["a2h"] = inp["rw_a2"][l][:, hs]
    d["g2h"] = inp["rw_g2"][l][:, hs]
    return {k: np.ascontiguousarray(v, dtype=f) for k, v in d.items()}


def build_fused(T, nlayers=2):
    nc = bass.Bass("TRN2", target_bir_lowering=False)
    P = Prog(nc)
    NT = T // 4
    TP = HALO + T
    EI = "ExternalInput"
    CH = 1024
    io = {
        "xT0": P.dram("xT0", [D, T], F32, EI),
        "xTh": P.dram("xTh", [D, HALO + NT], F32, EI),
        "hm": P.dram("hm", [128, 1], F32, EI),
        "oh": P.dram("oh", [128, 4], F32, EI),
        "ohp": P.dram("ohp", [128, 4], F32, EI),
        "yxk": [P.dram(f"yx{k}", [128, CH], F32, semkey="yx") for k in range(T // CH)],
        "ygk": [P.dram(f"yg{k}", [512, CH], F32, semkey="yg") for k in range(T // CH)],
        "xok": [[P.dram(f"xo{kc}_{cc}", [128, CH], F32, semkey="xo") for cc in range(NT // CH)] for kc in range(KC)],
        "xgk": [[P.dram(f"xg{kc}_{cc}", [512, CH], F32, semkey="xg") for cc in range(NT // CH)] for kc in range(KC)],
    }
    out = P.dram("out", [D, NT], F32, "ExternalOutput")
    RG = [[0, 1, 2, 3], [4, 5, 6, 7]]

    def gather(src, dst):
        P.dma("pool", dst, lambda g: g.collective_compute("AllGather", ALU.bypass, replica_groups=RG, ins=[src.t.ap().opt()], outs=[dst.t.ap().opt()]),
              reads=(src,), inc=1)

    def after_seq_tile(ti):
        if ti % 2 == 1:
            gather(io["yxk"][ti // 2], io["ygk"][ti // 2])

    for L in range(nlayers):
        lastL = (L == nlayers - 1)
        P.begin_phase()
        io["after_seq_tile"] = after_seq_tile
        emit_seqmix(P, T, L, io)
        P.end_phase()
        P.begin_phase()
        io["dense_out"] = out

        def after_dense_tile(ti):
            if ti % 2 == 1:
                for kc in range(KC):
                    gather(io["xok"][kc][ti // 2], io["xgk"][kc][ti // 2])
        io["after_dense_tile"] = None if lastL else after_dense_tile
        emit_dense(P, NT, L, L == 1, io, last_out=lastL)
        if lastL:
            P.final_wait("sp", (out,))
        P.end_phase()
    return nc, list(P.ext_in)


_CACHE = {}


def kernel(**inputs):
    inp = {k: np.asarray(v, dtype=np.float32) for k, v in inputs.items()}
    x = inp["x"]
    B, T, _ = x.shape
    NT = T // 4
    cores = list(range(8))
    nl = int(inputs.get("_nlayers", 2)) if "_nlayers" in inputs else 2
    if (T, nl) not in _CACHE:
        _CACHE[(T, nl)] = build_fused(T, nl)
    nc, ext_names = _CACHE[(T, nl)]
    xT_b = [np.ascontiguousarray(x[b].T) for b in range(B)]
    dw = [pack_dense_weights(inp, l) for l in range(2)]
    maps = []
    for c in cores:
        b, sg = c // 4, c % 4
        m = {"xT0": xT_b[b]}
        halo = np.zeros((D, HALO + NT), np.float32)
        if sg > 0:
            halo[:, :] = xT_b[b][:, sg * NT - HALO:(sg + 1) * NT]
        else:
            halo[:, HALO:] = xT_b[b][:, 0:NT]
        m["xTh"] = halo
        m["hm"] = np.full((128, 1), 0.0 if sg == 0 else 1.0, np.float32)
        oh = np.zeros((128, 4), np.float32)
        oh[:, sg] = 1.0
        ohp = np.zeros((128, 4), np.float32)
        if sg > 0:
            ohp[:, sg - 1] = 1.0
        m["oh"], m["ohp"] = oh, ohp
        for l in range(nl):
            for k, v in dw[l].items():
                m[f"{k}_{l}"] = v
            for k, v in pack_seqmix_weights(inp, l, sg).items():
                m[f"{k}_{l}"] = v
        maps.append(m)
    maps = [{k: m[k] for k in ext_names} for m in maps]
    res = run_bass_kernel_spmd(nc, maps, core_ids=cores)
    outp = np.zeros((B, T, D), np.float32)
    for c in cores:
        b, sg = c // 4, c % 4
        outp[b, sg * NT:(sg + 1) * NT, :] = res.results[c]["out"].T
    return outp
```
